# Optimizing a Trainium2 kernel written in Bass

```python
import math
import jax, jax.numpy as jnp
from jax import lax
import numpy as np

D_MODEL = 1024
BATCH = 2
SEQ = 8192
DEPTH = 4
DEC_BATCH = 128
DEC_SEQ = 8
PAST_LEN = 8192
PAGE_SIZE = 128

N_MIXERS = 3
EXPAND = 2
D_INNER = EXPAND * D_MODEL
NORM_EPS = 1e-6

GLA_HEADS = 4
GLA_KEY_DIM = D_MODEL // 2
GLA_HEAD_K = GLA_KEY_DIM // GLA_HEADS
GLA_HEAD_V = D_INNER // GLA_HEADS
GLA_GATE_RANK = 16
GLA_GATE_NORMALIZER = 16.0
GLA_CHUNK = 64
GLA_IN = 2 * GLA_KEY_DIM + 2 * D_INNER + GLA_GATE_RANK

SSD_HEAD_DIM = 64
SSD_HEADS = D_INNER // SSD_HEAD_DIM
SSD_GROUPS = 4
SSD_HEADS_PER_GROUP = SSD_HEADS // SSD_GROUPS
SSD_STATE = 128
SSD_CONV = 4
SSD_CONV_DIM = D_INNER + 2 * SSD_GROUPS * SSD_STATE
SSD_CHUNK = 64
SSD_IN = D_INNER + SSD_CONV_DIM + SSD_HEADS

SWA_HEAD_DIM = 64
SWA_Q_HEADS = D_INNER // SWA_HEAD_DIM
SWA_KV_HEADS = 4
SWA_GROUP = SWA_Q_HEADS // SWA_KV_HEADS
SWA_WINDOW = 128
SWA_KV_DIM = SWA_KV_HEADS * SWA_HEAD_DIM
SWA_IN = D_INNER + 2 * SWA_KV_DIM + D_INNER

kernel_name = 'hybrid_gla_ssd_swa_decode_step'


def rms_norm(x, w):
    x32 = x.astype(jnp.float32)
    y = x32 * lax.rsqrt(jnp.mean(x32 * x32, axis=-1, keepdims=True) + NORM_EPS)
    return (y * w.astype(jnp.float32)).astype(x.dtype)


def to_chunks(a, c):
    b, l = a.shape[0], a.shape[1]
    return jnp.moveaxis(a.reshape((b, l // c, c) + a.shape[2:]), 1, 0)


def from_chunks(a):
    a = jnp.moveaxis(a, 0, 1)
    return a.reshape((a.shape[0], a.shape[1] * a.shape[2]) + a.shape[3:])


def gla_chunk_scan(q, k, v, g, s0):
    L = q.shape[1]
    c = math.gcd(L, GLA_CHUNK)
    mask = jnp.tril(jnp.ones((c, c), dtype=bool))[None, :, :, None, None]

    def step(s, inp):
        qc, kc, vc, gc = inp
        cum = jnp.cumsum(gc, axis=1)
        decay = jnp.exp(jnp.where(mask, cum[:, :, None] - cum[:, None], -jnp.inf))
        att = jnp.einsum('bthk,bshk,btshk->btsh', qc, kc, decay)
        o = jnp.einsum('btsh,bshv->bthv', att, vc) + jnp.einsum('bthk,bhkv->bthv', qc * jnp.exp(cum), s)
        last = cum[:, -1]
        s_new = s * jnp.exp(last)[..., None] + jnp.einsum('bshk,bshv->bhkv', kc * jnp.exp(last[:, None] - cum), vc)
        return s_new.astype(s.dtype), o.astype(vc.dtype)

    s_fin, o = lax.scan(step, s0, (to_chunks(q, c), to_chunks(k, c), to_chunks(v, c), to_chunks(g, c)))
    return from_chunks(o), s_fin


def gla_mixer(h, w_in, w_gk2, b_gk, head_norm, w_out, s0):
    B, L, _ = h.shape
    proj = h @ w_in
    q, k, v, r, lr = jnp.split(proj, [GLA_KEY_DIM, 2 * GLA_KEY_DIM, 2 * GLA_KEY_DIM + D_INNER, 2 * GLA_KEY_DIM + 2 * D_INNER], axis=-1)
    q = q.reshape(B, L, GLA_HEADS, GLA_HEAD_K) * (GLA_HEAD_K ** -0.5)
    k = k.reshape(B, L, GLA_HEADS, GLA_HEAD_K)
    v = v.reshape(B, L, GLA_HEADS, GLA_HEAD_V)
    g = (jax.nn.log_sigmoid(lr @ w_gk2 + b_gk) / GLA_GATE_NORMALIZER).reshape(B, L, GLA_HEADS, GLA_HEAD_K)
    o, s_fin = gla_chunk_scan(q, k, v, g, s0)
    o = rms_norm(o, head_norm).reshape(B, L, D_INNER) * jax.nn.silu(r)
    return o @ w_out, s_fin


def ssd_chunk_scan(x, dt, la, bm, cm, s0):
    L = x.shape[1]
    c = math.gcd(L, SSD_CHUNK)
    mask = jnp.tril(jnp.ones((c, c), dtype=bool))[None, :, :, None, None]

    def step(s, inp):
        xc, dtc, lac, bc, cc = inp
        cum = jnp.cumsum(lac, axis=1)
        decay = jnp.exp(jnp.where(mask, cum[:, :, None] - cum[:, None], -jnp.inf))
        cb = jnp.einsum('btgn,bsgn->btsg', cc, bc)
        u = xc * dtc[..., None]
        y = jnp.einsum('btsge,btsg,bsgep->btgep', decay, cb, u) + jnp.einsum('btgn,bgepn->btgep', cc, s) * jnp.exp(cum)[..., None]
        last = cum[:, -1]
        s_new = s * jnp.exp(last)[..., None, None] + jnp.einsum('bsge,bsgn,bsgep->bgepn', jnp.exp(last[:, None] - cum), bc, u)
        return s_new.astype(s.dtype), y.astype(xc.dtype)

    s_fin, y = lax.scan(step, s0, (to_chunks(x, c), to_chunks(dt, c), to_chunks(la, c), to_chunks(bm, c), to_chunks(cm, c)))
    return from_chunks(y), s_fin


def ssd_mixer(h, w_in, conv_w, conv_b, dt_bias, a_log, d_skip, gate_norm, w_out, ssm0, conv0):
    B, L, _ = h.shape
    proj = h @ w_in
    z, xbc, dt_raw = jnp.split(proj, [D_INNER, D_INNER + SSD_CONV_DIM], axis=-1)
    ext = jnp.concatenate([conv0, xbc], axis=1)
    conv = conv_b + sum(ext[:, j:j + L] * conv_w[j] for j in range(SSD_CONV))
    xbc_c = jax.nn.silu(conv)
    xs, bm, cm = jnp.split(xbc_c, [D_INNER, D_INNER + SSD_GROUPS * SSD_STATE], axis=-1)
    xs = xs.reshape(B, L, SSD_GROUPS, SSD_HEADS_PER_GROUP, SSD_HEAD_DIM)
    bm = bm.reshape(B, L, SSD_GROUPS, SSD_STATE)
    cm = cm.reshape(B, L, SSD_GROUPS, SSD_STATE)
    dt = jax.nn.softplus(dt_raw + dt_bias).reshape(B, L, SSD_GROUPS, SSD_HEADS_PER_GROUP)
    a = -jnp.exp(a_log).reshape(SSD_GROUPS, SSD_HEADS_PER_GROUP)
    s0 = ssm0.reshape(B, SSD_GROUPS, SSD_HEADS_PER_GROUP, SSD_HEAD_DIM, SSD_STATE)
    y, s_fin = ssd_chunk_scan(xs, dt, dt * a, bm, cm, s0)
    y = y + xs * d_skip.reshape(SSD_GROUPS, SSD_HEADS_PER_GROUP, 1)
    y = y.reshape(B, L, D_INNER) * jax.nn.silu(z)
    y = rms_norm(y.reshape(B, L, SSD_GROUPS, D_INNER // SSD_GROUPS), gate_norm.reshape(SSD_GROUPS, D_INNER // SSD_GROUPS)).reshape(B, L, D_INNER)
    return y @ w_out, s_fin.reshape(B, SSD_HEADS, SSD_HEAD_DIM, SSD_STATE), ext[:, L:]


def swa_mixer(h, w_in, sinks, w_out, k0, v0, has_past):
    B, L, _ = h.shape
    W = SWA_WINDOW
    proj = h @ w_in
    q, k, v, gate = jnp.split(proj, [D_INNER, D_INNER + SWA_KV_DIM, D_INNER + 2 * SWA_KV_DIM], axis=-1)
    q = q.reshape(B, L, SWA_KV_HEADS, SWA_GROUP, SWA_HEAD_DIM) * (SWA_HEAD_DIM ** -0.5)
    k_ext = jnp.concatenate([k0, k.reshape(B, L, SWA_KV_HEADS, SWA_HEAD_DIM)], axis=1)
    v_ext = jnp.concatenate([v0, v.reshape(B, L, SWA_KV_HEADS, SWA_HEAD_DIM)], axis=1)
    bq = math.gcd(L, W)
    nb = L // bq
    idx = jnp.arange(nb)[:, None] * bq + jnp.arange(bq + W)[None, :]
    kb = k_ext[:, idx]
    vb = v_ext[:, idx]
    qb = q.reshape(B, nb, bq, SWA_KV_HEADS, SWA_GROUP, SWA_HEAD_DIM)
    scores = jnp.einsum('bnqkgd,bnskd->bnkgqs', qb, kb).astype(jnp.float32)
    diff = W + jnp.arange(bq)[:, None] - jnp.arange(bq + W)[None, :]
    band = (diff >= 0) & (diff < W)
    key_ok = (idx >= W) | has_past
    mask = band[None] & key_ok[:, None]
    scores = jnp.where(mask[None, :, None, None], scores, -jnp.inf)
    sink = sinks.astype(jnp.float32).reshape(1, 1, SWA_KV_HEADS, SWA_GROUP, 1, 1)
    m = jnp.maximum(jnp.max(scores, axis=-1, keepdims=True), sink)
    p = jnp.exp(scores - m)
    denom = jnp.sum(p, axis=-1, keepdims=True) + jnp.exp(sink - m)
    probs = (p / denom).astype(vb.dtype)
    o = jnp.einsum('bnkgqs,bnskd->bnqkgd', probs, vb).reshape(B, L, D_INNER)
    return (o * jax.nn.silu(gate)) @ w_out, k_ext[:, L:], v_ext[:, L:]


def init_states(batch, dtype):
    states = []
    for i in range(DEPTH):
        kind = i % N_MIXERS
        if kind == 0:
            states.append((jnp.zeros((batch, GLA_HEADS, GLA_HEAD_K, GLA_HEAD_V), dtype),))
        elif kind == 1:
            states.append((jnp.zeros((batch, SSD_HEADS, SSD_HEAD_DIM, SSD_STATE), dtype),
                           jnp.zeros((batch, SSD_CONV - 1, SSD_CONV_DIM), dtype)))
        else:
            states.append((jnp.zeros((batch, SWA_WINDOW, SWA_KV_HEADS, SWA_HEAD_DIM), dtype),
                           jnp.zeros((batch, SWA_WINDOW, SWA_KV_HEADS, SWA_HEAD_DIM), dtype)))
    return states


def run_trunk(x, layer_states, layer_params, final_norm, has_past):
    new_states = []
    for i in range(DEPTH):
        kind = i % N_MIXERS
        p = layer_params[i]
        st = layer_states[i]
        h = rms_norm(x, p[0])
        if kind == 0:
            out, s_new = gla_mixer(h, *p[1:], st[0])
            ns = (s_new,)
        elif kind == 1:
            out, s_new, c_new = ssd_mixer(h, *p[1:], st[0], st[1])
            ns = (s_new, c_new)
        else:
            out, k_new, v_new = swa_mixer(h, *p[1:], st[0], st[1], has_past)
            ns = (k_new, v_new)
        x = x + out
        new_states.append(ns)
    return rms_norm(x, final_norm), new_states


def setup_inputs(seed: int = 0) -> dict:
    key = jax.random.key(seed)
    ks = jax.random.split(key, 40)

    def nrm(k, shape, scale):
        return jax.random.normal(k, shape, jnp.float32) * scale

    def gain(k, n):
        return 1.0 + 0.02 * jax.random.normal(k, (n,), jnp.float32)

    dt0 = jnp.exp(jax.random.uniform(ks[20], (SSD_HEADS,), jnp.float32, math.log(1e-3), math.log(1e-1)))
    dt_bias = dt0 + jnp.log(-jnp.expm1(-dt0))
    return {
        'x_prompt': nrm(ks[0], (BATCH, SEQ, D_MODEL), 1.0),
        'x_sample': nrm(ks[1], (DEC_BATCH, DEC_SEQ, D_MODEL), 1.0),
        'state_gla_0': nrm(ks[2], (DEC_BATCH, GLA_HEADS, GLA_HEAD_K, GLA_HEAD_V), 0.5),
        'state_ssm_1': nrm(ks[3], (DEC_BATCH, SSD_HEADS, SSD_HEAD_DIM, SSD_STATE), 0.1),
        'state_conv_1': nrm(ks[4], (DEC_BATCH, SSD_CONV - 1, SSD_CONV_DIM), 1.0),
        'cache_swa_k_2': nrm(ks[5], (DEC_BATCH, SWA_WINDOW, SWA_KV_HEADS, SWA_HEAD_DIM), 1.0),
        'cache_swa_v_2': nrm(ks[6], (DEC_BATCH, SWA_WINDOW, SWA_KV_HEADS, SWA_HEAD_DIM), 1.0),
        'state_gla_3': nrm(ks[7], (DEC_BATCH, GLA_HEADS, GLA_HEAD_K, GLA_HEAD_V), 0.5),
        'l0_norm': gain(ks[8], D_MODEL),
        'l0_w_in': nrm(ks[9], (D_MODEL, GLA_IN), D_MODEL ** -0.5),
        'l0_w_gk2': nrm(ks[10], (GLA_GATE_RANK, GLA_KEY_DIM), GLA_GATE_RANK ** -0.5),
        'l0_b_gk': nrm(ks[11], (GLA_KEY_DIM,), 0.02),
        'l0_head_norm': gain(ks[12], GLA_HEAD_V),
        'l0_w_out': nrm(ks[13], (D_INNER, D_MODEL), D_INNER ** -0.5),
        'l1_norm': gain(ks[14], D_MODEL),
        'l1_w_in': nrm(ks[15], (D_MODEL, SSD_IN), D_MODEL ** -0.5),
        'l1_conv_w': nrm(ks[16], (SSD_CONV, SSD_CONV_DIM), SSD_CONV ** -0.5),
        'l1_conv_b': nrm(ks[17], (SSD_CONV_DIM,), 0.02),
        'l1_dt_bias': dt_bias,
        'l1_a_log': jnp.log(jax.random.uniform(ks[18], (SSD_HEADS,), jnp.float32, 1.0, 16.0)),
        'l1_d_skip': gain(ks[19], SSD_HEADS),
        'l1_gate_norm': gain(ks[21], D_INNER),
        'l1_w_out': nrm(ks[22], (D_INNER, D_MODEL), D_INNER ** -0.5),
        'l2_norm': gain(ks[23], D_MODEL),
        'l2_w_in': nrm(ks[24], (D_MODEL, SWA_IN), D_MODEL ** -0.5),
        'l2_sinks': nrm(ks[25], (SWA_Q_HEADS,), 1.0),
        'l2_w_out': nrm(ks[26], (D_INNER, D_MODEL), D_INNER ** -0.5),
        'l3_norm': gain(ks[27], D_MODEL),
        'l3_w_in': nrm(ks[28], (D_MODEL, GLA_IN), D_MODEL ** -0.5),
        'l3_w_gk2': nrm(ks[29], (GLA_GATE_RANK, GLA_KEY_DIM), GLA_GATE_RANK ** -0.5),
        'l3_b_gk': nrm(ks[30], (GLA_KEY_DIM,), 0.02),
        'l3_head_norm': gain(ks[31], GLA_HEAD_V),
        'l3_w_out': nrm(ks[32], (D_INNER, D_MODEL), D_INNER ** -0.5),
        'final_norm': gain(ks[33], D_MODEL),
    }


def reference(x_prompt, x_sample, state_gla_0, state_ssm_1, state_conv_1, cache_swa_k_2, cache_swa_v_2, state_gla_3,
              l0_norm, l0_w_in, l0_w_gk2, l0_b_gk, l0_head_norm, l0_w_out,
              l1_norm, l1_w_in, l1_conv_w, l1_conv_b, l1_dt_bias, l1_a_log, l1_d_skip, l1_gate_norm, l1_w_out,
              l2_norm, l2_w_in, l2_sinks, l2_w_out,
              l3_norm, l3_w_in, l3_w_gk2, l3_b_gk, l3_head_norm, l3_w_out,
              final_norm):
    layer_params = [
        (l0_norm, l0_w_in, l0_w_gk2, l0_b_gk, l0_head_norm, l0_w_out),
        (l1_norm, l1_w_in, l1_conv_w, l1_conv_b, l1_dt_bias, l1_a_log, l1_d_skip, l1_gate_norm, l1_w_out),
        (l2_norm, l2_w_in, l2_sinks, l2_w_out),
        (l3_norm, l3_w_in, l3_w_gk2, l3_b_gk, l3_head_norm, l3_w_out),
    ]
    y_prompt, ns_p = run_trunk(x_prompt, init_states(x_prompt.shape[0], x_prompt.dtype), layer_params, final_norm, False)
    sample_states = [(state_gla_0,), (state_ssm_1, state_conv_1), (cache_swa_k_2, cache_swa_v_2), (state_gla_3,)]
    y_sample, ns_s = run_trunk(x_sample, sample_states, layer_params, final_norm, True)
    return (y_prompt, y_sample,
            ns_p[0][0], ns_s[0][0],
            ns_p[1][0], ns_s[1][0],
            ns_p[1][1], ns_s[1][1],
            ns_p[2][0], ns_s[2][0],
            ns_p[2][1], ns_s[2][1],
            ns_p[3][0], ns_s[3][0])
```

```python
import numpy as np
from contextlib import ExitStack
import concourse.bass as bass
import concourse.mybir as mybir
from concourse.ap import AP
from concourse.bass_utils import run_bass_kernel_spmd

F32 = mybir.dt.float32
BF16 = mybir.dt.bfloat16
AF = mybir.ActivationFunctionType
ALU = mybir.AluOpType

D = 1024
DI = 2048
EPS = 1e-6
GLA_IN = 5136
SSD_IN = 5152
SWA_IN = 4608
NCORES = 8

ENGS = ("pe", "act", "dve", "pool", "sp")


def _conflict(a, b):
    n = min(len(a), len(b))
    return a[:n] == b[:n]


class Op:
    __slots__ = ("eng", "fn", "reads", "writes", "dma", "semkey", "inc", "deps",
                 "sem", "semval", "need_inc", "idx")


class Prog:
    def __init__(self, nc):
        self.nc = nc
        self.ops = []
        self.state = {}
        self.final_waits = []

    @staticmethod
    def _norm(keys):
        out = []
        for k in keys:
            if k is None:
                continue
            if not isinstance(k, tuple):
                k = (k,)
            out.append(k)
        return out

    def op(self, eng, fn, r=(), w=(), dma=False, semkey=None, inc=None, final=False):
        o = Op()
        o.eng = eng
        o.fn = fn
        o.reads = self._norm(r)
        o.writes = self._norm(w)
        o.dma = dma
        o.semkey = semkey
        o.inc = inc if inc is not None else (16 if dma else 1)
        o.idx = len(self.ops)
        o.need_inc = False
        deps = set()
        for k in o.reads:
            tab = self.state.setdefault(k[0], {})
            for kk, st in tab.items():
                if _conflict(k, kk):
                    if st[0] is not None:
                        deps.add(st[0])
                    if k[0] in ("ps", "pst"):
                        deps.update(r for r in st[1] if self.ops[r].eng != eng)
        for k in o.writes:
            tab = self.state.setdefault(k[0], {})
            for kk, st in tab.items():
                if _conflict(k, kk):
                    if st[0] is not None:
                        deps.add(st[0])
                    deps.update(st[1])
        for k in o.reads:
            tab = self.state[k[0]]
            if k not in tab:
                tab[k] = [None, []]
            tab[k][1].append(o.idx)
        for k in o.writes:
            tab = self.state[k[0]]
            for kk in [kk for kk in tab if len(kk) > len(k) and kk[:len(k)] == k]:
                del tab[kk]
            tab[k] = [o.idx, []]
        deps.discard(o.idx)
        keep = set()
        for d in deps:
            dop = self.ops[d]
            if (not dop.dma) and (not o.dma) and dop.eng == eng:
                raw = False
                for k in o.reads:
                    for kk in dop.writes:
                        if _conflict(k, kk):
                            raw = True
                if not raw:
                    continue
            keep.add(d)
        o.deps = sorted(keep)
        self.ops.append(o)
        if final:
            self.final_waits.append(o.idx)
        return o

    def pe(self, fn, r=(), w=(), **kw):
        return self.op("pe", fn, r, w, **kw)

    def act(self, fn, r=(), w=(), **kw):
        return self.op("act", fn, r, w, **kw)

    def dve(self, fn, r=(), w=(), **kw):
        return self.op("dve", fn, r, w, **kw)

    def pool(self, fn, r=(), w=(), **kw):
        return self.op("pool", fn, r, w, **kw)

    def dma(self, q, out, in_, r=(), w=(), semkey=None, final=False, **dkw):
        assert semkey is not None
        sk = ("dma",) + (tuple(semkey) if isinstance(semkey, tuple) else (semkey,))
        return self.op(q, lambda e: e.dma_start(out=out, in_=in_, **dkw), r, w,
                       dma=True, semkey=sk, final=final)

    def emit(self, stack):
        nc = self.nc
        ops = self.ops
        for o in ops:
            for d in o.deps:
                ops[d].need_inc = True
        for i in self.final_waits:
            ops[i].need_inc = True
        engsem = {}
        for e in ("pe", "act", "dve", "pool"):
            engsem[e] = stack.enter_context(nc.semaphore("sem_" + e))
        dmasem = {}
        cnt = {e: 0 for e in engsem}
        dcnt = {}
        for o in ops:
            if o.dma:
                if o.semkey not in dmasem:
                    dmasem[o.semkey] = stack.enter_context(
                        nc.semaphore("sd_" + "_".join(str(x) for x in o.semkey[1:])))
                    dcnt[o.semkey] = 0
                dcnt[o.semkey] += o.inc
                o.sem = dmasem[o.semkey]
                o.semval = dcnt[o.semkey]
                o.need_inc = True
            elif o.need_inc:
                cnt[o.eng] += 1
                o.sem = engsem[o.eng]
                o.semval = cnt[o.eng]
        self.nsems = len(engsem) + len(dmasem)
        self.counts = dict(cnt)
        streams = {e: [o for o in ops if o.eng == e] for e in ENGS}
        block = stack.enter_context(nc.Block())
        final_waits = self.final_waits

        def run_stream(e, eng):
            waited = {}
            issued = []
            for o in streams[e]:
                need = {}
                if e == "pool" and o.dma:
                    if len(issued) >= 2:
                        po = issued[-2]
                        need[po.sem.num] = (po.sem, po.semval)
                    issued.append(o)
                for d in o.deps:
                    dop = ops[d]
                    key = dop.sem.num
                    if key not in need or need[key][1] < dop.semval:
                        need[key] = (dop.sem, dop.semval)
                for key, (sem, val) in need.items():
                    if waited.get(key, 0) >= val:
                        continue
                    eng.wait_ge(sem, val)
                    waited[key] = val
                ins = o.fn(eng)
                if o.need_inc:
                    ins.then_inc(o.sem, o.inc)
            if e == "sp":
                need = {}
                for i in final_waits:
                    dop = ops[i]
                    key = dop.sem.num
                    if key not in need or need[key][1] < dop.semval:
                        need[key] = (dop.sem, dop.semval)
                for key, (sem, val) in need.items():
                    if waited.get(key, 0) >= val:
                        continue
                    eng.wait_ge(sem, val)

        @block.sync
        def _(eng):
            run_stream("sp", eng)

        @block.gpsimd
        def _(eng):
            run_stream("pool", eng)

        @block.scalar
        def _(eng):
            run_stream("act", eng)

        @block.vector
        def _(eng):
            run_stream("dve", eng)

        @block.tensor
        def _(eng):
            run_stream("pe", eng)


def bc_mid(ap2d, n):
    a = ap2d.ap
    return AP(ap2d.tensor, ap2d.offset, [list(a[0]), [0, n], list(a[1])])


def bc_last(ap2d, n):
    a = ap2d.ap
    return AP(ap2d.tensor, ap2d.offset, [list(a[0]), list(a[1]), [0, n]])


def make_masks(T, L):
    idx = np.arange(T)
    seq = idx // L
    same = seq[:, None] == seq[None, :]
    s = idx[:, None]
    t = idx[None, :]
    m = {}
    m["LE"] = (same & (s <= t)).astype(np.float32)
    m["GT"] = (same & (s > t)).astype(np.float32)
    nseq = T // L
    seg = (seq[:, None] == np.arange(nseq)[None, :]).astype(np.float32)
    m["seg"] = seg
    sh = np.zeros((T, 3, T), np.float32)
    for j in range(3):
        sh[:, j, :] = (same & (s == t + j - 3)).astype(np.float32)
    m["Sh"] = sh
    if L == 128:
        shp = np.zeros((128, 3, 128), np.float32)
        for j in range(3):
            for tt in range(3):
                if tt + j < 3:
                    shp[125 + tt + j, j, tt] = 1.0
        m["ShP"] = shp
    else:
        shp = np.zeros((nseq * 3, 3, T), np.float32)
        for j in range(3):
            for tt in range(T):
                q = tt % L
                if q + j < 3:
                    shp[(tt // L) * 3 + q + j, j, tt] = 1.0
        m["ShP"] = shp
    return m


class Cfg:
    def __init__(self, B, SEQ, DB, layers=(0, 1, 2, 3), G=1):
        assert B * G <= NCORES
        self.B = B
        self.G = G
        assert SEQ % (self.G * 128) == 0
        self.NT = SEQ // self.G // 128
        assert DB % NCORES == 0
        self.NS = DB // NCORES
        self.TS = self.NS * 8
        assert self.TS <= 128
        self.layers = tuple(layers)
        self.SEQ = SEQ
        self.DB = DB


LAYER_KIND = {0: "gla", 1: "ssd", 2: "swa", 3: "gla"}
LAYER_NIN = {0: GLA_IN, 1: SSD_IN, 2: SWA_IN, 3: GLA_IN}


class Builder:
    def __init__(self, cfg):
        self.cfg = cfg
        self.nc = bass.Bass("TRN2", target_bir_lowering=False)
        self.P = Prog(self.nc)
        self.stack = ExitStack()
        self.d = {}
        self.bufs = {}
        self.psrr = 0
        self.dbg_stop = 99
        self.dbg_on = False
        self.dbg_names = []
        self.held = set()

    def din(self, name, shape, dt=F32):
        self.d[name] = self.nc.dram_tensor(name, list(shape), dt, kind="ExternalInput").ap()
        return self.d[name]

    def dout(self, name, shape, dt=F32):
        self.d[name] = self.nc.dram_tensor(name, list(shape), dt, kind="ExternalOutput").ap()
        return self.d[name]

    def dscr(self, name, shape, dt=F32):
        self.d[name] = self.nc.dram_tensor(name, list(shape), dt).ap()
        return self.d[name]

    def sb(self, name, shape, dt=F32):
        if name in self.bufs:
            return self.bufs[name]
        t = self.stack.enter_context(self.nc.sbuf_tensor(name, list(shape), dt))
        self.bufs[name] = t
        return t

    def dbg(self, name, ap, rkeys, shape):
        if not getattr(self, "dbg_on", False):
            return
        o = self.dout("dbg_" + name, shape)
        self.P.dma("sp", o, ap, r=rkeys, semkey=("dbg", name), final=True)
        self.dbg_names.append("dbg_" + name)

    def psum(self, hold=False):
        for _ in range(len(self.psb)):
            i = self.psrr
            self.psrr = (self.psrr + 1) % len(self.psb)
            if i not in self.held:
                if hold:
                    self.held.add(i)
                return self.psb[i], ("ps", i)
        raise RuntimeError("no free PSUM bank")

    def psum_release(self, key):
        self.held.discard(key[1])

    def psum_t(self):
        i = self.pstrr
        self.pstrr = (self.pstrr + 1) % len(self.pst)
        return self.pst[i], ("pst", i)

    def declare(self):
        cfg = self.cfg
        NT, NS, TS, G = cfg.NT, cfg.NS, cfg.TS, cfg.G
        NP = NT * 128
        self.din("xp", [NP, D])
        self.din("xsamp", [TS, D])
        self.din("sg0", [NS, 4, 128, 512])
        self.din("ssm1", [NS, 32, 64, 128])
        self.din("conv1", [NS, 3, 3072])
        self.din("kc2", [NS, 128, 256])
        self.din("vc2", [NS, 128, 256])
        self.din("sg3", [NS, 4, 128, 512])
        for l in (0, 3):
            self.din(f"l{l}_norm", [D])
            self.din(f"l{l}_w_in", [D, GLA_IN])
            self.din(f"l{l}_w_gk2", [16, 512])
            self.din(f"l{l}_b_gk", [512])
            self.din(f"l{l}_head_norm", [512])
            self.din(f"l{l}_w_out", [DI, D])
        self.din("l1_norm", [D])
        self.din("l1_w_in", [D, SSD_IN])
        self.din("l1_conv_w", [4, 3072])
        self.din("l1_conv_b", [3072])
        self.din("l1_dt_bias", [32])
        self.din("l1_a_log", [32])
        self.din("l1_d_skip", [32])
        self.din("l1_gate_norm", [DI])
        self.din("l1_w_out", [DI, D])
        self.din("l2_norm", [D])
        self.din("l2_w_in", [D, SWA_IN])
        self.din("l2_sinks", [32])
        self.din("l2_w_out", [DI, D])
        self.din("final_norm", [D])
        self.din("c_ident", [128, 128])
        for kind, T in (("p", 128), ("s", TS)):
            self.din(f"c_{kind}_LE", [T, T])
            self.din(f"c_{kind}_GT", [T, T])
        self.din("c_s_seg", [TS, NS])
        self.din("c_p_Sh", [128, 3, 128])
        self.din("c_s_maskA", [128, TS])
        self.din("c_p_maskA0", [128, 128])
        self.din("c_s_Sh", [TS, 3, TS])
        self.din("c_p_ShP", [128, 3, 128])
        self.din("c_s_ShP", [NS * 3, 3, TS])
        self.din("c_pm", [1, 2 * G])
        self.dout("yp", [NP, D])
        self.dout("ysamp", [TS, D])
        self.dout("gla0_p", [4, 128, 512])
        self.dout("gla0_s", [NS, 4, 128, 512])
        self.dout("ssm1_p", [32, 64, 128])
        self.dout("ssm1_s", [NS, 32, 64, 128])
        self.dout("conv1_p", [3, 3072])
        self.dout("conv1_s", [NS, 3, 3072])
        self.dout("k2_p", [128, 256])
        self.dout("k2_s", [NS, 128, 256])
        self.dout("v2_p", [128, 256])
        self.dout("v2_s", [NS, 128, 256])
        self.dout("gla3_p", [4, 128, 512])
        self.dout("gla3_s", [NS, 4, 128, 512])
        self.dscr("xres", [NP + 128, D])
        self.dscr("cwb", [128, 5 * 3072], BF16)
        self.dscr("cvs", [TS, 3072])
        self.dscr("kvs", [TS, 512])

    def setup_consts(self):
        P, d, cfg = self.P, self.d, self.cfg
        NS, TS = cfg.NS, cfg.TS
        nc = self.nc
        self.psb = [self.stack.enter_context(nc.psum_tensor(f"ps{i}", [128, 512], F32)) for i in range(6)]
        self.pst = [self.stack.enter_context(nc.psum_tensor(f"pst{i}", [128, 1024], BF16)) for i in range(2)]
        self.pstrr = 0
        self.identf = self.sb("identf", [128, 128])
        self.identb = self.sb("identb", [128, 128], BF16)
        P.dma("sp", self.identf[:], d["c_ident"][:, :], w=["identf"], semkey="c0")
        P.dma("pool", self.identb[:], d["c_ident"][:, :], w=["identb"], semkey="c1")
        self.ones_row = self.sb("ones_row", [1, 128])
        P.dve(lambda e: e.memset(self.ones_row[:], 1.0), w=["ones_row"])
        self.M = {}
        for kind, T in (("p", 128), ("s", TS)):
            m = {}
            for nm in ("LE", "GT"):
                t = self.sb(f"m_{kind}_{nm}", [T, T])
                P.dma("sp", t[:], d[f"c_{kind}_{nm}"][:, :], w=[f"m_{kind}_{nm}"], semkey=f"c_{kind}_{nm}")
                m[nm] = t
            self.M[kind] = m
        seg = self.sb("m_s_seg", [TS, NS])
        P.dma("sp", seg[:], d["c_s_seg"][:, :], w=["m_s_seg"], semkey="c_seg")
        self.M["s"]["seg"] = seg
        segp = self.sb("m_p_seg", [128, 1])
        P.dve(lambda e: e.memset(segp[:], 1.0), w=["m_p_seg"])
        self.M["p"]["seg"] = segp

    def load_layer_weights(self, l):
        P, d = self.P, self.d
        nin = LAYER_NIN[l]
        win = self.sb("w_in", [128, 8, SSD_IN], BF16)
        wsrc = d[f"l{l}_w_in"].rearrange("(kc p) n -> p kc n", p=128)
        nblk = (nin + 511) // 512
        for b in range(nblk):
            c0, c1 = b * 512, min(nin, (b + 1) * 512)
            P.dma("pool", win[:, :, c0:c1], wsrc[:, :, c0:c1], w=[("w_in", b)], semkey=("win", b))
        wout = self.sb("w_out", [128, 16, D], BF16)
        wosrc = d[f"l{l}_w_out"].rearrange("(rc p) n -> p rc n", p=128)
        for b in range(4):
            P.dma("pool", wout[:, b * 4:(b + 1) * 4, :], wosrc[:, b * 4:(b + 1) * 4, :], w=[("w_out", b)], semkey=("wout", b))
        ncol = self.sb("normcol", [128, 8])
        P.dma("sp", ncol[:], d[f"l{l}_norm"].rearrange("(kc p) -> p kc", p=128), w=["normcol"], semkey="nrm",
              allow_slow_non_contiguous=True)
        self.win, self.wout, self.ncol = win, wout, ncol

    def tile_src(self, l_first, ti):
        cfg, d = self.cfg, self.d
        NT, TS = cfg.NT, cfg.TS
        if ti < NT:
            if l_first:
                return d["xp"][ti * 128:(ti + 1) * 128, :], ("xp", ti)
            return d["xres"][ti * 128:(ti + 1) * 128, :], ("xres", ti)
        if l_first:
            return d["xsamp"][:, :], ("xsamp",)
        return d["xres"][NT * 128:NT * 128 + TS, :], ("xres", NT)

    def load_x(self, l_first, ti, slot):
        T = 128 if ti < self.cfg.NT else self.cfg.TS
        xt = self.sb(f"xt{slot}", [128, D])
        src, key = self.tile_src(l_first, ti)
        self.P.dma("sp", xt[:T, :], src, r=[key], w=[f"xt{slot}"], semkey=("xt", slot))
        return xt

    def norm_transpose(self, xt, xkey, T):
        P = self.P
        junk = self.sb("junk", [128, DI], BF16)
        st = self.sb("nstat", [128, 4])
        hn = self.sb("hn", [128, D], BF16)
        hT = self.sb("hT", [128, 8, 128], BF16)
        P.act(lambda e: e.activation(out=junk[:T, 0:D], in_=xt[:T, :], func=AF.Square, accum_out=st[:T, 0:1]),
              r=[xkey], w=["junk", ("nstat", 0)])
        P.act(lambda e: e.activation(out=st[:T, 1:2], in_=st[:T, 0:1], func=AF.Sqrt, scale=1.0 / D, bias=self.epsc[:T, 0:1]),
              r=[("nstat", 0), "epsc"], w=[("nstat", 1)])
        P.dve(lambda e: e.reciprocal(out=st[:T, 2:3], in_=st[:T, 1:2]), r=[("nstat", 1)], w=[("nstat", 2)])
        P.act(lambda e: e.activation(out=hn[:T, :], in_=xt[:T, :], func=AF.Copy, scale=st[:T, 2:3]),
              r=[xkey, ("nstat", 2)], w=["hn"])
        pt, pk = self.psum_t()
        for kc in range(8):
            P.pe(lambda e, kc=kc: e.transpose(out=pt[:, kc * 128:kc * 128 + T], in_=hn[:T, kc * 128:(kc + 1) * 128],
                                              identity=self.identb[:T, :T]),
                 r=["hn", "identb"], w=[pk])
        ptv = pt[:].rearrange("p (k t) -> p k t", k=8)[:, :, :T]
        P.dve(lambda e: e.tensor_tensor(out=hT[:, :, :T], in0=ptv, in1=bc_last(self.ncol[:, :], T), op=ALU.mult),
              r=[pk, "normcol"], w=["hT"])
        return hT

    def masked_cols(self, dst, dkey, src, skey, si, T):
        P = self.P
        if si == 0:
            P.dve(lambda e: e.memset(dst[:, :, :T], 0.0), w=[dkey])
        else:
            P.dve(lambda e: e.memset(dst[:, :, 8 * (si - 1):8 * si], 0.0), w=[dkey])
        P.dve(lambda e: e.tensor_copy(out=dst[:, :, 8 * si:8 * si + 8], in_=src[:, :, 8 * si:8 * si + 8]), r=[skey], w=[dkey])

    def proj_block(self, hT, T, c0, c1):
        P = self.P
        ps, pk = self.psum()
        b = c0 // 512
        assert (c1 - 1) // 512 == b
        for kc in range(8):
            P.pe(lambda e, kc=kc: e.matmul(ps[:T, 0:c1 - c0], lhsT=hT[:, kc, :T], rhs=self.win[:, kc, c0:c1],
                                           start=(kc == 0), stop=(kc == 7)),
                 r=["hT", ("w_in", b)], w=[pk])
        return ps, pk

    def out_proj_residual(self, og, ogkey, xt, xkey, T):
        P = self.P
        ogT = self.sb("ogT", [128, 16, 128], BF16)
        for half in range(2):
            pt, pk = self.psum_t()
            for j in range(8):
                vc = half * 8 + j
                P.pe(lambda e, j=j, vc=vc, pt=pt: e.transpose(out=pt[:, j * 128:j * 128 + T], in_=og[:T, vc * 128:(vc + 1) * 128],
                                                              identity=self.identb[:T, :T]),
                     r=[ogkey, "identb"], w=[pk])
            ptv = pt[:].rearrange("p (k t) -> p k t", k=8)[:, :, :T]
            P.dve(lambda e, ptv=ptv, half=half: e.tensor_copy(out=ogT[:, half * 8:half * 8 + 8, :T], in_=ptv), r=[pk], w=[("ogT", half)])
        if self.dbg_stop <= 47:
            return
        for nb in range(2):
            if self.dbg_stop <= 48 and nb == 1:
                return
            ps, pk = self.psum()
            for vc in range(16):
                P.pe(lambda e, vc=vc, nb=nb, ps=ps: e.matmul(ps[:T, :], lhsT=ogT[:, vc, :T], rhs=self.wout[:, vc, nb * 512:(nb + 1) * 512],
                                                             start=(vc == 0), stop=(vc == 15)),
                     r=[("ogT", vc // 8), ("w_out", vc // 4)], w=[pk])
            P.dve(lambda e, nb=nb, ps=ps: e.tensor_tensor(out=xt[:T, nb * 512:(nb + 1) * 512], in0=xt[:T, nb * 512:(nb + 1) * 512],
                                                          in1=ps[:T, :], op=ALU.add),
                  r=[pk, xkey], w=[xkey])

    def store_x(self, xt, xkey, ti, T, last_layer):
        P, d, cfg = self.P, self.d, self.cfg
        NT = cfg.NT
        if not last_layer:
            dst = d["xres"][ti * 128:ti * 128 + T, :]
            P.dma("pool", dst, xt[:T, :], r=[xkey], w=[("xres", ti)], semkey=("xst", xkey))
            return
        junk = self.sb("junk", [128, DI], BF16)
        st = self.sb("nstat", [128, 4])
        P.act(lambda e: e.activation(out=junk[:T, 0:D], in_=xt[:T, :], func=AF.Square, accum_out=st[:T, 0:1]),
              r=[xkey], w=["junk", ("nstat", 0)])
        P.act(lambda e: e.activation(out=st[:T, 1:2], in_=st[:T, 0:1], func=AF.Sqrt, scale=1.0 / D, bias=self.epsc[:T, 0:1]),
              r=[("nstat", 0), "epsc"], w=[("nstat", 1)])
        P.dve(lambda e: e.reciprocal(out=st[:T, 2:3], in_=st[:T, 1:2]), r=[("nstat", 1)], w=[("nstat", 2)])
        for hf in range(2):
            fb = self.sb(f"on{hf}", [128, 512])
            P.dma("sp", fb[:], d["final_norm"][hf * 512:(hf + 1) * 512].partition_broadcast(128), w=[f"on{hf}"], semkey=("fnb", hf))
            P.dve(lambda e, hf=hf, fb=fb: e.scalar_tensor_tensor(out=xt[:T, hf * 512:(hf + 1) * 512], in0=xt[:T, hf * 512:(hf + 1) * 512],
                                                                 scalar=st[:T, 2:3], in1=fb[:T, :], op0=ALU.mult, op1=ALU.mult),
                  r=[xkey, ("nstat", 2), f"on{hf}"], w=[xkey])
        dst = d["yp"][ti * 128:(ti + 1) * 128, :] if ti < NT else d["ysamp"][:, :]
        P.dma("pool", dst, xt[:T, :], r=[xkey], semkey=("yst", xkey), final=True)

    def gla_setup(self, l):
        P, d = self.P, self.d
        wgk = self.sb("wgk2", [17, 512])
        P.dma("sp", wgk[0:16, :], d[f"l{l}_w_gk2"][:, :], w=["wgk2"], semkey="gs0")
        P.dma("sp", wgk[16:17, :], d[f"l{l}_b_gk"].rearrange("(o n) -> o n", o=1), w=["wgk2"], semkey="gs1")
        lrT = self.sb("lrT", [17, 128])
        P.dve(lambda e: e.memset(lrT[:, :], 1.0), w=["lrT"])
        bgk = None
        hnb = self.sb("hnb", [128, 512])
        P.dma("sp", hnb[:], d[f"l{l}_head_norm"].partition_broadcast(128), w=["hnb"], semkey="gs2")
        self.wgk, self.bgk, self.hnb = wgk, bgk, hnb

    def gla_tile(self, l, ti, xt, xkey, T, kind, state_only, seqs):
        P, d, cfg = self.P, self.d, self.cfg
        M = self.M[kind]
        nseq = len(seqs)
        hT = self.norm_transpose(xt, xkey, T)
        ps, pk = self.proj_block(hT, T, 5120, 5136)
        lrf = self.sb("lrf", [128, 16])
        P.dve(lambda e: e.tensor_copy(out=lrf[:T, :], in_=ps[:T, 0:16]), r=[pk], w=["lrf"])
        ps2, pk2 = self.psum()
        P.pe(lambda e: e.transpose(out=ps2[:16, :T], in_=lrf[:T, :], identity=self.identf[:T, :T]), r=["lrf", "identf"], w=[pk2])
        lrT = self.sb("lrT", [17, 128])
        P.dve(lambda e: e.tensor_copy(out=lrT[0:16, :T], in_=ps2[:16, :T]), r=[pk2], w=["lrT"])
        psz, pkz = self.psum()
        P.pe(lambda e: e.matmul(psz[:T, :], lhsT=lrT[:, :T], rhs=self.wgk[:, :], start=True, stop=True), r=["lrT", "wgk2"], w=[pkz])
        if self.dbg_stop <= 1:
            return
        g4 = self.sb("g4", [128, 4, 512])
        spf = g4[:, 0, :]
        P.act(lambda e: e.activation(out=spf[:T, :], in_=psz[:T, :], func=AF.Exp, scale=-1.0), r=[pkz], w=["spf"])
        P.act(lambda e: e.activation(out=spf[:T, :], in_=spf[:T, :], func=AF.Ln, bias=self.onec[:T, 0:1]), r=["spf", "onec"], w=["spf"])
        if self.dbg_stop <= 2:
            return
        erev = g4[:, 1, :]
        psr, pkr = self.psum()
        P.pe(lambda e: e.matmul(psr[:T, :], lhsT=M["GT"][:T, :T], rhs=spf[:T, :], start=True, stop=True), r=["spf", f"m_{kind}_GT"], w=[pkr])
        P.act(lambda e: e.activation(out=erev[:T, :], in_=psr[:T, :], func=AF.Exp, scale=-1.0 / 16.0), r=[pkr], w=["erev"])
        if not state_only:
            ecum = g4[:, 2, :]
            encum = g4[:, 3, :]
            psc, pkc = self.psum()
            P.pe(lambda e: e.matmul(psc[:T, :], lhsT=M["LE"][:T, :T], rhs=spf[:T, :], start=True, stop=True), r=["spf", f"m_{kind}_LE"], w=[pkc])
            P.act(lambda e: e.activation(out=ecum[:T, :], in_=psc[:T, :], func=AF.Exp, scale=-1.0 / 16.0), r=[pkc], w=["ecum"])
            P.act(lambda e: e.activation(out=encum[:T, :], in_=psc[:T, :], func=AF.Exp, scale=1.0 / 16.0), r=[pkc], w=["encum"])
        if self.dbg_stop <= 3:
            return
        elast = self.sb("elast", [128, 4, 16])
        psl, pkl = self.psum()
        for h in range(4):
            P.pe(lambda e, h=h: e.matmul(psl[:, h * 16:h * 16 + nseq], lhsT=spf[:T, h * 128:(h + 1) * 128], rhs=M["seg"][:T, :nseq],
                                         start=True, stop=True), r=["spf", "m_s_seg", "m_p_seg"], w=[pkl])
        P.act(lambda e: e.activation(out=elast[:, :, :nseq], in_=psl[:, 0:64].rearrange("p (h j) -> p h j", h=4)[:, :, :nseq], func=AF.Exp, scale=-1.0 / 16.0),
              r=[pkl], w=["elast"])
        if self.dbg_stop <= 4:
            return
        if kind == "s":
            self.dbg("spf", spf[:T, :], ["spf"], [T, 512])
            self.dbg("erev", erev[:T, :], ["erev"], [T, 512])
            self.dbg("elast", elast[:, :, :nseq], ["elast"], [128, 4, nseq])
        if not state_only:
            psq, pkq = self.proj_block(hT, T, 0, 512)
            qg = self.sb("qg", [128, 512], BF16)
            P.dve(lambda e: e.scalar_tensor_tensor(out=qg[:T, :], in0=psq[:T, :], scalar=float(128 ** -0.5), in1=ecum[:T, :],
                                                   op0=ALU.mult, op1=ALU.mult), r=[pkq, "ecum"], w=["qg"])
        psk, pkk = self.proj_block(hT, T, 512, 1024)
        kh = self.sb("kh", [128, 512], BF16)
        P.dve(lambda e: e.tensor_tensor(out=kh[:T, :], in0=psk[:T, :], in1=erev[:T, :], op=ALU.mult), r=[pkk, "erev"], w=["kh"])
        if not state_only:
            kg = self.sb("kg", [128, 512], BF16)
            P.dve(lambda e: e.tensor_tensor(out=kg[:T, :], in0=psk[:T, :], in1=encum[:T, :], op=ALU.mult), r=[pkk, "encum"], w=["kg"])
        if self.dbg_stop <= 5:
            return
        vb = self.sb("vb", [128, DI], BF16)
        for b in range(4):
            psv, pkv = self.proj_block(hT, T, 1024 + b * 512, 1536 + b * 512)
            P.act(lambda e, b=b, psv=psv: e.activation(out=vb[:T, b * 512:(b + 1) * 512], in_=psv[:T, :], func=AF.Copy),
                  r=[pkv], w=[("vb", b)])
        if not state_only:
            sr = self.sb("sr", [128, DI], BF16)
            for b in range(4):
                psr2, pkr2 = self.proj_block(hT, T, 3072 + b * 512, 3584 + b * 512)
                P.act(lambda e, b=b, psr2=psr2: e.activation(out=sr[:T, b * 512:(b + 1) * 512], in_=psr2[:T, :], func=AF.Silu),
                      r=[pkr2], w=[("sr", b)])
            if self.dbg_stop <= 6:
                return
            qT = self.sb("qT", [128, 4, 128], BF16)
            kT = self.sb("kT", [128, 4, 128], BF16)
            pt, ptk = self.psum_t()
            for h in range(4):
                P.pe(lambda e, h=h: e.transpose(out=pt[:, h * 128:h * 128 + T], in_=qg[:T, h * 128:(h + 1) * 128], identity=self.identb[:T, :T]),
                     r=["qg", "identb"], w=[ptk])
                P.pe(lambda e, h=h: e.transpose(out=pt[:, 512 + h * 128:512 + h * 128 + T], in_=kg[:T, h * 128:(h + 1) * 128],
                                                identity=self.identb[:T, :T]), r=["kg", "identb"], w=[ptk])
            ptv = pt[:].rearrange("p (k t) -> p k t", k=8)
            P.dve(lambda e: e.tensor_copy(out=qT[:, :, :T], in_=ptv[:, 0:4, :T]), r=[ptk], w=["qT"])
            P.dve(lambda e: e.tensor_copy(out=kT[:, :, :T], in_=ptv[:, 4:8, :T]), r=[ptk], w=["kT"])
            if self.dbg_stop <= 7:
                return
            psa, pka = self.psum()
            for h in range(4):
                P.pe(lambda e, h=h: e.matmul(psa[:T, h * 128:h * 128 + T], lhsT=kT[:, h, :T], rhs=qT[:, h, :T], start=True, stop=True),
                     r=["qT", "kT"], w=[pka])
            attT = self.sb("attT", [128, 4, 128], BF16)
            P.dve(lambda e: e.tensor_tensor(out=attT[:T, :, :T], in0=psa[:T, :].rearrange("p (h t) -> p h t", h=4)[:, :, :T],
                                            in1=bc_mid(M["LE"][:T, :T], 4), op=ALU.mult), r=[pka, f"m_{kind}_LE"], w=["attT"])
            if self.dbg_stop <= 8:
                return
            pso = [self.psum(hold=True) for _ in range(4)]
            for h in range(4):
                P.pe(lambda e, h=h: e.matmul(pso[h][0][:T, :], lhsT=attT[:T, h, :T], rhs=vb[:T, h * 512:(h + 1) * 512], start=True, stop=False),
                     r=["attT", ("vb", h)], w=[pso[h][1]])
        if self.dbg_stop <= 9:
            return
        for si, sq in enumerate(seqs):
            j = sq["j"]
            if sq.get("load") is not None:
                sq["load"]()
            S, Sk, Sb, Sbk = sq["S"], sq["Skey"], sq["Sb"], sq["Sbkey"]
            last = si == nseq - 1
            if not state_only:
                if sq["masked"]:
                    qTm = self.sb("qTm", [128, 4, 128], BF16)
                    self.masked_cols(qTm, "qTm", qT, "qT", si, T)
                    qsrc, qk = qTm, "qTm"
                else:
                    qsrc, qk = qT, "qT"
                for h in range(4):
                    P.pe(lambda e, h=h, qsrc=qsrc, Sb=Sb, last=last: e.matmul(pso[h][0][:T, :], lhsT=qsrc[:, h, :T], rhs=Sb[:, h, :],
                                                                             start=False, stop=last),
                         r=[qk, Sbk], w=[pso[h][1]])
            if sq["masked"]:
                khm = self.sb("khm", [128, 512], BF16)
                P.dve(lambda e, j=j: e.tensor_scalar(out=khm[:T, :], in0=kh[:T, :], scalar1=M["seg"][:T, j:j + 1], scalar2=None, op0=ALU.mult),
                      r=["kh", "m_s_seg"], w=["khm"])
                ksrc, kk = khm, "khm"
            else:
                ksrc, kk = kh, "kh"
            for h in range(4):
                psu, pku = self.psum()
                P.pe(lambda e, h=h, psu=psu, ksrc=ksrc: e.matmul(psu[:, :], lhsT=ksrc[:T, h * 128:(h + 1) * 128], rhs=vb[:T, h * 512:(h + 1) * 512],
                                                                start=True, stop=True), r=[kk, ("vb", h)], w=[pku])
                P.dve(lambda e, h=h, psu=psu, S=S, j=j: e.scalar_tensor_tensor(out=S[:, h, :], in0=S[:, h, :], scalar=elast[:, h, j:j + 1],
                                                                                in1=psu[:, :], op0=ALU.mult, op1=ALU.add),
                      r=[pku, "elast", Sk], w=[Sk])
            if sq.get("done") is not None:
                sq["done"]()
        if state_only:
            return
        if self.dbg_stop <= 10:
            for h in range(4):
                self.psum_release(pso[h][1])
            return
        st = self.sb("ostat", [128, 12])
        junk = self.sb("junk", [128, DI], BF16)
        for h in range(4):
            P.act(lambda e, h=h: e.activation(out=junk[:T, h * 512:(h + 1) * 512], in_=pso[h][0][:T, :], func=AF.Square, accum_out=st[:T, h:h + 1]),
                  r=[pso[h][1]], w=[("junk", h), ("ostat", h)])
        P.act(lambda e: e.activation(out=st[:T, 4:8], in_=st[:T, 0:4], func=AF.Sqrt, scale=1.0 / 512, bias=self.epsc[:T, 0:1]),
              r=["ostat", "epsc"], w=[("ostat", 4)])
        P.dve(lambda e: e.reciprocal(out=st[:T, 8:12], in_=st[:T, 4:8]), r=[("ostat", 4)], w=[("ostat", 8)])
        og = self.sb("og", [128, DI], BF16)
        for h in range(4):
            on = self.sb(f"on{h % 2}", [128, 512])
            onk = f"on{h % 2}"
            P.dve(lambda e, h=h, on=on: e.scalar_tensor_tensor(out=on[:T, :], in0=pso[h][0][:T, :], scalar=st[:T, 8 + h:9 + h],
                                                               in1=self.hnb[:T, :], op0=ALU.mult, op1=ALU.mult),
                  r=[pso[h][1], ("ostat", 8), "hnb"], w=[onk])
            self.psum_release(pso[h][1])
            P.dve(lambda e, h=h, on=on: e.tensor_tensor(out=og[:T, h * 512:(h + 1) * 512], in0=on[:T, :],
                                                        in1=sr[:T, h * 512:(h + 1) * 512], op=ALU.mult),
                  r=[onk, ("sr", h)], w=[("og", h)])
        self.out_proj_residual(og, "og", xt, xkey, T)

    def run_layer_gla(self, l, first, lastl):
        P, d, cfg = self.P, self.d, self.cfg
        NT, NS, TS = cfg.NT, cfg.NS, cfg.TS
        self.gla_setup(l)
        sg_in = d["sg0"] if l == 0 else d["sg3"]
        out_p = d["gla0_p"] if l == 0 else d["gla3_p"]
        out_s = d["gla0_s"] if l == 0 else d["gla3_s"]
        S = [self.sb("gS0", [128, 4, 512])]
        Sb = [self.sb("gSb0", [128, 4, 512], BF16)]
        P.dve(lambda e: e.memset(S[0][:], 0.0), w=["gS0"])
        P.dve(lambda e: e.memset(Sb[0][:], 0.0), w=["gSb0"])
        ntiles = NT + 1
        for ti in range(ntiles):
            T = 128 if ti < NT else TS
            xkey = "xt0"
            xt = self.load_x(first, ti, 0)
            if ti < NT:
                def done(S0=S[0], Sb0=Sb[0]):
                    P.act(lambda e: e.activation(out=Sb0[:], in_=S0[:], func=AF.Copy), r=["gS0"], w=["gSb0"])
                seqs = [dict(j=0, S=S[0], Skey="gS0", Sb=Sb[0], Sbkey="gSb0", masked=False, done=done)]
                self.gla_tile(l, ti, xt, xkey, T, "p", False, seqs)
                if ti == NT - 1:
                    P.dma("pool", out_p.rearrange("h k v -> k h v"), S[0][:], r=["gS0"], semkey="gst_p", final=True)
            else:
                seqs = []
                for j in range(NS):
                    sl = 0
                    def load(j=j, sl=sl):
                        P.dma("sp", S[sl][:], sg_in[j].rearrange("h k v -> k h v"), w=[f"gS{sl}"], semkey=("gsl", sl))
                        P.act(lambda e: e.activation(out=Sb[sl][:], in_=S[sl][:], func=AF.Copy), r=[f"gS{sl}"], w=[f"gSb{sl}"])
                    def done(j=j, sl=sl):
                        P.dma("pool", out_s[j].rearrange("h k v -> k h v"), S[sl][:], r=[f"gS{sl}"], semkey=("gss", sl), final=True)
                    seqs.append(dict(j=j, S=S[sl], Skey=f"gS{sl}", Sb=Sb[sl], Sbkey=f"gSb{sl}", masked=True, load=load, done=done))
                self.gla_tile(l, ti, xt, xkey, T, "s", False, seqs)
            self.store_x(xt, xkey, ti, T, lastl)


    def ssd_setup(self):
        P, d, cfg = self.P, self.d, self.cfg
        NS, TS = cfg.NS, cfg.TS
        P.dma("pool", d["cwb"][:, 0:4 * 3072], d["l1_conv_w"].rearrange("j c -> (j c)").partition_broadcast(128),
              w=["cwb_d"], semkey="cwbd")
        P.dma("pool", d["cwb"][:, 4 * 3072:5 * 3072], d["l1_conv_b"].partition_broadcast(128), w=["cwb_d2"], semkey="cwbd2")
        cbrow = None
        onesb = self.sb("ones_row_b", [1, 128], BF16)
        P.dve(lambda e: e.memset(onesb[:], 1.0), w=["ones_row_b"])
        dtb = self.sb("dtb", [128, 32])
        P.dma("sp", dtb[:], d["l1_dt_bias"].partition_broadcast(128), w=["dtb"], semkey="dtb")
        aneg = self.sb("aneg", [128, 32])
        P.dma("sp", aneg[:], d["l1_a_log"].partition_broadcast(128), w=["aneg"], semkey="aneg")
        P.act(lambda e: e.activation(out=aneg[:], in_=aneg[:], func=AF.Exp), r=["aneg"], w=["aneg"])
        P.dve(lambda e: e.tensor_scalar(out=aneg[:], in0=aneg[:], scalar1=-1.0, scalar2=None, op0=ALU.mult), r=["aneg"], w=["aneg"])
        dsk = self.sb("dsk", [128, 32])
        P.dma("sp", dsk[:], d["l1_d_skip"].partition_broadcast(128), w=["dsk"], semkey="dsk")
        gcol = self.sb("gcol", [128, 16])
        P.dma("sp", gcol[:], d["l1_gate_norm"].rearrange("(rc p) -> p rc", p=128), w=["gcol"], semkey="gcol",
              allow_slow_non_contiguous=True)
        for rc in range(16):
            P.dve(lambda e, rc=rc: e.tensor_scalar(out=self.wout[:, rc, :], in0=self.wout[:, rc, :], scalar1=gcol[:, rc:rc + 1],
                                                   scalar2=None, op0=ALU.mult), r=[("w_out", rc // 4), "gcol"], w=[("w_out", rc // 4)])
        self.Sh = {}
        for kind, T, K in (("p", 128, 128), ("s", TS, NS * 3)):
            sh = self.sb(f"m_{kind}_Sh", [T, 3, T], BF16)
            P.dma("pool", sh[:], d[f"c_{kind}_Sh"][:, :, :], w=[f"m_{kind}_Sh"], semkey=f"c_{kind}_Sh")
            shp = self.sb(f"m_{kind}_ShP", [128, 3, T], BF16)
            P.dma("pool", shp[:K], d[f"c_{kind}_ShP"][:, :, :], w=[f"m_{kind}_ShP"], semkey=f"c_{kind}_ShP")
            self.Sh[kind] = (sh, shp)
        self.cbrow, self.onesb, self.dtb, self.aneg, self.dsk = cbrow, onesb, dtb, aneg, dsk

    def ssd_tile(self, ti, xt, xkey, T, kind, state_only, seqs, rows, conv_out):
        P, d, cfg = self.P, self.d, self.cfg
        M = self.M[kind]
        sh, shp = self.Sh[kind]
        nseq = len(seqs)
        r0, r1 = rows
        hT = self.norm_transpose(xt, xkey, T)
        if self.dbg_stop <= 20:
            return
        xs = self.sb("vb", [128, DI], BF16)
        Bb = self.sb("qg", [128, 512], BF16)
        Cb = self.sb("kg", [128, 512], BF16)
        xtail = self.sb("xtail", [128, 3072], BF16)
        ytail = self.sb("ytail", [128, 3, 512], BF16)
        cwblk = self.sb("cwblk", [128, 5, 512], BF16)
        Yblk = self.sb("ogT", [128, 16, 128], BF16)[:].rearrange("p a b -> p (a b)").rearrange("p (j c) -> p j c", j=4)
        cwsrc = d["cwb"].rearrange("p (j c) -> p j c", j=5)
        nblk = 5 if state_only else 6
        for blk in range(nblk):
            c0 = 2048 + blk * 512
            ps, pk = self.proj_block(hT, T, c0, c0 + 512)
            P.dma("sp", cwblk[:, :, :], cwsrc[:, :, blk * 512:(blk + 1) * 512], r=["cwb_d", "cwb_d2"], w=["cwblk"], semkey="cwblk")
            psb = AP(ps[:T, :].tensor, ps[:T, :].offset, [list(ps[:T, :].ap[0]), [0, 4], list(ps[:T, :].ap[1])])
            P.dve(lambda e, psb=psb: e.tensor_tensor(out=Yblk[:T, :, :], in0=psb, in1=cwblk[:T, 0:4, :], op=ALU.mult),
                  r=[pk, "cwblk"], w=["ogT"])
            if self.dbg_stop <= 31:
                return
            xtb = xtail[r0:r1, blk * 512:(blk + 1) * 512]
            xtb3 = AP(xtb.tensor, xtb.offset, [list(xtb.ap[0]), [0, 3], list(xtb.ap[1])])
            P.dve(lambda e, xtb3=xtb3: e.tensor_tensor(out=ytail[r0:r1, :, :], in0=xtb3, in1=cwblk[r0:r1, 0:3, :], op=ALU.mult),
                  r=[("xtail", blk), "cwblk"], w=["ytail"])
            if self.dbg_stop <= 32:
                return
            if conv_out is not None:
                stg = self.sb(f"on{blk % 2}", [128, 512])
                P.dve(lambda e, ps=ps, stg=stg: e.tensor_copy(out=stg[:T, :], in_=ps[:T, :]), r=[pk], w=[f"on{blk % 2}"])
                conv_out(stg, f"on{blk % 2}", blk)
            if kind == "p":
                P.dve(lambda e, ps=ps, blk=blk: e.tensor_copy(out=xtail[64:128, blk * 512:(blk + 1) * 512], in_=ps[64:128, :]),
                      r=[pk, "ytail"], w=[("xtail", blk)])
            if self.dbg_stop <= 33:
                return
            pc, pck = self.psum()
            for j in range(3):
                P.pe(lambda e, j=j, pc=pc: e.matmul(pc[:T, :], lhsT=sh[:T, j, :T], rhs=Yblk[:T, j, :], start=(j == 0), stop=False),
                     r=["ogT", f"m_{kind}_Sh"], w=[pck])
            P.pe(lambda e, pc=pc: e.matmul(pc[:T, :], lhsT=self.identb[:T, :T], rhs=Yblk[:T, 3, :], start=False, stop=False),
                 r=["ogT", "identb"], w=[pck])
            if self.dbg_stop <= 34:
                return
            for j in range(3):
                P.pe(lambda e, j=j, pc=pc: e.matmul(pc[:T, :], lhsT=shp[r0:r1, j, :T], rhs=ytail[r0:r1, j, :], start=False, stop=False),
                     r=["ytail", f"m_{kind}_ShP"], w=[pck])
            if self.dbg_stop <= 35:
                return
            P.pe(lambda e, pc=pc, blk=blk: e.matmul(pc[:T, :], lhsT=self.onesb[0:1, :T], rhs=cwblk[0:1, 4, :],
                                                    start=False, stop=True), r=["ones_row_b", "cwblk"], w=[pck])
            if blk < 4:
                P.act(lambda e, pc=pc, blk=blk: e.activation(out=xs[:T, blk * 512:(blk + 1) * 512], in_=pc[:T, :], func=AF.Silu),
                      r=[pck], w=[("vb", blk)])
            elif blk == 4:
                P.act(lambda e, pc=pc: e.activation(out=Bb[:T, :], in_=pc[:T, :], func=AF.Silu), r=[pck], w=["qg"])
            else:
                P.act(lambda e, pc=pc: e.activation(out=Cb[:T, :], in_=pc[:T, :], func=AF.Silu), r=[pck], w=["kg"])
        if self.dbg_stop <= 41:
            return
        sm = self.sb("ssm_small", [128, 8, 32])
        psd, pkd = self.proj_block(hT, T, 5120, 5152)
        P.dve(lambda e: e.tensor_tensor(out=sm[:T, 4, :], in0=psd[:T, 0:32], in1=self.dtb[:T, :], op=ALU.add), r=[pkd, "dtb"], w=[("sm", 4)])
        P.act(lambda e: e.activation(out=sm[:T, 4, :], in_=sm[:T, 4, :], func=AF.Exp), r=[("sm", 4)], w=[("sm", 4)])
        P.act(lambda e: e.activation(out=sm[:T, 0, :], in_=sm[:T, 4, :], func=AF.Ln, bias=self.onec[:T, 0:1]), r=[("sm", 4), "onec"], w=[("sm", 0)])
        P.dve(lambda e: e.tensor_tensor(out=sm[:T, 1, :], in0=sm[:T, 0, :], in1=self.aneg[:T, :], op=ALU.mult), r=[("sm", 0), "aneg"], w=[("sm", 1)])
        la = sm[:T, 1, :]
        psr, pkr = self.psum()
        P.pe(lambda e: e.matmul(psr[:T, 0:32], lhsT=M["GT"][:T, :T], rhs=la, start=True, stop=True), r=[("sm", 1), f"m_{kind}_GT"], w=[pkr])
        P.act(lambda e: e.activation(out=sm[:T, 2, :], in_=psr[:T, 0:32], func=AF.Exp), r=[pkr], w=[("sm", 2)])
        P.dve(lambda e: e.tensor_tensor(out=sm[:T, 2, :], in0=sm[:T, 2, :], in1=sm[:T, 0, :], op=ALU.mult), r=[("sm", 2), ("sm", 0)], w=[("sm", 2)])
        if not state_only:
            psc, pkc = self.psum()
            P.pe(lambda e: e.matmul(psc[:T, 0:32], lhsT=M["LE"][:T, :T], rhs=la, start=True, stop=True), r=[("sm", 1), f"m_{kind}_LE"], w=[pkc])
            P.act(lambda e: e.activation(out=sm[:T, 3, :], in_=psc[:T, 0:32], func=AF.Exp), r=[pkc], w=[("sm", 3)])
        elb = self.sb("elb", [128, 16, 32])
        pse, pke = self.psum()
        for sq in seqs:
            j = sq["j"]
            sc = M["seg"][:T, j:j + 1]
            scb = AP(sc.tensor, sc.offset, [list(sc.ap[0]), [0, 128]])
            P.pe(lambda e, j=j, scb=scb: e.matmul(pse[:, j * 32:(j + 1) * 32], lhsT=scb, rhs=la, start=True, stop=True),
                 r=[("sm", 1), f"m_{kind}_seg"], w=[pke])
        P.act(lambda e: e.activation(out=elb[:, :nseq, :], in_=pse[:, 0:nseq * 32].rearrange("p (j h) -> p j h", h=32), func=AF.Exp),
              r=[pke], w=["elb"])
        if self.dbg_stop <= 42:
            return
        if not state_only:
            zs = self.sb("sr", [128, DI], BF16)
            for b in range(4):
                psz, pkz = self.proj_block(hT, T, b * 512, (b + 1) * 512)
                P.act(lambda e, b=b, psz=psz: e.activation(out=zs[:T, b * 512:(b + 1) * 512], in_=psz[:T, :], func=AF.Silu), r=[pkz], w=[("sr", b)])
            BT = self.sb("qT", [128, 4, 128], BF16)
            CT = self.sb("kT", [128, 4, 128], BF16)
            pt, ptk = self.psum_t()
            for g in range(4):
                P.pe(lambda e, g=g: e.transpose(out=pt[:, g * 128:g * 128 + T], in_=Bb[:T, g * 128:(g + 1) * 128], identity=self.identb[:T, :T]),
                     r=["qg", "identb"], w=[ptk])
                P.pe(lambda e, g=g: e.transpose(out=pt[:, 512 + g * 128:512 + g * 128 + T], in_=Cb[:T, g * 128:(g + 1) * 128],
                                                identity=self.identb[:T, :T]), r=["kg", "identb"], w=[ptk])
            ptv = pt[:].rearrange("p (k t) -> p k t", k=8)
            P.dve(lambda e: e.tensor_copy(out=BT[:, :, :T], in_=ptv[:, 0:4, :T]), r=[ptk], w=["qT"])
            P.dve(lambda e: e.tensor_copy(out=CT[:, :, :T], in_=ptv[:, 4:8, :T]), r=[ptk], w=["kT"])
            psa, pka = self.psum()
            for g in range(4):
                P.pe(lambda e, g=g: e.matmul(psa[:T, g * 128:g * 128 + T], lhsT=BT[:, g, :T], rhs=CT[:, g, :T], start=True, stop=True),
                     r=["qT", "kT"], w=[pka])
            cbT = self.sb("attT", [128, 4, 128], BF16)
            P.dve(lambda e: e.tensor_tensor(out=cbT[:T, :, :T], in0=psa[:T, :].rearrange("p (g t) -> p g t", g=4)[:, :, :T],
                                            in1=bc_mid(M["LE"][:T, :T], 4), op=ALU.mult), r=[pka, f"m_{kind}_LE"], w=["attT"])
            psy = [self.psum(hold=True) for _ in range(4)]
        if self.dbg_stop <= 43:
            for g in range(4):
                self.psum_release(psy[g][1])
            return
        uu = self.sb("junk", [128, DI], BF16)
        P.dve(lambda e: e.tensor_tensor(out=uu[:T, :].rearrange("p (h q) -> p h q", h=32), in0=xs[:T, :].rearrange("p (h q) -> p h q", h=32),
                                        in1=bc_last(sm[:T, 2, :], 64), op=ALU.mult), r=["vb", ("sm", 2)], w=["junk"])
        if self.dbg_stop <= 43.1:
            for g in range(4):
                self.psum_release(psy[g][1])
            return
        for si, sq in enumerate(seqs):
            j = sq["j"]
            if sq.get("load") is not None:
                sq["load"]()
            if self.dbg_stop <= 43.2:
                for g in range(4):
                    self.psum_release(psy[g][1])
                return
            ST, STk, STb, STbk = sq["S"], sq["Skey"], sq["Sb"], sq["Sbkey"]
            last = si == nseq - 1
            if not state_only:
                if sq["masked"]:
                    CTm = self.sb("qTm", [128, 4, 128], BF16)
                    self.masked_cols(CTm, "qTm", CT, "kT", si, T)
                    csrc, ck = CTm, "qTm"
                else:
                    csrc, ck = CT, "kT"
                for g in range(4):
                    P.pe(lambda e, g=g, csrc=csrc, STb=STb, si=si, last=last: e.matmul(psy[g][0][:T, :], lhsT=csrc[:, g, :T],
                                                                                      rhs=STb[:, g * 512:(g + 1) * 512],
                                                                                      start=(si == 0), stop=last),
                         r=[ck, STbk], w=[psy[g][1]])
            if self.dbg_stop <= 43.4:
                for g in range(4):
                    self.psum_release(psy[g][1])
                return
            if sq["masked"]:
                Bm = self.sb("khm", [128, 512], BF16)
                P.dve(lambda e, j=j: e.tensor_scalar(out=Bm[:T, :], in0=Bb[:T, :], scalar1=M["seg"][:T, j:j + 1], scalar2=None, op0=ALU.mult),
                      r=["qg", "m_s_seg"], w=["khm"])
                bsrc, bk = Bm, "khm"
            else:
                bsrc, bk = Bb, "qg"
            for g in range(4):
                psu, pku = self.psum()
                P.pe(lambda e, g=g, psu=psu, bsrc=bsrc: e.matmul(psu[:, :], lhsT=bsrc[:T, g * 128:(g + 1) * 128], rhs=uu[:T, g * 512:(g + 1) * 512],
                                                                start=True, stop=True), r=[bk, "junk"], w=[pku])
                stv = ST[:, g * 512:(g + 1) * 512].rearrange("p (h q) -> p h q", h=8)
                P.dve(lambda e, g=g, stv=stv, j=j: e.tensor_tensor(out=stv, in0=stv, in1=bc_last(elb[:, j, g * 8:(g + 1) * 8], 64), op=ALU.mult),
                      r=[STk, "elb"], w=[STk])
                P.dve(lambda e, g=g, psu=psu, ST=ST: e.tensor_tensor(out=ST[:, g * 512:(g + 1) * 512], in0=ST[:, g * 512:(g + 1) * 512],
                                                                     in1=psu[:, :], op=ALU.add), r=[pku, STk], w=[STk])
            if self.dbg_stop <= 43.6:
                for g in range(4):
                    self.psum_release(psy[g][1])
                return
            if sq.get("done") is not None:
                sq["done"]()
        if state_only:
            return
        if self.dbg_stop <= 44:
            for g in range(4):
                self.psum_release(psy[g][1])
            return
        og = self.sb("og", [128, DI], BF16)
        for g in range(4):
            P.dve(lambda e, g=g: e.tensor_tensor(out=og[:T, g * 512:(g + 1) * 512].rearrange("p (h q) -> p h q", h=8),
                                                 in0=psy[g][0][:T, :].rearrange("p (h q) -> p h q", h=8),
                                                 in1=bc_last(sm[:T, 3, g * 8:(g + 1) * 8], 64), op=ALU.mult),
                  r=[psy[g][1], ("sm", 3)], w=[("og", g)])
            self.psum_release(psy[g][1])
        if self.dbg_stop <= 45:
            return
        P.dve(lambda e: e.tensor_tensor(out=uu[:T, :].rearrange("p (h q) -> p h q", h=32), in0=xs[:T, :].rearrange("p (h q) -> p h q", h=32),
                                        in1=bc_last(sm[:T, 0, :], 64), op=ALU.mult), r=["vb", ("sm", 0)], w=["junk"])
        g4 = self.sb("g4", [128, 4, 512])
        laexp = g4[:, 0:2, :].rearrange("p a (b t) -> p (a b) t", t=128)
        Eg = self.sb("Eg", [128, 8, 128], BF16)
        Mg = self.sb("Mg", [128, 8, 128], BF16)
        st = self.sb("ostat", [128, 12])
        sq_junk = g4[:, 3, :]
        for g in range(4):
            P.dve(lambda e, g=g: e.tensor_tensor(out=laexp[:T, :, :T], in0=bc_last(sm[:T, 1, g * 8:(g + 1) * 8], T), in1=bc_mid(M["LE"][:T, :T], 8),
                                                 op=ALU.mult), r=[("sm", 1), f"m_{kind}_LE"], w=["spf", "erev"])
            nmm = 2 if T == 128 else 1
            for hb in range(nmm):
                psd2, pkd2 = self.psum()
                e0, e1 = (hb * 4, hb * 4 + 4) if nmm == 2 else (0, 8)
                P.pe(lambda e, psd2=psd2, e0=e0, e1=e1: e.matmul(psd2[:T, 0:(e1 - e0) * T].rearrange("p (a t) -> p a t", t=T),
                                                                 lhsT=M["GT"][:T, :T], rhs=laexp[:T, e0:e1, :T], start=True, stop=True),
                     r=["spf", "erev", f"m_{kind}_GT"], w=[pkd2])
                P.act(lambda e, psd2=psd2, e0=e0, e1=e1: e.activation(out=Eg[:T, e0:e1, :T],
                                                                      in_=psd2[:T, 0:(e1 - e0) * T].rearrange("p (a t) -> p a t", t=T), func=AF.Exp),
                      r=[pkd2], w=[("Eg", hb)])
            P.dve(lambda e, g=g: e.tensor_tensor(out=Mg[:T, :, :T], in0=Eg[:T, :, :T], in1=bc_mid(cbT[:T, g, :T], 8), op=ALU.mult),
                  r=["Eg", "attT"], w=["Mg"])
            pyi, pyik = self.psum()
            for eh in range(8):
                h = g * 8 + eh
                P.pe(lambda e, eh=eh, h=h, pyi=pyi: e.matmul(pyi[:T, eh * 64:(eh + 1) * 64], lhsT=Mg[:T, eh, :T], rhs=uu[:T, h * 64:(h + 1) * 64],
                                                             start=True, stop=True), r=["Mg", "junk"], w=[pyik])
            on = self.sb(f"on{g % 2}", [128, 512])
            onk = f"on{g % 2}"
            on2 = g4[:, 2, :]
            gs = slice(g * 512, (g + 1) * 512)
            P.dve(lambda e, on=on, pyi=pyi, gs=gs: e.tensor_tensor(out=on[:T, :], in0=pyi[:T, :], in1=og[:T, gs], op=ALU.add),
                  r=[pyik, ("og", g)], w=[onk])
            P.dve(lambda e, g=g, gs=gs: e.tensor_tensor(out=on2[:T, :].rearrange("p (h q) -> p h q", h=8),
                                                        in0=xs[:T, gs].rearrange("p (h q) -> p h q", h=8),
                                                        in1=bc_last(self.dsk[:T, g * 8:(g + 1) * 8], 64), op=ALU.mult),
                  r=[("vb", g), "dsk"], w=["ecum"])
            P.dve(lambda e, on=on: e.tensor_tensor(out=on[:T, :], in0=on[:T, :], in1=on2[:T, :], op=ALU.add), r=[onk, "ecum"], w=[onk])
            P.dve(lambda e, on=on, gs=gs: e.tensor_tensor(out=on[:T, :], in0=on[:T, :], in1=zs[:T, gs], op=ALU.mult), r=[onk, ("sr", g)], w=[onk])
            P.act(lambda e, on=on, g=g: e.activation(out=sq_junk[:T, :], in_=on[:T, :], func=AF.Square, accum_out=st[:T, g:g + 1]),
                  r=[onk], w=["encum", ("ostat", g)])
            P.act(lambda e, g=g: e.activation(out=st[:T, 4 + g:5 + g], in_=st[:T, g:g + 1], func=AF.Sqrt, scale=1.0 / 512, bias=self.epsc[:T, 0:1]),
                  r=[("ostat", g), "epsc"], w=[("ostat", 4 + g)])
            P.dve(lambda e, g=g: e.reciprocal(out=st[:T, 8 + g:9 + g], in_=st[:T, 4 + g:5 + g]), r=[("ostat", 4 + g)], w=[("ostat", 8 + g)])
            P.dve(lambda e, on=on, g=g, gs=gs: e.tensor_scalar(out=og[:T, gs], in0=on[:T, :], scalar1=st[:T, 8 + g:9 + g], scalar2=None, op0=ALU.mult),
                  r=[onk, ("ostat", 8 + g)], w=[("og", g)])
        if self.dbg_stop <= 46:
            return
        self.out_proj_residual(og, "og", xt, xkey, T)

    def run_layer_ssd(self, l, first, lastl):
        P, d, cfg = self.P, self.d, self.cfg
        NT, NS, TS = cfg.NT, cfg.NS, cfg.TS
        self.ssd_setup()
        ST = self.sb("gS0", [128, 4, 512])[:].rearrange("p a b -> p (a b)")
        STb = self.sb("gSb0", [128, 4, 512], BF16)[:].rearrange("p a b -> p (a b)")
        xtail = self.sb("xtail", [128, 3072], BF16)
        P.dve(lambda e: e.memset(ST, 0.0), w=["gS0"])
        P.dve(lambda e: e.memset(STb, 0.0), w=["gSb0"])
        P.dve(lambda e: e.memset(xtail[:, :], 0.0), w=["xtail"])

        def st_load(src_seq):
            srcv = src_seq.rearrange("(c h2) q n -> (h2 q) c n", c=16)
            for cg in range(4):
                stg = self.sb(f"on{cg % 2}", [128, 512])
                P.dma("sp", stg[:].rearrange("p (c n) -> p c n", c=4), srcv[:, cg * 4:(cg + 1) * 4, :], w=[f"on{cg % 2}"], semkey=("stl", cg % 2))
                ps, pk = self.psum()
                for c in range(4):
                    P.pe(lambda e, c=c, ps=ps, stg=stg: e.transpose(out=ps[:, c * 128:(c + 1) * 128], in_=stg[:, c * 128:(c + 1) * 128],
                                                                    identity=self.identf[:, :]), r=[f"on{cg % 2}", "identf"], w=[pk])
                P.dve(lambda e, ps=ps, cg=cg: e.tensor_copy(out=ST[:, cg * 512:(cg + 1) * 512], in_=ps[:, :]), r=[pk], w=["gS0"])
                P.dve(lambda e, ps=ps, cg=cg: e.tensor_copy(out=STb[:, cg * 512:(cg + 1) * 512], in_=ps[:, :]), r=[pk], w=["gSb0"])

        def st_store(dst_seq, semname):
            dstv = dst_seq.rearrange("(c h2) q n -> (h2 q) c n", c=16)
            for cg in range(4):
                ps, pk = self.psum()
                for c in range(4):
                    cc = cg * 4 + c
                    P.pe(lambda e, c=c, cc=cc, ps=ps: e.transpose(out=ps[:, c * 128:(c + 1) * 128], in_=ST[:, cc * 128:(cc + 1) * 128],
                                                                  identity=self.identf[:, :]), r=["gS0", "identf"], w=[pk])
                stg = self.sb(f"on{cg % 2}", [128, 512])
                P.dve(lambda e, ps=ps, stg=stg: e.tensor_copy(out=stg[:, :], in_=ps[:, :]), r=[pk], w=[f"on{cg % 2}"])
                P.dma("pool", dstv[:, cg * 4:(cg + 1) * 4, :], stg[:].rearrange("p (c n) -> p c n", c=4), r=[f"on{cg % 2}"],
                      semkey=(semname, cg % 2), final=True)

        ntiles = NT + 1
        for ti in range(ntiles):
            T = 128 if ti < NT else TS
            xkey = "xt0"
            xt = self.load_x(first, ti, 0)
            if ti < NT:
                def done():
                    P.act(lambda e: e.activation(out=STb, in_=ST, func=AF.Copy), r=["gS0"], w=["gSb0"])
                seqs = [dict(j=0, S=ST, Skey="gS0", Sb=STb, Sbkey="gSb0", masked=False, done=done)]
                conv_out = None
                if ti == NT - 1:
                    def conv_out(stg, skey, blk):
                        P.dma("pool", d["conv1_p"][:, blk * 512:(blk + 1) * 512], stg[125:128, :], r=[skey], semkey=("cvo", skey), final=True)
                        if blk == 0:
                            self.dbg("stg0", stg[:, :], [skey], [128, 512])
                self.ssd_tile(ti, xt, xkey, T, "p", False, seqs, (64, 128), conv_out)
                if ti == NT - 1:
                    st_store(d["ssm1_p"], "sst_p")
            else:
                P.dma("pool", xtail[0:NS * 3, :], d["conv1"].rearrange("s r c -> (s r) c"), w=["xtail"], semkey="xtl_s")
                seqs = []
                for j in range(NS):
                    def load(j=j):
                        st_load(d["ssm1"][j])
                    def done(j=j):
                        st_store(d["ssm1_s"][j], "sst_s")
                    seqs.append(dict(j=j, S=ST, Skey="gS0", Sb=STb, Sbkey="gSb0", masked=True, load=load, done=done))
                def conv_out(stg, skey, blk):
                    P.dma("pool", d["cvs"][:, blk * 512:(blk + 1) * 512], stg[:TS, :], r=[skey], w=[("cvs", blk)], semkey=("cvo", skey))
                    P.dma("pool", d["conv1_s"][:, :, blk * 512:(blk + 1) * 512],
                          d["cvs"][:, blk * 512:(blk + 1) * 512].rearrange("(s t) c -> s t c", t=8)[:, 5:8, :],
                          r=[("cvs", blk)], semkey=("cvo2", blk), final=True)
                self.ssd_tile(ti, xt, xkey, T, "s", False, seqs, (0, NS * 3), conv_out)
            self.store_x(xt, xkey, ti, T, lastl)


    def swa_setup(self):
        P, d, cfg = self.P, self.d, self.cfg
        NS, TS = cfg.NS, cfg.TS
        esink = self.sb("esink", [128, 32])
        P.dma("sp", esink[:], d["l2_sinks"].partition_broadcast(128), w=["esink"], semkey="esink")
        P.act(lambda e: e.activation(out=esink[:], in_=esink[:], func=AF.Exp), r=["esink"], w=["esink"])
        mas = self.sb("m_s_maskA", [128, TS])
        P.dma("sp", mas[:], d["c_s_maskA"][:, :], w=["m_s_maskA"], semkey="c_s_maskA")
        ma0 = self.sb("m_p_maskA0", [128, 128])
        P.dma("sp", ma0[:], d["c_p_maskA0"][:, :], w=["m_p_maskA0"], semkey="c_p_maskA0")
        zl = self.sb("zeros_b", [128, 128], BF16)
        P.dve(lambda e: e.memset(zl[:], 0.0), w=["zeros_b"])
        self.esink, self.mas, self.ma0, self.zl = esink, mas, ma0, zl

    def swa_views(self):
        cw = self.sb("cwblk", [128, 5, 512], BF16)
        yt = self.sb("ytail", [128, 3, 512], BF16)
        kT2 = [cw[:, i, :].rearrange("p (k t) -> p k t", k=4) for i in range(3)]
        vaug = [yt[:, i, 0:260].rearrange("p (k c) -> p k c", k=4) for i in range(3)]
        return kT2, vaug

    def swa_kv_prep(self, kvf, kvkey, T, slot):
        P = self.P
        kT2, vaug = self.swa_views()
        kd = self.sb("qg", [128, 512], BF16)
        kdv = kd[:T, :].rearrange("p (k r c) -> p k r c", k=4, r=2)
        kin = kvf[:T, 0:256].rearrange("p (k c) -> p k c", k=4)
        for r in range(2):
            P.dve(lambda e, r=r: e.tensor_copy(out=kdv[:, :, r, :], in_=kin), r=[kvkey], w=["qg"])
        P.dve(lambda e: e.tensor_copy(out=vaug[slot][:T, :, 0:64], in_=kvf[:T, 256:512].rearrange("p (k c) -> p k c", k=4)),
              r=[kvkey], w=[("ytail", slot)])
        P.dve(lambda e: e.memset(vaug[slot][:T, :, 64:65], 1.0), w=[("ytail", slot)])
        pt, ptk = self.psum_t()
        for k in range(4):
            P.pe(lambda e, k=k: e.transpose(out=pt[:, k * 128:k * 128 + T], in_=kd[:T, k * 128:(k + 1) * 128], identity=self.identb[:T, :T]),
                 r=["qg", "identb"], w=[ptk])
        P.dve(lambda e: e.tensor_copy(out=kT2[slot][:, :, :T], in_=pt[:, 0:512].rearrange("p (k t) -> p k t", k=4)[:, :, :T]),
              r=[ptk], w=[("cwblk", slot)])

    def swa_tile(self, ti, xt, xkey, T, kind, cur, prev, maskA, maskAkey, seqs_cache, kv_out):
        P, d, cfg = self.P, self.d, self.cfg
        M = self.M[kind]
        kT2, vaug = self.swa_views()
        hT = self.norm_transpose(xt, xkey, T)
        qb = self.sb("vb", [128, DI], BF16)
        for b in range(4):
            ps, pk = self.proj_block(hT, T, b * 512, (b + 1) * 512)
            P.act(lambda e, b=b, ps=ps: e.activation(out=qb[:T, b * 512:(b + 1) * 512], in_=ps[:T, :], func=AF.Copy, scale=0.125),
                  r=[pk], w=[("vb", b)])
        qT = self.sb("ogT", [128, 16, 128], BF16)
        for half in range(2):
            pt, ptk = self.psum_t()
            for jj in range(8):
                c = half * 8 + jj
                P.pe(lambda e, jj=jj, c=c, pt=pt: e.transpose(out=pt[:, jj * 128:jj * 128 + T], in_=qb[:T, c * 128:(c + 1) * 128],
                                                              identity=self.identb[:T, :T]), r=[("vb", c // 4), "identb"], w=[ptk])
            P.dve(lambda e, half=half, pt=pt: e.tensor_copy(out=qT[:, half * 8:half * 8 + 8, :T],
                                                            in_=pt[:].rearrange("p (k t) -> p k t", k=8)[:, :, :T]), r=[ptk], w=[("ogT", half)])
        psk, pkk = self.proj_block(hT, T, 2048, 2560)
        kvf = self.sb("on0", [128, 512])
        P.dve(lambda e: e.tensor_copy(out=kvf[:T, :], in_=psk[:T, :]), r=[pkk], w=["on0"])
        self.swa_kv_prep(kvf, "on0", T, cur)
        if kv_out is not None:
            kv_out(kvf, "on0")
        sg = self.sb("sr", [128, DI], BF16)
        for b in range(4):
            ps, pk = self.proj_block(hT, T, 2560 + b * 512, 3072 + b * 512)
            P.act(lambda e, b=b, ps=ps: e.activation(out=sg[:T, b * 512:(b + 1) * 512], in_=ps[:T, :], func=AF.Silu), r=[pk], w=[("sr", b)])
        og = self.sb("og", [128, DI], BF16)
        PA = self.sb("qT", [128, 4, 128], BF16)
        PB = self.sb("kT", [128, 4, 128], BF16)
        PAm = self.sb("qTm", [128, 4, 128], BF16)
        st = self.sb("swstat", [128, 8])
        for par in range(2):
            pbase = par * 64
            if seqs_cache is None:
                combos = [[kvh] for kvh in range(4)]
            else:
                combos = [[0, 1, 2, 3]]
            for cb in combos:
                pv = {}
                for kvh in cb:
                    pv[kvh] = self.psum(hold=True)
                    P.pe(lambda e, pv=pv, kvh=kvh: e.matmul(pv[kvh][0][:T, 0:260], lhsT=self.zl[:T, :T], rhs=vaug[cur][:T, :, :].rearrange("p k c -> p (k c)"),
                                                    start=True, stop=False), r=["zeros_b", ("ytail", cur)], w=[pv[kvh][1]])
                for kvh in cb:
                    qrhs = qT[pbase:pbase + 64, kvh * 4:kvh * 4 + 4, :T]
                    psb, pkb = self.psum()
                    klhs = kT2[cur][pbase:pbase + 64, kvh, :T]
                    P.pe(lambda e, kvh=kvh, psb=psb, qrhs=qrhs, klhs=klhs: e.matmul(psb[:T, 0:4 * T].rearrange("p (a t) -> p a t", a=4),
                                                                        lhsT=klhs, rhs=qrhs, start=True, stop=True),
                         r=[("cwblk", cur), "ogT"], w=[pkb])
                    P.act(lambda e, psb=psb: e.activation(out=PB[:T, :, :T], in_=psb[:T, 0:4 * T].rearrange("p (a t) -> p a t", a=4), func=AF.Exp),
                          r=[pkb], w=["kT"])
                    P.dve(lambda e: e.tensor_tensor(out=PB[:T, :, :T], in0=PB[:T, :, :T], in1=bc_mid(M["LE"][:T, :T], 4), op=ALU.mult),
                          r=["kT", f"m_{kind}_LE"], w=["kT"])
                    for i in range(4):
                        P.pe(lambda e, pv=pv, kvh=kvh, i=i: e.matmul(pv[kvh][0][:T, i * 65:(i + 1) * 65], lhsT=PB[:T, i, :T], rhs=vaug[cur][:T, kvh, :],
                                                             start=False, stop=False), r=["kT", ("ytail", cur)], w=[pv[kvh][1]])
                if seqs_cache is None:
                    kvh = cb[0]
                    qrhs = qT[pbase:pbase + 64, kvh * 4:kvh * 4 + 4, :T]
                    psa, pka = self.psum()
                    klhs = kT2[prev][pbase:pbase + 64, kvh, :]
                    P.pe(lambda e, kvh=kvh, psa=psa, qrhs=qrhs, klhs=klhs: e.matmul(psa[:, 0:4 * T].rearrange("p (a t) -> p a t", a=4),
                                                                        lhsT=klhs, rhs=qrhs, start=True, stop=True),
                         r=[("cwblk", prev), "ogT"], w=[pka])
                    P.act(lambda e, psa=psa: e.activation(out=PA[:, :, :T], in_=psa[:, 0:4 * T].rearrange("p (a t) -> p a t", a=4), func=AF.Exp),
                          r=[pka], w=["qT"])
                    P.dve(lambda e: e.tensor_tensor(out=PA[:, :, :T], in0=PA[:, :, :T], in1=bc_mid(maskA[:, :T], 4), op=ALU.mult),
                          r=["qT", maskAkey], w=["qT"])
                    for i in range(4):
                        P.pe(lambda e, pv=pv, kvh=kvh, i=i: e.matmul(pv[kvh][0][:T, i * 65:(i + 1) * 65], lhsT=PA[:, i, :T], rhs=vaug[prev][:, kvh, :],
                                                             start=False, stop=False), r=["qT", ("ytail", prev)], w=[pv[kvh][1]])
                else:
                    for si, sq in enumerate(seqs_cache):
                        sq["load"]()
                        for kvh in cb:
                            qrhs = qT[pbase:pbase + 64, kvh * 4:kvh * 4 + 4, 8 * si:8 * si + 8]
                            psa, pka = self.psum()
                            klhs = kT2[2][pbase:pbase + 64, kvh, :]
                            P.pe(lambda e, kvh=kvh, psa=psa, qrhs=qrhs, klhs=klhs: e.matmul(psa[:, 0:32].rearrange("p (a t) -> p a t", a=4),
                                                                                lhsT=klhs, rhs=qrhs, start=True, stop=True),
                                 r=[("cwblk", 2), "ogT"], w=[pka])
                            P.act(lambda e, psa=psa: e.activation(out=PA[:, :, 0:8], in_=psa[:, 0:32].rearrange("p (a t) -> p a t", a=4), func=AF.Exp),
                                  r=[pka], w=["qT"])
                            first_use = (si == 0 and kvh == cb[0])
                            if first_use:
                                P.dve(lambda e: e.memset(PAm[:, :, :T], 0.0), w=["qTm"])
                            elif si > 0 and kvh == cb[0]:
                                P.dve(lambda e, si=si: e.memset(PAm[:, :, 8 * (si - 1):8 * si], 0.0), w=["qTm"])
                            P.dve(lambda e, si=si: e.tensor_tensor(out=PAm[:, :, 8 * si:8 * si + 8], in0=PA[:, :, 0:8],
                                                                   in1=bc_mid(self.mas[:, 8 * si:8 * si + 8], 4), op=ALU.mult),
                                  r=["qT", "m_s_maskA"], w=["qTm"])
                            for i in range(4):
                                P.pe(lambda e, pv=pv, kvh=kvh, i=i: e.matmul(pv[kvh][0][:T, i * 65:(i + 1) * 65], lhsT=PAm[:, i, :T], rhs=vaug[2][:, kvh, :],
                                                                     start=False, stop=False), r=["qTm", ("ytail", 2)], w=[pv[kvh][1]])
                for kvh in cb:
                    P.pe(lambda e, pv=pv, kvh=kvh: e.matmul(pv[kvh][0][:T, 0:260], lhsT=self.zl[:T, :T], rhs=vaug[cur][:T, :, :].rearrange("p k c -> p (k c)"),
                                                    start=False, stop=True), r=["zeros_b", ("ytail", cur)], w=[pv[kvh][1]])
                    pvv = pv[kvh][0][:T, 0:260].rearrange("p (a c) -> p a c", a=4)
                    h0 = kvh * 8 + par
                    es = self.esink[:T, h0:h0 + 7:2]
                    P.dve(lambda e, pvv=pvv, es=es: e.tensor_tensor(out=st[:T, 0:4], in0=pvv[:, :, 64], in1=es, op=ALU.add),
                          r=[pv[kvh][1], "esink"], w=[("swstat", 0)])
                    P.dve(lambda e: e.reciprocal(out=st[:T, 4:8], in_=st[:T, 0:4]), r=[("swstat", 0)], w=[("swstat", 4)])
                    on = self.sb("on1", [128, 512])
                    onv = on[:T, 0:256].rearrange("p (a c) -> p a c", a=4)
                    P.dve(lambda e, pvv=pvv, onv=onv: e.tensor_tensor(out=onv, in0=pvv[:, :, 0:64], in1=bc_last(st[:T, 4:8], 64), op=ALU.mult),
                          r=[pv[kvh][1], ("swstat", 4)], w=["on1"])
                    self.psum_release(pv[kvh][1])
                    c0 = (kvh * 8 + par) * 64
                    ogv = AP(og[:T, c0:c0 + 64].tensor, og[:T, c0:c0 + 64].offset, [list(og[:T, c0:c0 + 64].ap[0]), [128, 4], [1, 64]])
                    sgv = AP(sg[:T, c0:c0 + 64].tensor, sg[:T, c0:c0 + 64].offset, [list(sg[:T, c0:c0 + 64].ap[0]), [128, 4], [1, 64]])
                    P.dve(lambda e, ogv=ogv, sgv=sgv, onv=onv: e.tensor_tensor(out=ogv, in0=onv, in1=sgv, op=ALU.mult),
                          r=["on1", "sr"], w=["og"])
        self.out_proj_residual(og, "og", xt, xkey, T)

    def run_layer_swa(self, l, first, lastl):
        P, d, cfg = self.P, self.d, self.cfg
        NT, NS, TS = cfg.NT, cfg.NS, cfg.TS
        self.swa_setup()
        kT2, vaug = self.swa_views()
        cw = self.sb("cwblk", [128, 5, 512], BF16)
        yt = self.sb("ytail", [128, 3, 512], BF16)
        P.dve(lambda e: e.memset(cw[:], 0.0), w=["cwblk"])
        P.dve(lambda e: e.memset(yt[:], 0.0), w=["ytail"])
        ntiles = NT + 1
        for ti in range(ntiles):
            T = 128 if ti < NT else TS
            xkey = "xt0"
            xt = self.load_x(first, ti, 0)
            if ti < NT:
                cur, prev = ti % 2, 1 - ti % 2
                if ti == 0:
                    maskA, mk = self.ma0, "m_p_maskA0"
                else:
                    maskA, mk = self.M["p"]["GT"], "m_p_GT"
                kv_out = None
                if ti == NT - 1:
                    def kv_out(kvf, key):
                        P.dma("pool", d["k2_p"][:, :], kvf[:, 0:256], r=[key], semkey="k2p", final=True)
                        P.dma("pool", d["v2_p"][:, :], kvf[:, 256:512], r=[key], semkey="v2p", final=True)
                self.swa_tile(ti, xt, xkey, T, "p", cur, prev, maskA, mk, None, kv_out)
            else:
                seqs = []
                for j in range(NS):
                    def load(j=j):
                        cst = self.sb("on1", [128, 512])
                        P.dma("sp", cst[:, 0:256], d["kc2"][j], w=["on1"], semkey="kc2l")
                        P.dma("sp", cst[:, 256:512], d["vc2"][j], w=["on1"], semkey="vc2l")
                        self.swa_kv_prep(cst, "on1", 128, 2)
                    seqs.append(dict(j=j, load=load))
                def kv_out(kvf, key):
                    P.dma("pool", d["kvs"][:, :], kvf[:TS, :], r=[key], w=["kvs"], semkey="kvs")
                    kvv = d["kvs"].rearrange("(s t) c -> s t c", t=8)
                    P.dma("pool", d["k2_s"][:, 120:128, :], kvv[:, :, 0:256], r=["kvs"], semkey="k2s", final=True)
                    P.dma("pool", d["v2_s"][:, 120:128, :], kvv[:, :, 256:512], r=["kvs"], semkey="v2s", final=True)
                    P.dma("pool", d["k2_s"][:, 0:120, :], d["kc2"][:, 8:128, :], semkey="k2s2", final=True)
                    P.dma("pool", d["v2_s"][:, 0:120, :], d["vc2"][:, 8:128, :], semkey="v2s2", final=True)
                self.swa_tile(ti, xt, xkey, T, "s", 0, 1, None, None, seqs, kv_out)
            self.store_x(xt, xkey, ti, T, lastl)

    def build(self):
        cfg, P, d = self.cfg, self.P, self.d
        self.declare()
        self.setup_consts()
        self.epsc = self.sb("epsc", [128, 1])
        P.dve(lambda e: e.memset(self.epsc[:], EPS), w=["epsc"])
        self.onec = self.sb("onec", [128, 1])
        P.dve(lambda e: e.memset(self.onec[:], 1.0), w=["onec"])
        layers = cfg.layers
        for li, l in enumerate(layers):
            first = li == 0
            lastl = li == len(layers) - 1
            self.load_layer_weights(l)
            kind = LAYER_KIND[l]
            if kind == "gla":
                self.run_layer_gla(l, first, lastl)
            elif kind == "ssd":
                self.run_layer_ssd(l, first, lastl)
            elif kind == "swa":
                self.run_layer_swa(l, first, lastl)
            else:
                raise NotImplementedError(kind)
        P.emit(self.stack)
        return self.nc


_PROG_CACHE = {}


def _run(inputs, layers=(0, 1, 2, 3)):
    xp = np.asarray(inputs["x_prompt"], dtype=np.float32)
    xs = np.asarray(inputs["x_sample"], dtype=np.float32)
    B, SEQ, _ = xp.shape
    DB = xs.shape[0]
    cfg = Cfg(B, SEQ, DB, layers, G=1)
    G, NT, NS, TS = cfg.G, cfg.NT, cfg.NS, cfg.TS
    key = (B, SEQ, DB, tuple(layers))
    if key not in _PROG_CACHE:
        bld = Builder(cfg)
        nc = bld.build()
        _PROG_CACHE[key] = (bld, nc)
    bld, nc = _PROG_CACHE[key]
    mp = make_masks(128, 128)
    ms = make_masks(TS, 8)
    colmask = np.ascontiguousarray(ms["seg"].T)
    shared = {}
    for k, v in inputs.items():
        if k.startswith("l") or k == "final_norm":
            shared[k] = np.ascontiguousarray(np.asarray(v, dtype=np.float32))
    shared["c_ident"] = np.eye(128, dtype=np.float32)
    shared["c_p_LE"], shared["c_p_GT"] = mp["LE"], mp["GT"]
    shared["c_s_LE"], shared["c_s_GT"] = ms["LE"], ms["GT"]
    shared["c_s_seg"] = ms["seg"]
    shared["c_p_Sh"], shared["c_s_Sh"] = mp["Sh"], ms["Sh"]
    shared["c_s_maskA"] = (np.arange(128)[:, None] > (np.arange(TS)[None, :] % 8)).astype(np.float32)
    shared["c_p_ShP"], shared["c_s_ShP"] = mp["ShP"], ms["ShP"]
    in_maps = []
    NP = NT * 128
    for c in range(NCORES):
        b, g = (c // G, c % G) if c < B * G else (0, 0)
        m = dict(shared)
        m["xp"] = np.ascontiguousarray(xp[b, g * NP:(g + 1) * NP, :])
        sl = slice(c * NS, (c + 1) * NS)
        m["xsamp"] = np.ascontiguousarray(xs[sl].reshape(TS, D))
        m["sg0"] = np.ascontiguousarray(inputs["state_gla_0"][sl])
        m["ssm1"] = np.ascontiguousarray(inputs["state_ssm_1"][sl])
        m["conv1"] = np.ascontiguousarray(inputs["state_conv_1"][sl])
        m["kc2"] = np.ascontiguousarray(np.asarray(inputs["cache_swa_k_2"][sl]).reshape(NS, 128, 256))
        m["vc2"] = np.ascontiguousarray(np.asarray(inputs["cache_swa_v_2"][sl]).reshape(NS, 128, 256))
        m["sg3"] = np.ascontiguousarray(inputs["state_gla_3"][sl])
        pm = np.zeros((1, 2 * G), np.float32)
        for j in range(G):
            pm[0, j] = 1.0 if j < g else 0.0
            pm[0, G + j] = 1.0 - pm[0, j]
        m["c_pm"] = pm
        m["c_p_maskA0"] = (mp["GT"] * (1.0 if g > 0 else 0.0)).astype(np.float32)
        in_maps.append(m)
    res = run_bass_kernel_spmd(nc, in_maps, core_ids=list(range(NCORES)))
    R = res.results
    last = [b * G + G - 1 for b in range(B)]
    y_prompt = np.stack([np.concatenate([R[b * G + g]["yp"] for g in range(G)], axis=0) for b in range(B)])
    y_sample = np.concatenate([R[c]["ysamp"].reshape(NS, 8, D) for c in range(NCORES)], axis=0)

    def pst(name, shape):
        return np.stack([R[c][name].reshape(shape) for c in last])

    def sst(name, shape):
        return np.concatenate([R[c][name].reshape((NS,) + shape) for c in range(NCORES)], axis=0)

    outs = (y_prompt, y_sample,
            pst("gla0_p", (4, 128, 512)), sst("gla0_s", (4, 128, 512)),
            pst("ssm1_p", (32, 64, 128)), sst("ssm1_s", (32, 64, 128)),
            pst("conv1_p", (3, 3072)), sst("conv1_s", (3, 3072)),
            pst("k2_p", (128, 4, 64)), sst("k2_s", (128, 4, 64)),
            pst("v2_p", (128, 4, 64)), sst("v2_s", (128, 4, 64)),
            pst("gla3_p", (4, 128, 512)), sst("gla3_s", (4, 128, 512)))
    return tuple(np.ascontiguousarray(o, dtype=np.float32) for o in outs)


def kernel(**inputs):
    return _run(inputs)
```

```python
import numpy as np
from contextlib import ExitStack
import concourse.bass as bass
import concourse.mybir as mybir
from concourse.ap import AP
from concourse.bass_utils import run_bass_kernel_spmd

F32 = mybir.dt.float32
BF16 = mybir.dt.bfloat16
AF = mybir.ActivationFunctionType
ALU = mybir.AluOpType

D = 1024
DI = 2048
EPS = 1e-6
GLA_IN = 5136
SSD_IN = 5152
SWA_IN = 4608
NCORES = 8
FORCE_G = None

ENGS = ("pe", "act", "dve", "pool", "sp")


def _conflict(a, b):
    n = min(len(a), len(b))
    return a[:n] == b[:n]


class Op:
    __slots__ = ("eng", "fn", "reads", "writes", "dma", "semkey", "inc", "deps",
                 "sem", "semval", "need_inc", "idx")


class Prog:
    def __init__(self, nc):
        self.nc = nc
        self.ops = []
        self.state = {}
        self.final_waits = []

    @staticmethod
    def _norm(keys):
        out = []
        for k in keys:
            if k is None:
                continue
            if not isinstance(k, tuple):
                k = (k,)
            out.append(k)
        return out

    def op(self, eng, fn, r=(), w=(), dma=False, semkey=None, inc=None, final=False):
        o = Op()
        o.eng = eng
        o.fn = fn
        o.reads = self._norm(r)
        o.writes = self._norm(w)
        o.dma = dma
        o.semkey = semkey
        o.inc = inc if inc is not None else (16 if dma else 1)
        o.idx = len(self.ops)
        o.need_inc = False
        deps = set()
        for k in o.reads:
            tab = self.state.setdefault(k[0], {})
            for kk, st in tab.items():
                if _conflict(k, kk):
                    if st[0] is not None:
                        deps.add(st[0])
                    if k[0] in ("ps", "pst"):
                        deps.update(r for r in st[1] if self.ops[r].eng != eng)
        for k in o.writes:
            tab = self.state.setdefault(k[0], {})
            for kk, st in tab.items():
                if _conflict(k, kk):
                    if st[0] is not None:
                        deps.add(st[0])
                    deps.update(st[1])
        for k in o.reads:
            tab = self.state[k[0]]
            if k not in tab:
                tab[k] = [None, []]
            tab[k][1].append(o.idx)
        for k in o.writes:
            tab = self.state[k[0]]
            for kk in [kk for kk in tab if len(kk) > len(k) and kk[:len(k)] == k]:
                del tab[kk]
            tab[k] = [o.idx, []]
        deps.discard(o.idx)
        keep = set()
        for d in deps:
            dop = self.ops[d]
            if (not dop.dma) and (not o.dma) and dop.eng == eng:
                raw = False
                for k in o.reads:
                    for kk in dop.writes:
                        if _conflict(k, kk):
                            raw = True
                if not raw:
                    continue
            keep.add(d)
        o.deps = sorted(keep)
        self.ops.append(o)
        if final:
            self.final_waits.append(o.idx)
        return o

    def pe(self, fn, r=(), w=(), **kw):
        return self.op("pe", fn, r, w, **kw)

    def act(self, fn, r=(), w=(), **kw):
        return self.op("act", fn, r, w, **kw)

    def dve(self, fn, r=(), w=(), **kw):
        return self.op("dve", fn, r, w, **kw)

    def pool(self, fn, r=(), w=(), **kw):
        return self.op("pool", fn, r, w, **kw)

    def dma(self, q, out, in_, r=(), w=(), semkey=None, final=False, **dkw):
        assert semkey is not None
        sk = ("dma",) + (tuple(semkey) if isinstance(semkey, tuple) else (semkey,))
        return self.op(q, lambda e: e.dma_start(out=out, in_=in_, **dkw), r, w,
                       dma=True, semkey=sk, final=final)

    def emit(self, stack):
        nc = self.nc
        ops = self.ops
        for o in ops:
            for d in o.deps:
                ops[d].need_inc = True
        for i in self.final_waits:
            ops[i].need_inc = True
        engsem = {}
        for e in ("pe", "act", "dve", "pool"):
            engsem[e] = stack.enter_context(nc.semaphore("sem_" + e))
        dmasem = {}
        cnt = {e: 0 for e in engsem}
        dcnt = {}
        for o in ops:
            if o.dma:
                if o.semkey not in dmasem:
                    dmasem[o.semkey] = stack.enter_context(
                        nc.semaphore("sd_" + "_".join(str(x) for x in o.semkey[1:])))
                    dcnt[o.semkey] = 0
                dcnt[o.semkey] += o.inc
                o.sem = dmasem[o.semkey]
                o.semval = dcnt[o.semkey]
                o.need_inc = True
            elif o.need_inc:
                cnt[o.eng] += 1
                o.sem = engsem[o.eng]
                o.semval = cnt[o.eng]
        self.nsems = len(engsem) + len(dmasem)
        self.counts = dict(cnt)
        streams = {e: [o for o in ops if o.eng == e] for e in ENGS}
        block = stack.enter_context(nc.Block())
        final_waits = self.final_waits

        def run_stream(e, eng):
            waited = {}
            issued = []
            for o in streams[e]:
                need = {}
                if e == "pool" and o.dma:
                    if len(issued) >= 2:
                        po = issued[-2]
                        need[po.sem.num] = (po.sem, po.semval)
                    issued.append(o)
                for d in o.deps:
                    dop = ops[d]
                    key = dop.sem.num
                    if key not in need or need[key][1] < dop.semval:
                        need[key] = (dop.sem, dop.semval)
                for key, (sem, val) in need.items():
                    if waited.get(key, 0) >= val:
                        continue
                    eng.wait_ge(sem, val)
                    waited[key] = val
                ins = o.fn(eng)
                if o.need_inc:
                    ins.then_inc(o.sem, o.inc)
            if e == "sp":
                need = {}
                for i in final_waits:
                    dop = ops[i]
                    key = dop.sem.num
                    if key not in need or need[key][1] < dop.semval:
                        need[key] = (dop.sem, dop.semval)
                for key, (sem, val) in need.items():
                    if waited.get(key, 0) >= val:
                        continue
                    eng.wait_ge(sem, val)

        @block.sync
        def _(eng):
            run_stream("sp", eng)

        @block.gpsimd
        def _(eng):
            run_stream("pool", eng)

        @block.scalar
        def _(eng):
            run_stream("act", eng)

        @block.vector
        def _(eng):
            run_stream("dve", eng)

        @block.tensor
        def _(eng):
            run_stream("pe", eng)


def bc_mid(ap2d, n):
    a = ap2d.ap
    return AP(ap2d.tensor, ap2d.offset, [list(a[0]), [0, n], list(a[1])])


def bc_last(ap2d, n):
    a = ap2d.ap
    return AP(ap2d.tensor, ap2d.offset, [list(a[0]), list(a[1]), [0, n]])


def make_masks(T, L):
    idx = np.arange(T)
    seq = idx // L
    same = seq[:, None] == seq[None, :]
    s = idx[:, None]
    t = idx[None, :]
    m = {}
    m["LE"] = (same & (s <= t)).astype(np.float32)
    m["GT"] = (same & (s > t)).astype(np.float32)
    nseq = T // L
    seg = (seq[:, None] == np.arange(nseq)[None, :]).astype(np.float32)
    m["seg"] = seg
    sh = np.zeros((T, 3, T), np.float32)
    for j in range(3):
        sh[:, j, :] = (same & (s == t + j - 3)).astype(np.float32)
    m["Sh"] = sh
    if L == 128:
        shp = np.zeros((128, 3, 128), np.float32)
        for j in range(3):
            for tt in range(3):
                if tt + j < 3:
                    shp[125 + tt + j, j, tt] = 1.0
        m["ShP"] = shp
    else:
        shp = np.zeros((nseq * 3, 3, T), np.float32)
        for j in range(3):
            for tt in range(T):
                q = tt % L
                if q + j < 3:
                    shp[(tt // L) * 3 + q + j, j, tt] = 1.0
        m["ShP"] = shp
    return m


class Cfg:
    def __init__(self, B, SEQ, DB, layers=(0, 1, 2, 3), G=1):
        assert B * G <= NCORES
        self.B = B
        self.G = G
        assert SEQ % (self.G * 128) == 0
        self.NT = SEQ // self.G // 128
        assert DB % NCORES == 0
        self.NS = DB // NCORES
        self.TS = self.NS * 8
        assert self.TS <= 128
        self.layers = tuple(layers)
        self.SEQ = SEQ
        self.DB = DB


LAYER_KIND = {0: "gla", 1: "ssd", 2: "swa", 3: "gla"}
LAYER_NIN = {0: GLA_IN, 1: SSD_IN, 2: SWA_IN, 3: GLA_IN}


class Builder:
    def __init__(self, cfg):
        self.cfg = cfg
        self.nc = bass.Bass("TRN2", target_bir_lowering=False)
        self.P = Prog(self.nc)
        self.stack = ExitStack()
        self.d = {}
        self.bufs = {}
        self.psrr = 0
        self.dbg_stop = 99
        self.dbg_on = False
        self.dbg_names = []
        self.held = set()

    def din(self, name, shape, dt=F32):
        self.d[name] = self.nc.dram_tensor(name, list(shape), dt, kind="ExternalInput").ap()
        return self.d[name]

    def dout(self, name, shape, dt=F32):
        self.d[name] = self.nc.dram_tensor(name, list(shape), dt, kind="ExternalOutput").ap()
        return self.d[name]

    def dscr(self, name, shape, dt=F32):
        self.d[name] = self.nc.dram_tensor(name, list(shape), dt).ap()
        return self.d[name]

    def sb(self, name, shape, dt=F32):
        if name in self.bufs:
            return self.bufs[name]
        t = self.stack.enter_context(self.nc.sbuf_tensor(name, list(shape), dt))
        self.bufs[name] = t
        return t

    def dbg(self, name, ap, rkeys, shape):
        if not getattr(self, "dbg_on", False):
            return
        o = self.dout("dbg_" + name, shape)
        self.P.dma("sp", o, ap, r=rkeys, semkey=("dbg", name), final=True)
        self.dbg_names.append("dbg_" + name)

    def psum(self, hold=False):
        for _ in range(len(self.psb)):
            i = self.psrr
            self.psrr = (self.psrr + 1) % len(self.psb)
            if i not in self.held:
                if hold:
                    self.held.add(i)
                return self.psb[i], ("ps", i)
        raise RuntimeError("no free PSUM bank")

    def psum_release(self, key):
        self.held.discard(key[1])

    def psum_t(self):
        i = self.pstrr
        self.pstrr = (self.pstrr + 1) % len(self.pst)
        return self.pst[i], ("pst", i)

    def declare(self):
        cfg = self.cfg
        NT, NS, TS, G = cfg.NT, cfg.NS, cfg.TS, cfg.G
        NP = NT * 128
        self.din("xp", [NP, D])
        self.din("xsamp", [TS, D])
        self.din("sg0", [NS, 4, 128, 512])
        self.din("ssm1", [NS, 32, 64, 128])
        self.din("conv1", [NS, 3, 3072])
        self.din("kc2", [NS, 128, 256])
        self.din("vc2", [NS, 128, 256])
        self.din("sg3", [NS, 4, 128, 512])
        for l in (0, 3):
            self.din(f"l{l}_norm", [D])
            self.din(f"l{l}_w_in", [D, GLA_IN])
            self.din(f"l{l}_w_gk2", [16, 512])
            self.din(f"l{l}_b_gk", [512])
            self.din(f"l{l}_head_norm", [512])
            self.din(f"l{l}_w_out", [DI, D])
        self.din("l1_norm", [D])
        self.din("l1_w_in", [D, SSD_IN])
        self.din("l1_conv_w", [4, 3072])
        self.din("l1_conv_b", [3072])
        self.din("l1_dt_bias", [32])
        self.din("l1_a_log", [32])
        self.din("l1_d_skip", [32])
        self.din("l1_gate_norm", [DI])
        self.din("l1_w_out", [DI, D])
        self.din("l2_norm", [D])
        self.din("l2_w_in", [D, SWA_IN])
        self.din("l2_sinks", [32])
        self.din("l2_w_out", [DI, D])
        self.din("final_norm", [D])
        self.din("c_ident", [128, 128])
        for kind, T in (("p", 128), ("s", TS)):
            self.din(f"c_{kind}_LE", [T, T])
            self.din(f"c_{kind}_GT", [T, T])
        self.din("c_s_seg", [TS, NS])
        self.din("c_p_Sh", [128, 3, 128])
        self.din("c_s_maskA", [128, TS])
        self.din("c_p_maskA0", [128, 128])
        self.din("c_s_Sh", [TS, 3, TS])
        self.din("c_p_ShP", [128, 3, 128])
        self.din("c_s_ShP", [NS * 3, 3, TS])
        self.din("c_pm", [1, 2 * G])
        self.din("c_oh", [1, G])
        self.din("c_sel", [G * 3, 128])
        self.dout("yp", [NP, D])
        self.dout("ysamp", [TS, D])
        self.dout("gla0_p", [4, 128, 512])
        self.dout("gla0_s", [NS, 4, 128, 512])
        self.dout("ssm1_p", [32, 64, 128])
        self.dout("ssm1_s", [NS, 32, 64, 128])
        self.dout("conv1_p", [3, 3072])
        self.dout("conv1_s", [NS, 3, 3072])
        self.dout("k2_p", [128, 256])
        self.dout("k2_s", [NS, 128, 256])
        self.dout("v2_p", [128, 256])
        self.dout("v2_s", [NS, 128, 256])
        self.dout("gla3_p", [4, 128, 512])
        self.dout("gla3_s", [NS, 4, 128, 512])
        self.dscr("xres", [NP + 128, D])
        self.dscr("cwb", [128, 5 * 3072], BF16)
        self.dscr("cvs", [TS, 3072])
        self.dscr("kvs", [TS, 512])

    def setup_consts(self):
        P, d, cfg = self.P, self.d, self.cfg
        NS, TS = cfg.NS, cfg.TS
        nc = self.nc
        self.psb = [self.stack.enter_context(nc.psum_tensor(f"ps{i}", [128, 512], F32)) for i in range(6)]
        self.pst = [self.stack.enter_context(nc.psum_tensor(f"pst{i}", [128, 1024], BF16)) for i in range(2)]
        self.pstrr = 0
        self.identf = self.sb("identf", [128, 128])
        self.identb = self.sb("identb", [128, 128], BF16)
        P.dma("sp", self.identf[:], d["c_ident"][:, :], w=["identf"], semkey="c0")
        P.dma("pool", self.identb[:], d["c_ident"][:, :], w=["identb"], semkey="c1")
        self.ones_row = self.sb("ones_row", [1, 128])
        P.dve(lambda e: e.memset(self.ones_row[:], 1.0), w=["ones_row"])
        self.M = {}
        for kind, T in (("p", 128), ("s", TS)):
            m = {}
            for nm in ("LE", "GT"):
                t = self.sb(f"m_{kind}_{nm}", [T, T])
                P.dma("sp", t[:], d[f"c_{kind}_{nm}"][:, :], w=[f"m_{kind}_{nm}"], semkey=f"c_{kind}_{nm}")
                m[nm] = t
            self.M[kind] = m
        seg = self.sb("m_s_seg", [TS, NS])
        P.dma("sp", seg[:], d["c_s_seg"][:, :], w=["m_s_seg"], semkey="c_seg")
        self.M["s"]["seg"] = seg
        segp = self.sb("m_p_seg", [128, 1])
        P.dve(lambda e: e.memset(segp[:], 1.0), w=["m_p_seg"])
        self.M["p"]["seg"] = segp

    def load_layer_weights(self, l):
        P, d = self.P, self.d
        nin = LAYER_NIN[l]
        win = self.sb("w_in", [128, 8, SSD_IN], BF16)
        wsrc = d[f"l{l}_w_in"].rearrange("(kc p) n -> p kc n", p=128)
        nblk = (nin + 511) // 512
        for b in range(nblk):
            c0, c1 = b * 512, min(nin, (b + 1) * 512)
            P.dma("pool", win[:, :, c0:c1], wsrc[:, :, c0:c1], w=[("w_in", b)], semkey=("win", b))
        wout = self.sb("w_out", [128, 16, D], BF16)
        wosrc = d[f"l{l}_w_out"].rearrange("(rc p) n -> p rc n", p=128)
        for b in range(4):
            P.dma("pool", wout[:, b * 4:(b + 1) * 4, :], wosrc[:, b * 4:(b + 1) * 4, :], w=[("w_out", b)], semkey=("wout", b))
        ncol = self.sb("normcol", [128, 8])
        P.dma("sp", ncol[:], d[f"l{l}_norm"].rearrange("(kc p) -> p kc", p=128), w=["normcol"], semkey="nrm",
              allow_slow_non_contiguous=True)
        self.win, self.wout, self.ncol = win, wout, ncol

    def tile_src(self, l_first, ti):
        cfg, d = self.cfg, self.d
        NT, TS = cfg.NT, cfg.TS
        if ti < NT:
            if l_first:
                return d["xp"][ti * 128:(ti + 1) * 128, :], ("xp", ti)
            return d["xres"][ti * 128:(ti + 1) * 128, :], ("xres", ti)
        if l_first:
            return d["xsamp"][:, :], ("xsamp",)
        return d["xres"][NT * 128:NT * 128 + TS, :], ("xres", NT)

    def load_x(self, l_first, ti, slot):
        T = 128 if ti < self.cfg.NT else self.cfg.TS
        xt = self.sb(f"xt{slot}", [128, D])
        src, key = self.tile_src(l_first, ti)
        self.P.dma("sp", xt[:T, :], src, r=[key], w=[f"xt{slot}"], semkey=("xt", slot))
        return xt

    def norm_transpose(self, xt, xkey, T):
        P = self.P
        junk = self.sb("junk", [128, DI], BF16)
        st = self.sb("nstat", [128, 4])
        hn = self.sb("hn", [128, D], BF16)
        hT = self.sb("hT", [128, 8, 128], BF16)
        P.act(lambda e: e.activation(out=junk[:T, 0:D], in_=xt[:T, :], func=AF.Square, accum_out=st[:T, 0:1]),
              r=[xkey], w=["junk", ("nstat", 0)])
        P.act(lambda e: e.activation(out=st[:T, 1:2], in_=st[:T, 0:1], func=AF.Sqrt, scale=1.0 / D, bias=self.epsc[:T, 0:1]),
              r=[("nstat", 0), "epsc"], w=[("nstat", 1)])
        P.dve(lambda e: e.reciprocal(out=st[:T, 2:3], in_=st[:T, 1:2]), r=[("nstat", 1)], w=[("nstat", 2)])
        P.act(lambda e: e.activation(out=hn[:T, :], in_=xt[:T, :], func=AF.Copy, scale=st[:T, 2:3]),
              r=[xkey, ("nstat", 2)], w=["hn"])
        pt, pk = self.psum_t()
        for kc in range(8):
            P.pe(lambda e, kc=kc: e.transpose(out=pt[:, kc * 128:kc * 128 + T], in_=hn[:T, kc * 128:(kc + 1) * 128],
                                              identity=self.identb[:T, :T]),
                 r=["hn", "identb"], w=[pk])
        ptv = pt[:].rearrange("p (k t) -> p k t", k=8)[:, :, :T]
        P.dve(lambda e: e.tensor_tensor(out=hT[:, :, :T], in0=ptv, in1=bc_last(self.ncol[:, :], T), op=ALU.mult),
              r=[pk, "normcol"], w=["hT"])
        return hT

    def masked_cols(self, dst, dkey, src, skey, si, T):
        P = self.P
        if si == 0:
            P.dve(lambda e: e.memset(dst[:, :, :T], 0.0), w=[dkey])
        else:
            P.dve(lambda e: e.memset(dst[:, :, 8 * (si - 1):8 * si], 0.0), w=[dkey])
        P.dve(lambda e: e.tensor_copy(out=dst[:, :, 8 * si:8 * si + 8], in_=src[:, :, 8 * si:8 * si + 8]), r=[skey], w=[dkey])

    def proj_block(self, hT, T, c0, c1):
        P = self.P
        ps, pk = self.psum()
        b = c0 // 512
        assert (c1 - 1) // 512 == b
        for kc in range(8):
            P.pe(lambda e, kc=kc: e.matmul(ps[:T, 0:c1 - c0], lhsT=hT[:, kc, :T], rhs=self.win[:, kc, c0:c1],
                                           start=(kc == 0), stop=(kc == 7)),
                 r=["hT", ("w_in", b)], w=[pk])
        return ps, pk

    def out_proj_residual(self, og, ogkey, xt, xkey, T):
        P = self.P
        ogT = self.sb("ogT", [128, 16, 128], BF16)
        for half in range(2):
            pt, pk = self.psum_t()
            for j in range(8):
                vc = half * 8 + j
                P.pe(lambda e, j=j, vc=vc, pt=pt: e.transpose(out=pt[:, j * 128:j * 128 + T], in_=og[:T, vc * 128:(vc + 1) * 128],
                                                              identity=self.identb[:T, :T]),
                     r=[ogkey, "identb"], w=[pk])
            ptv = pt[:].rearrange("p (k t) -> p k t", k=8)[:, :, :T]
            P.dve(lambda e, ptv=ptv, half=half: e.tensor_copy(out=ogT[:, half * 8:half * 8 + 8, :T], in_=ptv), r=[pk], w=[("ogT", half)])
        if self.dbg_stop <= 47:
            return
        for nb in range(2):
            if self.dbg_stop <= 48 and nb == 1:
                return
            ps, pk = self.psum()
            for vc in range(16):
                P.pe(lambda e, vc=vc, nb=nb, ps=ps: e.matmul(ps[:T, :], lhsT=ogT[:, vc, :T], rhs=self.wout[:, vc, nb * 512:(nb + 1) * 512],
                                                             start=(vc == 0), stop=(vc == 15)),
                     r=[("ogT", vc // 8), ("w_out", vc // 4)], w=[pk])
            P.dve(lambda e, nb=nb, ps=ps: e.tensor_tensor(out=xt[:T, nb * 512:(nb + 1) * 512], in0=xt[:T, nb * 512:(nb + 1) * 512],
                                                          in1=ps[:T, :], op=ALU.add),
                  r=[pk, xkey], w=[xkey])

    def store_x(self, xt, xkey, ti, T, last_layer):
        P, d, cfg = self.P, self.d, self.cfg
        NT = cfg.NT
        if not last_layer:
            dst = d["xres"][ti * 128:ti * 128 + T, :]
            P.dma("pool", dst, xt[:T, :], r=[xkey], w=[("xres", ti)], semkey=("xst", xkey))
            return
        junk = self.sb("junk", [128, DI], BF16)
        st = self.sb("nstat", [128, 4])
        P.act(lambda e: e.activation(out=junk[:T, 0:D], in_=xt[:T, :], func=AF.Square, accum_out=st[:T, 0:1]),
              r=[xkey], w=["junk", ("nstat", 0)])
        P.act(lambda e: e.activation(out=st[:T, 1:2], in_=st[:T, 0:1], func=AF.Sqrt, scale=1.0 / D, bias=self.epsc[:T, 0:1]),
              r=[("nstat", 0), "epsc"], w=[("nstat", 1)])
        P.dve(lambda e: e.reciprocal(out=st[:T, 2:3], in_=st[:T, 1:2]), r=[("nstat", 1)], w=[("nstat", 2)])
        for hf in range(2):
            fb = self.sb(f"on{hf}", [128, 512])
            P.dma("sp", fb[:], d["final_norm"][hf * 512:(hf + 1) * 512].partition_broadcast(128), w=[f"on{hf}"], semkey=("fnb", hf))
            P.dve(lambda e, hf=hf, fb=fb: e.scalar_tensor_tensor(out=xt[:T, hf * 512:(hf + 1) * 512], in0=xt[:T, hf * 512:(hf + 1) * 512],
                                                                 scalar=st[:T, 2:3], in1=fb[:T, :], op0=ALU.mult, op1=ALU.mult),
                  r=[xkey, ("nstat", 2), f"on{hf}"], w=[xkey])
        dst = d["yp"][ti * 128:(ti + 1) * 128, :] if ti < NT else d["ysamp"][:, :]
        P.dma("pool", dst, xt[:T, :], r=[xkey], semkey=("yst", xkey), final=True)


    def rank_consts(self):
        P, d, G = self.P, self.d, self.cfg.G
        pm = self.sb("pm", [128, 2 * G])
        P.dma("sp", pm[:], d["c_pm"].rearrange("o n -> (o n)").partition_broadcast(128), w=["pm"], semkey="c_pm")
        oh = self.sb("oh", [128, G])
        P.dma("sp", oh[:], d["c_oh"].rearrange("o n -> (o n)").partition_broadcast(128), w=["oh"], semkey="c_oh")
        sel = self.sb("sel", [G * 3, 128])
        P.dma("sp", sel[:], d["c_sel"][:, :], w=["sel"], semkey="c_sel")
        self.pm, self.oh, self.sel = pm, oh, sel

    def allgather(self, tag, rows, W, writes):
        P, G = self.P, self.cfg.G
        xin = self.dscr(f"xin_{tag}", [rows, W])
        xout = self.dscr(f"xout_{tag}", [G * rows, W])
        for i, (c0, c1, src, rk) in enumerate(writes):
            P.dma("sp", xin[:, c0:c1], src, r=rk, w=[(f"xin_{tag}", i)], semkey=("xi", i))
        groups = [list(range(b * G, (b + 1) * G)) for b in range(self.cfg.B)]
        P.op("pool", lambda e: e.collective_compute("AllGather", ALU.bypass, replica_groups=groups, ins=[xin[:, :]], outs=[xout[:, :]]),
             r=[f"xin_{tag}"], w=[f"xout_{tag}"], dma=True, semkey=("dma", "cc"), inc=1)
        return xout, f"xout_{tag}"

    def state_combine(self, tag, Sview, Skey, Dview, Dkey, nd):
        P, G = self.P, self.cfg.G
        xout, xk = self.allgather(tag, 128, 2048, [(0, 2048, Sview, [Skey])])
        xoutd, xkd = self.allgather(tag + "d", 128, 256, [(0, nd, Dview, [Dkey])])
        P.dve(lambda e: e.memset(Sview, 0.0), r=[(f"xin_{tag}", 0)], w=[Skey])
        g4 = self.sb("g4", [128, 4, 512])
        cand = g4[:].rearrange("p a b -> p (a b)")
        ck = ["spf", "erev", "ecum", "encum"]
        dj = self.sb("xD", [128, 32])
        S3 = Sview.rearrange("p (a b) -> p a b", a=nd)
        for j in range(G):
            P.dma("sp", cand, xout[j * 128:(j + 1) * 128, 0:2048], r=[xk], w=ck, semkey="xc")
            P.dma("sp", dj[:, 0:nd], xoutd[j * 128:(j + 1) * 128, 0:nd], r=[xkd], w=["xD"], semkey="xd")
            P.dve(lambda e, j=j: e.tensor_scalar(out=dj[:, 0:nd], in0=dj[:, 0:nd], scalar1=self.pm[:, j:j + 1], scalar2=self.pm[:, G + j:G + j + 1],
                                                 op0=ALU.mult, op1=ALU.add), r=["xD", "pm"], w=["xD"])
            P.dve(lambda e: e.tensor_tensor(out=S3, in0=S3, in1=bc_last(dj[:, 0:nd], 2048 // nd), op=ALU.mult), r=[Skey, "xD"], w=[Skey])
            P.dve(lambda e, j=j: e.scalar_tensor_tensor(out=Sview, in0=cand, scalar=self.pm[:, j:j + 1], in1=Sview, op0=ALU.mult, op1=ALU.add),
                  r=ck + [Skey, "pm"], w=[Skey])

    def gla_setup(self, l):
        P, d = self.P, self.d
        wgk = self.sb("wgk2", [17, 512])
        P.dma("sp", wgk[0:16, :], d[f"l{l}_w_gk2"][:, :], w=["wgk2"], semkey="gs0")
        P.dma("sp", wgk[16:17, :], d[f"l{l}_b_gk"].rearrange("(o n) -> o n", o=1), w=["wgk2"], semkey="gs1")
        lrT = self.sb("lrT", [17, 128])
        P.dve(lambda e: e.memset(lrT[:, :], 1.0), w=["lrT"])
        bgk = None
        hnb = self.sb("hnb", [128, 512])
        P.dma("sp", hnb[:], d[f"l{l}_head_norm"].partition_broadcast(128), w=["hnb"], semkey="gs2")
        self.wgk, self.bgk, self.hnb = wgk, bgk, hnb

    def gla_tile(self, l, ti, xt, xkey, T, kind, state_only, seqs):
        P, d, cfg = self.P, self.d, self.cfg
        M = self.M[kind]
        nseq = len(seqs)
        hT = self.norm_transpose(xt, xkey, T)
        ps, pk = self.proj_block(hT, T, 5120, 5136)
        lrf = self.sb("lrf", [128, 16])
        P.dve(lambda e: e.tensor_copy(out=lrf[:T, :], in_=ps[:T, 0:16]), r=[pk], w=["lrf"])
        ps2, pk2 = self.psum()
        P.pe(lambda e: e.transpose(out=ps2[:16, :T], in_=lrf[:T, :], identity=self.identf[:T, :T]), r=["lrf", "identf"], w=[pk2])
        lrT = self.sb("lrT", [17, 128])
        P.dve(lambda e: e.tensor_copy(out=lrT[0:16, :T], in_=ps2[:16, :T]), r=[pk2], w=["lrT"])
        psz, pkz = self.psum()
        P.pe(lambda e: e.matmul(psz[:T, :], lhsT=lrT[:, :T], rhs=self.wgk[:, :], start=True, stop=True), r=["lrT", "wgk2"], w=[pkz])
        if self.dbg_stop <= 1:
            return
        g4 = self.sb("g4", [128, 4, 512])
        spf = g4[:, 0, :]
        P.act(lambda e: e.activation(out=spf[:T, :], in_=psz[:T, :], func=AF.Exp, scale=-1.0), r=[pkz], w=["spf"])
        P.act(lambda e: e.activation(out=spf[:T, :], in_=spf[:T, :], func=AF.Ln, bias=self.onec[:T, 0:1]), r=["spf", "onec"], w=["spf"])
        if self.dbg_stop <= 2:
            return
        erev = g4[:, 1, :]
        psr, pkr = self.psum()
        P.pe(lambda e: e.matmul(psr[:T, :], lhsT=M["GT"][:T, :T], rhs=spf[:T, :], start=True, stop=True), r=["spf", f"m_{kind}_GT"], w=[pkr])
        P.act(lambda e: e.activation(out=erev[:T, :], in_=psr[:T, :], func=AF.Exp, scale=-1.0 / 16.0), r=[pkr], w=["erev"])
        if not state_only:
            ecum = g4[:, 2, :]
            encum = g4[:, 3, :]
            psc, pkc = self.psum()
            P.pe(lambda e: e.matmul(psc[:T, :], lhsT=M["LE"][:T, :T], rhs=spf[:T, :], start=True, stop=True), r=["spf", f"m_{kind}_LE"], w=[pkc])
            P.act(lambda e: e.activation(out=ecum[:T, :], in_=psc[:T, :], func=AF.Exp, scale=-1.0 / 16.0), r=[pkc], w=["ecum"])
            P.act(lambda e: e.activation(out=encum[:T, :], in_=psc[:T, :], func=AF.Exp, scale=1.0 / 16.0), r=[pkc], w=["encum"])
        if self.dbg_stop <= 3:
            return
        elast = self.sb("elast", [128, 4, 16])
        psl, pkl = self.psum()
        for h in range(4):
            P.pe(lambda e, h=h: e.matmul(psl[:, h * 16:h * 16 + nseq], lhsT=spf[:T, h * 128:(h + 1) * 128], rhs=M["seg"][:T, :nseq],
                                         start=True, stop=True), r=["spf", "m_s_seg", "m_p_seg"], w=[pkl])
        P.act(lambda e: e.activation(out=elast[:, :, :nseq], in_=psl[:, 0:64].rearrange("p (h j) -> p h j", h=4)[:, :, :nseq], func=AF.Exp, scale=-1.0 / 16.0),
              r=[pkl], w=["elast"])
        if self.dbg_stop <= 4:
            return
        if kind == "s":
            self.dbg("spf", spf[:T, :], ["spf"], [T, 512])
            self.dbg("erev", erev[:T, :], ["erev"], [T, 512])
            self.dbg("elast", elast[:, :, :nseq], ["elast"], [128, 4, nseq])
        if not state_only:
            psq, pkq = self.proj_block(hT, T, 0, 512)
            qg = self.sb("qg", [128, 512], BF16)
            P.dve(lambda e: e.scalar_tensor_tensor(out=qg[:T, :], in0=psq[:T, :], scalar=float(128 ** -0.5), in1=ecum[:T, :],
                                                   op0=ALU.mult, op1=ALU.mult), r=[pkq, "ecum"], w=["qg"])
        psk, pkk = self.proj_block(hT, T, 512, 1024)
        kh = self.sb("kh", [128, 512], BF16)
        P.dve(lambda e: e.tensor_tensor(out=kh[:T, :], in0=psk[:T, :], in1=erev[:T, :], op=ALU.mult), r=[pkk, "erev"], w=["kh"])
        if not state_only:
            kg = self.sb("kg", [128, 512], BF16)
            P.dve(lambda e: e.tensor_tensor(out=kg[:T, :], in0=psk[:T, :], in1=encum[:T, :], op=ALU.mult), r=[pkk, "encum"], w=["kg"])
        if self.dbg_stop <= 5:
            return
        vb = self.sb("vb", [128, DI], BF16)
        for b in range(4):
            psv, pkv = self.proj_block(hT, T, 1024 + b * 512, 1536 + b * 512)
            P.act(lambda e, b=b, psv=psv: e.activation(out=vb[:T, b * 512:(b + 1) * 512], in_=psv[:T, :], func=AF.Copy),
                  r=[pkv], w=[("vb", b)])
        if not state_only:
            sr = self.sb("sr", [128, DI], BF16)
            for b in range(4):
                psr2, pkr2 = self.proj_block(hT, T, 3072 + b * 512, 3584 + b * 512)
                P.act(lambda e, b=b, psr2=psr2: e.activation(out=sr[:T, b * 512:(b + 1) * 512], in_=psr2[:T, :], func=AF.Silu),
                      r=[pkr2], w=[("sr", b)])
            if self.dbg_stop <= 6:
                return
            qT = self.sb("qT", [128, 4, 128], BF16)
            kT = self.sb("kT", [128, 4, 128], BF16)
            pt, ptk = self.psum_t()
            for h in range(4):
                P.pe(lambda e, h=h: e.transpose(out=pt[:, h * 128:h * 128 + T], in_=qg[:T, h * 128:(h + 1) * 128], identity=self.identb[:T, :T]),
                     r=["qg", "identb"], w=[ptk])
                P.pe(lambda e, h=h: e.transpose(out=pt[:, 512 + h * 128:512 + h * 128 + T], in_=kg[:T, h * 128:(h + 1) * 128],
                                                identity=self.identb[:T, :T]), r=["kg", "identb"], w=[ptk])
            ptv = pt[:].rearrange("p (k t) -> p k t", k=8)
            P.dve(lambda e: e.tensor_copy(out=qT[:, :, :T], in_=ptv[:, 0:4, :T]), r=[ptk], w=["qT"])
            P.dve(lambda e: e.tensor_copy(out=kT[:, :, :T], in_=ptv[:, 4:8, :T]), r=[ptk], w=["kT"])
            if self.dbg_stop <= 7:
                return
            psa, pka = self.psum()
            for h in range(4):
                P.pe(lambda e, h=h: e.matmul(psa[:T, h * 128:h * 128 + T], lhsT=kT[:, h, :T], rhs=qT[:, h, :T], start=True, stop=True),
                     r=["qT", "kT"], w=[pka])
            attT = self.sb("attT", [128, 4, 128], BF16)
            P.dve(lambda e: e.tensor_tensor(out=attT[:T, :, :T], in0=psa[:T, :].rearrange("p (h t) -> p h t", h=4)[:, :, :T],
                                            in1=bc_mid(M["LE"][:T, :T], 4), op=ALU.mult), r=[pka, f"m_{kind}_LE"], w=["attT"])
            if self.dbg_stop <= 8:
                return
            pso = [self.psum(hold=True) for _ in range(4)]
            for h in range(4):
                P.pe(lambda e, h=h: e.matmul(pso[h][0][:T, :], lhsT=attT[:T, h, :T], rhs=vb[:T, h * 512:(h + 1) * 512], start=True, stop=False),
                     r=["attT", ("vb", h)], w=[pso[h][1]])
        if self.dbg_stop <= 9:
            return
        for si, sq in enumerate(seqs):
            j = sq["j"]
            if sq.get("load") is not None:
                sq["load"]()
            S, Sk, Sb, Sbk = sq["S"], sq["Skey"], sq["Sb"], sq["Sbkey"]
            last = si == nseq - 1
            if not state_only:
                if sq["masked"]:
                    qTm = self.sb("qTm", [128, 4, 128], BF16)
                    self.masked_cols(qTm, "qTm", qT, "qT", si, T)
                    qsrc, qk = qTm, "qTm"
                else:
                    qsrc, qk = qT, "qT"
                for h in range(4):
                    P.pe(lambda e, h=h, qsrc=qsrc, Sb=Sb, last=last: e.matmul(pso[h][0][:T, :], lhsT=qsrc[:, h, :T], rhs=Sb[:, h, :],
                                                                             start=False, stop=last),
                         r=[qk, Sbk], w=[pso[h][1]])
            if sq["masked"]:
                khm = self.sb("khm", [128, 512], BF16)
                P.dve(lambda e, j=j: e.tensor_scalar(out=khm[:T, :], in0=kh[:T, :], scalar1=M["seg"][:T, j:j + 1], scalar2=None, op0=ALU.mult),
                      r=["kh", "m_s_seg"], w=["khm"])
                ksrc, kk = khm, "khm"
            else:
                ksrc, kk = kh, "kh"
            for h in range(4):
                psu, pku = self.psum()
                P.pe(lambda e, h=h, psu=psu, ksrc=ksrc: e.matmul(psu[:, :], lhsT=ksrc[:T, h * 128:(h + 1) * 128], rhs=vb[:T, h * 512:(h + 1) * 512],
                                                                start=True, stop=True), r=[kk, ("vb", h)], w=[pku])
                P.dve(lambda e, h=h, psu=psu, S=S, j=j: e.scalar_tensor_tensor(out=S[:, h, :], in0=S[:, h, :], scalar=elast[:, h, j:j + 1],
                                                                                in1=psu[:, :], op0=ALU.mult, op1=ALU.add),
                      r=[pku, "elast", Sk], w=[Sk])
            if sq.get("done") is not None:
                sq["done"]()
        if state_only:
            return
        if self.dbg_stop <= 10:
            for h in range(4):
                self.psum_release(pso[h][1])
            return
        st = self.sb("ostat", [128, 12])
        junk = self.sb("junk", [128, DI], BF16)
        for h in range(4):
            P.act(lambda e, h=h: e.activation(out=junk[:T, h * 512:(h + 1) * 512], in_=pso[h][0][:T, :], func=AF.Square, accum_out=st[:T, h:h + 1]),
                  r=[pso[h][1]], w=[("junk", h), ("ostat", h)])
        P.act(lambda e: e.activation(out=st[:T, 4:8], in_=st[:T, 0:4], func=AF.Sqrt, scale=1.0 / 512, bias=self.epsc[:T, 0:1]),
              r=["ostat", "epsc"], w=[("ostat", 4)])
        P.dve(lambda e: e.reciprocal(out=st[:T, 8:12], in_=st[:T, 4:8]), r=[("ostat", 4)], w=[("ostat", 8)])
        og = self.sb("og", [128, DI], BF16)
        for h in range(4):
            on = self.sb(f"on{h % 2}", [128, 512])
            onk = f"on{h % 2}"
            P.dve(lambda e, h=h, on=on: e.scalar_tensor_tensor(out=on[:T, :], in0=pso[h][0][:T, :], scalar=st[:T, 8 + h:9 + h],
                                                               in1=self.hnb[:T, :], op0=ALU.mult, op1=ALU.mult),
                  r=[pso[h][1], ("ostat", 8), "hnb"], w=[onk])
            self.psum_release(pso[h][1])
            P.dve(lambda e, h=h, on=on: e.tensor_tensor(out=og[:T, h * 512:(h + 1) * 512], in0=on[:T, :],
                                                        in1=sr[:T, h * 512:(h + 1) * 512], op=ALU.mult),
                  r=[onk, ("sr", h)], w=[("og", h)])
        self.out_proj_residual(og, "og", xt, xkey, T)

    def run_layer_gla(self, l, first, lastl):
        P, d, cfg = self.P, self.d, self.cfg
        NT, NS, TS = cfg.NT, cfg.NS, cfg.TS
        self.gla_setup(l)
        sg_in = d["sg0"] if l == 0 else d["sg3"]
        out_p = d["gla0_p"] if l == 0 else d["gla3_p"]
        out_s = d["gla0_s"] if l == 0 else d["gla3_s"]
        S = [self.sb("gS0", [128, 4, 512])]
        Sb = [self.sb("gSb0", [128, 4, 512], BF16)]
        P.dve(lambda e: e.memset(S[0][:], 0.0), w=["gS0"])
        P.dve(lambda e: e.memset(Sb[0][:], 0.0), w=["gSb0"])
        if cfg.G > 1:
            Dt = self.sb("Dtot", [128, 32])
            P.dve(lambda e: e.memset(Dt[:], 1.0), w=["Dtot"])
            for ti in range(NT):
                xt = self.load_x(first, ti, 0)
                seqs = [dict(j=0, S=S[0], Skey="gS0", Sb=Sb[0], Sbkey="gSb0", masked=False)]
                self.gla_tile(l, ti, xt, "xt0", 128, "p", True, seqs)
                el = self.bufs["elast"]
                P.dve(lambda e, el=el: e.tensor_tensor(out=Dt[:, 0:4], in0=Dt[:, 0:4], in1=el[:, :, 0], op=ALU.mult), r=["Dtot", "elast"], w=["Dtot"])
            self.state_combine(f"g{l}", S[0][:].rearrange("p a b -> p (a b)"), "gS0", Dt[:, 0:4], "Dtot", 4)
            P.act(lambda e: e.activation(out=Sb[0][:], in_=S[0][:], func=AF.Copy), r=["gS0"], w=["gSb0"])
        ntiles = NT + 1
        for ti in range(ntiles):
            T = 128 if ti < NT else TS
            xkey = "xt0"
            xt = self.load_x(first, ti, 0)
            if ti < NT:
                def done(S0=S[0], Sb0=Sb[0]):
                    P.act(lambda e: e.activation(out=Sb0[:], in_=S0[:], func=AF.Copy), r=["gS0"], w=["gSb0"])
                seqs = [dict(j=0, S=S[0], Skey="gS0", Sb=Sb[0], Sbkey="gSb0", masked=False, done=done)]
                self.gla_tile(l, ti, xt, xkey, T, "p", False, seqs)
                if ti == NT - 1:
                    P.dma("pool", out_p.rearrange("h k v -> k h v"), S[0][:], r=["gS0"], semkey="gst_p", final=True)
            else:
                seqs = []
                for j in range(NS):
                    sl = 0
                    def load(j=j, sl=sl):
                        P.dma("sp", S[sl][:], sg_in[j].rearrange("h k v -> k h v"), w=[f"gS{sl}"], semkey=("gsl", sl))
                        P.act(lambda e: e.activation(out=Sb[sl][:], in_=S[sl][:], func=AF.Copy), r=[f"gS{sl}"], w=[f"gSb{sl}"])
                    def done(j=j, sl=sl):
                        P.dma("pool", out_s[j].rearrange("h k v -> k h v"), S[sl][:], r=[f"gS{sl}"], semkey=("gss", sl), final=True)
                    seqs.append(dict(j=j, S=S[sl], Skey=f"gS{sl}", Sb=Sb[sl], Sbkey=f"gSb{sl}", masked=True, load=load, done=done))
                self.gla_tile(l, ti, xt, xkey, T, "s", False, seqs)
            self.store_x(xt, xkey, ti, T, lastl)


    def ssd_setup(self):
        P, d, cfg = self.P, self.d, self.cfg
        NS, TS = cfg.NS, cfg.TS
        P.dma("pool", d["cwb"][:, 0:4 * 3072], d["l1_conv_w"].rearrange("j c -> (j c)").partition_broadcast(128),
              w=["cwb_d"], semkey="cwbd")
        P.dma("pool", d["cwb"][:, 4 * 3072:5 * 3072], d["l1_conv_b"].partition_broadcast(128), w=["cwb_d2"], semkey="cwbd2")
        cbrow = None
        onesb = self.sb("ones_row_b", [1, 128], BF16)
        P.dve(lambda e: e.memset(onesb[:], 1.0), w=["ones_row_b"])
        dtb = self.sb("dtb", [128, 32])
        P.dma("sp", dtb[:], d["l1_dt_bias"].partition_broadcast(128), w=["dtb"], semkey="dtb")
        aneg = self.sb("aneg", [128, 32])
        P.dma("sp", aneg[:], d["l1_a_log"].partition_broadcast(128), w=["aneg"], semkey="aneg")
        P.act(lambda e: e.activation(out=aneg[:], in_=aneg[:], func=AF.Exp), r=["aneg"], w=["aneg"])
        P.dve(lambda e: e.tensor_scalar(out=aneg[:], in0=aneg[:], scalar1=-1.0, scalar2=None, op0=ALU.mult), r=["aneg"], w=["aneg"])
        dsk = self.sb("dsk", [128, 32])
        P.dma("sp", dsk[:], d["l1_d_skip"].partition_broadcast(128), w=["dsk"], semkey="dsk")
        gcol = self.sb("gcol", [128, 16])
        P.dma("sp", gcol[:], d["l1_gate_norm"].rearrange("(rc p) -> p rc", p=128), w=["gcol"], semkey="gcol",
              allow_slow_non_contiguous=True)
        for rc in range(16):
            P.dve(lambda e, rc=rc: e.tensor_scalar(out=self.wout[:, rc, :], in0=self.wout[:, rc, :], scalar1=gcol[:, rc:rc + 1],
                                                   scalar2=None, op0=ALU.mult), r=[("w_out", rc // 4), "gcol"], w=[("w_out", rc // 4)])
        self.Sh = {}
        for kind, T, K in (("p", 128, 128), ("s", TS, NS * 3)):
            sh = self.sb(f"m_{kind}_Sh", [T, 3, T], BF16)
            P.dma("pool", sh[:], d[f"c_{kind}_Sh"][:, :, :], w=[f"m_{kind}_Sh"], semkey=f"c_{kind}_Sh")
            shp = self.sb(f"m_{kind}_ShP", [128, 3, T], BF16)
            P.dma("pool", shp[:K], d[f"c_{kind}_ShP"][:, :, :], w=[f"m_{kind}_ShP"], semkey=f"c_{kind}_ShP")
            self.Sh[kind] = (sh, shp)
        self.cbrow, self.onesb, self.dtb, self.aneg, self.dsk = cbrow, onesb, dtb, aneg, dsk

    def ssd_tile(self, ti, xt, xkey, T, kind, state_only, seqs, rows, conv_out):
        P, d, cfg = self.P, self.d, self.cfg
        M = self.M[kind]
        sh, shp = self.Sh[kind]
        nseq = len(seqs)
        r0, r1 = rows
        hT = self.norm_transpose(xt, xkey, T)
        if self.dbg_stop <= 20:
            return
        xs = self.sb("vb", [128, DI], BF16)
        Bb = self.sb("qg", [128, 512], BF16)
        Cb = self.sb("kg", [128, 512], BF16)
        xtail = self.sb("xtail", [128, 3072], BF16)
        ytail = self.sb("ytail", [128, 3, 512], BF16)
        cwblk = self.sb("cwblk", [128, 5, 512], BF16)
        Yblk = self.sb("ogT", [128, 16, 128], BF16)[:].rearrange("p a b -> p (a b)").rearrange("p (j c) -> p j c", j=4)
        cwsrc = d["cwb"].rearrange("p (j c) -> p j c", j=5)
        nblk = 5 if state_only else 6
        for blk in range(nblk):
            c0 = 2048 + blk * 512
            ps, pk = self.proj_block(hT, T, c0, c0 + 512)
            P.dma("sp", cwblk[:, :, :], cwsrc[:, :, blk * 512:(blk + 1) * 512], r=["cwb_d", "cwb_d2"], w=["cwblk"], semkey="cwblk")
            psb = AP(ps[:T, :].tensor, ps[:T, :].offset, [list(ps[:T, :].ap[0]), [0, 4], list(ps[:T, :].ap[1])])
            P.dve(lambda e, psb=psb: e.tensor_tensor(out=Yblk[:T, :, :], in0=psb, in1=cwblk[:T, 0:4, :], op=ALU.mult),
                  r=[pk, "cwblk"], w=["ogT"])
            if self.dbg_stop <= 31:
                return
            xtb = xtail[r0:r1, blk * 512:(blk + 1) * 512]
            xtb3 = AP(xtb.tensor, xtb.offset, [list(xtb.ap[0]), [0, 3], list(xtb.ap[1])])
            P.dve(lambda e, xtb3=xtb3: e.tensor_tensor(out=ytail[r0:r1, :, :], in0=xtb3, in1=cwblk[r0:r1, 0:3, :], op=ALU.mult),
                  r=[("xtail", blk), "cwblk"], w=["ytail"])
            if self.dbg_stop <= 32:
                return
            if conv_out is not None:
                stg = self.sb(f"on{blk % 2}", [128, 512])
                P.dve(lambda e, ps=ps, stg=stg: e.tensor_copy(out=stg[:T, :], in_=ps[:T, :]), r=[pk], w=[f"on{blk % 2}"])
                conv_out(stg, f"on{blk % 2}", blk)
            if kind == "p":
                P.dve(lambda e, ps=ps, blk=blk: e.tensor_copy(out=xtail[64:128, blk * 512:(blk + 1) * 512], in_=ps[64:128, :]),
                      r=[pk, "ytail"], w=[("xtail", blk)])
            if self.dbg_stop <= 33:
                return
            pc, pck = self.psum()
            for j in range(3):
                P.pe(lambda e, j=j, pc=pc: e.matmul(pc[:T, :], lhsT=sh[:T, j, :T], rhs=Yblk[:T, j, :], start=(j == 0), stop=False),
                     r=["ogT", f"m_{kind}_Sh"], w=[pck])
            P.pe(lambda e, pc=pc: e.matmul(pc[:T, :], lhsT=self.identb[:T, :T], rhs=Yblk[:T, 3, :], start=False, stop=False),
                 r=["ogT", "identb"], w=[pck])
            if self.dbg_stop <= 34:
                return
            for j in range(3):
                P.pe(lambda e, j=j, pc=pc: e.matmul(pc[:T, :], lhsT=shp[r0:r1, j, :T], rhs=ytail[r0:r1, j, :], start=False, stop=False),
                     r=["ytail", f"m_{kind}_ShP"], w=[pck])
            if self.dbg_stop <= 35:
                return
            P.pe(lambda e, pc=pc, blk=blk: e.matmul(pc[:T, :], lhsT=self.onesb[0:1, :T], rhs=cwblk[0:1, 4, :],
                                                    start=False, stop=True), r=["ones_row_b", "cwblk"], w=[pck])
            if blk < 4:
                P.act(lambda e, pc=pc, blk=blk: e.activation(out=xs[:T, blk * 512:(blk + 1) * 512], in_=pc[:T, :], func=AF.Silu),
                      r=[pck], w=[("vb", blk)])
            elif blk == 4:
                P.act(lambda e, pc=pc: e.activation(out=Bb[:T, :], in_=pc[:T, :], func=AF.Silu), r=[pck], w=["qg"])
            else:
                P.act(lambda e, pc=pc: e.activation(out=Cb[:T, :], in_=pc[:T, :], func=AF.Silu), r=[pck], w=["kg"])
        if self.dbg_stop <= 41:
            return
        sm = self.sb("ssm_small", [128, 8, 32])
        psd, pkd = self.proj_block(hT, T, 5120, 5152)
        P.dve(lambda e: e.tensor_tensor(out=sm[:T, 4, :], in0=psd[:T, 0:32], in1=self.dtb[:T, :], op=ALU.add), r=[pkd, "dtb"], w=[("sm", 4)])
        P.act(lambda e: e.activation(out=sm[:T, 4, :], in_=sm[:T, 4, :], func=AF.Exp), r=[("sm", 4)], w=[("sm", 4)])
        P.act(lambda e: e.activation(out=sm[:T, 0, :], in_=sm[:T, 4, :], func=AF.Ln, bias=self.onec[:T, 0:1]), r=[("sm", 4), "onec"], w=[("sm", 0)])
        P.dve(lambda e: e.tensor_tensor(out=sm[:T, 1, :], in0=sm[:T, 0, :], in1=self.aneg[:T, :], op=ALU.mult), r=[("sm", 0), "aneg"], w=[("sm", 1)])
        la = sm[:T, 1, :]
        psr, pkr = self.psum()
        P.pe(lambda e: e.matmul(psr[:T, 0:32], lhsT=M["GT"][:T, :T], rhs=la, start=True, stop=True), r=[("sm", 1), f"m_{kind}_GT"], w=[pkr])
        P.act(lambda e: e.activation(out=sm[:T, 2, :], in_=psr[:T, 0:32], func=AF.Exp), r=[pkr], w=[("sm", 2)])
        P.dve(lambda e: e.tensor_tensor(out=sm[:T, 2, :], in0=sm[:T, 2, :], in1=sm[:T, 0, :], op=ALU.mult), r=[("sm", 2), ("sm", 0)], w=[("sm", 2)])
        if not state_only:
            psc, pkc = self.psum()
            P.pe(lambda e: e.matmul(psc[:T, 0:32], lhsT=M["LE"][:T, :T], rhs=la, start=True, stop=True), r=[("sm", 1), f"m_{kind}_LE"], w=[pkc])
            P.act(lambda e: e.activation(out=sm[:T, 3, :], in_=psc[:T, 0:32], func=AF.Exp), r=[pkc], w=[("sm", 3)])
        elb = self.sb("elb", [128, 16, 32])
        pse, pke = self.psum()
        for sq in seqs:
            j = sq["j"]
            sc = M["seg"][:T, j:j + 1]
            scb = AP(sc.tensor, sc.offset, [list(sc.ap[0]), [0, 128]])
            P.pe(lambda e, j=j, scb=scb: e.matmul(pse[:, j * 32:(j + 1) * 32], lhsT=scb, rhs=la, start=True, stop=True),
                 r=[("sm", 1), f"m_{kind}_seg"], w=[pke])
        P.act(lambda e: e.activation(out=elb[:, :nseq, :], in_=pse[:, 0:nseq * 32].rearrange("p (j h) -> p j h", h=32), func=AF.Exp),
              r=[pke], w=["elb"])
        if self.dbg_stop <= 42:
            return
        if not state_only:
            zs = self.sb("sr", [128, DI], BF16)
            for b in range(4):
                psz, pkz = self.proj_block(hT, T, b * 512, (b + 1) * 512)
                P.act(lambda e, b=b, psz=psz: e.activation(out=zs[:T, b * 512:(b + 1) * 512], in_=psz[:T, :], func=AF.Silu), r=[pkz], w=[("sr", b)])
            BT = self.sb("qT", [128, 4, 128], BF16)
            CT = self.sb("kT", [128, 4, 128], BF16)
            pt, ptk = self.psum_t()
            for g in range(4):
                P.pe(lambda e, g=g: e.transpose(out=pt[:, g * 128:g * 128 + T], in_=Bb[:T, g * 128:(g + 1) * 128], identity=self.identb[:T, :T]),
                     r=["qg", "identb"], w=[ptk])
                P.pe(lambda e, g=g: e.transpose(out=pt[:, 512 + g * 128:512 + g * 128 + T], in_=Cb[:T, g * 128:(g + 1) * 128],
                                                identity=self.identb[:T, :T]), r=["kg", "identb"], w=[ptk])
            ptv = pt[:].rearrange("p (k t) -> p k t", k=8)
            P.dve(lambda e: e.tensor_copy(out=BT[:, :, :T], in_=ptv[:, 0:4, :T]), r=[ptk], w=["qT"])
            P.dve(lambda e: e.tensor_copy(out=CT[:, :, :T], in_=ptv[:, 4:8, :T]), r=[ptk], w=["kT"])
            psa, pka = self.psum()
            for g in range(4):
                P.pe(lambda e, g=g: e.matmul(psa[:T, g * 128:g * 128 + T], lhsT=BT[:, g, :T], rhs=CT[:, g, :T], start=True, stop=True),
                     r=["qT", "kT"], w=[pka])
            cbT = self.sb("attT", [128, 4, 128], BF16)
            P.dve(lambda e: e.tensor_tensor(out=cbT[:T, :, :T], in0=psa[:T, :].rearrange("p (g t) -> p g t", g=4)[:, :, :T],
                                            in1=bc_mid(M["LE"][:T, :T], 4), op=ALU.mult), r=[pka, f"m_{kind}_LE"], w=["attT"])
            psy = [self.psum(hold=True) for _ in range(4)]
        if self.dbg_stop <= 43:
            for g in range(4):
                self.psum_release(psy[g][1])
            return
        uu = self.sb("junk", [128, DI], BF16)
        P.dve(lambda e: e.tensor_tensor(out=uu[:T, :].rearrange("p (h q) -> p h q", h=32), in0=xs[:T, :].rearrange("p (h q) -> p h q", h=32),
                                        in1=bc_last(sm[:T, 2, :], 64), op=ALU.mult), r=["vb", ("sm", 2)], w=["junk"])
        if self.dbg_stop <= 43.1:
            for g in range(4):
                self.psum_release(psy[g][1])
            return
        for si, sq in enumerate(seqs):
            j = sq["j"]
            if sq.get("load") is not None:
                sq["load"]()
            if self.dbg_stop <= 43.2:
                for g in range(4):
                    self.psum_release(psy[g][1])
                return
            ST, STk, STb, STbk = sq["S"], sq["Skey"], sq["Sb"], sq["Sbkey"]
            last = si == nseq - 1
            if not state_only:
                if sq["masked"]:
                    CTm = self.sb("qTm", [128, 4, 128], BF16)
                    self.masked_cols(CTm, "qTm", CT, "kT", si, T)
                    csrc, ck = CTm, "qTm"
                else:
                    csrc, ck = CT, "kT"
                for g in range(4):
                    P.pe(lambda e, g=g, csrc=csrc, STb=STb, si=si, last=last: e.matmul(psy[g][0][:T, :], lhsT=csrc[:, g, :T],
                                                                                      rhs=STb[:, g * 512:(g + 1) * 512],
                                                                                      start=(si == 0), stop=last),
                         r=[ck, STbk], w=[psy[g][1]])
            if self.dbg_stop <= 43.4:
                for g in range(4):
                    self.psum_release(psy[g][1])
                return
            if sq["masked"]:
                Bm = self.sb("khm", [128, 512], BF16)
                P.dve(lambda e, j=j: e.tensor_scalar(out=Bm[:T, :], in0=Bb[:T, :], scalar1=M["seg"][:T, j:j + 1], scalar2=None, op0=ALU.mult),
                      r=["qg", "m_s_seg"], w=["khm"])
                bsrc, bk = Bm, "khm"
            else:
                bsrc, bk = Bb, "qg"
            for g in range(4):
                psu, pku = self.psum()
                P.pe(lambda e, g=g, psu=psu, bsrc=bsrc: e.matmul(psu[:, :], lhsT=bsrc[:T, g * 128:(g + 1) * 128], rhs=uu[:T, g * 512:(g + 1) * 512],
                                                                start=True, stop=True), r=[bk, "junk"], w=[pku])
                stv = ST[:, g * 512:(g + 1) * 512].rearrange("p (h q) -> p h q", h=8)
                P.dve(lambda e, g=g, stv=stv, j=j: e.tensor_tensor(out=stv, in0=stv, in1=bc_last(elb[:, j, g * 8:(g + 1) * 8], 64), op=ALU.mult),
                      r=[STk, "elb"], w=[STk])
                P.dve(lambda e, g=g, psu=psu, ST=ST: e.tensor_tensor(out=ST[:, g * 512:(g + 1) * 512], in0=ST[:, g * 512:(g + 1) * 512],
                                                                     in1=psu[:, :], op=ALU.add), r=[pku, STk], w=[STk])
            if self.dbg_stop <= 43.6:
                for g in range(4):
                    self.psum_release(psy[g][1])
                return
            if sq.get("done") is not None:
                sq["done"]()
        if state_only:
            return
        if self.dbg_stop <= 44:
            for g in range(4):
                self.psum_release(psy[g][1])
            return
        og = self.sb("og", [128, DI], BF16)
        for g in range(4):
            P.dve(lambda e, g=g: e.tensor_tensor(out=og[:T, g * 512:(g + 1) * 512].rearrange("p (h q) -> p h q", h=8),
                                                 in0=psy[g][0][:T, :].rearrange("p (h q) -> p h q", h=8),
                                                 in1=bc_last(sm[:T, 3, g * 8:(g + 1) * 8], 64), op=ALU.mult),
                  r=[psy[g][1], ("sm", 3)], w=[("og", g)])
            self.psum_release(psy[g][1])
        if self.dbg_stop <= 45:
            return
        P.dve(lambda e: e.tensor_tensor(out=uu[:T, :].rearrange("p (h q) -> p h q", h=32), in0=xs[:T, :].rearrange("p (h q) -> p h q", h=32),
                                        in1=bc_last(sm[:T, 0, :], 64), op=ALU.mult), r=["vb", ("sm", 0)], w=["junk"])
        g4 = self.sb("g4", [128, 4, 512])
        laexp = g4[:, 0:2, :].rearrange("p a (b t) -> p (a b) t", t=128)
        Eg = self.sb("Eg", [128, 8, 128], BF16)
        Mg = self.sb("Mg", [128, 8, 128], BF16)
        st = self.sb("ostat", [128, 12])
        sq_junk = g4[:, 3, :]
        for g in range(4):
            P.dve(lambda e, g=g: e.tensor_tensor(out=laexp[:T, :, :T], in0=bc_last(sm[:T, 1, g * 8:(g + 1) * 8], T), in1=bc_mid(M["LE"][:T, :T], 8),
                                                 op=ALU.mult), r=[("sm", 1), f"m_{kind}_LE"], w=["spf", "erev"])
            nmm = 2 if T == 128 else 1
            for hb in range(nmm):
                psd2, pkd2 = self.psum()
                e0, e1 = (hb * 4, hb * 4 + 4) if nmm == 2 else (0, 8)
                P.pe(lambda e, psd2=psd2, e0=e0, e1=e1: e.matmul(psd2[:T, 0:(e1 - e0) * T].rearrange("p (a t) -> p a t", t=T),
                                                                 lhsT=M["GT"][:T, :T], rhs=laexp[:T, e0:e1, :T], start=True, stop=True),
                     r=["spf", "erev", f"m_{kind}_GT"], w=[pkd2])
                P.act(lambda e, psd2=psd2, e0=e0, e1=e1: e.activation(out=Eg[:T, e0:e1, :T],
                                                                      in_=psd2[:T, 0:(e1 - e0) * T].rearrange("p (a t) -> p a t", t=T), func=AF.Exp),
                      r=[pkd2], w=[("Eg", hb)])
            P.dve(lambda e, g=g: e.tensor_tensor(out=Mg[:T, :, :T], in0=Eg[:T, :, :T], in1=bc_mid(cbT[:T, g, :T], 8), op=ALU.mult),
                  r=["Eg", "attT"], w=["Mg"])
            pyi, pyik = self.psum()
            for eh in range(8):
                h = g * 8 + eh
                P.pe(lambda e, eh=eh, h=h, pyi=pyi: e.matmul(pyi[:T, eh * 64:(eh + 1) * 64], lhsT=Mg[:T, eh, :T], rhs=uu[:T, h * 64:(h + 1) * 64],
                                                             start=True, stop=True), r=["Mg", "junk"], w=[pyik])
            on = self.sb(f"on{g % 2}", [128, 512])
            onk = f"on{g % 2}"
            on2 = g4[:, 2, :]
            gs = slice(g * 512, (g + 1) * 512)
            P.dve(lambda e, on=on, pyi=pyi, gs=gs: e.tensor_tensor(out=on[:T, :], in0=pyi[:T, :], in1=og[:T, gs], op=ALU.add),
                  r=[pyik, ("og", g)], w=[onk])
            P.dve(lambda e, g=g, gs=gs: e.tensor_tensor(out=on2[:T, :].rearrange("p (h q) -> p h q", h=8),
                                                        in0=xs[:T, gs].rearrange("p (h q) -> p h q", h=8),
                                                        in1=bc_last(self.dsk[:T, g * 8:(g + 1) * 8], 64), op=ALU.mult),
                  r=[("vb", g), "dsk"], w=["ecum"])
            P.dve(lambda e, on=on: e.tensor_tensor(out=on[:T, :], in0=on[:T, :], in1=on2[:T, :], op=ALU.add), r=[onk, "ecum"], w=[onk])
            P.dve(lambda e, on=on, gs=gs: e.tensor_tensor(out=on[:T, :], in0=on[:T, :], in1=zs[:T, gs], op=ALU.mult), r=[onk, ("sr", g)], w=[onk])
            P.act(lambda e, on=on, g=g: e.activation(out=sq_junk[:T, :], in_=on[:T, :], func=AF.Square, accum_out=st[:T, g:g + 1]),
                  r=[onk], w=["encum", ("ostat", g)])
            P.act(lambda e, g=g: e.activation(out=st[:T, 4 + g:5 + g], in_=st[:T, g:g + 1], func=AF.Sqrt, scale=1.0 / 512, bias=self.epsc[:T, 0:1]),
                  r=[("ostat", g), "epsc"], w=[("ostat", 4 + g)])
            P.dve(lambda e, g=g: e.reciprocal(out=st[:T, 8 + g:9 + g], in_=st[:T, 4 + g:5 + g]), r=[("ostat", 4 + g)], w=[("ostat", 8 + g)])
            P.dve(lambda e, on=on, g=g, gs=gs: e.tensor_scalar(out=og[:T, gs], in0=on[:T, :], scalar1=st[:T, 8 + g:9 + g], scalar2=None, op0=ALU.mult),
                  r=[onk, ("ostat", 8 + g)], w=[("og", g)])
        if self.dbg_stop <= 46:
            return
        self.out_proj_residual(og, "og", xt, xkey, T)

    def run_layer_ssd(self, l, first, lastl):
        P, d, cfg = self.P, self.d, self.cfg
        NT, NS, TS = cfg.NT, cfg.NS, cfg.TS
        self.ssd_setup()
        ST = self.sb("gS0", [128, 4, 512])[:].rearrange("p a b -> p (a b)")
        STb = self.sb("gSb0", [128, 4, 512], BF16)[:].rearrange("p a b -> p (a b)")
        xtail = self.sb("xtail", [128, 3072], BF16)
        P.dve(lambda e: e.memset(ST, 0.0), w=["gS0"])
        P.dve(lambda e: e.memset(STb, 0.0), w=["gSb0"])
        P.dve(lambda e: e.memset(xtail[:, :], 0.0), w=["xtail"])
        if cfg.G > 1:
            xt = self.load_x(first, NT - 1, 0)
            hT = self.norm_transpose(xt, "xt0", 128)
            writes = []
            for blk in range(6):
                ps, pk = self.proj_block(hT, 128, 2048 + blk * 512, 2560 + blk * 512)
                so = self.sb(f"on{blk % 2}", [128, 512])
                P.dve(lambda e, ps=ps, so=so: e.tensor_copy(out=so[:, :], in_=ps[:, :]), r=[pk], w=[f"on{blk % 2}"])
                writes.append((blk * 512, (blk + 1) * 512, so[125:128, :], [f"on{blk % 2}"]))
                if blk == 0:
                    xin_cvp = self.dscr("xin_cv", [128, 256])
                    xout_cvp = self.dscr("xout_cv", [cfg.G * 128, 256])
                    xin_cv = xin_cvp.rearrange("p w -> (p w)")[0:9216].rearrange("(r c) -> r c", c=3072)
                    xout_cv = xout_cvp.rearrange("(g p) w -> g (p w)", g=cfg.G)[:, 0:9216].rearrange("g (r c) -> g r c", c=3072)
                P.dma("sp", xin_cv[:, blk * 512:(blk + 1) * 512], so[125:128, :], r=[f"on{blk % 2}"], w=[("xin_cv", blk)], semkey=("xi_cv", blk % 2))
            groups = [list(range(b * cfg.G, (b + 1) * cfg.G)) for b in range(cfg.B)]
            P.op("pool", lambda e: e.collective_compute("AllGather", ALU.bypass, replica_groups=groups, ins=[xin_cvp[:, :]], outs=[xout_cvp[:, :]]),
                 r=["xin_cv"], w=["xout_cv"], dma=True, semkey=("dma", "cc"), inc=1)

            def init_tail():
                for blk in range(6):
                    so = self.sb(f"on{blk % 2}", [128, 512])
                    for gg in range(cfg.G):
                        P.dma("sp", so[gg * 3:gg * 3 + 3, :], xout_cv[gg, :, blk * 512:(blk + 1) * 512], r=["xout_cv"], w=[f"on{blk % 2}"],
                              semkey=("xo_cv", blk % 2))
                    ps, pk = self.psum()
                    P.pe(lambda e, ps=ps, so=so: e.matmul(ps[:, :], lhsT=self.sel[:, :], rhs=so[0:cfg.G * 3, :], start=True, stop=True),
                         r=[f"on{blk % 2}", "sel"], w=[pk])
                    P.dve(lambda e, ps=ps, blk=blk: e.tensor_copy(out=xtail[64:128, blk * 512:(blk + 1) * 512], in_=ps[64:128, :]),
                          r=[pk], w=[("xtail", blk)])
            init_tail()
            Dt = self.sb("Dtot", [128, 32])
            P.dve(lambda e: e.memset(Dt[:], 1.0), w=["Dtot"])
            for ti in range(NT):
                xt = self.load_x(first, ti, 0)
                seqs = [dict(j=0, S=ST, Skey="gS0", Sb=STb, Sbkey="gSb0", masked=False)]
                self.ssd_tile(ti, xt, "xt0", 128, "p", True, seqs, (64, 128), None)
                elb = self.bufs["elb"]
                P.dve(lambda e, elb=elb: e.tensor_tensor(out=Dt[:, :], in0=Dt[:, :], in1=elb[:, 0, :], op=ALU.mult), r=["Dtot", "elb"], w=["Dtot"])
            self.state_combine("s1", ST, "gS0", Dt[:, :], "Dtot", 32)
            P.act(lambda e: e.activation(out=STb, in_=ST, func=AF.Copy), r=["gS0"], w=["gSb0"])
            init_tail()

        def st_load(src_seq):
            srcv = src_seq.rearrange("(c h2) q n -> (h2 q) c n", c=16)
            for cg in range(4):
                stg = self.sb(f"on{cg % 2}", [128, 512])
                P.dma("sp", stg[:].rearrange("p (c n) -> p c n", c=4), srcv[:, cg * 4:(cg + 1) * 4, :], w=[f"on{cg % 2}"], semkey=("stl", cg % 2))
                ps, pk = self.psum()
                for c in range(4):
                    P.pe(lambda e, c=c, ps=ps, stg=stg: e.transpose(out=ps[:, c * 128:(c + 1) * 128], in_=stg[:, c * 128:(c + 1) * 128],
                                                                    identity=self.identf[:, :]), r=[f"on{cg % 2}", "identf"], w=[pk])
                P.dve(lambda e, ps=ps, cg=cg: e.tensor_copy(out=ST[:, cg * 512:(cg + 1) * 512], in_=ps[:, :]), r=[pk], w=["gS0"])
                P.dve(lambda e, ps=ps, cg=cg: e.tensor_copy(out=STb[:, cg * 512:(cg + 1) * 512], in_=ps[:, :]), r=[pk], w=["gSb0"])

        def st_store(dst_seq, semname):
            dstv = dst_seq.rearrange("(c h2) q n -> (h2 q) c n", c=16)
            for cg in range(4):
                ps, pk = self.psum()
                for c in range(4):
                    cc = cg * 4 + c
                    P.pe(lambda e, c=c, cc=cc, ps=ps: e.transpose(out=ps[:, c * 128:(c + 1) * 128], in_=ST[:, cc * 128:(cc + 1) * 128],
                                                                  identity=self.identf[:, :]), r=["gS0", "identf"], w=[pk])
                stg = self.sb(f"on{cg % 2}", [128, 512])
                P.dve(lambda e, ps=ps, stg=stg: e.tensor_copy(out=stg[:, :], in_=ps[:, :]), r=[pk], w=[f"on{cg % 2}"])
                P.dma("pool", dstv[:, cg * 4:(cg + 1) * 4, :], stg[:].rearrange("p (c n) -> p c n", c=4), r=[f"on{cg % 2}"],
                      semkey=(semname, cg % 2), final=True)

        ntiles = NT + 1
        for ti in range(ntiles):
            T = 128 if ti < NT else TS
            xkey = "xt0"
            xt = self.load_x(first, ti, 0)
            if ti < NT:
                def done():
                    P.act(lambda e: e.activation(out=STb, in_=ST, func=AF.Copy), r=["gS0"], w=["gSb0"])
                seqs = [dict(j=0, S=ST, Skey="gS0", Sb=STb, Sbkey="gSb0", masked=False, done=done)]
                conv_out = None
                if ti == NT - 1:
                    def conv_out(stg, skey, blk):
                        P.dma("pool", d["conv1_p"][:, blk * 512:(blk + 1) * 512], stg[125:128, :], r=[skey], semkey=("cvo", skey), final=True)
                        if blk == 0:
                            self.dbg("stg0", stg[:, :], [skey], [128, 512])
                self.ssd_tile(ti, xt, xkey, T, "p", False, seqs, (64, 128), conv_out)
                if ti == NT - 1:
                    st_store(d["ssm1_p"], "sst_p")
            else:
                P.dma("pool", xtail[0:NS * 3, :], d["conv1"].rearrange("s r c -> (s r) c"), w=["xtail"], semkey="xtl_s")
                seqs = []
                for j in range(NS):
                    def load(j=j):
                        st_load(d["ssm1"][j])
                    def done(j=j):
                        st_store(d["ssm1_s"][j], "sst_s")
                    seqs.append(dict(j=j, S=ST, Skey="gS0", Sb=STb, Sbkey="gSb0", masked=True, load=load, done=done))
                def conv_out(stg, skey, blk):
                    P.dma("pool", d["cvs"][:, blk * 512:(blk + 1) * 512], stg[:TS, :], r=[skey], w=[("cvs", blk)], semkey=("cvo", skey))
                    P.dma("pool", d["conv1_s"][:, :, blk * 512:(blk + 1) * 512],
                          d["cvs"][:, blk * 512:(blk + 1) * 512].rearrange("(s t) c -> s t c", t=8)[:, 5:8, :],
                          r=[("cvs", blk)], semkey="fin2", final=True)
                self.ssd_tile(ti, xt, xkey, T, "s", False, seqs, (0, NS * 3), conv_out)
            self.store_x(xt, xkey, ti, T, lastl)


    def swa_setup(self):
        P, d, cfg = self.P, self.d, self.cfg
        NS, TS = cfg.NS, cfg.TS
        esink = self.sb("esink", [128, 32])
        P.dma("sp", esink[:], d["l2_sinks"].partition_broadcast(128), w=["esink"], semkey="esink")
        P.act(lambda e: e.activation(out=esink[:], in_=esink[:], func=AF.Exp), r=["esink"], w=["esink"])
        mas = self.sb("m_s_maskA", [128, TS])
        P.dma("sp", mas[:], d["c_s_maskA"][:, :], w=["m_s_maskA"], semkey="c_s_maskA")
        ma0 = self.sb("m_p_maskA0", [128, 128])
        P.dma("sp", ma0[:], d["c_p_maskA0"][:, :], w=["m_p_maskA0"], semkey="c_p_maskA0")
        zl = self.sb("zeros_b", [128, 128], BF16)
        P.dve(lambda e: e.memset(zl[:], 0.0), w=["zeros_b"])
        self.esink, self.mas, self.ma0, self.zl = esink, mas, ma0, zl

    def swa_views(self):
        cw = self.sb("cwblk", [128, 5, 512], BF16)
        yt = self.sb("ytail", [128, 3, 512], BF16)
        kT2 = [cw[:, i, :].rearrange("p (k t) -> p k t", k=4) for i in range(3)]
        vaug = [yt[:, i, 0:260].rearrange("p (k c) -> p k c", k=4) for i in range(3)]
        return kT2, vaug

    def swa_kv_prep(self, kvf, kvkey, T, slot):
        P = self.P
        kT2, vaug = self.swa_views()
        kd = self.sb("qg", [128, 512], BF16)
        kdv = kd[:T, :].rearrange("p (k r c) -> p k r c", k=4, r=2)
        kin = kvf[:T, 0:256].rearrange("p (k c) -> p k c", k=4)
        for r in range(2):
            P.dve(lambda e, r=r: e.tensor_copy(out=kdv[:, :, r, :], in_=kin), r=[kvkey], w=["qg"])
        P.dve(lambda e: e.tensor_copy(out=vaug[slot][:T, :, 0:64], in_=kvf[:T, 256:512].rearrange("p (k c) -> p k c", k=4)),
              r=[kvkey], w=[("ytail", slot)])
        P.dve(lambda e: e.memset(vaug[slot][:T, :, 64:65], 1.0), w=[("ytail", slot)])
        pt, ptk = self.psum_t()
        for k in range(4):
            P.pe(lambda e, k=k: e.transpose(out=pt[:, k * 128:k * 128 + T], in_=kd[:T, k * 128:(k + 1) * 128], identity=self.identb[:T, :T]),
                 r=["qg", "identb"], w=[ptk])
        P.dve(lambda e: e.tensor_copy(out=kT2[slot][:, :, :T], in_=pt[:, 0:512].rearrange("p (k t) -> p k t", k=4)[:, :, :T]),
              r=[ptk], w=[("cwblk", slot)])

    def swa_tile(self, ti, xt, xkey, T, kind, cur, prev, maskA, maskAkey, seqs_cache, kv_out):
        P, d, cfg = self.P, self.d, self.cfg
        M = self.M[kind]
        kT2, vaug = self.swa_views()
        hT = self.norm_transpose(xt, xkey, T)
        qb = self.sb("vb", [128, DI], BF16)
        for b in range(4):
            ps, pk = self.proj_block(hT, T, b * 512, (b + 1) * 512)
            P.act(lambda e, b=b, ps=ps: e.activation(out=qb[:T, b * 512:(b + 1) * 512], in_=ps[:T, :], func=AF.Copy, scale=0.125),
                  r=[pk], w=[("vb", b)])
        qT = self.sb("ogT", [128, 16, 128], BF16)
        for half in range(2):
            pt, ptk = self.psum_t()
            for jj in range(8):
                c = half * 8 + jj
                P.pe(lambda e, jj=jj, c=c, pt=pt: e.transpose(out=pt[:, jj * 128:jj * 128 + T], in_=qb[:T, c * 128:(c + 1) * 128],
                                                              identity=self.identb[:T, :T]), r=[("vb", c // 4), "identb"], w=[ptk])
            P.dve(lambda e, half=half, pt=pt: e.tensor_copy(out=qT[:, half * 8:half * 8 + 8, :T],
                                                            in_=pt[:].rearrange("p (k t) -> p k t", k=8)[:, :, :T]), r=[ptk], w=[("ogT", half)])
        psk, pkk = self.proj_block(hT, T, 2048, 2560)
        kvf = self.sb("on0", [128, 512])
        P.dve(lambda e: e.tensor_copy(out=kvf[:T, :], in_=psk[:T, :]), r=[pkk], w=["on0"])
        self.swa_kv_prep(kvf, "on0", T, cur)
        if kv_out is not None:
            kv_out(kvf, "on0")
        sg = self.sb("sr", [128, DI], BF16)
        for b in range(4):
            ps, pk = self.proj_block(hT, T, 2560 + b * 512, 3072 + b * 512)
            P.act(lambda e, b=b, ps=ps: e.activation(out=sg[:T, b * 512:(b + 1) * 512], in_=ps[:T, :], func=AF.Silu), r=[pk], w=[("sr", b)])
        og = self.sb("og", [128, DI], BF16)
        PA = self.sb("qT", [128, 4, 128], BF16)
        PB = self.sb("kT", [128, 4, 128], BF16)
        PAm = self.sb("qTm", [128, 4, 128], BF16)
        st = self.sb("swstat", [128, 8])
        for par in range(2):
            pbase = par * 64
            if seqs_cache is None:
                combos = [[kvh] for kvh in range(4)]
            else:
                combos = [[0, 1, 2, 3]]
            for cb in combos:
                pv = {}
                for kvh in cb:
                    pv[kvh] = self.psum(hold=True)
                    P.pe(lambda e, pv=pv, kvh=kvh: e.matmul(pv[kvh][0][:T, 0:260], lhsT=self.zl[:T, :T], rhs=vaug[cur][:T, :, :].rearrange("p k c -> p (k c)"),
                                                    start=True, stop=False), r=["zeros_b", ("ytail", cur)], w=[pv[kvh][1]])
                for kvh in cb:
                    qrhs = qT[pbase:pbase + 64, kvh * 4:kvh * 4 + 4, :T]
                    psb, pkb = self.psum()
                    klhs = kT2[cur][pbase:pbase + 64, kvh, :T]
                    P.pe(lambda e, kvh=kvh, psb=psb, qrhs=qrhs, klhs=klhs: e.matmul(psb[:T, 0:4 * T].rearrange("p (a t) -> p a t", a=4),
                                                                        lhsT=klhs, rhs=qrhs, start=True, stop=True),
                         r=[("cwblk", cur), "ogT"], w=[pkb])
                    P.act(lambda e, psb=psb: e.activation(out=PB[:T, :, :T], in_=psb[:T, 0:4 * T].rearrange("p (a t) -> p a t", a=4), func=AF.Exp),
                          r=[pkb], w=["kT"])
                    P.dve(lambda e: e.tensor_tensor(out=PB[:T, :, :T], in0=PB[:T, :, :T], in1=bc_mid(M["LE"][:T, :T], 4), op=ALU.mult),
                          r=["kT", f"m_{kind}_LE"], w=["kT"])
                    for i in range(4):
                        P.pe(lambda e, pv=pv, kvh=kvh, i=i: e.matmul(pv[kvh][0][:T, i * 65:(i + 1) * 65], lhsT=PB[:T, i, :T], rhs=vaug[cur][:T, kvh, :],
                                                             start=False, stop=False), r=["kT", ("ytail", cur)], w=[pv[kvh][1]])
                if seqs_cache is None:
                    kvh = cb[0]
                    qrhs = qT[pbase:pbase + 64, kvh * 4:kvh * 4 + 4, :T]
                    psa, pka = self.psum()
                    klhs = kT2[prev][pbase:pbase + 64, kvh, :]
                    P.pe(lambda e, kvh=kvh, psa=psa, qrhs=qrhs, klhs=klhs: e.matmul(psa[:, 0:4 * T].rearrange("p (a t) -> p a t", a=4),
                                                                        lhsT=klhs, rhs=qrhs, start=True, stop=True),
                         r=[("cwblk", prev), "ogT"], w=[pka])
                    P.act(lambda e, psa=psa: e.activation(out=PA[:, :, :T], in_=psa[:, 0:4 * T].rearrange("p (a t) -> p a t", a=4), func=AF.Exp),
                          r=[pka], w=["qT"])
                    P.dve(lambda e: e.tensor_tensor(out=PA[:, :, :T], in0=PA[:, :, :T], in1=bc_mid(maskA[:, :T], 4), op=ALU.mult),
                          r=["qT", maskAkey], w=["qT"])
                    for i in range(4):
                        P.pe(lambda e, pv=pv, kvh=kvh, i=i: e.matmul(pv[kvh][0][:T, i * 65:(i + 1) * 65], lhsT=PA[:, i, :T], rhs=vaug[prev][:, kvh, :],
                                                             start=False, stop=False), r=["qT", ("ytail", prev)], w=[pv[kvh][1]])
                else:
                    for si, sq in enumerate(seqs_cache):
                        sq["load"]()
                        for kvh in cb:
                            qrhs = qT[pbase:pbase + 64, kvh * 4:kvh * 4 + 4, 8 * si:8 * si + 8]
                            psa, pka = self.psum()
                            klhs = kT2[2][pbase:pbase + 64, kvh, :]
                            P.pe(lambda e, kvh=kvh, psa=psa, qrhs=qrhs, klhs=klhs: e.matmul(psa[:, 0:32].rearrange("p (a t) -> p a t", a=4),
                                                                                lhsT=klhs, rhs=qrhs, start=True, stop=True),
                                 r=[("cwblk", 2), "ogT"], w=[pka])
                            P.act(lambda e, psa=psa: e.activation(out=PA[:, :, 0:8], in_=psa[:, 0:32].rearrange("p (a t) -> p a t", a=4), func=AF.Exp),
                                  r=[pka], w=["qT"])
                            first_use = (si == 0 and kvh == cb[0])
                            if first_use:
                                P.dve(lambda e: e.memset(PAm[:, :, :T], 0.0), w=["qTm"])
                            elif si > 0 and kvh == cb[0]:
                                P.dve(lambda e, si=si: e.memset(PAm[:, :, 8 * (si - 1):8 * si], 0.0), w=["qTm"])
                            P.dve(lambda e, si=si: e.tensor_tensor(out=PAm[:, :, 8 * si:8 * si + 8], in0=PA[:, :, 0:8],
                                                                   in1=bc_mid(self.mas[:, 8 * si:8 * si + 8], 4), op=ALU.mult),
                                  r=["qT", "m_s_maskA"], w=["qTm"])
                            for i in range(4):
                                P.pe(lambda e, pv=pv, kvh=kvh, i=i: e.matmul(pv[kvh][0][:T, i * 65:(i + 1) * 65], lhsT=PAm[:, i, :T], rhs=vaug[2][:, kvh, :],
                                                                     start=False, stop=False), r=["qTm", ("ytail", 2)], w=[pv[kvh][1]])
                for kvh in cb:
                    P.pe(lambda e, pv=pv, kvh=kvh: e.matmul(pv[kvh][0][:T, 0:260], lhsT=self.zl[:T, :T], rhs=vaug[cur][:T, :, :].rearrange("p k c -> p (k c)"),
                                                    start=False, stop=True), r=["zeros_b", ("ytail", cur)], w=[pv[kvh][1]])
                    pvv = pv[kvh][0][:T, 0:260].rearrange("p (a c) -> p a c", a=4)
                    h0 = kvh * 8 + par
                    es = self.esink[:T, h0:h0 + 7:2]
                    P.dve(lambda e, pvv=pvv, es=es: e.tensor_tensor(out=st[:T, 0:4], in0=pvv[:, :, 64], in1=es, op=ALU.add),
                          r=[pv[kvh][1], "esink"], w=[("swstat", 0)])
                    P.dve(lambda e: e.reciprocal(out=st[:T, 4:8], in_=st[:T, 0:4]), r=[("swstat", 0)], w=[("swstat", 4)])
                    on = self.sb("on1", [128, 512])
                    onv = on[:T, 0:256].rearrange("p (a c) -> p a c", a=4)
                    P.dve(lambda e, pvv=pvv, onv=onv: e.tensor_tensor(out=onv, in0=pvv[:, :, 0:64], in1=bc_last(st[:T, 4:8], 64), op=ALU.mult),
                          r=[pv[kvh][1], ("swstat", 4)], w=["on1"])
                    self.psum_release(pv[kvh][1])
                    c0 = (kvh * 8 + par) * 64
                    ogv = AP(og[:T, c0:c0 + 64].tensor, og[:T, c0:c0 + 64].offset, [list(og[:T, c0:c0 + 64].ap[0]), [128, 4], [1, 64]])
                    sgv = AP(sg[:T, c0:c0 + 64].tensor, sg[:T, c0:c0 + 64].offset, [list(sg[:T, c0:c0 + 64].ap[0]), [128, 4], [1, 64]])
                    P.dve(lambda e, ogv=ogv, sgv=sgv, onv=onv: e.tensor_tensor(out=ogv, in0=onv, in1=sgv, op=ALU.mult),
                          r=["on1", "sr"], w=["og"])
        self.out_proj_residual(og, "og", xt, xkey, T)

    def run_layer_swa(self, l, first, lastl):
        P, d, cfg = self.P, self.d, self.cfg
        NT, NS, TS = cfg.NT, cfg.NS, cfg.TS
        self.swa_setup()
        kT2, vaug = self.swa_views()
        cw = self.sb("cwblk", [128, 5, 512], BF16)
        yt = self.sb("ytail", [128, 3, 512], BF16)
        P.dve(lambda e: e.memset(cw[:], 0.0), w=["cwblk"])
        P.dve(lambda e: e.memset(yt[:], 0.0), w=["ytail"])
        if cfg.G > 1:
            xt = self.load_x(first, NT - 1, 0)
            hT = self.norm_transpose(xt, "xt0", 128)
            psk, pkk = self.proj_block(hT, 128, 2048, 2560)
            kvf = self.sb("on0", [128, 512])
            P.dve(lambda e: e.tensor_copy(out=kvf[:, :], in_=psk[:, :]), r=[pkk], w=["on0"])
            xout, xk = self.allgather("kv", 128, 512, [(0, 512, kvf[:, :], ["on0"])])
            P.dve(lambda e: e.memset(kvf[:, :], 0.0), r=[("xin_kv", 0)], w=["on0"])
            cand = self.sb("on1", [128, 512])
            for j in range(cfg.G):
                P.dma("sp", cand[:, :], xout[j * 128:(j + 1) * 128, :], r=[xk], w=["on1"], semkey="xkv")
                P.dve(lambda e, j=j: e.scalar_tensor_tensor(out=kvf[:, :], in0=cand[:, :], scalar=self.oh[:, j:j + 1], in1=kvf[:, :],
                                                            op0=ALU.mult, op1=ALU.add), r=["on1", "on0", "oh"], w=["on0"])
            self.swa_kv_prep(kvf, "on0", 128, 1)
        ntiles = NT + 1
        for ti in range(ntiles):
            T = 128 if ti < NT else TS
            xkey = "xt0"
            xt = self.load_x(first, ti, 0)
            if ti < NT:
                cur, prev = ti % 2, 1 - ti % 2
                if ti == 0:
                    maskA, mk = self.ma0, "m_p_maskA0"
                else:
                    maskA, mk = self.M["p"]["GT"], "m_p_GT"
                kv_out = None
                if ti == NT - 1:
                    def kv_out(kvf, key):
                        P.dma("pool", d["k2_p"][:, :], kvf[:, 0:256], r=[key], semkey="k2p", final=True)
                        P.dma("pool", d["v2_p"][:, :], kvf[:, 256:512], r=[key], semkey="v2p", final=True)
                self.swa_tile(ti, xt, xkey, T, "p", cur, prev, maskA, mk, None, kv_out)
            else:
                seqs = []
                for j in range(NS):
                    def load(j=j):
                        cst = self.sb("on1", [128, 512])
                        P.dma("sp", cst[:, 0:256], d["kc2"][j], w=["on1"], semkey="kc2l")
                        P.dma("sp", cst[:, 256:512], d["vc2"][j], w=["on1"], semkey="vc2l")
                        self.swa_kv_prep(cst, "on1", 128, 2)
                    seqs.append(dict(j=j, load=load))
                def kv_out(kvf, key):
                    P.dma("pool", d["kvs"][:, :], kvf[:TS, :], r=[key], w=["kvs"], semkey="kvs")
                    kvv = d["kvs"].rearrange("(s t) c -> s t c", t=8)
                    P.dma("pool", d["k2_s"][:, 120:128, :], kvv[:, :, 0:256], r=["kvs"], semkey="fin2", final=True)
                    P.dma("pool", d["v2_s"][:, 120:128, :], kvv[:, :, 256:512], r=["kvs"], semkey="fin2", final=True)
                    P.dma("pool", d["k2_s"][:, 0:120, :], d["kc2"][:, 8:128, :], semkey="fin2", final=True)
                    P.dma("pool", d["v2_s"][:, 0:120, :], d["vc2"][:, 8:128, :], semkey="fin2", final=True)
                self.swa_tile(ti, xt, xkey, T, "s", 0, 1, None, None, seqs, kv_out)
            self.store_x(xt, xkey, ti, T, lastl)

    def build(self):
        cfg, P, d = self.cfg, self.P, self.d
        self.declare()
        self.setup_consts()
        self.epsc = self.sb("epsc", [128, 1])
        P.dve(lambda e: e.memset(self.epsc[:], EPS), w=["epsc"])
        self.onec = self.sb("onec", [128, 1])
        P.dve(lambda e: e.memset(self.onec[:], 1.0), w=["onec"])
        if cfg.G > 1:
            self.rank_consts()
        layers = cfg.layers
        for li, l in enumerate(layers):
            first = li == 0
            lastl = li == len(layers) - 1
            self.load_layer_weights(l)
            kind = LAYER_KIND[l]
            if kind == "gla":
                self.run_layer_gla(l, first, lastl)
            elif kind == "ssd":
                self.run_layer_ssd(l, first, lastl)
            elif kind == "swa":
                self.run_layer_swa(l, first, lastl)
            else:
                raise NotImplementedError(kind)
        P.emit(self.stack)
        return self.nc


_PROG_CACHE = {}


def _run(inputs, layers=(0, 1, 2, 3)):
    xp = np.asarray(inputs["x_prompt"], dtype=np.float32)
    xs = np.asarray(inputs["x_sample"], dtype=np.float32)
    B, SEQ, _ = xp.shape
    DB = xs.shape[0]
    G = NCORES // B if (NCORES % B == 0 and SEQ % ((NCORES // B) * 128) == 0) else 1
    if FORCE_G is not None:
        G = FORCE_G
    cfg = Cfg(B, SEQ, DB, layers, G=G)
    G, NT, NS, TS = cfg.G, cfg.NT, cfg.NS, cfg.TS
    key = (B, SEQ, DB, tuple(layers))
    if key not in _PROG_CACHE:
        bld = Builder(cfg)
        nc = bld.build()
        _PROG_CACHE[key] = (bld, nc)
    bld, nc = _PROG_CACHE[key]
    mp = make_masks(128, 128)
    ms = make_masks(TS, 8)
    colmask = np.ascontiguousarray(ms["seg"].T)
    shared = {}
    for k, v in inputs.items():
        if k.startswith("l") or k == "final_norm":
            shared[k] = np.ascontiguousarray(np.asarray(v, dtype=np.float32))
    shared["c_ident"] = np.eye(128, dtype=np.float32)
    shared["c_p_LE"], shared["c_p_GT"] = mp["LE"], mp["GT"]
    shared["c_s_LE"], shared["c_s_GT"] = ms["LE"], ms["GT"]
    shared["c_s_seg"] = ms["seg"]
    shared["c_p_Sh"], shared["c_s_Sh"] = mp["Sh"], ms["Sh"]
    shared["c_s_maskA"] = (np.arange(128)[:, None] > (np.arange(TS)[None, :] % 8)).astype(np.float32)
    shared["c_p_ShP"], shared["c_s_ShP"] = mp["ShP"], ms["ShP"]
    in_maps = []
    NP = NT * 128
    for c in range(NCORES):
        b, g = (c // G, c % G) if c < B * G else (0, 0)
        m = dict(shared)
        m["xp"] = np.ascontiguousarray(xp[b, g * NP:(g + 1) * NP, :])
        sl = slice(c * NS, (c + 1) * NS)
        m["xsamp"] = np.ascontiguousarray(xs[sl].reshape(TS, D))
        m["sg0"] = np.ascontiguousarray(inputs["state_gla_0"][sl])
        m["ssm1"] = np.ascontiguousarray(inputs["state_ssm_1"][sl])
        m["conv1"] = np.ascontiguousarray(inputs["state_conv_1"][sl])
        m["kc2"] = np.ascontiguousarray(np.asarray(inputs["cache_swa_k_2"][sl]).reshape(NS, 128, 256))
        m["vc2"] = np.ascontiguousarray(np.asarray(inputs["cache_swa_v_2"][sl]).reshape(NS, 128, 256))
        m["sg3"] = np.ascontiguousarray(inputs["state_gla_3"][sl])
        pm = np.zeros((1, 2 * G), np.float32)
        for j in range(G):
            pm[0, j] = 1.0 if j < g else 0.0
            pm[0, G + j] = 1.0 - pm[0, j]
        m["c_pm"] = pm
        ohv = np.zeros((1, G), np.float32)
        if g > 0:
            ohv[0, g - 1] = 1.0
        m["c_oh"] = ohv
        selv = np.zeros((G * 3, 128), np.float32)
        if g > 0:
            for r in range(3):
                selv[(g - 1) * 3 + r, 125 + r] = 1.0
        m["c_sel"] = selv
        m["c_p_maskA0"] = (mp["GT"] * (1.0 if g > 0 else 0.0)).astype(np.float32)
        in_maps.append(m)
    res = run_bass_kernel_spmd(nc, in_maps, core_ids=list(range(NCORES)))
    R = res.results
    last = [b * G + G - 1 for b in range(B)]
    y_prompt = np.stack([np.concatenate([R[b * G + g]["yp"] for g in range(G)], axis=0) for b in range(B)])
    y_sample = np.concatenate([R[c]["ysamp"].reshape(NS, 8, D) for c in range(NCORES)], axis=0)

    def pst(name, shape):
        return np.stack([R[c][name].reshape(shape) for c in last])

    def sst(name, shape):
        return np.concatenate([R[c][name].reshape((NS,) + shape) for c in range(NCORES)], axis=0)

    outs = (y_prompt, y_sample,
            pst("gla0_p", (4, 128, 512)), sst("gla0_s", (4, 128, 512)),
            pst("ssm1_p", (32, 64, 128)), sst("ssm1_s", (32, 64, 128)),
            pst("conv1_p", (3, 3072)), sst("conv1_s", (3, 3072)),
            pst("k2_p", (128, 4, 64)), sst("k2_s", (128, 4, 64)),
            pst("v2_p", (128, 4, 64)), sst("v2_s", (128, 4, 64)),
            pst("gla3_p", (4, 128, 512)), sst("gla3_s", (4, 128, 512)))
    return tuple(np.ascontiguousarray(o, dtype=np.float32) for o in outs)


def kernel(**inputs):
    return _run(inputs)
```

```python
import numpy as np
from contextlib import ExitStack
import concourse.bass as bass
import concourse.mybir as mybir
from concourse.ap import AP
from concourse.bass_utils import run_bass_kernel_spmd

F32 = mybir.dt.float32
BF16 = mybir.dt.bfloat16
AF = mybir.ActivationFunctionType
ALU = mybir.AluOpType

D = 1024
DI = 2048
EPS = 1e-6
GLA_IN = 5136
SSD_IN = 5152
SWA_IN = 4608
NCORES = 8
FORCE_G = None

ENGS = ("pe", "act", "dve", "pool", "sp")


def _conflict(a, b):
    n = min(len(a), len(b))
    return a[:n] == b[:n]


class Op:
    __slots__ = ("eng", "fn", "reads", "writes", "dma", "semkey", "inc", "deps",
                 "sem", "semval", "need_inc", "idx")


class Prog:
    def __init__(self, nc):
        self.nc = nc
        self.ops = []
        self.state = {}
        self.final_waits = []

    @staticmethod
    def _norm(keys):
        out = []
        for k in keys:
            if k is None:
                continue
            if not isinstance(k, tuple):
                k = (k,)
            out.append(k)
        return out

    def op(self, eng, fn, r=(), w=(), dma=False, semkey=None, inc=None, final=False):
        o = Op()
        o.eng = eng
        o.fn = fn
        o.reads = self._norm(r)
        o.writes = self._norm(w)
        o.dma = dma
        o.semkey = semkey
        o.inc = inc if inc is not None else (16 if dma else 1)
        o.idx = len(self.ops)
        o.need_inc = False
        deps = set()
        for k in o.reads:
            tab = self.state.setdefault(k[0], {})
            for kk, st in tab.items():
                if _conflict(k, kk):
                    if st[0] is not None:
                        deps.add(st[0])
                    if k[0] in ("ps", "pst"):
                        deps.update(r for r in st[1] if self.ops[r].eng != eng)
        for k in o.writes:
            tab = self.state.setdefault(k[0], {})
            for kk, st in tab.items():
                if _conflict(k, kk):
                    if st[0] is not None:
                        deps.add(st[0])
                    deps.update(st[1])
        for k in o.reads:
            tab = self.state[k[0]]
            if k not in tab:
                tab[k] = [None, []]
            tab[k][1].append(o.idx)
        for k in o.writes:
            tab = self.state[k[0]]
            for kk in [kk for kk in tab if len(kk) > len(k) and kk[:len(k)] == k]:
                del tab[kk]
            tab[k] = [o.idx, []]
        deps.discard(o.idx)
        keep = set()
        for d in deps:
            dop = self.ops[d]
            if (not dop.dma) and (not o.dma) and dop.eng == eng:
                raw = False
                for k in o.reads:
                    for kk in dop.writes:
                        if _conflict(k, kk):
                            raw = True
                if not raw:
                    continue
            keep.add(d)
        o.deps = sorted(keep)
        self.ops.append(o)
        if final:
            self.final_waits.append(o.idx)
        return o

    def pe(self, fn, r=(), w=(), **kw):
        return self.op("pe", fn, r, w, **kw)

    def act(self, fn, r=(), w=(), **kw):
        return self.op("act", fn, r, w, **kw)

    def dve(self, fn, r=(), w=(), **kw):
        return self.op("dve", fn, r, w, **kw)

    def pool(self, fn, r=(), w=(), **kw):
        return self.op("pool", fn, r, w, **kw)

    def dma(self, q, out, in_, r=(), w=(), semkey=None, final=False, **dkw):
        assert semkey is not None
        sk = ("dma",) + (tuple(semkey) if isinstance(semkey, tuple) else (semkey,))
        return self.op(q, lambda e: e.dma_start(out=out, in_=in_, **dkw), r, w,
                       dma=True, semkey=sk, final=final)

    def emit(self, stack):
        nc = self.nc
        ops = self.ops
        for o in ops:
            for d in o.deps:
                ops[d].need_inc = True
        for i in self.final_waits:
            ops[i].need_inc = True
        engsem = {}
        for e in ("pe", "act", "dve", "pool"):
            engsem[e] = stack.enter_context(nc.semaphore("sem_" + e))
        dmasem = {}
        cnt = {e: 0 for e in engsem}
        dcnt = {}
        for o in ops:
            if o.dma:
                if o.semkey not in dmasem:
                    dmasem[o.semkey] = stack.enter_context(
                        nc.semaphore("sd_" + "_".join(str(x) for x in o.semkey[1:])))
                    dcnt[o.semkey] = 0
                dcnt[o.semkey] += o.inc
                o.sem = dmasem[o.semkey]
                o.semval = dcnt[o.semkey]
                o.need_inc = True
            elif o.need_inc:
                cnt[o.eng] += 1
                o.sem = engsem[o.eng]
                o.semval = cnt[o.eng]
        self.nsems = len(engsem) + len(dmasem)
        self.counts = dict(cnt)
        streams = {e: [o for o in ops if o.eng == e] for e in ENGS}
        block = stack.enter_context(nc.Block())
        final_waits = self.final_waits

        def run_stream(e, eng):
            waited = {}
            issued = []
            for o in streams[e]:
                need = {}
                if e == "pool" and o.dma:
                    if len(issued) >= 2:
                        po = issued[-2]
                        need[po.sem.num] = (po.sem, po.semval)
                    issued.append(o)
                for d in o.deps:
                    dop = ops[d]
                    key = dop.sem.num
                    if key not in need or need[key][1] < dop.semval:
                        need[key] = (dop.sem, dop.semval)
                for key, (sem, val) in need.items():
                    if waited.get(key, 0) >= val:
                        continue
                    eng.wait_ge(sem, val)
                    waited[key] = val
                ins = o.fn(eng)
                if o.need_inc:
                    ins.then_inc(o.sem, o.inc)
            if e == "sp":
                need = {}
                for i in final_waits:
                    dop = ops[i]
                    key = dop.sem.num
                    if key not in need or need[key][1] < dop.semval:
                        need[key] = (dop.sem, dop.semval)
                for key, (sem, val) in need.items():
                    if waited.get(key, 0) >= val:
                        continue
                    eng.wait_ge(sem, val)

        @block.sync
        def _(eng):
            run_stream("sp", eng)

        @block.gpsimd
        def _(eng):
            run_stream("pool", eng)

        @block.scalar
        def _(eng):
            run_stream("act", eng)

        @block.vector
        def _(eng):
            run_stream("dve", eng)

        @block.tensor
        def _(eng):
            run_stream("pe", eng)


def bc_mid(ap2d, n):
    a = ap2d.ap
    return AP(ap2d.tensor, ap2d.offset, [list(a[0]), [0, n], list(a[1])])


def bc_last(ap2d, n):
    a = ap2d.ap
    return AP(ap2d.tensor, ap2d.offset, [list(a[0]), list(a[1]), [0, n]])


def make_masks(T, L):
    idx = np.arange(T)
    seq = idx // L
    same = seq[:, None] == seq[None, :]
    s = idx[:, None]
    t = idx[None, :]
    m = {}
    m["LE"] = (same & (s <= t)).astype(np.float32)
    m["GT"] = (same & (s > t)).astype(np.float32)
    nseq = T // L
    seg = (seq[:, None] == np.arange(nseq)[None, :]).astype(np.float32)
    m["seg"] = seg
    sh = np.zeros((T, 3, T), np.float32)
    for j in range(3):
        sh[:, j, :] = (same & (s == t + j - 3)).astype(np.float32)
    m["Sh"] = sh
    if L == 128:
        shp = np.zeros((128, 3, 128), np.float32)
        for j in range(3):
            for tt in range(3):
                if tt + j < 3:
                    shp[125 + tt + j, j, tt] = 1.0
        m["ShP"] = shp
    else:
        shp = np.zeros((nseq * 3, 3, T), np.float32)
        for j in range(3):
            for tt in range(T):
                q = tt % L
                if q + j < 3:
                    shp[(tt // L) * 3 + q + j, j, tt] = 1.0
        m["ShP"] = shp
    return m


class Cfg:
    def __init__(self, B, SEQ, DB, layers=(0, 1, 2, 3), G=1):
        assert B * G <= NCORES
        self.B = B
        self.G = G
        assert SEQ % (self.G * 128) == 0
        self.NT = SEQ // self.G // 128
        assert DB % NCORES == 0
        self.NS = DB // NCORES
        self.TS = self.NS * 8
        assert self.TS <= 128
        self.layers = tuple(layers)
        self.SEQ = SEQ
        self.DB = DB


LAYER_KIND = {0: "gla", 1: "ssd", 2: "swa", 3: "gla"}
LAYER_NIN = {0: GLA_IN, 1: SSD_IN, 2: SWA_IN, 3: GLA_IN}


class Builder:
    def __init__(self, cfg):
        self.cfg = cfg
        self.nc = bass.Bass("TRN2", target_bir_lowering=False)
        self.P = Prog(self.nc)
        self.stack = ExitStack()
        self.d = {}
        self.bufs = {}
        self.psrr = 0
        self.dbg_stop = 99
        self.dbg_on = False
        self.dbg_names = []
        self.held = set()

    def din(self, name, shape, dt=F32):
        self.d[name] = self.nc.dram_tensor(name, list(shape), dt, kind="ExternalInput").ap()
        return self.d[name]

    def dout(self, name, shape, dt=F32):
        self.d[name] = self.nc.dram_tensor(name, list(shape), dt, kind="ExternalOutput").ap()
        return self.d[name]

    def dscr(self, name, shape, dt=F32):
        self.d[name] = self.nc.dram_tensor(name, list(shape), dt).ap()
        return self.d[name]

    def sb(self, name, shape, dt=F32):
        if name in self.bufs:
            return self.bufs[name]
        t = self.stack.enter_context(self.nc.sbuf_tensor(name, list(shape), dt))
        self.bufs[name] = t
        return t

    def dbg(self, name, ap, rkeys, shape):
        if not getattr(self, "dbg_on", False):
            return
        o = self.dout("dbg_" + name, shape)
        self.P.dma("sp", o, ap, r=rkeys, semkey=("dbg", name), final=True)
        self.dbg_names.append("dbg_" + name)

    def psum(self, hold=False):
        for _ in range(len(self.psb)):
            i = self.psrr
            self.psrr = (self.psrr + 1) % len(self.psb)
            if i not in self.held:
                if hold:
                    self.held.add(i)
                return self.psb[i], ("ps", i)
        raise RuntimeError("no free PSUM bank")

    def psum_release(self, key):
        self.held.discard(key[1])

    def psum_t(self):
        i = self.pstrr
        self.pstrr = (self.pstrr + 1) % len(self.pst)
        return self.pst[i], ("pst", i)

    def declare(self):
        cfg = self.cfg
        NT, NS, TS, G = cfg.NT, cfg.NS, cfg.TS, cfg.G
        NP = NT * 128
        self.din("xp", [NP, D])
        self.din("xsamp", [TS, D])
        self.din("sg0", [NS, 4, 128, 512])
        self.din("ssm1", [NS, 32, 64, 128])
        self.din("conv1", [NS, 3, 3072])
        self.din("kc2", [NS, 128, 256])
        self.din("vc2", [NS, 128, 256])
        self.din("sg3", [NS, 4, 128, 512])
        for l in (0, 3):
            self.din(f"l{l}_norm", [D])
            self.din(f"l{l}_w_in", [D, GLA_IN])
            self.din(f"l{l}_w_gk2", [16, 512])
            self.din(f"l{l}_b_gk", [512])
            self.din(f"l{l}_head_norm", [512])
            self.din(f"l{l}_w_out", [DI, D])
        self.din("l1_norm", [D])
        self.din("l1_w_in", [D, SSD_IN])
        self.din("l1_conv_w", [4, 3072])
        self.din("l1_conv_b", [3072])
        self.din("l1_dt_bias", [32])
        self.din("l1_a_log", [32])
        self.din("l1_d_skip", [32])
        self.din("l1_gate_norm", [DI])
        self.din("l1_w_out", [DI, D])
        self.din("l2_norm", [D])
        self.din("l2_w_in", [D, SWA_IN])
        self.din("l2_sinks", [32])
        self.din("l2_w_out", [DI, D])
        self.din("final_norm", [D])
        self.din("c_ident", [128, 128])
        for kind, T in (("p", 128), ("s", TS)):
            self.din(f"c_{kind}_LE", [T, T])
            self.din(f"c_{kind}_GT", [T, T])
        self.din("c_s_seg", [TS, NS])
        self.din("c_p_Sh", [128, 3, 128])
        self.din("c_s_maskA", [128, TS])
        self.din("c_p_maskA0", [128, 128])
        self.din("c_s_Sh", [TS, 3, TS])
        self.din("c_p_ShP", [128, 3, 128])
        self.din("c_s_ShP", [NS * 3, 3, TS])
        self.din("c_pm", [1, 2 * G])
        self.din("c_oh", [1, G])
        self.din("c_sel", [G * 3, 128])
        self.dout("yp", [NP, D])
        self.dout("ysamp", [TS, D])
        self.dout("gla0_p", [4, 128, 512])
        self.dout("gla0_s", [NS, 4, 128, 512])
        self.dout("ssm1_p", [32, 64, 128])
        self.dout("ssm1_s", [NS, 32, 64, 128])
        self.dout("conv1_p", [3, 3072])
        self.dout("conv1_s", [NS, 3, 3072])
        self.dout("k2_p", [128, 256])
        self.dout("k2_s", [NS, 128, 256])
        self.dout("v2_p", [128, 256])
        self.dout("v2_s", [NS, 128, 256])
        self.dout("gla3_p", [4, 128, 512])
        self.dout("gla3_s", [NS, 4, 128, 512])
        self.dscr("xres", [NP + 128, D])
        self.dscr("cwb", [128, 5 * 3072], BF16)
        self.dscr("cvs", [TS, 3072])
        self.dscr("kvs", [TS, 512])

    def setup_consts(self):
        P, d, cfg = self.P, self.d, self.cfg
        NS, TS = cfg.NS, cfg.TS
        nc = self.nc
        self.psb = [self.stack.enter_context(nc.psum_tensor(f"ps{i}", [128, 512], F32)) for i in range(6)]
        self.pst = [self.stack.enter_context(nc.psum_tensor(f"pst{i}", [128, 1024], BF16)) for i in range(2)]
        self.pstrr = 0
        self.identf = self.sb("identf", [128, 128])
        self.identb = self.sb("identb", [128, 128], BF16)
        P.dma("sp", self.identf[:], d["c_ident"][:, :], w=["identf"], semkey="c0")
        P.dma("pool", self.identb[:], d["c_ident"][:, :], w=["identb"], semkey="c1")
        self.ones_row = self.sb("ones_row", [1, 128])
        P.dve(lambda e: e.memset(self.ones_row[:], 1.0), w=["ones_row"])
        self.M = {}
        for kind, T in (("p", 128), ("s", TS)):
            m = {}
            for nm in ("LE", "GT"):
                t = self.sb(f"m_{kind}_{nm}", [T, T])
                P.dma("sp", t[:], d[f"c_{kind}_{nm}"][:, :], w=[f"m_{kind}_{nm}"], semkey=f"c_{kind}_{nm}")
                m[nm] = t
            self.M[kind] = m
        seg = self.sb("m_s_seg", [TS, NS])
        P.dma("sp", seg[:], d["c_s_seg"][:, :], w=["m_s_seg"], semkey="c_seg")
        self.M["s"]["seg"] = seg
        segp = self.sb("m_p_seg", [128, 1])
        P.dve(lambda e: e.memset(segp[:], 1.0), w=["m_p_seg"])
        self.M["p"]["seg"] = segp

    def load_layer_weights(self, l):
        P, d = self.P, self.d
        nin = LAYER_NIN[l]
        win = self.sb("w_in", [128, 8, SSD_IN], BF16)
        wsrc = d[f"l{l}_w_in"].rearrange("(kc p) n -> p kc n", p=128)
        nblk = (nin + 511) // 512
        for b in range(nblk):
            c0, c1 = b * 512, min(nin, (b + 1) * 512)
            P.dma("pool", win[:, :, c0:c1], wsrc[:, :, c0:c1], w=[("w_in", b)], semkey=("win", b))
        wout = self.sb("w_out", [128, 16, D], BF16)
        wosrc = d[f"l{l}_w_out"].rearrange("(rc p) n -> p rc n", p=128)
        for b in range(4):
            P.dma("pool", wout[:, b * 4:(b + 1) * 4, :], wosrc[:, b * 4:(b + 1) * 4, :], w=[("w_out", b)], semkey=("wout", b))
        ncol = self.sb("normcol", [128, 8])
        P.dma("sp", ncol[:], d[f"l{l}_norm"].rearrange("(kc p) -> p kc", p=128), w=["normcol"], semkey="nrm",
              allow_slow_non_contiguous=True)
        self.win, self.wout, self.ncol = win, wout, ncol

    def tile_src(self, l_first, ti):
        cfg, d = self.cfg, self.d
        NT, TS = cfg.NT, cfg.TS
        if ti < NT:
            if l_first:
                return d["xp"][ti * 128:(ti + 1) * 128, :], ("xp", ti)
            return d["xres"][ti * 128:(ti + 1) * 128, :], ("xres", ti)
        if l_first:
            return d["xsamp"][:, :], ("xsamp",)
        return d["xres"][NT * 128:NT * 128 + TS, :], ("xres", NT)

    def load_x(self, l_first, ti, slot):
        T = 128 if ti < self.cfg.NT else self.cfg.TS
        xt = self.sb(f"xt{slot}", [128, D])
        src, key = self.tile_src(l_first, ti)
        self.P.dma("sp", xt[:T, :], src, r=[key], w=[f"xt{slot}"], semkey=("xt", slot))
        return xt

    def norm_transpose(self, xt, xkey, T):
        P = self.P
        junk = self.sb("junk", [128, DI], BF16)
        st = self.sb("nstat", [128, 4])
        hn = self.sb("hn", [128, D], BF16)
        hT = self.sb("hT", [128, 8, 128], BF16)
        P.act(lambda e: e.activation(out=junk[:T, 0:D], in_=xt[:T, :], func=AF.Square, accum_out=st[:T, 0:1]),
              r=[xkey], w=["junk", ("nstat", 0)])
        P.act(lambda e: e.activation(out=st[:T, 1:2], in_=st[:T, 0:1], func=AF.Sqrt, scale=1.0 / D, bias=self.epsc[:T, 0:1]),
              r=[("nstat", 0), "epsc"], w=[("nstat", 1)])
        P.dve(lambda e: e.reciprocal(out=st[:T, 2:3], in_=st[:T, 1:2]), r=[("nstat", 1)], w=[("nstat", 2)])
        P.act(lambda e: e.activation(out=hn[:T, :], in_=xt[:T, :], func=AF.Copy, scale=st[:T, 2:3]),
              r=[xkey, ("nstat", 2)], w=["hn"])
        pt, pk = self.psum_t()
        for kc in range(8):
            P.pe(lambda e, kc=kc: e.transpose(out=pt[:, kc * 128:kc * 128 + T], in_=hn[:T, kc * 128:(kc + 1) * 128],
                                              identity=self.identb[:T, :T]),
                 r=["hn", "identb"], w=[pk])
        ptv = pt[:].rearrange("p (k t) -> p k t", k=8)[:, :, :T]
        P.dve(lambda e: e.tensor_tensor(out=hT[:, :, :T], in0=ptv, in1=bc_last(self.ncol[:, :], T), op=ALU.mult),
              r=[pk, "normcol"], w=["hT"])
        return hT

    def masked_cols(self, dst, dkey, src, skey, si, T):
        P = self.P
        if si == 0:
            P.dve(lambda e: e.memset(dst[:, :, :T], 0.0), w=[dkey])
        else:
            P.dve(lambda e: e.memset(dst[:, :, 8 * (si - 1):8 * si], 0.0), w=[dkey])
        P.dve(lambda e: e.tensor_copy(out=dst[:, :, 8 * si:8 * si + 8], in_=src[:, :, 8 * si:8 * si + 8]), r=[skey], w=[dkey])

    def proj_block(self, hT, T, c0, c1):
        P = self.P
        ps, pk = self.psum()
        b = c0 // 512
        assert (c1 - 1) // 512 == b
        for kc in range(8):
            P.pe(lambda e, kc=kc: e.matmul(ps[:T, 0:c1 - c0], lhsT=hT[:, kc, :T], rhs=self.win[:, kc, c0:c1],
                                           start=(kc == 0), stop=(kc == 7)),
                 r=["hT", ("w_in", b)], w=[pk])
        return ps, pk

    def out_proj_residual(self, og, ogkey, xt, xkey, T):
        P = self.P
        ogT = self.sb("ogT", [128, 16, 128], BF16)
        for half in range(2):
            pt, pk = self.psum_t()
            for j in range(8):
                vc = half * 8 + j
                P.pe(lambda e, j=j, vc=vc, pt=pt: e.transpose(out=pt[:, j * 128:j * 128 + T], in_=og[:T, vc * 128:(vc + 1) * 128],
                                                              identity=self.identb[:T, :T]),
                     r=[ogkey, "identb"], w=[pk])
            ptv = pt[:].rearrange("p (k t) -> p k t", k=8)[:, :, :T]
            P.dve(lambda e, ptv=ptv, half=half: e.tensor_copy(out=ogT[:, half * 8:half * 8 + 8, :T], in_=ptv), r=[pk], w=[("ogT", half)])
        if self.dbg_stop <= 47:
            return
        for nb in range(2):
            if self.dbg_stop <= 48 and nb == 1:
                return
            ps, pk = self.psum()
            for vc in range(16):
                P.pe(lambda e, vc=vc, nb=nb, ps=ps: e.matmul(ps[:T, :], lhsT=ogT[:, vc, :T], rhs=self.wout[:, vc, nb * 512:(nb + 1) * 512],
                                                             start=(vc == 0), stop=(vc == 15)),
                     r=[("ogT", vc // 8), ("w_out", vc // 4)], w=[pk])
            P.dve(lambda e, nb=nb, ps=ps: e.tensor_tensor(out=xt[:T, nb * 512:(nb + 1) * 512], in0=xt[:T, nb * 512:(nb + 1) * 512],
                                                          in1=ps[:T, :], op=ALU.add),
                  r=[pk, xkey], w=[xkey])

    def store_x(self, xt, xkey, ti, T, last_layer):
        P, d, cfg = self.P, self.d, self.cfg
        NT = cfg.NT
        if not last_layer:
            dst = d["xres"][ti * 128:ti * 128 + T, :]
            P.dma("pool", dst, xt[:T, :], r=[xkey], w=[("xres", ti)], semkey=("xst", xkey))
            return
        junk = self.sb("junk", [128, DI], BF16)
        st = self.sb("nstat", [128, 4])
        P.act(lambda e: e.activation(out=junk[:T, 0:D], in_=xt[:T, :], func=AF.Square, accum_out=st[:T, 0:1]),
              r=[xkey], w=["junk", ("nstat", 0)])
        P.act(lambda e: e.activation(out=st[:T, 1:2], in_=st[:T, 0:1], func=AF.Sqrt, scale=1.0 / D, bias=self.epsc[:T, 0:1]),
              r=[("nstat", 0), "epsc"], w=[("nstat", 1)])
        P.dve(lambda e: e.reciprocal(out=st[:T, 2:3], in_=st[:T, 1:2]), r=[("nstat", 1)], w=[("nstat", 2)])
        for hf in range(2):
            fb = self.sb(f"on{hf}", [128, 512])
            P.dma("sp", fb[:], d["final_norm"][hf * 512:(hf + 1) * 512].partition_broadcast(128), w=[f"on{hf}"], semkey=("fnb", hf))
            P.dve(lambda e, hf=hf, fb=fb: e.scalar_tensor_tensor(out=xt[:T, hf * 512:(hf + 1) * 512], in0=xt[:T, hf * 512:(hf + 1) * 512],
                                                                 scalar=st[:T, 2:3], in1=fb[:T, :], op0=ALU.mult, op1=ALU.mult),
                  r=[xkey, ("nstat", 2), f"on{hf}"], w=[xkey])
        dst = d["yp"][ti * 128:(ti + 1) * 128, :] if ti < NT else d["ysamp"][:, :]
        P.dma("pool", dst, xt[:T, :], r=[xkey], semkey=("yst", xkey), final=True)


    def rank_consts(self):
        P, d, G = self.P, self.d, self.cfg.G
        pm = self.sb("pm", [128, 2 * G])
        P.dma("sp", pm[:], d["c_pm"].rearrange("o n -> (o n)").partition_broadcast(128), w=["pm"], semkey="c_pm")
        oh = self.sb("oh", [128, G])
        P.dma("sp", oh[:], d["c_oh"].rearrange("o n -> (o n)").partition_broadcast(128), w=["oh"], semkey="c_oh")
        sel = self.sb("sel", [G * 3, 128])
        P.dma("sp", sel[:], d["c_sel"][:, :], w=["sel"], semkey="c_sel")
        self.pm, self.oh, self.sel = pm, oh, sel

    def allgather(self, tag, rows, W, writes, cls=0):
        P, G = self.P, self.cfg.G
        xin = self.dscr(f"xin_{tag}", [rows, W])
        xout = self.dscr(f"xout_{tag}", [G * rows, W])
        for i, (c0, c1, src, rk) in enumerate(writes):
            P.dma("sp", xin[:, c0:c1], src, r=rk, w=[(f"xin_{tag}", i)], semkey=("xi", cls, i))
        groups = [list(range(b * G, (b + 1) * G)) for b in range(self.cfg.B)]
        P.op("pool", lambda e: e.collective_compute("AllGather", ALU.bypass, replica_groups=groups, ins=[xin[:, :]], outs=[xout[:, :]]),
             r=[f"xin_{tag}"], w=[f"xout_{tag}"], dma=True, semkey=("dma", "cc", cls), inc=1)
        return xout, f"xout_{tag}"

    def state_combine(self, tag, Sview, Skey, Dview, Dkey, nd):
        P, G = self.P, self.cfg.G
        xout, xk = self.allgather(tag, 128, 2048, [(0, 2048, Sview, [Skey])])
        xoutd, xkd = self.allgather(tag + "d", 128, 256, [(0, nd, Dview, [Dkey])], cls=1)
        P.dve(lambda e: e.memset(Sview, 0.0), r=[(f"xin_{tag}", 0)], w=[Skey])
        g4 = self.sb("g4", [128, 4, 512])
        cand = g4[:].rearrange("p a b -> p (a b)")
        ck = ["spf", "erev", "ecum", "encum"]
        dj = self.sb("xD", [128, 32])
        S3 = Sview.rearrange("p (a b) -> p a b", a=nd)
        for j in range(G):
            P.dma("sp", cand, xout[j * 128:(j + 1) * 128, 0:2048], r=[xk], w=ck, semkey="xc")
            P.dma("sp", dj[:, 0:nd], xoutd[j * 128:(j + 1) * 128, 0:nd], r=[xkd], w=["xD"], semkey="xd")
            P.dve(lambda e, j=j: e.tensor_scalar(out=dj[:, 0:nd], in0=dj[:, 0:nd], scalar1=self.pm[:, j:j + 1], scalar2=self.pm[:, G + j:G + j + 1],
                                                 op0=ALU.mult, op1=ALU.add), r=["xD", "pm"], w=["xD"])
            P.dve(lambda e: e.tensor_tensor(out=S3, in0=S3, in1=bc_last(dj[:, 0:nd], 2048 // nd), op=ALU.mult), r=[Skey, "xD"], w=[Skey])
            P.dve(lambda e, j=j: e.scalar_tensor_tensor(out=Sview, in0=cand, scalar=self.pm[:, j:j + 1], in1=Sview, op0=ALU.mult, op1=ALU.add),
                  r=ck + [Skey, "pm"], w=[Skey])

    def gla_setup(self, l):
        P, d = self.P, self.d
        wgk = self.sb("wgk2", [17, 512])
        P.dma("sp", wgk[0:16, :], d[f"l{l}_w_gk2"][:, :], w=["wgk2"], semkey="gs0")
        P.dma("sp", wgk[16:17, :], d[f"l{l}_b_gk"].rearrange("(o n) -> o n", o=1), w=["wgk2"], semkey="gs1")
        lrT = self.sb("lrT", [17, 128])
        P.dve(lambda e: e.memset(lrT[:, :], 1.0), w=["lrT"])
        bgk = None
        hnb = self.sb("hnb", [128, 512])
        P.dma("sp", hnb[:], d[f"l{l}_head_norm"].partition_broadcast(128), w=["hnb"], semkey="gs2")
        self.wgk, self.bgk, self.hnb = wgk, bgk, hnb

    def gla_tile(self, l, ti, xt, xkey, T, kind, state_only, seqs):
        P, d, cfg = self.P, self.d, self.cfg
        M = self.M[kind]
        nseq = len(seqs)
        hT = self.norm_transpose(xt, xkey, T)
        ps, pk = self.proj_block(hT, T, 5120, 5136)
        lrf = self.sb("lrf", [128, 16])
        P.dve(lambda e: e.tensor_copy(out=lrf[:T, :], in_=ps[:T, 0:16]), r=[pk], w=["lrf"])
        ps2, pk2 = self.psum()
        P.pe(lambda e: e.transpose(out=ps2[:16, :T], in_=lrf[:T, :], identity=self.identf[:T, :T]), r=["lrf", "identf"], w=[pk2])
        lrT = self.sb("lrT", [17, 128])
        P.dve(lambda e: e.tensor_copy(out=lrT[0:16, :T], in_=ps2[:16, :T]), r=[pk2], w=["lrT"])
        psz, pkz = self.psum()
        P.pe(lambda e: e.matmul(psz[:T, :], lhsT=lrT[:, :T], rhs=self.wgk[:, :], start=True, stop=True), r=["lrT", "wgk2"], w=[pkz])
        if self.dbg_stop <= 1:
            return
        g4 = self.sb("g4", [128, 4, 512])
        spf = g4[:, 0, :]
        P.act(lambda e: e.activation(out=spf[:T, :], in_=psz[:T, :], func=AF.Exp, scale=-1.0), r=[pkz], w=["spf"])
        P.act(lambda e: e.activation(out=spf[:T, :], in_=spf[:T, :], func=AF.Ln, bias=self.onec[:T, 0:1]), r=["spf", "onec"], w=["spf"])
        if self.dbg_stop <= 2:
            return
        erev = g4[:, 1, :]
        psr, pkr = self.psum()
        P.pe(lambda e: e.matmul(psr[:T, :], lhsT=M["GT"][:T, :T], rhs=spf[:T, :], start=True, stop=True), r=["spf", f"m_{kind}_GT"], w=[pkr])
        P.act(lambda e: e.activation(out=erev[:T, :], in_=psr[:T, :], func=AF.Exp, scale=-1.0 / 16.0), r=[pkr], w=["erev"])
        if not state_only:
            ecum = g4[:, 2, :]
            encum = g4[:, 3, :]
            psc, pkc = self.psum()
            P.pe(lambda e: e.matmul(psc[:T, :], lhsT=M["LE"][:T, :T], rhs=spf[:T, :], start=True, stop=True), r=["spf", f"m_{kind}_LE"], w=[pkc])
            P.act(lambda e: e.activation(out=ecum[:T, :], in_=psc[:T, :], func=AF.Exp, scale=-1.0 / 16.0), r=[pkc], w=["ecum"])
            P.act(lambda e: e.activation(out=encum[:T, :], in_=psc[:T, :], func=AF.Exp, scale=1.0 / 16.0), r=[pkc], w=["encum"])
        if self.dbg_stop <= 3:
            return
        elast = self.sb("elast", [128, 4, 16])
        psl, pkl = self.psum()
        for h in range(4):
            P.pe(lambda e, h=h: e.matmul(psl[:, h * 16:h * 16 + nseq], lhsT=spf[:T, h * 128:(h + 1) * 128], rhs=M["seg"][:T, :nseq],
                                         start=True, stop=True), r=["spf", "m_s_seg", "m_p_seg"], w=[pkl])
        P.act(lambda e: e.activation(out=elast[:, :, :nseq], in_=psl[:, 0:64].rearrange("p (h j) -> p h j", h=4)[:, :, :nseq], func=AF.Exp, scale=-1.0 / 16.0),
              r=[pkl], w=["elast"])
        if self.dbg_stop <= 4:
            return
        if kind == "s":
            self.dbg("spf", spf[:T, :], ["spf"], [T, 512])
            self.dbg("erev", erev[:T, :], ["erev"], [T, 512])
            self.dbg("elast", elast[:, :, :nseq], ["elast"], [128, 4, nseq])
        if not state_only:
            psq, pkq = self.proj_block(hT, T, 0, 512)
            qg = self.sb("qg", [128, 512], BF16)
            P.dve(lambda e: e.scalar_tensor_tensor(out=qg[:T, :], in0=psq[:T, :], scalar=float(128 ** -0.5), in1=ecum[:T, :],
                                                   op0=ALU.mult, op1=ALU.mult), r=[pkq, "ecum"], w=["qg"])
        psk, pkk = self.proj_block(hT, T, 512, 1024)
        kh = self.sb("kh", [128, 512], BF16)
        P.dve(lambda e: e.tensor_tensor(out=kh[:T, :], in0=psk[:T, :], in1=erev[:T, :], op=ALU.mult), r=[pkk, "erev"], w=["kh"])
        if not state_only:
            kg = self.sb("kg", [128, 512], BF16)
            P.dve(lambda e: e.tensor_tensor(out=kg[:T, :], in0=psk[:T, :], in1=encum[:T, :], op=ALU.mult), r=[pkk, "encum"], w=["kg"])
        if self.dbg_stop <= 5:
            return
        vb = self.sb("vb", [128, DI], BF16)
        for b in range(4):
            psv, pkv = self.proj_block(hT, T, 1024 + b * 512, 1536 + b * 512)
            P.act(lambda e, b=b, psv=psv: e.activation(out=vb[:T, b * 512:(b + 1) * 512], in_=psv[:T, :], func=AF.Copy),
                  r=[pkv], w=[("vb", b)])
        if not state_only:
            sr = self.sb("sr", [128, DI], BF16)
            for b in range(4):
                psr2, pkr2 = self.proj_block(hT, T, 3072 + b * 512, 3584 + b * 512)
                P.act(lambda e, b=b, psr2=psr2: e.activation(out=sr[:T, b * 512:(b + 1) * 512], in_=psr2[:T, :], func=AF.Silu),
                      r=[pkr2], w=[("sr", b)])
            if self.dbg_stop <= 6:
                return
            qT = self.sb("qT", [128, 4, 128], BF16)
            kT = self.sb("kT", [128, 4, 128], BF16)
            pt, ptk = self.psum_t()
            for h in range(4):
                P.pe(lambda e, h=h: e.transpose(out=pt[:, h * 128:h * 128 + T], in_=qg[:T, h * 128:(h + 1) * 128], identity=self.identb[:T, :T]),
                     r=["qg", "identb"], w=[ptk])
                P.pe(lambda e, h=h: e.transpose(out=pt[:, 512 + h * 128:512 + h * 128 + T], in_=kg[:T, h * 128:(h + 1) * 128],
                                                identity=self.identb[:T, :T]), r=["kg", "identb"], w=[ptk])
            ptv = pt[:].rearrange("p (k t) -> p k t", k=8)
            P.dve(lambda e: e.tensor_copy(out=qT[:, :, :T], in_=ptv[:, 0:4, :T]), r=[ptk], w=["qT"])
            P.dve(lambda e: e.tensor_copy(out=kT[:, :, :T], in_=ptv[:, 4:8, :T]), r=[ptk], w=["kT"])
            if self.dbg_stop <= 7:
                return
            psa, pka = self.psum()
            for h in range(4):
                P.pe(lambda e, h=h: e.matmul(psa[:T, h * 128:h * 128 + T], lhsT=kT[:, h, :T], rhs=qT[:, h, :T], start=True, stop=True),
                     r=["qT", "kT"], w=[pka])
            attT = self.sb("attT", [128, 4, 128], BF16)
            P.dve(lambda e: e.tensor_tensor(out=attT[:T, :, :T], in0=psa[:T, :].rearrange("p (h t) -> p h t", h=4)[:, :, :T],
                                            in1=bc_mid(M["LE"][:T, :T], 4), op=ALU.mult), r=[pka, f"m_{kind}_LE"], w=["attT"])
            if self.dbg_stop <= 8:
                return
            pso = [self.psum(hold=True) for _ in range(4)]
            for h in range(4):
                P.pe(lambda e, h=h: e.matmul(pso[h][0][:T, :], lhsT=attT[:T, h, :T], rhs=vb[:T, h * 512:(h + 1) * 512], start=True, stop=False),
                     r=["attT", ("vb", h)], w=[pso[h][1]])
        if self.dbg_stop <= 9:
            return
        for si, sq in enumerate(seqs):
            j = sq["j"]
            if sq.get("load") is not None:
                sq["load"]()
            S, Sk, Sb, Sbk = sq["S"], sq["Skey"], sq["Sb"], sq["Sbkey"]
            last = si == nseq - 1
            if not state_only:
                if sq["masked"]:
                    qTm = self.sb("qTm", [128, 4, 128], BF16)
                    self.masked_cols(qTm, "qTm", qT, "qT", si, T)
                    qsrc, qk = qTm, "qTm"
                else:
                    qsrc, qk = qT, "qT"
                for h in range(4):
                    P.pe(lambda e, h=h, qsrc=qsrc, Sb=Sb, last=last: e.matmul(pso[h][0][:T, :], lhsT=qsrc[:, h, :T], rhs=Sb[:, h, :],
                                                                             start=False, stop=last),
                         r=[qk, Sbk], w=[pso[h][1]])
            if sq["masked"]:
                khm = self.sb("khm", [128, 512], BF16)
                P.dve(lambda e, j=j: e.tensor_scalar(out=khm[:T, :], in0=kh[:T, :], scalar1=M["seg"][:T, j:j + 1], scalar2=None, op0=ALU.mult),
                      r=["kh", "m_s_seg"], w=["khm"])
                ksrc, kk = khm, "khm"
            else:
                ksrc, kk = kh, "kh"
            for h in range(4):
                psu, pku = self.psum()
                P.pe(lambda e, h=h, psu=psu, ksrc=ksrc: e.matmul(psu[:, :], lhsT=ksrc[:T, h * 128:(h + 1) * 128], rhs=vb[:T, h * 512:(h + 1) * 512],
                                                                start=True, stop=True), r=[kk, ("vb", h)], w=[pku])
                P.dve(lambda e, h=h, psu=psu, S=S, j=j: e.scalar_tensor_tensor(out=S[:, h, :], in0=S[:, h, :], scalar=elast[:, h, j:j + 1],
                                                                                in1=psu[:, :], op0=ALU.mult, op1=ALU.add),
                      r=[pku, "elast", Sk], w=[Sk])
            if sq.get("done") is not None:
                sq["done"]()
        if state_only:
            return
        if self.dbg_stop <= 10:
            for h in range(4):
                self.psum_release(pso[h][1])
            return
        st = self.sb("ostat", [128, 12])
        junk = self.sb("junk", [128, DI], BF16)
        for h in range(4):
            P.act(lambda e, h=h: e.activation(out=junk[:T, h * 512:(h + 1) * 512], in_=pso[h][0][:T, :], func=AF.Square, accum_out=st[:T, h:h + 1]),
                  r=[pso[h][1]], w=[("junk", h), ("ostat", h)])
        P.act(lambda e: e.activation(out=st[:T, 4:8], in_=st[:T, 0:4], func=AF.Sqrt, scale=1.0 / 512, bias=self.epsc[:T, 0:1]),
              r=["ostat", "epsc"], w=[("ostat", 4)])
        P.dve(lambda e: e.reciprocal(out=st[:T, 8:12], in_=st[:T, 4:8]), r=[("ostat", 4)], w=[("ostat", 8)])
        og = self.sb("og", [128, DI], BF16)
        for h in range(4):
            on = self.sb(f"on{h % 2}", [128, 512])
            onk = f"on{h % 2}"
            P.dve(lambda e, h=h, on=on: e.scalar_tensor_tensor(out=on[:T, :], in0=pso[h][0][:T, :], scalar=st[:T, 8 + h:9 + h],
                                                               in1=self.hnb[:T, :], op0=ALU.mult, op1=ALU.mult),
                  r=[pso[h][1], ("ostat", 8), "hnb"], w=[onk])
            self.psum_release(pso[h][1])
            P.dve(lambda e, h=h, on=on: e.tensor_tensor(out=og[:T, h * 512:(h + 1) * 512], in0=on[:T, :],
                                                        in1=sr[:T, h * 512:(h + 1) * 512], op=ALU.mult),
                  r=[onk, ("sr", h)], w=[("og", h)])
        self.out_proj_residual(og, "og", xt, xkey, T)

    def run_layer_gla(self, l, first, lastl):
        P, d, cfg = self.P, self.d, self.cfg
        NT, NS, TS = cfg.NT, cfg.NS, cfg.TS
        self.gla_setup(l)
        sg_in = d["sg0"] if l == 0 else d["sg3"]
        out_p = d["gla0_p"] if l == 0 else d["gla3_p"]
        out_s = d["gla0_s"] if l == 0 else d["gla3_s"]
        S = [self.sb("gS0", [128, 4, 512])]
        Sb = [self.sb("gSb0", [128, 4, 512], BF16)]
        P.dve(lambda e: e.memset(S[0][:], 0.0), w=["gS0"])
        P.dve(lambda e: e.memset(Sb[0][:], 0.0), w=["gSb0"])
        if cfg.G > 1:
            Dt = self.sb("Dtot", [128, 32])
            P.dve(lambda e: e.memset(Dt[:], 1.0), w=["Dtot"])
            for ti in range(NT):
                xt = self.load_x(first, ti, 0)
                seqs = [dict(j=0, S=S[0], Skey="gS0", Sb=Sb[0], Sbkey="gSb0", masked=False)]
                self.gla_tile(l, ti, xt, "xt0", 128, "p", True, seqs)
                el = self.bufs["elast"]
                P.dve(lambda e, el=el: e.tensor_tensor(out=Dt[:, 0:4], in0=Dt[:, 0:4], in1=el[:, :, 0], op=ALU.mult), r=["Dtot", "elast"], w=["Dtot"])
            self.state_combine(f"g{l}", S[0][:].rearrange("p a b -> p (a b)"), "gS0", Dt[:, 0:4], "Dtot", 4)
            P.act(lambda e: e.activation(out=Sb[0][:], in_=S[0][:], func=AF.Copy), r=["gS0"], w=["gSb0"])
        ntiles = NT + 1
        for ti in range(ntiles):
            T = 128 if ti < NT else TS
            xkey = "xt0"
            xt = self.load_x(first, ti, 0)
            if ti < NT:
                def done(S0=S[0], Sb0=Sb[0]):
                    P.act(lambda e: e.activation(out=Sb0[:], in_=S0[:], func=AF.Copy), r=["gS0"], w=["gSb0"])
                seqs = [dict(j=0, S=S[0], Skey="gS0", Sb=Sb[0], Sbkey="gSb0", masked=False, done=done)]
                self.gla_tile(l, ti, xt, xkey, T, "p", False, seqs)
                if ti == NT - 1:
                    P.dma("pool", out_p.rearrange("h k v -> k h v"), S[0][:], r=["gS0"], semkey="gst_p", final=True)
            else:
                seqs = []
                for j in range(NS):
                    sl = 0
                    def load(j=j, sl=sl):
                        P.dma("sp", S[sl][:], sg_in[j].rearrange("h k v -> k h v"), w=[f"gS{sl}"], semkey=("gsl", sl))
                        P.act(lambda e: e.activation(out=Sb[sl][:], in_=S[sl][:], func=AF.Copy), r=[f"gS{sl}"], w=[f"gSb{sl}"])
                    def done(j=j, sl=sl):
                        P.dma("pool", out_s[j].rearrange("h k v -> k h v"), S[sl][:], r=[f"gS{sl}"], semkey=("gss", sl), final=True)
                    seqs.append(dict(j=j, S=S[sl], Skey=f"gS{sl}", Sb=Sb[sl], Sbkey=f"gSb{sl}", masked=True, load=load, done=done))
                self.gla_tile(l, ti, xt, xkey, T, "s", False, seqs)
            self.store_x(xt, xkey, ti, T, lastl)


    def ssd_setup(self):
        P, d, cfg = self.P, self.d, self.cfg
        NS, TS = cfg.NS, cfg.TS
        P.dma("pool", d["cwb"][:, 0:4 * 3072], d["l1_conv_w"].rearrange("j c -> (j c)").partition_broadcast(128),
              w=["cwb_d"], semkey="cwbd")
        P.dma("pool", d["cwb"][:, 4 * 3072:5 * 3072], d["l1_conv_b"].partition_broadcast(128), w=["cwb_d2"], semkey="cwbd2")
        cbrow = None
        onesb = self.sb("ones_row_b", [1, 128], BF16)
        P.dve(lambda e: e.memset(onesb[:], 1.0), w=["ones_row_b"])
        dtb = self.sb("dtb", [128, 32])
        P.dma("sp", dtb[:], d["l1_dt_bias"].partition_broadcast(128), w=["dtb"], semkey="dtb")
        aneg = self.sb("aneg", [128, 32])
        P.dma("sp", aneg[:], d["l1_a_log"].partition_broadcast(128), w=["aneg"], semkey="aneg")
        P.act(lambda e: e.activation(out=aneg[:], in_=aneg[:], func=AF.Exp), r=["aneg"], w=["aneg"])
        P.dve(lambda e: e.tensor_scalar(out=aneg[:], in0=aneg[:], scalar1=-1.0, scalar2=None, op0=ALU.mult), r=["aneg"], w=["aneg"])
        dsk = self.sb("dsk", [128, 32])
        P.dma("sp", dsk[:], d["l1_d_skip"].partition_broadcast(128), w=["dsk"], semkey="dsk")
        gcol = self.sb("gcol", [128, 16])
        P.dma("sp", gcol[:], d["l1_gate_norm"].rearrange("(rc p) -> p rc", p=128), w=["gcol"], semkey="gcol",
              allow_slow_non_contiguous=True)
        for rc in range(16):
            P.dve(lambda e, rc=rc: e.tensor_scalar(out=self.wout[:, rc, :], in0=self.wout[:, rc, :], scalar1=gcol[:, rc:rc + 1],
                                                   scalar2=None, op0=ALU.mult), r=[("w_out", rc // 4), "gcol"], w=[("w_out", rc // 4)])
        self.Sh = {}
        for kind, T, K in (("p", 128, 128), ("s", TS, NS * 3)):
            sh = self.sb(f"m_{kind}_Sh", [T, 3, T], BF16)
            P.dma("pool", sh[:], d[f"c_{kind}_Sh"][:, :, :], w=[f"m_{kind}_Sh"], semkey=f"c_{kind}_Sh")
            shp = self.sb(f"m_{kind}_ShP", [128, 3, T], BF16)
            P.dma("pool", shp[:K], d[f"c_{kind}_ShP"][:, :, :], w=[f"m_{kind}_ShP"], semkey=f"c_{kind}_ShP")
            self.Sh[kind] = (sh, shp)
        self.cbrow, self.onesb, self.dtb, self.aneg, self.dsk = cbrow, onesb, dtb, aneg, dsk

    def ssd_tile(self, ti, xt, xkey, T, kind, state_only, seqs, rows, conv_out):
        P, d, cfg = self.P, self.d, self.cfg
        M = self.M[kind]
        sh, shp = self.Sh[kind]
        nseq = len(seqs)
        r0, r1 = rows
        hT = self.norm_transpose(xt, xkey, T)
        if self.dbg_stop <= 20:
            return
        xs = self.sb("vb", [128, DI], BF16)
        Bb = self.sb("qg", [128, 512], BF16)
        Cb = self.sb("kg", [128, 512], BF16)
        xtail = self.sb("xtail", [128, 3072], BF16)
        ytail = self.sb("ytail", [128, 3, 512], BF16)
        cwblk = self.sb("cwblk", [128, 5, 512], BF16)
        Yblk = self.sb("ogT", [128, 16, 128], BF16)[:].rearrange("p a b -> p (a b)").rearrange("p (j c) -> p j c", j=4)
        cwsrc = d["cwb"].rearrange("p (j c) -> p j c", j=5)
        nblk = 5 if state_only else 6
        for blk in range(nblk):
            c0 = 2048 + blk * 512
            ps, pk = self.proj_block(hT, T, c0, c0 + 512)
            P.dma("sp", cwblk[:, :, :], cwsrc[:, :, blk * 512:(blk + 1) * 512], r=["cwb_d", "cwb_d2"], w=["cwblk"], semkey="cwblk")
            psb = AP(ps[:T, :].tensor, ps[:T, :].offset, [list(ps[:T, :].ap[0]), [0, 4], list(ps[:T, :].ap[1])])
            P.dve(lambda e, psb=psb: e.tensor_tensor(out=Yblk[:T, :, :], in0=psb, in1=cwblk[:T, 0:4, :], op=ALU.mult),
                  r=[pk, "cwblk"], w=["ogT"])
            if self.dbg_stop <= 31:
                return
            xtb = xtail[r0:r1, blk * 512:(blk + 1) * 512]
            xtb3 = AP(xtb.tensor, xtb.offset, [list(xtb.ap[0]), [0, 3], list(xtb.ap[1])])
            P.dve(lambda e, xtb3=xtb3: e.tensor_tensor(out=ytail[r0:r1, :, :], in0=xtb3, in1=cwblk[r0:r1, 0:3, :], op=ALU.mult),
                  r=[("xtail", blk), "cwblk"], w=["ytail"])
            if self.dbg_stop <= 32:
                return
            if conv_out is not None:
                stg = self.sb(f"on{blk % 2}", [128, 512])
                P.dve(lambda e, ps=ps, stg=stg: e.tensor_copy(out=stg[:T, :], in_=ps[:T, :]), r=[pk], w=[f"on{blk % 2}"])
                conv_out(stg, f"on{blk % 2}", blk)
            if kind == "p":
                P.dve(lambda e, ps=ps, blk=blk: e.tensor_copy(out=xtail[64:128, blk * 512:(blk + 1) * 512], in_=ps[64:128, :]),
                      r=[pk, "ytail"], w=[("xtail", blk)])
            if self.dbg_stop <= 33:
                return
            pc, pck = self.psum()
            for j in range(3):
                P.pe(lambda e, j=j, pc=pc: e.matmul(pc[:T, :], lhsT=sh[:T, j, :T], rhs=Yblk[:T, j, :], start=(j == 0), stop=False),
                     r=["ogT", f"m_{kind}_Sh"], w=[pck])
            P.pe(lambda e, pc=pc: e.matmul(pc[:T, :], lhsT=self.identb[:T, :T], rhs=Yblk[:T, 3, :], start=False, stop=False),
                 r=["ogT", "identb"], w=[pck])
            if self.dbg_stop <= 34:
                return
            for j in range(3):
                P.pe(lambda e, j=j, pc=pc: e.matmul(pc[:T, :], lhsT=shp[r0:r1, j, :T], rhs=ytail[r0:r1, j, :], start=False, stop=False),
                     r=["ytail", f"m_{kind}_ShP"], w=[pck])
            if self.dbg_stop <= 35:
                return
            P.pe(lambda e, pc=pc, blk=blk: e.matmul(pc[:T, :], lhsT=self.onesb[0:1, :T], rhs=cwblk[0:1, 4, :],
                                                    start=False, stop=True), r=["ones_row_b", "cwblk"], w=[pck])
            if blk < 4:
                P.act(lambda e, pc=pc, blk=blk: e.activation(out=xs[:T, blk * 512:(blk + 1) * 512], in_=pc[:T, :], func=AF.Silu),
                      r=[pck], w=[("vb", blk)])
            elif blk == 4:
                P.act(lambda e, pc=pc: e.activation(out=Bb[:T, :], in_=pc[:T, :], func=AF.Silu), r=[pck], w=["qg"])
            else:
                P.act(lambda e, pc=pc: e.activation(out=Cb[:T, :], in_=pc[:T, :], func=AF.Silu), r=[pck], w=["kg"])
        if self.dbg_stop <= 41:
            return
        sm = self.sb("ssm_small", [128, 8, 32])
        psd, pkd = self.proj_block(hT, T, 5120, 5152)
        P.dve(lambda e: e.tensor_tensor(out=sm[:T, 4, :], in0=psd[:T, 0:32], in1=self.dtb[:T, :], op=ALU.add), r=[pkd, "dtb"], w=[("sm", 4)])
        P.act(lambda e: e.activation(out=sm[:T, 4, :], in_=sm[:T, 4, :], func=AF.Exp), r=[("sm", 4)], w=[("sm", 4)])
        P.act(lambda e: e.activation(out=sm[:T, 0, :], in_=sm[:T, 4, :], func=AF.Ln, bias=self.onec[:T, 0:1]), r=[("sm", 4), "onec"], w=[("sm", 0)])
        P.dve(lambda e: e.tensor_tensor(out=sm[:T, 1, :], in0=sm[:T, 0, :], in1=self.aneg[:T, :], op=ALU.mult), r=[("sm", 0), "aneg"], w=[("sm", 1)])
        la = sm[:T, 1, :]
        psr, pkr = self.psum()
        P.pe(lambda e: e.matmul(psr[:T, 0:32], lhsT=M["GT"][:T, :T], rhs=la, start=True, stop=True), r=[("sm", 1), f"m_{kind}_GT"], w=[pkr])
        P.act(lambda e: e.activation(out=sm[:T, 2, :], in_=psr[:T, 0:32], func=AF.Exp), r=[pkr], w=[("sm", 2)])
        P.dve(lambda e: e.tensor_tensor(out=sm[:T, 2, :], in0=sm[:T, 2, :], in1=sm[:T, 0, :], op=ALU.mult), r=[("sm", 2), ("sm", 0)], w=[("sm", 2)])
        if not state_only:
            psc, pkc = self.psum()
            P.pe(lambda e: e.matmul(psc[:T, 0:32], lhsT=M["LE"][:T, :T], rhs=la, start=True, stop=True), r=[("sm", 1), f"m_{kind}_LE"], w=[pkc])
            P.act(lambda e: e.activation(out=sm[:T, 3, :], in_=psc[:T, 0:32], func=AF.Exp), r=[pkc], w=[("sm", 3)])
        elb = self.sb("elb", [128, 16, 32])
        pse, pke = self.psum()
        for sq in seqs:
            j = sq["j"]
            sc = M["seg"][:T, j:j + 1]
            scb = AP(sc.tensor, sc.offset, [list(sc.ap[0]), [0, 128]])
            P.pe(lambda e, j=j, scb=scb: e.matmul(pse[:, j * 32:(j + 1) * 32], lhsT=scb, rhs=la, start=True, stop=True),
                 r=[("sm", 1), f"m_{kind}_seg"], w=[pke])
        P.act(lambda e: e.activation(out=elb[:, :nseq, :], in_=pse[:, 0:nseq * 32].rearrange("p (j h) -> p j h", h=32), func=AF.Exp),
              r=[pke], w=["elb"])
        if self.dbg_stop <= 42:
            return
        if not state_only:
            zs = self.sb("sr", [128, DI], BF16)
            for b in range(4):
                psz, pkz = self.proj_block(hT, T, b * 512, (b + 1) * 512)
                P.act(lambda e, b=b, psz=psz: e.activation(out=zs[:T, b * 512:(b + 1) * 512], in_=psz[:T, :], func=AF.Silu), r=[pkz], w=[("sr", b)])
            BT = self.sb("qT", [128, 4, 128], BF16)
            CT = self.sb("kT", [128, 4, 128], BF16)
            pt, ptk = self.psum_t()
            for g in range(4):
                P.pe(lambda e, g=g: e.transpose(out=pt[:, g * 128:g * 128 + T], in_=Bb[:T, g * 128:(g + 1) * 128], identity=self.identb[:T, :T]),
                     r=["qg", "identb"], w=[ptk])
                P.pe(lambda e, g=g: e.transpose(out=pt[:, 512 + g * 128:512 + g * 128 + T], in_=Cb[:T, g * 128:(g + 1) * 128],
                                                identity=self.identb[:T, :T]), r=["kg", "identb"], w=[ptk])
            ptv = pt[:].rearrange("p (k t) -> p k t", k=8)
            P.dve(lambda e: e.tensor_copy(out=BT[:, :, :T], in_=ptv[:, 0:4, :T]), r=[ptk], w=["qT"])
            P.dve(lambda e: e.tensor_copy(out=CT[:, :, :T], in_=ptv[:, 4:8, :T]), r=[ptk], w=["kT"])
            psa, pka = self.psum()
            for g in range(4):
                P.pe(lambda e, g=g: e.matmul(psa[:T, g * 128:g * 128 + T], lhsT=BT[:, g, :T], rhs=CT[:, g, :T], start=True, stop=True),
                     r=["qT", "kT"], w=[pka])
            cbT = self.sb("attT", [128, 4, 128], BF16)
            P.dve(lambda e: e.tensor_tensor(out=cbT[:T, :, :T], in0=psa[:T, :].rearrange("p (g t) -> p g t", g=4)[:, :, :T],
                                            in1=bc_mid(M["LE"][:T, :T], 4), op=ALU.mult), r=[pka, f"m_{kind}_LE"], w=["attT"])
            psy = [self.psum(hold=True) for _ in range(4)]
        if self.dbg_stop <= 43:
            for g in range(4):
                self.psum_release(psy[g][1])
            return
        uu = self.sb("junk", [128, DI], BF16)
        P.dve(lambda e: e.tensor_tensor(out=uu[:T, :].rearrange("p (h q) -> p h q", h=32), in0=xs[:T, :].rearrange("p (h q) -> p h q", h=32),
                                        in1=bc_last(sm[:T, 2, :], 64), op=ALU.mult), r=["vb", ("sm", 2)], w=["junk"])
        if self.dbg_stop <= 43.1:
            for g in range(4):
                self.psum_release(psy[g][1])
            return
        for si, sq in enumerate(seqs):
            j = sq["j"]
            if sq.get("load") is not None:
                sq["load"]()
            if self.dbg_stop <= 43.2:
                for g in range(4):
                    self.psum_release(psy[g][1])
                return
            ST, STk, STb, STbk = sq["S"], sq["Skey"], sq["Sb"], sq["Sbkey"]
            last = si == nseq - 1
            if not state_only:
                if sq["masked"]:
                    CTm = self.sb("qTm", [128, 4, 128], BF16)
                    self.masked_cols(CTm, "qTm", CT, "kT", si, T)
                    csrc, ck = CTm, "qTm"
                else:
                    csrc, ck = CT, "kT"
                for g in range(4):
                    P.pe(lambda e, g=g, csrc=csrc, STb=STb, si=si, last=last: e.matmul(psy[g][0][:T, :], lhsT=csrc[:, g, :T],
                                                                                      rhs=STb[:, g * 512:(g + 1) * 512],
                                                                                      start=(si == 0), stop=last),
                         r=[ck, STbk], w=[psy[g][1]])
            if self.dbg_stop <= 43.4:
                for g in range(4):
                    self.psum_release(psy[g][1])
                return
            if sq["masked"]:
                Bm = self.sb("khm", [128, 512], BF16)
                P.dve(lambda e, j=j: e.tensor_scalar(out=Bm[:T, :], in0=Bb[:T, :], scalar1=M["seg"][:T, j:j + 1], scalar2=None, op0=ALU.mult),
                      r=["qg", "m_s_seg"], w=["khm"])
                bsrc, bk = Bm, "khm"
            else:
                bsrc, bk = Bb, "qg"
            for g in range(4):
                psu, pku = self.psum()
                P.pe(lambda e, g=g, psu=psu, bsrc=bsrc: e.matmul(psu[:, :], lhsT=bsrc[:T, g * 128:(g + 1) * 128], rhs=uu[:T, g * 512:(g + 1) * 512],
                                                                start=True, stop=True), r=[bk, "junk"], w=[pku])
                stv = ST[:, g * 512:(g + 1) * 512].rearrange("p (h q) -> p h q", h=8)
                P.dve(lambda e, g=g, stv=stv, j=j: e.tensor_tensor(out=stv, in0=stv, in1=bc_last(elb[:, j, g * 8:(g + 1) * 8], 64), op=ALU.mult),
                      r=[STk, "elb"], w=[STk])
                P.dve(lambda e, g=g, psu=psu, ST=ST: e.tensor_tensor(out=ST[:, g * 512:(g + 1) * 512], in0=ST[:, g * 512:(g + 1) * 512],
                                                                     in1=psu[:, :], op=ALU.add), r=[pku, STk], w=[STk])
            if self.dbg_stop <= 43.6:
                for g in range(4):
                    self.psum_release(psy[g][1])
                return
            if sq.get("done") is not None:
                sq["done"]()
        if state_only:
            return
        if self.dbg_stop <= 44:
            for g in range(4):
                self.psum_release(psy[g][1])
            return
        og = self.sb("og", [128, DI], BF16)
        for g in range(4):
            P.dve(lambda e, g=g: e.tensor_tensor(out=og[:T, g * 512:(g + 1) * 512].rearrange("p (h q) -> p h q", h=8),
                                                 in0=psy[g][0][:T, :].rearrange("p (h q) -> p h q", h=8),
                                                 in1=bc_last(sm[:T, 3, g * 8:(g + 1) * 8], 64), op=ALU.mult),
                  r=[psy[g][1], ("sm", 3)], w=[("og", g)])
            self.psum_release(psy[g][1])
        if self.dbg_stop <= 45:
            return
        P.dve(lambda e: e.tensor_tensor(out=uu[:T, :].rearrange("p (h q) -> p h q", h=32), in0=xs[:T, :].rearrange("p (h q) -> p h q", h=32),
                                        in1=bc_last(sm[:T, 0, :], 64), op=ALU.mult), r=["vb", ("sm", 0)], w=["junk"])
        g4 = self.sb("g4", [128, 4, 512])
        laexp = g4[:, 0:2, :].rearrange("p a (b t) -> p (a b) t", t=128)
        Eg = self.sb("Eg", [128, 8, 128], BF16)
        Mg = self.sb("Mg", [128, 8, 128], BF16)
        st = self.sb("ostat", [128, 12])
        sq_junk = g4[:, 3, :]
        for g in range(4):
            P.dve(lambda e, g=g: e.tensor_tensor(out=laexp[:T, :, :T], in0=bc_last(sm[:T, 1, g * 8:(g + 1) * 8], T), in1=bc_mid(M["LE"][:T, :T], 8),
                                                 op=ALU.mult), r=[("sm", 1), f"m_{kind}_LE"], w=["spf", "erev"])
            nmm = 2 if T == 128 else 1
            for hb in range(nmm):
                psd2, pkd2 = self.psum()
                e0, e1 = (hb * 4, hb * 4 + 4) if nmm == 2 else (0, 8)
                P.pe(lambda e, psd2=psd2, e0=e0, e1=e1: e.matmul(psd2[:T, 0:(e1 - e0) * T].rearrange("p (a t) -> p a t", t=T),
                                                                 lhsT=M["GT"][:T, :T], rhs=laexp[:T, e0:e1, :T], start=True, stop=True),
                     r=["spf", "erev", f"m_{kind}_GT"], w=[pkd2])
                P.act(lambda e, psd2=psd2, e0=e0, e1=e1: e.activation(out=Eg[:T, e0:e1, :T],
                                                                      in_=psd2[:T, 0:(e1 - e0) * T].rearrange("p (a t) -> p a t", t=T), func=AF.Exp),
                      r=[pkd2], w=[("Eg", hb)])
            P.dve(lambda e, g=g: e.tensor_tensor(out=Mg[:T, :, :T], in0=Eg[:T, :, :T], in1=bc_mid(cbT[:T, g, :T], 8), op=ALU.mult),
                  r=["Eg", "attT"], w=["Mg"])
            pyi, pyik = self.psum()
            for eh in range(8):
                h = g * 8 + eh
                P.pe(lambda e, eh=eh, h=h, pyi=pyi: e.matmul(pyi[:T, eh * 64:(eh + 1) * 64], lhsT=Mg[:T, eh, :T], rhs=uu[:T, h * 64:(h + 1) * 64],
                                                             start=True, stop=True), r=["Mg", "junk"], w=[pyik])
            on = self.sb(f"on{g % 2}", [128, 512])
            onk = f"on{g % 2}"
            on2 = g4[:, 2, :]
            gs = slice(g * 512, (g + 1) * 512)
            P.dve(lambda e, on=on, pyi=pyi, gs=gs: e.tensor_tensor(out=on[:T, :], in0=pyi[:T, :], in1=og[:T, gs], op=ALU.add),
                  r=[pyik, ("og", g)], w=[onk])
            P.dve(lambda e, g=g, gs=gs: e.tensor_tensor(out=on2[:T, :].rearrange("p (h q) -> p h q", h=8),
                                                        in0=xs[:T, gs].rearrange("p (h q) -> p h q", h=8),
                                                        in1=bc_last(self.dsk[:T, g * 8:(g + 1) * 8], 64), op=ALU.mult),
                  r=[("vb", g), "dsk"], w=["ecum"])
            P.dve(lambda e, on=on: e.tensor_tensor(out=on[:T, :], in0=on[:T, :], in1=on2[:T, :], op=ALU.add), r=[onk, "ecum"], w=[onk])
            P.dve(lambda e, on=on, gs=gs: e.tensor_tensor(out=on[:T, :], in0=on[:T, :], in1=zs[:T, gs], op=ALU.mult), r=[onk, ("sr", g)], w=[onk])
            P.act(lambda e, on=on, g=g: e.activation(out=sq_junk[:T, :], in_=on[:T, :], func=AF.Square, accum_out=st[:T, g:g + 1]),
                  r=[onk], w=["encum", ("ostat", g)])
            P.act(lambda e, g=g: e.activation(out=st[:T, 4 + g:5 + g], in_=st[:T, g:g + 1], func=AF.Sqrt, scale=1.0 / 512, bias=self.epsc[:T, 0:1]),
                  r=[("ostat", g), "epsc"], w=[("ostat", 4 + g)])
            P.dve(lambda e, g=g: e.reciprocal(out=st[:T, 8 + g:9 + g], in_=st[:T, 4 + g:5 + g]), r=[("ostat", 4 + g)], w=[("ostat", 8 + g)])
            P.dve(lambda e, on=on, g=g, gs=gs: e.tensor_scalar(out=og[:T, gs], in0=on[:T, :], scalar1=st[:T, 8 + g:9 + g], scalar2=None, op0=ALU.mult),
                  r=[onk, ("ostat", 8 + g)], w=[("og", g)])
        if self.dbg_stop <= 46:
            return
        self.out_proj_residual(og, "og", xt, xkey, T)

    def run_layer_ssd(self, l, first, lastl):
        P, d, cfg = self.P, self.d, self.cfg
        NT, NS, TS = cfg.NT, cfg.NS, cfg.TS
        self.ssd_setup()
        ST = self.sb("gS0", [128, 4, 512])[:].rearrange("p a b -> p (a b)")
        STb = self.sb("gSb0", [128, 4, 512], BF16)[:].rearrange("p a b -> p (a b)")
        xtail = self.sb("xtail", [128, 3072], BF16)
        P.dve(lambda e: e.memset(ST, 0.0), w=["gS0"])
        P.dve(lambda e: e.memset(STb, 0.0), w=["gSb0"])
        P.dve(lambda e: e.memset(xtail[:, :], 0.0), w=["xtail"])
        if cfg.G > 1:
            xt = self.load_x(first, NT - 1, 0)
            hT = self.norm_transpose(xt, "xt0", 128)
            writes = []
            for blk in range(6):
                ps, pk = self.proj_block(hT, 128, 2048 + blk * 512, 2560 + blk * 512)
                so = self.sb(f"on{blk % 2}", [128, 512])
                P.dve(lambda e, ps=ps, so=so: e.tensor_copy(out=so[:, :], in_=ps[:, :]), r=[pk], w=[f"on{blk % 2}"])
                writes.append((blk * 512, (blk + 1) * 512, so[125:128, :], [f"on{blk % 2}"]))
                if blk == 0:
                    xin_cvp = self.dscr("xin_cv", [128, 256])
                    xout_cvp = self.dscr("xout_cv", [cfg.G * 128, 256])
                    xin_cv = xin_cvp.rearrange("p w -> (p w)")[0:9216].rearrange("(r c) -> r c", c=3072)
                    xout_cv = xout_cvp.rearrange("(g p) w -> g (p w)", g=cfg.G)[:, 0:9216].rearrange("g (r c) -> g r c", c=3072)
                P.dma("sp", xin_cv[:, blk * 512:(blk + 1) * 512], so[125:128, :], r=[f"on{blk % 2}"], w=[("xin_cv", blk)], semkey=("xi_cv", blk % 2))
            groups = [list(range(b * cfg.G, (b + 1) * cfg.G)) for b in range(cfg.B)]
            P.op("pool", lambda e: e.collective_compute("AllGather", ALU.bypass, replica_groups=groups, ins=[xin_cvp[:, :]], outs=[xout_cvp[:, :]]),
                 r=["xin_cv"], w=["xout_cv"], dma=True, semkey=("dma", "cc", 3), inc=1)

            def init_tail():
                for blk in range(6):
                    so = self.sb(f"on{blk % 2}", [128, 512])
                    for gg in range(cfg.G):
                        P.dma("sp", so[gg * 3:gg * 3 + 3, :], xout_cv[gg, :, blk * 512:(blk + 1) * 512], r=["xout_cv"], w=[f"on{blk % 2}"],
                              semkey=("xo_cv", blk % 2))
                    ps, pk = self.psum()
                    P.pe(lambda e, ps=ps, so=so: e.matmul(ps[:, :], lhsT=self.sel[:, :], rhs=so[0:cfg.G * 3, :], start=True, stop=True),
                         r=[f"on{blk % 2}", "sel"], w=[pk])
                    P.dve(lambda e, ps=ps, blk=blk: e.tensor_copy(out=xtail[64:128, blk * 512:(blk + 1) * 512], in_=ps[64:128, :]),
                          r=[pk], w=[("xtail", blk)])
            init_tail()
            Dt = self.sb("Dtot", [128, 32])
            P.dve(lambda e: e.memset(Dt[:], 1.0), w=["Dtot"])
            for ti in range(NT):
                xt = self.load_x(first, ti, 0)
                seqs = [dict(j=0, S=ST, Skey="gS0", Sb=STb, Sbkey="gSb0", masked=False)]
                self.ssd_tile(ti, xt, "xt0", 128, "p", True, seqs, (64, 128), None)
                elb = self.bufs["elb"]
                P.dve(lambda e, elb=elb: e.tensor_tensor(out=Dt[:, :], in0=Dt[:, :], in1=elb[:, 0, :], op=ALU.mult), r=["Dtot", "elb"], w=["Dtot"])
            self.state_combine("s1", ST, "gS0", Dt[:, :], "Dtot", 32)
            P.act(lambda e: e.activation(out=STb, in_=ST, func=AF.Copy), r=["gS0"], w=["gSb0"])
            init_tail()

        def st_load(src_seq):
            srcv = src_seq.rearrange("(c h2) q n -> (h2 q) c n", c=16)
            for cg in range(4):
                stg = self.sb(f"on{cg % 2}", [128, 512])
                P.dma("sp", stg[:].rearrange("p (c n) -> p c n", c=4), srcv[:, cg * 4:(cg + 1) * 4, :], w=[f"on{cg % 2}"], semkey=("stl", cg % 2))
                ps, pk = self.psum()
                for c in range(4):
                    P.pe(lambda e, c=c, ps=ps, stg=stg: e.transpose(out=ps[:, c * 128:(c + 1) * 128], in_=stg[:, c * 128:(c + 1) * 128],
                                                                    identity=self.identf[:, :]), r=[f"on{cg % 2}", "identf"], w=[pk])
                P.dve(lambda e, ps=ps, cg=cg: e.tensor_copy(out=ST[:, cg * 512:(cg + 1) * 512], in_=ps[:, :]), r=[pk], w=["gS0"])
                P.dve(lambda e, ps=ps, cg=cg: e.tensor_copy(out=STb[:, cg * 512:(cg + 1) * 512], in_=ps[:, :]), r=[pk], w=["gSb0"])

        def st_store(dst_seq, semname):
            dstv = dst_seq.rearrange("(c h2) q n -> (h2 q) c n", c=16)
            for cg in range(4):
                ps, pk = self.psum()
                for c in range(4):
                    cc = cg * 4 + c
                    P.pe(lambda e, c=c, cc=cc, ps=ps: e.transpose(out=ps[:, c * 128:(c + 1) * 128], in_=ST[:, cc * 128:(cc + 1) * 128],
                                                                  identity=self.identf[:, :]), r=["gS0", "identf"], w=[pk])
                stg = self.sb(f"on{cg % 2}", [128, 512])
                P.dve(lambda e, ps=ps, stg=stg: e.tensor_copy(out=stg[:, :], in_=ps[:, :]), r=[pk], w=[f"on{cg % 2}"])
                P.dma("pool", dstv[:, cg * 4:(cg + 1) * 4, :], stg[:].rearrange("p (c n) -> p c n", c=4), r=[f"on{cg % 2}"],
                      semkey=(semname, cg % 2), final=True)

        ntiles = NT + 1
        for ti in range(ntiles):
            T = 128 if ti < NT else TS
            xkey = "xt0"
            xt = self.load_x(first, ti, 0)
            if ti < NT:
                def done():
                    P.act(lambda e: e.activation(out=STb, in_=ST, func=AF.Copy), r=["gS0"], w=["gSb0"])
                seqs = [dict(j=0, S=ST, Skey="gS0", Sb=STb, Sbkey="gSb0", masked=False, done=done)]
                conv_out = None
                if ti == NT - 1:
                    def conv_out(stg, skey, blk):
                        P.dma("pool", d["conv1_p"][:, blk * 512:(blk + 1) * 512], stg[125:128, :], r=[skey], semkey=("cvo", skey), final=True)
                        if blk == 0:
                            self.dbg("stg0", stg[:, :], [skey], [128, 512])
                self.ssd_tile(ti, xt, xkey, T, "p", False, seqs, (64, 128), conv_out)
                if ti == NT - 1:
                    st_store(d["ssm1_p"], "sst_p")
            else:
                P.dma("pool", xtail[0:NS * 3, :], d["conv1"].rearrange("s r c -> (s r) c"), w=["xtail"], semkey="xtl_s")
                seqs = []
                for j in range(NS):
                    def load(j=j):
                        st_load(d["ssm1"][j])
                    def done(j=j):
                        st_store(d["ssm1_s"][j], "sst_s")
                    seqs.append(dict(j=j, S=ST, Skey="gS0", Sb=STb, Sbkey="gSb0", masked=True, load=load, done=done))
                def conv_out(stg, skey, blk):
                    P.dma("pool", d["cvs"][:, blk * 512:(blk + 1) * 512], stg[:TS, :], r=[skey], w=[("cvs", blk)], semkey=("cvo", skey))
                    P.dma("pool", d["conv1_s"][:, :, blk * 512:(blk + 1) * 512],
                          d["cvs"][:, blk * 512:(blk + 1) * 512].rearrange("(s t) c -> s t c", t=8)[:, 5:8, :],
                          r=[("cvs", blk)], semkey="fin2", final=True)
                self.ssd_tile(ti, xt, xkey, T, "s", False, seqs, (0, NS * 3), conv_out)
            self.store_x(xt, xkey, ti, T, lastl)


    def swa_setup(self):
        P, d, cfg = self.P, self.d, self.cfg
        NS, TS = cfg.NS, cfg.TS
        esink = self.sb("esink", [128, 32])
        P.dma("sp", esink[:], d["l2_sinks"].partition_broadcast(128), w=["esink"], semkey="esink")
        P.act(lambda e: e.activation(out=esink[:], in_=esink[:], func=AF.Exp), r=["esink"], w=["esink"])
        mas = self.sb("m_s_maskA", [128, TS])
        P.dma("sp", mas[:], d["c_s_maskA"][:, :], w=["m_s_maskA"], semkey="c_s_maskA")
        ma0 = self.sb("m_p_maskA0", [128, 128])
        P.dma("sp", ma0[:], d["c_p_maskA0"][:, :], w=["m_p_maskA0"], semkey="c_p_maskA0")
        zl = self.sb("zeros_b", [128, 128], BF16)
        P.dve(lambda e: e.memset(zl[:], 0.0), w=["zeros_b"])
        self.esink, self.mas, self.ma0, self.zl = esink, mas, ma0, zl

    def swa_views(self):
        cw = self.sb("cwblk", [128, 5, 512], BF16)
        yt = self.sb("ytail", [128, 3, 512], BF16)
        kT2 = [cw[:, i, :].rearrange("p (k t) -> p k t", k=4) for i in range(3)]
        vaug = [yt[:, i, 0:260].rearrange("p (k c) -> p k c", k=4) for i in range(3)]
        return kT2, vaug

    def swa_kv_prep(self, kvf, kvkey, T, slot):
        P = self.P
        kT2, vaug = self.swa_views()
        kd = self.sb("qg", [128, 512], BF16)
        kdv = kd[:T, :].rearrange("p (k r c) -> p k r c", k=4, r=2)
        kin = kvf[:T, 0:256].rearrange("p (k c) -> p k c", k=4)
        for r in range(2):
            P.dve(lambda e, r=r: e.tensor_copy(out=kdv[:, :, r, :], in_=kin), r=[kvkey], w=["qg"])
        P.dve(lambda e: e.tensor_copy(out=vaug[slot][:T, :, 0:64], in_=kvf[:T, 256:512].rearrange("p (k c) -> p k c", k=4)),
              r=[kvkey], w=[("ytail", slot)])
        P.dve(lambda e: e.memset(vaug[slot][:T, :, 64:65], 1.0), w=[("ytail", slot)])
        pt, ptk = self.psum_t()
        for k in range(4):
            P.pe(lambda e, k=k: e.transpose(out=pt[:, k * 128:k * 128 + T], in_=kd[:T, k * 128:(k + 1) * 128], identity=self.identb[:T, :T]),
                 r=["qg", "identb"], w=[ptk])
        P.dve(lambda e: e.tensor_copy(out=kT2[slot][:, :, :T], in_=pt[:, 0:512].rearrange("p (k t) -> p k t", k=4)[:, :, :T]),
              r=[ptk], w=[("cwblk", slot)])

    def swa_tile(self, ti, xt, xkey, T, kind, cur, prev, maskA, maskAkey, seqs_cache, kv_out):
        P, d, cfg = self.P, self.d, self.cfg
        M = self.M[kind]
        kT2, vaug = self.swa_views()
        hT = self.norm_transpose(xt, xkey, T)
        qb = self.sb("vb", [128, DI], BF16)
        for b in range(4):
            ps, pk = self.proj_block(hT, T, b * 512, (b + 1) * 512)
            P.act(lambda e, b=b, ps=ps: e.activation(out=qb[:T, b * 512:(b + 1) * 512], in_=ps[:T, :], func=AF.Copy, scale=0.125),
                  r=[pk], w=[("vb", b)])
        qT = self.sb("ogT", [128, 16, 128], BF16)
        for half in range(2):
            pt, ptk = self.psum_t()
            for jj in range(8):
                c = half * 8 + jj
                P.pe(lambda e, jj=jj, c=c, pt=pt: e.transpose(out=pt[:, jj * 128:jj * 128 + T], in_=qb[:T, c * 128:(c + 1) * 128],
                                                              identity=self.identb[:T, :T]), r=[("vb", c // 4), "identb"], w=[ptk])
            P.dve(lambda e, half=half, pt=pt: e.tensor_copy(out=qT[:, half * 8:half * 8 + 8, :T],
                                                            in_=pt[:].rearrange("p (k t) -> p k t", k=8)[:, :, :T]), r=[ptk], w=[("ogT", half)])
        psk, pkk = self.proj_block(hT, T, 2048, 2560)
        kvf = self.sb("on0", [128, 512])
        P.dve(lambda e: e.tensor_copy(out=kvf[:T, :], in_=psk[:T, :]), r=[pkk], w=["on0"])
        self.swa_kv_prep(kvf, "on0", T, cur)
        if kv_out is not None:
            kv_out(kvf, "on0")
        sg = self.sb("sr", [128, DI], BF16)
        for b in range(4):
            ps, pk = self.proj_block(hT, T, 2560 + b * 512, 3072 + b * 512)
            P.act(lambda e, b=b, ps=ps: e.activation(out=sg[:T, b * 512:(b + 1) * 512], in_=ps[:T, :], func=AF.Silu), r=[pk], w=[("sr", b)])
        og = self.sb("og", [128, DI], BF16)
        PA = self.sb("qT", [128, 4, 128], BF16)
        PB = self.sb("kT", [128, 4, 128], BF16)
        PAm = self.sb("qTm", [128, 4, 128], BF16)
        st = self.sb("swstat", [128, 8])
        for par in range(2):
            pbase = par * 64
            if seqs_cache is None:
                combos = [[kvh] for kvh in range(4)]
            else:
                combos = [[0, 1, 2, 3]]
            for cb in combos:
                pv = {}
                for kvh in cb:
                    pv[kvh] = self.psum(hold=True)
                    P.pe(lambda e, pv=pv, kvh=kvh: e.matmul(pv[kvh][0][:T, 0:260], lhsT=self.zl[:T, :T], rhs=vaug[cur][:T, :, :].rearrange("p k c -> p (k c)"),
                                                    start=True, stop=False), r=["zeros_b", ("ytail", cur)], w=[pv[kvh][1]])
                for kvh in cb:
                    qrhs = qT[pbase:pbase + 64, kvh * 4:kvh * 4 + 4, :T]
                    psb, pkb = self.psum()
                    klhs = kT2[cur][pbase:pbase + 64, kvh, :T]
                    P.pe(lambda e, kvh=kvh, psb=psb, qrhs=qrhs, klhs=klhs: e.matmul(psb[:T, 0:4 * T].rearrange("p (a t) -> p a t", a=4),
                                                                        lhsT=klhs, rhs=qrhs, start=True, stop=True),
                         r=[("cwblk", cur), "ogT"], w=[pkb])
                    P.act(lambda e, psb=psb: e.activation(out=PB[:T, :, :T], in_=psb[:T, 0:4 * T].rearrange("p (a t) -> p a t", a=4), func=AF.Exp),
                          r=[pkb], w=["kT"])
                    P.dve(lambda e: e.tensor_tensor(out=PB[:T, :, :T], in0=PB[:T, :, :T], in1=bc_mid(M["LE"][:T, :T], 4), op=ALU.mult),
                          r=["kT", f"m_{kind}_LE"], w=["kT"])
                    for i in range(4):
                        P.pe(lambda e, pv=pv, kvh=kvh, i=i: e.matmul(pv[kvh][0][:T, i * 65:(i + 1) * 65], lhsT=PB[:T, i, :T], rhs=vaug[cur][:T, kvh, :],
                                                             start=False, stop=False), r=["kT", ("ytail", cur)], w=[pv[kvh][1]])
                if seqs_cache is None:
                    kvh = cb[0]
                    qrhs = qT[pbase:pbase + 64, kvh * 4:kvh * 4 + 4, :T]
                    psa, pka = self.psum()
                    klhs = kT2[prev][pbase:pbase + 64, kvh, :]
                    P.pe(lambda e, kvh=kvh, psa=psa, qrhs=qrhs, klhs=klhs: e.matmul(psa[:, 0:4 * T].rearrange("p (a t) -> p a t", a=4),
                                                                        lhsT=klhs, rhs=qrhs, start=True, stop=True),
                         r=[("cwblk", prev), "ogT"], w=[pka])
                    P.act(lambda e, psa=psa: e.activation(out=PA[:, :, :T], in_=psa[:, 0:4 * T].rearrange("p (a t) -> p a t", a=4), func=AF.Exp),
                          r=[pka], w=["qT"])
                    P.dve(lambda e: e.tensor_tensor(out=PA[:, :, :T], in0=PA[:, :, :T], in1=bc_mid(maskA[:, :T], 4), op=ALU.mult),
                          r=["qT", maskAkey], w=["qT"])
                    for i in range(4):
                        P.pe(lambda e, pv=pv, kvh=kvh, i=i: e.matmul(pv[kvh][0][:T, i * 65:(i + 1) * 65], lhsT=PA[:, i, :T], rhs=vaug[prev][:, kvh, :],
                                                             start=False, stop=False), r=["qT", ("ytail", prev)], w=[pv[kvh][1]])
                else:
                    for si, sq in enumerate(seqs_cache):
                        sq["load"]()
                        for kvh in cb:
                            qrhs = qT[pbase:pbase + 64, kvh * 4:kvh * 4 + 4, 8 * si:8 * si + 8]
                            psa, pka = self.psum()
                            klhs = kT2[2][pbase:pbase + 64, kvh, :]
                            P.pe(lambda e, kvh=kvh, psa=psa, qrhs=qrhs, klhs=klhs: e.matmul(psa[:, 0:32].rearrange("p (a t) -> p a t", a=4),
                                                                                lhsT=klhs, rhs=qrhs, start=True, stop=True),
                                 r=[("cwblk", 2), "ogT"], w=[pka])
                            P.act(lambda e, psa=psa: e.activation(out=PA[:, :, 0:8], in_=psa[:, 0:32].rearrange("p (a t) -> p a t", a=4), func=AF.Exp),
                                  r=[pka], w=["qT"])
                            first_use = (si == 0 and kvh == cb[0])
                            if first_use:
                                P.dve(lambda e: e.memset(PAm[:, :, :T], 0.0), w=["qTm"])
                            elif si > 0 and kvh == cb[0]:
                                P.dve(lambda e, si=si: e.memset(PAm[:, :, 8 * (si - 1):8 * si], 0.0), w=["qTm"])
                            P.dve(lambda e, si=si: e.tensor_tensor(out=PAm[:, :, 8 * si:8 * si + 8], in0=PA[:, :, 0:8],
                                                                   in1=bc_mid(self.mas[:, 8 * si:8 * si + 8], 4), op=ALU.mult),
                                  r=["qT", "m_s_maskA"], w=["qTm"])
                            for i in range(4):
                                P.pe(lambda e, pv=pv, kvh=kvh, i=i: e.matmul(pv[kvh][0][:T, i * 65:(i + 1) * 65], lhsT=PAm[:, i, :T], rhs=vaug[2][:, kvh, :],
                                                                     start=False, stop=False), r=["qTm", ("ytail", 2)], w=[pv[kvh][1]])
                for kvh in cb:
                    P.pe(lambda e, pv=pv, kvh=kvh: e.matmul(pv[kvh][0][:T, 0:260], lhsT=self.zl[:T, :T], rhs=vaug[cur][:T, :, :].rearrange("p k c -> p (k c)"),
                                                    start=False, stop=True), r=["zeros_b", ("ytail", cur)], w=[pv[kvh][1]])
                    pvv = pv[kvh][0][:T, 0:260].rearrange("p (a c) -> p a c", a=4)
                    h0 = kvh * 8 + par
                    es = self.esink[:T, h0:h0 + 7:2]
                    P.dve(lambda e, pvv=pvv, es=es: e.tensor_tensor(out=st[:T, 0:4], in0=pvv[:, :, 64], in1=es, op=ALU.add),
                          r=[pv[kvh][1], "esink"], w=[("swstat", 0)])
                    P.dve(lambda e: e.reciprocal(out=st[:T, 4:8], in_=st[:T, 0:4]), r=[("swstat", 0)], w=[("swstat", 4)])
                    on = self.sb("on1", [128, 512])
                    onv = on[:T, 0:256].rearrange("p (a c) -> p a c", a=4)
                    P.dve(lambda e, pvv=pvv, onv=onv: e.tensor_tensor(out=onv, in0=pvv[:, :, 0:64], in1=bc_last(st[:T, 4:8], 64), op=ALU.mult),
                          r=[pv[kvh][1], ("swstat", 4)], w=["on1"])
                    self.psum_release(pv[kvh][1])
                    c0 = (kvh * 8 + par) * 64
                    ogv = AP(og[:T, c0:c0 + 64].tensor, og[:T, c0:c0 + 64].offset, [list(og[:T, c0:c0 + 64].ap[0]), [128, 4], [1, 64]])
                    sgv = AP(sg[:T, c0:c0 + 64].tensor, sg[:T, c0:c0 + 64].offset, [list(sg[:T, c0:c0 + 64].ap[0]), [128, 4], [1, 64]])
                    P.dve(lambda e, ogv=ogv, sgv=sgv, onv=onv: e.tensor_tensor(out=ogv, in0=onv, in1=sgv, op=ALU.mult),
                          r=["on1", "sr"], w=["og"])
        self.out_proj_residual(og, "og", xt, xkey, T)

    def run_layer_swa(self, l, first, lastl):
        P, d, cfg = self.P, self.d, self.cfg
        NT, NS, TS = cfg.NT, cfg.NS, cfg.TS
        self.swa_setup()
        kT2, vaug = self.swa_views()
        cw = self.sb("cwblk", [128, 5, 512], BF16)
        yt = self.sb("ytail", [128, 3, 512], BF16)
        P.dve(lambda e: e.memset(cw[:], 0.0), w=["cwblk"])
        P.dve(lambda e: e.memset(yt[:], 0.0), w=["ytail"])
        if cfg.G > 1:
            xt = self.load_x(first, NT - 1, 0)
            hT = self.norm_transpose(xt, "xt0", 128)
            psk, pkk = self.proj_block(hT, 128, 2048, 2560)
            kvf = self.sb("on0", [128, 512])
            P.dve(lambda e: e.tensor_copy(out=kvf[:, :], in_=psk[:, :]), r=[pkk], w=["on0"])
            xout, xk = self.allgather("kv", 128, 512, [(0, 512, kvf[:, :], ["on0"])], cls=2)
            P.dve(lambda e: e.memset(kvf[:, :], 0.0), r=[("xin_kv", 0)], w=["on0"])
            cand = self.sb("on1", [128, 512])
            for j in range(cfg.G):
                P.dma("sp", cand[:, :], xout[j * 128:(j + 1) * 128, :], r=[xk], w=["on1"], semkey="xkv")
                P.dve(lambda e, j=j: e.scalar_tensor_tensor(out=kvf[:, :], in0=cand[:, :], scalar=self.oh[:, j:j + 1], in1=kvf[:, :],
                                                            op0=ALU.mult, op1=ALU.add), r=["on1", "on0", "oh"], w=["on0"])
            self.swa_kv_prep(kvf, "on0", 128, 1)
        ntiles = NT + 1
        for ti in range(ntiles):
            T = 128 if ti < NT else TS
            xkey = "xt0"
            xt = self.load_x(first, ti, 0)
            if ti < NT:
                cur, prev = ti % 2, 1 - ti % 2
                if ti == 0:
                    maskA, mk = self.ma0, "m_p_maskA0"
                else:
                    maskA, mk = self.M["p"]["GT"], "m_p_GT"
                kv_out = None
                if ti == NT - 1:
                    def kv_out(kvf, key):
                        P.dma("pool", d["k2_p"][:, :], kvf[:, 0:256], r=[key], semkey="k2p", final=True)
                        P.dma("pool", d["v2_p"][:, :], kvf[:, 256:512], r=[key], semkey="v2p", final=True)
                self.swa_tile(ti, xt, xkey, T, "p", cur, prev, maskA, mk, None, kv_out)
            else:
                seqs = []
                for j in range(NS):
                    def load(j=j):
                        cst = self.sb("on1", [128, 512])
                        P.dma("sp", cst[:, 0:256], d["kc2"][j], w=["on1"], semkey="kc2l")
                        P.dma("sp", cst[:, 256:512], d["vc2"][j], w=["on1"], semkey="vc2l")
                        self.swa_kv_prep(cst, "on1", 128, 2)
                    seqs.append(dict(j=j, load=load))
                def kv_out(kvf, key):
                    P.dma("pool", d["kvs"][:, :], kvf[:TS, :], r=[key], w=["kvs"], semkey="kvs")
                    kvv = d["kvs"].rearrange("(s t) c -> s t c", t=8)
                    P.dma("pool", d["k2_s"][:, 120:128, :], kvv[:, :, 0:256], r=["kvs"], semkey="fin2", final=True)
                    P.dma("pool", d["v2_s"][:, 120:128, :], kvv[:, :, 256:512], r=["kvs"], semkey="fin2", final=True)
                    P.dma("pool", d["k2_s"][:, 0:120, :], d["kc2"][:, 8:128, :], semkey="fin2", final=True)
                    P.dma("pool", d["v2_s"][:, 0:120, :], d["vc2"][:, 8:128, :], semkey="fin2", final=True)
                self.swa_tile(ti, xt, xkey, T, "s", 0, 1, None, None, seqs, kv_out)
            self.store_x(xt, xkey, ti, T, lastl)

    def build(self):
        cfg, P, d = self.cfg, self.P, self.d
        self.declare()
        self.setup_consts()
        self.epsc = self.sb("epsc", [128, 1])
        P.dve(lambda e: e.memset(self.epsc[:], EPS), w=["epsc"])
        self.onec = self.sb("onec", [128, 1])
        P.dve(lambda e: e.memset(self.onec[:], 1.0), w=["onec"])
        if cfg.G > 1:
            self.rank_consts()
        layers = cfg.layers
        for li, l in enumerate(layers):
            first = li == 0
            lastl = li == len(layers) - 1
            self.load_layer_weights(l)
            kind = LAYER_KIND[l]
            if kind == "gla":
                self.run_layer_gla(l, first, lastl)
            elif kind == "ssd":
                self.run_layer_ssd(l, first, lastl)
            elif kind == "swa":
                self.run_layer_swa(l, first, lastl)
            else:
                raise NotImplementedError(kind)
        P.emit(self.stack)
        return self.nc


_PROG_CACHE = {}


def _run(inputs, layers=(0, 1, 2, 3)):
    xp = np.asarray(inputs["x_prompt"], dtype=np.float32)
    xs = np.asarray(inputs["x_sample"], dtype=np.float32)
    B, SEQ, _ = xp.shape
    DB = xs.shape[0]
    G = NCORES // B if (NCORES % B == 0 and SEQ % ((NCORES // B) * 128) == 0) else 1
    if FORCE_G is not None:
        G = FORCE_G
    cfg = Cfg(B, SEQ, DB, layers, G=G)
    G, NT, NS, TS = cfg.G, cfg.NT, cfg.NS, cfg.TS
    key = (B, SEQ, DB, tuple(layers))
    if key not in _PROG_CACHE:
        bld = Builder(cfg)
        nc = bld.build()
        _PROG_CACHE[key] = (bld, nc)
    bld, nc = _PROG_CACHE[key]
    mp = make_masks(128, 128)
    ms = make_masks(TS, 8)
    colmask = np.ascontiguousarray(ms["seg"].T)
    shared = {}
    for k, v in inputs.items():
        if k.startswith("l") or k == "final_norm":
            shared[k] = np.ascontiguousarray(np.asarray(v, dtype=np.float32))
    shared["c_ident"] = np.eye(128, dtype=np.float32)
    shared["c_p_LE"], shared["c_p_GT"] = mp["LE"], mp["GT"]
    shared["c_s_LE"], shared["c_s_GT"] = ms["LE"], ms["GT"]
    shared["c_s_seg"] = ms["seg"]
    shared["c_p_Sh"], shared["c_s_Sh"] = mp["Sh"], ms["Sh"]
    shared["c_s_maskA"] = (np.arange(128)[:, None] > (np.arange(TS)[None, :] % 8)).astype(np.float32)
    shared["c_p_ShP"], shared["c_s_ShP"] = mp["ShP"], ms["ShP"]
    in_maps = []
    NP = NT * 128
    for c in range(NCORES):
        b, g = (c // G, c % G) if c < B * G else (0, 0)
        m = dict(shared)
        m["xp"] = np.ascontiguousarray(xp[b, g * NP:(g + 1) * NP, :])
        sl = slice(c * NS, (c + 1) * NS)
        m["xsamp"] = np.ascontiguousarray(xs[sl].reshape(TS, D))
        m["sg0"] = np.ascontiguousarray(inputs["state_gla_0"][sl])
        m["ssm1"] = np.ascontiguousarray(inputs["state_ssm_1"][sl])
        m["conv1"] = np.ascontiguousarray(inputs["state_conv_1"][sl])
        m["kc2"] = np.ascontiguousarray(np.asarray(inputs["cache_swa_k_2"][sl]).reshape(NS, 128, 256))
        m["vc2"] = np.ascontiguousarray(np.asarray(inputs["cache_swa_v_2"][sl]).reshape(NS, 128, 256))
        m["sg3"] = np.ascontiguousarray(inputs["state_gla_3"][sl])
        pm = np.zeros((1, 2 * G), np.float32)
        for j in range(G):
            pm[0, j] = 1.0 if j < g else 0.0
            pm[0, G + j] = 1.0 - pm[0, j]
        m["c_pm"] = pm
        ohv = np.zeros((1, G), np.float32)
        if g > 0:
            ohv[0, g - 1] = 1.0
        m["c_oh"] = ohv
        selv = np.zeros((G * 3, 128), np.float32)
        if g > 0:
            for r in range(3):
                selv[(g - 1) * 3 + r, 125 + r] = 1.0
        m["c_sel"] = selv
        m["c_p_maskA0"] = (mp["GT"] * (1.0 if g > 0 else 0.0)).astype(np.float32)
        in_maps.append(m)
    res = run_bass_kernel_spmd(nc, in_maps, core_ids=list(range(NCORES)))
    R = res.results
    last = [b * G + G - 1 for b in range(B)]
    y_prompt = np.stack([np.concatenate([R[b * G + g]["yp"] for g in range(G)], axis=0) for b in range(B)])
    y_sample = np.concatenate([R[c]["ysamp"].reshape(NS, 8, D) for c in range(NCORES)], axis=0)

    def pst(name, shape):
        return np.stack([R[c][name].reshape(shape) for c in last])

    def sst(name, shape):
        return np.concatenate([R[c][name].reshape((NS,) + shape) for c in range(NCORES)], axis=0)

    outs = (y_prompt, y_sample,
            pst("gla0_p", (4, 128, 512)), sst("gla0_s", (4, 128, 512)),
            pst("ssm1_p", (32, 64, 128)), sst("ssm1_s", (32, 64, 128)),
            pst("conv1_p", (3, 3072)), sst("conv1_s", (3, 3072)),
            pst("k2_p", (128, 4, 64)), sst("k2_s", (128, 4, 64)),
            pst("v2_p", (128, 4, 64)), sst("v2_s", (128, 4, 64)),
            pst("gla3_p", (4, 128, 512)), sst("gla3_s", (4, 128, 512)))
    return tuple(np.ascontiguousarray(o, dtype=np.float32) for o in outs)


def kernel(**inputs):
    return _run(inputs)
```

```python
import numpy as np
from contextlib import ExitStack
import concourse.bass as bass
import concourse.mybir as mybir
from concourse.ap import AP
from concourse.bass_utils import run_bass_kernel_spmd

F32 = mybir.dt.float32
BF16 = mybir.dt.bfloat16
AF = mybir.ActivationFunctionType
ALU = mybir.AluOpType

D = 1024
DI = 2048
EPS = 1e-6
GLA_IN = 5136
SSD_IN = 5152
SWA_IN = 4608
NCORES = 8
FORCE_G = None

ENGS = ("pe", "act", "dve", "pool", "sp")


def _conflict(a, b):
    n = min(len(a), len(b))
    return a[:n] == b[:n]


class Op:
    __slots__ = ("eng", "fn", "reads", "writes", "dma", "semkey", "inc", "deps",
                 "sem", "semval", "need_inc", "idx")


class Prog:
    def __init__(self, nc):
        self.nc = nc
        self.ops = []
        self.state = {}
        self.final_waits = []

    @staticmethod
    def _norm(keys):
        out = []
        for k in keys:
            if k is None:
                continue
            if not isinstance(k, tuple):
                k = (k,)
            out.append(k)
        return out

    def op(self, eng, fn, r=(), w=(), dma=False, semkey=None, inc=None, final=False):
        o = Op()
        o.eng = eng
        o.fn = fn
        o.reads = self._norm(r)
        o.writes = self._norm(w)
        o.dma = dma
        o.semkey = semkey
        o.inc = inc if inc is not None else (16 if dma else 1)
        o.idx = len(self.ops)
        o.need_inc = False
        deps = set()
        for k in o.reads:
            tab = self.state.setdefault(k[0], {})
            for kk, st in tab.items():
                if _conflict(k, kk):
                    if st[0] is not None:
                        deps.add(st[0])
                    if k[0] in ("ps", "pst"):
                        deps.update(r for r in st[1] if self.ops[r].eng != eng)
        for k in o.writes:
            tab = self.state.setdefault(k[0], {})
            for kk, st in tab.items():
                if _conflict(k, kk):
                    if st[0] is not None:
                        deps.add(st[0])
                    deps.update(st[1])
        for k in o.reads:
            tab = self.state[k[0]]
            if k not in tab:
                tab[k] = [None, []]
            tab[k][1].append(o.idx)
        for k in o.writes:
            tab = self.state[k[0]]
            for kk in [kk for kk in tab if len(kk) > len(k) and kk[:len(k)] == k]:
                del tab[kk]
            tab[k] = [o.idx, []]
        deps.discard(o.idx)
        keep = set()
        for d in deps:
            dop = self.ops[d]
            if (not dop.dma) and (not o.dma) and dop.eng == eng:
                raw = False
                for k in o.reads:
                    for kk in dop.writes:
                        if _conflict(k, kk):
                            raw = True
                if not raw:
                    continue
            keep.add(d)
        o.deps = sorted(keep)
        self.ops.append(o)
        if final:
            self.final_waits.append(o.idx)
        return o

    def pe(self, fn, r=(), w=(), **kw):
        return self.op("pe", fn, r, w, **kw)

    def act(self, fn, r=(), w=(), **kw):
        return self.op("act", fn, r, w, **kw)

    def dve(self, fn, r=(), w=(), **kw):
        return self.op("dve", fn, r, w, **kw)

    def pool(self, fn, r=(), w=(), **kw):
        return self.op("pool", fn, r, w, **kw)

    def dma(self, q, out, in_, r=(), w=(), semkey=None, final=False, **dkw):
        assert semkey is not None
        sk = ("dma",) + (tuple(semkey) if isinstance(semkey, tuple) else (semkey,))
        return self.op(q, lambda e: e.dma_start(out=out, in_=in_, **dkw), r, w,
                       dma=True, semkey=sk, final=final)

    def emit(self, stack):
        nc = self.nc
        ops = self.ops
        for o in ops:
            for d in o.deps:
                ops[d].need_inc = True
        for i in self.final_waits:
            ops[i].need_inc = True
        engsem = {}
        for e in ("pe", "act", "dve", "pool"):
            engsem[e] = stack.enter_context(nc.semaphore("sem_" + e))
        dmasem = {}
        cnt = {e: 0 for e in engsem}
        dcnt = {}
        for o in ops:
            if o.dma:
                if o.semkey not in dmasem:
                    dmasem[o.semkey] = stack.enter_context(
                        nc.semaphore("sd_" + "_".join(str(x) for x in o.semkey[1:])))
                    dcnt[o.semkey] = 0
                dcnt[o.semkey] += o.inc
                o.sem = dmasem[o.semkey]
                o.semval = dcnt[o.semkey]
                o.need_inc = True
            elif o.need_inc:
                cnt[o.eng] += 1
                o.sem = engsem[o.eng]
                o.semval = cnt[o.eng]
        self.nsems = len(engsem) + len(dmasem)
        self.counts = dict(cnt)
        streams = {e: [o for o in ops if o.eng == e] for e in ENGS}
        block = stack.enter_context(nc.Block())
        final_waits = self.final_waits

        def run_stream(e, eng):
            waited = {}
            issued = []
            for o in streams[e]:
                need = {}
                if e == "pool" and o.dma:
                    if len(issued) >= 2:
                        po = issued[-2]
                        need[po.sem.num] = (po.sem, po.semval)
                    issued.append(o)
                for d in o.deps:
                    dop = ops[d]
                    key = dop.sem.num
                    if key not in need or need[key][1] < dop.semval:
                        need[key] = (dop.sem, dop.semval)
                for key, (sem, val) in need.items():
                    if waited.get(key, 0) >= val:
                        continue
                    eng.wait_ge(sem, val)
                    waited[key] = val
                ins = o.fn(eng)
                if o.need_inc:
                    ins.then_inc(o.sem, o.inc)
            if e == "sp":
                need = {}
                for i in final_waits:
                    dop = ops[i]
                    key = dop.sem.num
                    if key not in need or need[key][1] < dop.semval:
                        need[key] = (dop.sem, dop.semval)
                for key, (sem, val) in need.items():
                    if waited.get(key, 0) >= val:
                        continue
                    eng.wait_ge(sem, val)

        @block.sync
        def _(eng):
            run_stream("sp", eng)

        @block.gpsimd
        def _(eng):
            run_stream("pool", eng)

        @block.scalar
        def _(eng):
            run_stream("act", eng)

        @block.vector
        def _(eng):
            run_stream("dve", eng)

        @block.tensor
        def _(eng):
            run_stream("pe", eng)


def bc_mid(ap2d, n):
    a = ap2d.ap
    return AP(ap2d.tensor, ap2d.offset, [list(a[0]), [0, n], list(a[1])])


def bc_last(ap2d, n):
    a = ap2d.ap
    return AP(ap2d.tensor, ap2d.offset, [list(a[0]), list(a[1]), [0, n]])


def make_masks(T, L):
    idx = np.arange(T)
    seq = idx // L
    same = seq[:, None] == seq[None, :]
    s = idx[:, None]
    t = idx[None, :]
    m = {}
    m["LE"] = (same & (s <= t)).astype(np.float32)
    m["GT"] = (same & (s > t)).astype(np.float32)
    nseq = T // L
    seg = (seq[:, None] == np.arange(nseq)[None, :]).astype(np.float32)
    m["seg"] = seg
    sh = np.zeros((T, 3, T), np.float32)
    for j in range(3):
        sh[:, j, :] = (same & (s == t + j - 3)).astype(np.float32)
    m["Sh"] = sh
    if L == 128:
        shp = np.zeros((128, 3, 128), np.float32)
        for j in range(3):
            for tt in range(3):
                if tt + j < 3:
                    shp[125 + tt + j, j, tt] = 1.0
        m["ShP"] = shp
    else:
        shp = np.zeros((nseq * 3, 3, T), np.float32)
        for j in range(3):
            for tt in range(T):
                q = tt % L
                if q + j < 3:
                    shp[(tt // L) * 3 + q + j, j, tt] = 1.0
        m["ShP"] = shp
    return m


class Cfg:
    def __init__(self, B, SEQ, DB, layers=(0, 1, 2, 3), G=1):
        assert B * G <= NCORES
        self.B = B
        self.G = G
        assert SEQ % (self.G * 128) == 0
        self.NT = SEQ // self.G // 128
        assert DB % NCORES == 0
        self.NS = DB // NCORES
        self.TS = self.NS * 8
        assert self.TS <= 128
        self.layers = tuple(layers)
        self.SEQ = SEQ
        self.DB = DB


LAYER_KIND = {0: "gla", 1: "ssd", 2: "swa", 3: "gla"}
LAYER_NIN = {0: GLA_IN, 1: SSD_IN, 2: SWA_IN, 3: GLA_IN}


class Builder:
    def __init__(self, cfg):
        self.cfg = cfg
        self.nc = bass.Bass("TRN2", target_bir_lowering=False)
        self.P = Prog(self.nc)
        self.stack = ExitStack()
        self.d = {}
        self.bufs = {}
        self.psrr = 0
        self.dbg_stop = 99
        self.dbg_on = False
        self.dbg_names = []
        self.held = set()

    def din(self, name, shape, dt=F32):
        self.d[name] = self.nc.dram_tensor(name, list(shape), dt, kind="ExternalInput").ap()
        return self.d[name]

    def dout(self, name, shape, dt=F32):
        self.d[name] = self.nc.dram_tensor(name, list(shape), dt, kind="ExternalOutput").ap()
        return self.d[name]

    def dscr(self, name, shape, dt=F32):
        self.d[name] = self.nc.dram_tensor(name, list(shape), dt).ap()
        return self.d[name]

    def sb(self, name, shape, dt=F32):
        if name in self.bufs:
            return self.bufs[name]
        t = self.stack.enter_context(self.nc.sbuf_tensor(name, list(shape), dt))
        self.bufs[name] = t
        return t

    def dbg(self, name, ap, rkeys, shape):
        if not getattr(self, "dbg_on", False):
            return
        o = self.dout("dbg_" + name, shape)
        self.P.dma("sp", o, ap, r=rkeys, semkey=("dbg", name), final=True)
        self.dbg_names.append("dbg_" + name)

    def psum(self, hold=False):
        for _ in range(len(self.psb)):
            i = self.psrr
            self.psrr = (self.psrr + 1) % len(self.psb)
            if i not in self.held:
                if hold:
                    self.held.add(i)
                return self.psb[i], ("ps", i)
        raise RuntimeError("no free PSUM bank")

    def psum_release(self, key):
        self.held.discard(key[1])

    def psum_t(self):
        i = self.pstrr
        self.pstrr = (self.pstrr + 1) % len(self.pst)
        return self.pst[i], ("pst", i)

    def declare(self):
        cfg = self.cfg
        NT, NS, TS, G = cfg.NT, cfg.NS, cfg.TS, cfg.G
        NP = NT * 128
        self.din("xp", [NP, D])
        self.din("xsamp", [TS, D])
        self.din("sg0", [NS, 4, 128, 512])
        self.din("ssm1", [NS, 32, 64, 128])
        self.din("conv1", [NS, 3, 3072])
        self.din("kc2", [NS, 128, 256])
        self.din("vc2", [NS, 128, 256])
        self.din("sg3", [NS, 4, 128, 512])
        for l in (0, 3):
            self.din(f"l{l}_norm", [D])
            self.din(f"l{l}_w_in", [D, GLA_IN])
            self.din(f"l{l}_w_gk2", [16, 512])
            self.din(f"l{l}_b_gk", [512])
            self.din(f"l{l}_head_norm", [512])
            self.din(f"l{l}_w_out", [DI, D])
        self.din("l1_norm", [D])
        self.din("l1_w_in", [D, SSD_IN])
        self.din("l1_conv_w", [4, 3072])
        self.din("l1_conv_b", [3072])
        self.din("l1_dt_bias", [32])
        self.din("l1_a_log", [32])
        self.din("l1_d_skip", [32])
        self.din("l1_gate_norm", [DI])
        self.din("l1_w_out", [DI, D])
        self.din("l2_norm", [D])
        self.din("l2_w_in", [D, SWA_IN])
        self.din("l2_sinks", [32])
        self.din("l2_w_out", [DI, D])
        self.din("final_norm", [D])
        self.din("c_ident", [128, 128])
        for kind, T in (("p", 128), ("s", TS)):
            self.din(f"c_{kind}_LE", [T, T])
            self.din(f"c_{kind}_GT", [T, T])
        self.din("c_s_seg", [TS, NS])
        self.din("c_p_Sh", [128, 3, 128])
        self.din("c_s_maskA", [128, TS])
        self.din("c_p_maskA0", [128, 128])
        self.din("c_s_Sh", [TS, 3, TS])
        self.din("c_p_ShP", [128, 3, 128])
        self.din("c_s_ShP", [NS * 3, 3, TS])
        self.din("c_pm", [1, 2 * G])
        self.din("c_oh", [1, G])
        self.din("c_sel", [G * 3, 128])
        self.dout("yp", [NP, D])
        self.dout("ysamp", [TS, D])
        self.dout("gla0_p", [4, 128, 512])
        self.dout("gla0_s", [NS, 4, 128, 512])
        self.dout("ssm1_p", [32, 64, 128])
        self.dout("ssm1_s", [NS, 32, 64, 128])
        self.dout("conv1_p", [3, 3072])
        self.dout("conv1_s", [NS, 3, 3072])
        self.dout("k2_p", [128, 256])
        self.dout("k2_s", [NS, 128, 256])
        self.dout("v2_p", [128, 256])
        self.dout("v2_s", [NS, 128, 256])
        self.dout("gla3_p", [4, 128, 512])
        self.dout("gla3_s", [NS, 4, 128, 512])
        self.dscr("xres", [NP + 128, D])
        self.dscr("cwb", [128, 5 * 3072], BF16)
        self.dscr("cvs", [TS, 3072])
        self.dscr("kvs", [TS, 512])

    def setup_consts(self):
        P, d, cfg = self.P, self.d, self.cfg
        NS, TS = cfg.NS, cfg.TS
        nc = self.nc
        self.psb = [self.stack.enter_context(nc.psum_tensor(f"ps{i}", [128, 512], F32)) for i in range(6)]
        self.pst = [self.stack.enter_context(nc.psum_tensor(f"pst{i}", [128, 1024], BF16)) for i in range(2)]
        self.pstrr = 0
        self.identf = self.sb("identf", [128, 128])
        self.identb = self.sb("identb", [128, 128], BF16)
        P.dma("sp", self.identf[:], d["c_ident"][:, :], w=["identf"], semkey="c0")
        P.dma("pool", self.identb[:], d["c_ident"][:, :], w=["identb"], semkey="c1")
        self.ones_row = self.sb("ones_row", [1, 128])
        P.dve(lambda e: e.memset(self.ones_row[:], 1.0), w=["ones_row"])
        self.M = {}
        for kind, T in (("p", 128), ("s", TS)):
            m = {}
            for nm in ("LE", "GT"):
                t = self.sb(f"m_{kind}_{nm}", [T, T])
                P.dma("sp", t[:], d[f"c_{kind}_{nm}"][:, :], w=[f"m_{kind}_{nm}"], semkey=f"c_{kind}_{nm}")
                m[nm] = t
            self.M[kind] = m
        seg = self.sb("m_s_seg", [TS, NS])
        P.dma("sp", seg[:], d["c_s_seg"][:, :], w=["m_s_seg"], semkey="c_seg")
        self.M["s"]["seg"] = seg
        segp = self.sb("m_p_seg", [128, 1])
        P.dve(lambda e: e.memset(segp[:], 1.0), w=["m_p_seg"])
        self.M["p"]["seg"] = segp

    def load_layer_weights(self, l):
        P, d = self.P, self.d
        nin = LAYER_NIN[l]
        win = self.sb("w_in", [128, 8, SSD_IN], BF16)
        wsrc = d[f"l{l}_w_in"].rearrange("(kc p) n -> p kc n", p=128)
        nblk = (nin + 511) // 512
        for b in range(nblk):
            c0, c1 = b * 512, min(nin, (b + 1) * 512)
            P.dma("pool", win[:, :, c0:c1], wsrc[:, :, c0:c1], w=[("w_in", b)], semkey=("win", b))
        wout = self.sb("w_out", [128, 16, D], BF16)
        wosrc = d[f"l{l}_w_out"].rearrange("(rc p) n -> p rc n", p=128)
        for b in range(4):
            P.dma("pool", wout[:, b * 4:(b + 1) * 4, :], wosrc[:, b * 4:(b + 1) * 4, :], w=[("w_out", b)], semkey=("wout", b))
        ncol = self.sb("normcol", [128, 8])
        P.dma("sp", ncol[:], d[f"l{l}_norm"].rearrange("(kc p) -> p kc", p=128), w=["normcol"], semkey="nrm",
              allow_slow_non_contiguous=True)
        self.win, self.wout, self.ncol = win, wout, ncol

    def tile_src(self, l_first, ti):
        cfg, d = self.cfg, self.d
        NT, TS = cfg.NT, cfg.TS
        if ti < NT:
            if l_first:
                return d["xp"][ti * 128:(ti + 1) * 128, :], ("xp", ti)
            return d["xres"][ti * 128:(ti + 1) * 128, :], ("xres", ti)
        if l_first:
            return d["xsamp"][:, :], ("xsamp",)
        return d["xres"][NT * 128:NT * 128 + TS, :], ("xres", NT)

    def load_x(self, l_first, ti, slot):
        T = 128 if ti < self.cfg.NT else self.cfg.TS
        xt = self.sb(f"xt{slot}", [128, D])
        src, key = self.tile_src(l_first, ti)
        self.P.dma("sp", xt[:T, :], src, r=[key], w=[f"xt{slot}"], semkey=("xt", slot))
        return xt

    def tiles_iter(self, first, tile_ids):
        slot = 0
        xt = self.load_x(first, tile_ids[0], slot)
        for i, ti in enumerate(tile_ids):
            nxt = None
            if i + 1 < len(tile_ids):
                nxt = self.load_x(first, tile_ids[i + 1], 1 - slot)
            yield ti, xt, f"xt{slot}"
            xt = nxt
            slot = 1 - slot

    def norm_transpose(self, xt, xkey, T):
        P = self.P
        junk = self.sb("junk", [128, DI], BF16)
        st = self.sb("nstat", [128, 4])
        hn = junk[:, D:2 * D]
        hT = self.sb("hT", [128, 8, 128], BF16)
        P.act(lambda e: e.activation(out=junk[:T, 0:D], in_=xt[:T, :], func=AF.Square, accum_out=st[:T, 0:1]),
              r=[xkey], w=["junk", ("nstat", 0)])
        P.act(lambda e: e.activation(out=st[:T, 1:2], in_=st[:T, 0:1], func=AF.Sqrt, scale=1.0 / D, bias=self.epsc[:T, 0:1]),
              r=[("nstat", 0), "epsc"], w=[("nstat", 1)])
        P.dve(lambda e: e.reciprocal(out=st[:T, 2:3], in_=st[:T, 1:2]), r=[("nstat", 1)], w=[("nstat", 2)])
        P.act(lambda e: e.activation(out=hn[:T, :], in_=xt[:T, :], func=AF.Copy, scale=st[:T, 2:3]),
              r=[xkey, ("nstat", 2)], w=["junk"])
        pt, pk = self.psum_t()
        for kc in range(8):
            P.pe(lambda e, kc=kc: e.transpose(out=pt[:, kc * 128:kc * 128 + T], in_=hn[:T, kc * 128:(kc + 1) * 128],
                                              identity=self.identb[:T, :T]),
                 r=["junk", "identb"], w=[pk])
        ptv = pt[:].rearrange("p (k t) -> p k t", k=8)[:, :, :T]
        P.dve(lambda e: e.tensor_tensor(out=hT[:, :, :T], in0=ptv, in1=bc_last(self.ncol[:, :], T), op=ALU.mult),
              r=[pk, "normcol"], w=["hT"])
        return hT

    def masked_cols(self, dst, dkey, src, skey, si, T):
        P = self.P
        if si == 0:
            P.dve(lambda e: e.memset(dst[:, :, :T], 0.0), w=[dkey])
        else:
            P.dve(lambda e: e.memset(dst[:, :, 8 * (si - 1):8 * si], 0.0), w=[dkey])
        P.dve(lambda e: e.tensor_copy(out=dst[:, :, 8 * si:8 * si + 8], in_=src[:, :, 8 * si:8 * si + 8]), r=[skey], w=[dkey])

    def proj_block(self, hT, T, c0, c1):
        P = self.P
        ps, pk = self.psum()
        b = c0 // 512
        assert (c1 - 1) // 512 == b
        for kc in range(8):
            P.pe(lambda e, kc=kc: e.matmul(ps[:T, 0:c1 - c0], lhsT=hT[:, kc, :T], rhs=self.win[:, kc, c0:c1],
                                           start=(kc == 0), stop=(kc == 7)),
                 r=["hT", ("w_in", b)], w=[pk])
        return ps, pk

    def out_proj_residual(self, og, ogkey, xt, xkey, T):
        P = self.P
        ogT = self.sb("ogT", [128, 16, 128], BF16)
        for half in range(2):
            pt, pk = self.psum_t()
            for j in range(8):
                vc = half * 8 + j
                P.pe(lambda e, j=j, vc=vc, pt=pt: e.transpose(out=pt[:, j * 128:j * 128 + T], in_=og[:T, vc * 128:(vc + 1) * 128],
                                                              identity=self.identb[:T, :T]),
                     r=[ogkey, "identb"], w=[pk])
            ptv = pt[:].rearrange("p (k t) -> p k t", k=8)[:, :, :T]
            P.dve(lambda e, ptv=ptv, half=half: e.tensor_copy(out=ogT[:, half * 8:half * 8 + 8, :T], in_=ptv), r=[pk], w=[("ogT", half)])
        if self.dbg_stop <= 47:
            return
        for nb in range(2):
            if self.dbg_stop <= 48 and nb == 1:
                return
            ps, pk = self.psum()
            for vc in range(16):
                P.pe(lambda e, vc=vc, nb=nb, ps=ps: e.matmul(ps[:T, :], lhsT=ogT[:, vc, :T], rhs=self.wout[:, vc, nb * 512:(nb + 1) * 512],
                                                             start=(vc == 0), stop=(vc == 15)),
                     r=[("ogT", vc // 8), ("w_out", vc // 4)], w=[pk])
            P.dve(lambda e, nb=nb, ps=ps: e.tensor_tensor(out=xt[:T, nb * 512:(nb + 1) * 512], in0=xt[:T, nb * 512:(nb + 1) * 512],
                                                          in1=ps[:T, :], op=ALU.add),
                  r=[pk, xkey], w=[xkey])

    def store_x(self, xt, xkey, ti, T, last_layer):
        P, d, cfg = self.P, self.d, self.cfg
        NT = cfg.NT
        if not last_layer:
            dst = d["xres"][ti * 128:ti * 128 + T, :]
            P.dma("pool", dst, xt[:T, :], r=[xkey], w=[("xres", ti)], semkey=("xst", xkey))
            return
        junk = self.sb("junk", [128, DI], BF16)
        st = self.sb("nstat", [128, 4])
        P.act(lambda e: e.activation(out=junk[:T, 0:D], in_=xt[:T, :], func=AF.Square, accum_out=st[:T, 0:1]),
              r=[xkey], w=["junk", ("nstat", 0)])
        P.act(lambda e: e.activation(out=st[:T, 1:2], in_=st[:T, 0:1], func=AF.Sqrt, scale=1.0 / D, bias=self.epsc[:T, 0:1]),
              r=[("nstat", 0), "epsc"], w=[("nstat", 1)])
        P.dve(lambda e: e.reciprocal(out=st[:T, 2:3], in_=st[:T, 1:2]), r=[("nstat", 1)], w=[("nstat", 2)])
        for hf in range(2):
            fb = self.sb(f"on{hf}", [128, 512])
            P.dma("sp", fb[:], d["final_norm"][hf * 512:(hf + 1) * 512].partition_broadcast(128), w=[f"on{hf}"], semkey=("fnb", hf))
            P.dve(lambda e, hf=hf, fb=fb: e.scalar_tensor_tensor(out=xt[:T, hf * 512:(hf + 1) * 512], in0=xt[:T, hf * 512:(hf + 1) * 512],
                                                                 scalar=st[:T, 2:3], in1=fb[:T, :], op0=ALU.mult, op1=ALU.mult),
                  r=[xkey, ("nstat", 2), f"on{hf}"], w=[xkey])
        dst = d["yp"][ti * 128:(ti + 1) * 128, :] if ti < NT else d["ysamp"][:, :]
        P.dma("pool", dst, xt[:T, :], r=[xkey], semkey=("yst", xkey), final=True)


    def rank_consts(self):
        P, d, G = self.P, self.d, self.cfg.G
        pm = self.sb("pm", [128, 2 * G])
        P.dma("sp", pm[:], d["c_pm"].rearrange("o n -> (o n)").partition_broadcast(128), w=["pm"], semkey="c_pm")
        oh = self.sb("oh", [128, G])
        P.dma("sp", oh[:], d["c_oh"].rearrange("o n -> (o n)").partition_broadcast(128), w=["oh"], semkey="c_oh")
        sel = self.sb("sel", [G * 3, 128])
        P.dma("sp", sel[:], d["c_sel"][:, :], w=["sel"], semkey="c_sel")
        self.pm, self.oh, self.sel = pm, oh, sel

    def allgather(self, tag, rows, W, writes, cls=0):
        P, G = self.P, self.cfg.G
        xin = self.dscr(f"xin_{tag}", [rows, W])
        xout = self.dscr(f"xout_{tag}", [G * rows, W])
        for i, (c0, c1, src, rk) in enumerate(writes):
            P.dma("sp", xin[:, c0:c1], src, r=rk, w=[(f"xin_{tag}", i)], semkey=("xi", cls, i))
        groups = [list(range(b * G, (b + 1) * G)) for b in range(self.cfg.B)]
        P.op("pool", lambda e: e.collective_compute("AllGather", ALU.bypass, replica_groups=groups, ins=[xin[:, :]], outs=[xout[:, :]]),
             r=[f"xin_{tag}"], w=[f"xout_{tag}"], dma=True, semkey=("dma", "cc", cls), inc=1)
        return xout, f"xout_{tag}"

    def state_combine(self, tag, Sview, Skey, Dview, Dkey, nd):
        P, G = self.P, self.cfg.G
        xout, xk = self.allgather(tag, 128, 2048, [(0, 2048, Sview, [Skey])])
        xoutd, xkd = self.allgather(tag + "d", 128, 256, [(0, nd, Dview, [Dkey])], cls=1)
        P.dve(lambda e: e.memset(Sview, 0.0), r=[(f"xin_{tag}", 0)], w=[Skey])
        g4 = self.sb("g4", [128, 4, 512])
        cand = g4[:].rearrange("p a b -> p (a b)")
        ck = ["spf", "erev", "ecum", "encum"]
        dj = self.sb("xD", [128, 32])
        S3 = Sview.rearrange("p (a b) -> p a b", a=nd)
        for j in range(G):
            P.dma("sp", cand, xout[j * 128:(j + 1) * 128, 0:2048], r=[xk], w=ck, semkey="xc")
            P.dma("sp", dj[:, 0:nd], xoutd[j * 128:(j + 1) * 128, 0:nd], r=[xkd], w=["xD"], semkey="xd")
            P.dve(lambda e, j=j: e.tensor_scalar(out=dj[:, 0:nd], in0=dj[:, 0:nd], scalar1=self.pm[:, j:j + 1], scalar2=self.pm[:, G + j:G + j + 1],
                                                 op0=ALU.mult, op1=ALU.add), r=["xD", "pm"], w=["xD"])
            P.dve(lambda e: e.tensor_tensor(out=S3, in0=S3, in1=bc_last(dj[:, 0:nd], 2048 // nd), op=ALU.mult), r=[Skey, "xD"], w=[Skey])
            P.dve(lambda e, j=j: e.scalar_tensor_tensor(out=Sview, in0=cand, scalar=self.pm[:, j:j + 1], in1=Sview, op0=ALU.mult, op1=ALU.add),
                  r=ck + [Skey, "pm"], w=[Skey])

    def gla_setup(self, l):
        P, d = self.P, self.d
        wgk = self.sb("wgk2", [17, 512])
        P.dma("sp", wgk[0:16, :], d[f"l{l}_w_gk2"][:, :], w=["wgk2"], semkey="gs0")
        P.dma("sp", wgk[16:17, :], d[f"l{l}_b_gk"].rearrange("(o n) -> o n", o=1), w=["wgk2"], semkey="gs1")
        lrT = self.sb("lrT", [17, 128])
        P.dve(lambda e: e.memset(lrT[:, :], 1.0), w=["lrT"])
        bgk = None
        hnb = self.sb("hnb", [128, 512])
        P.dma("sp", hnb[:], d[f"l{l}_head_norm"].partition_broadcast(128), w=["hnb"], semkey="gs2")
        self.wgk, self.bgk, self.hnb = wgk, bgk, hnb

    def gla_tile(self, l, ti, xt, xkey, T, kind, state_only, seqs):
        P, d, cfg = self.P, self.d, self.cfg
        M = self.M[kind]
        nseq = len(seqs)
        hT = self.norm_transpose(xt, xkey, T)
        ps, pk = self.proj_block(hT, T, 5120, 5136)
        lrf = self.sb("lrf", [128, 16])
        P.dve(lambda e: e.tensor_copy(out=lrf[:T, :], in_=ps[:T, 0:16]), r=[pk], w=["lrf"])
        ps2, pk2 = self.psum()
        P.pe(lambda e: e.transpose(out=ps2[:16, :T], in_=lrf[:T, :], identity=self.identf[:T, :T]), r=["lrf", "identf"], w=[pk2])
        lrT = self.sb("lrT", [17, 128])
        P.dve(lambda e: e.tensor_copy(out=lrT[0:16, :T], in_=ps2[:16, :T]), r=[pk2], w=["lrT"])
        psz, pkz = self.psum()
        P.pe(lambda e: e.matmul(psz[:T, :], lhsT=lrT[:, :T], rhs=self.wgk[:, :], start=True, stop=True), r=["lrT", "wgk2"], w=[pkz])
        if self.dbg_stop <= 1:
            return
        g4 = self.sb("g4", [128, 4, 512])
        spf = g4[:, 0, :]
        P.act(lambda e: e.activation(out=spf[:T, :], in_=psz[:T, :], func=AF.Exp, scale=-1.0), r=[pkz], w=["spf"])
        P.act(lambda e: e.activation(out=spf[:T, :], in_=spf[:T, :], func=AF.Ln, bias=self.onec[:T, 0:1]), r=["spf", "onec"], w=["spf"])
        if self.dbg_stop <= 2:
            return
        erev = g4[:, 1, :]
        psr, pkr = self.psum()
        P.pe(lambda e: e.matmul(psr[:T, :], lhsT=M["GT"][:T, :T], rhs=spf[:T, :], start=True, stop=True), r=["spf", f"m_{kind}_GT"], w=[pkr])
        P.act(lambda e: e.activation(out=erev[:T, :], in_=psr[:T, :], func=AF.Exp, scale=-1.0 / 16.0), r=[pkr], w=["erev"])
        if not state_only:
            ecum = g4[:, 2, :]
            encum = g4[:, 3, :]
            psc, pkc = self.psum()
            P.pe(lambda e: e.matmul(psc[:T, :], lhsT=M["LE"][:T, :T], rhs=spf[:T, :], start=True, stop=True), r=["spf", f"m_{kind}_LE"], w=[pkc])
            P.act(lambda e: e.activation(out=ecum[:T, :], in_=psc[:T, :], func=AF.Exp, scale=-1.0 / 16.0), r=[pkc], w=["ecum"])
            P.act(lambda e: e.activation(out=encum[:T, :], in_=psc[:T, :], func=AF.Exp, scale=1.0 / 16.0), r=[pkc], w=["encum"])
        if self.dbg_stop <= 3:
            return
        elast = self.sb("elast", [128, 4, 16])
        psl, pkl = self.psum()
        for h in range(4):
            P.pe(lambda e, h=h: e.matmul(psl[:, h * 16:h * 16 + nseq], lhsT=spf[:T, h * 128:(h + 1) * 128], rhs=M["seg"][:T, :nseq],
                                         start=True, stop=True), r=["spf", "m_s_seg", "m_p_seg"], w=[pkl])
        P.act(lambda e: e.activation(out=elast[:, :, :nseq], in_=psl[:, 0:64].rearrange("p (h j) -> p h j", h=4)[:, :, :nseq], func=AF.Exp, scale=-1.0 / 16.0),
              r=[pkl], w=["elast"])
        if self.dbg_stop <= 4:
            return
        if kind == "s":
            self.dbg("spf", spf[:T, :], ["spf"], [T, 512])
            self.dbg("erev", erev[:T, :], ["erev"], [T, 512])
            self.dbg("elast", elast[:, :, :nseq], ["elast"], [128, 4, nseq])
        if not state_only:
            psq, pkq = self.proj_block(hT, T, 0, 512)
            qg = self.sb("qg", [128, 512], BF16)
            P.dve(lambda e: e.scalar_tensor_tensor(out=qg[:T, :], in0=psq[:T, :], scalar=float(128 ** -0.5), in1=ecum[:T, :],
                                                   op0=ALU.mult, op1=ALU.mult), r=[pkq, "ecum"], w=["qg"])
        psk, pkk = self.proj_block(hT, T, 512, 1024)
        kh = self.sb("kh", [128, 512], BF16)
        P.dve(lambda e: e.tensor_tensor(out=kh[:T, :], in0=psk[:T, :], in1=erev[:T, :], op=ALU.mult), r=[pkk, "erev"], w=["kh"])
        if not state_only:
            kg = self.sb("kg", [128, 512], BF16)
            P.dve(lambda e: e.tensor_tensor(out=kg[:T, :], in0=psk[:T, :], in1=encum[:T, :], op=ALU.mult), r=[pkk, "encum"], w=["kg"])
        if self.dbg_stop <= 5:
            return
        vb = self.sb("vb", [128, DI], BF16)
        for b in range(4):
            psv, pkv = self.proj_block(hT, T, 1024 + b * 512, 1536 + b * 512)
            P.act(lambda e, b=b, psv=psv: e.activation(out=vb[:T, b * 512:(b + 1) * 512], in_=psv[:T, :], func=AF.Copy),
                  r=[pkv], w=[("vb", b)])
        if not state_only:
            sr = self.sb("sr", [128, DI], BF16)
            for b in range(4):
                psr2, pkr2 = self.proj_block(hT, T, 3072 + b * 512, 3584 + b * 512)
                P.act(lambda e, b=b, psr2=psr2: e.activation(out=sr[:T, b * 512:(b + 1) * 512], in_=psr2[:T, :], func=AF.Silu),
                      r=[pkr2], w=[("sr", b)])
            if self.dbg_stop <= 6:
                return
            qT = self.sb("qT", [128, 4, 128], BF16)
            kT = self.sb("kT", [128, 4, 128], BF16)
            pt, ptk = self.psum_t()
            for h in range(4):
                P.pe(lambda e, h=h: e.transpose(out=pt[:, h * 128:h * 128 + T], in_=qg[:T, h * 128:(h + 1) * 128], identity=self.identb[:T, :T]),
                     r=["qg", "identb"], w=[ptk])
                P.pe(lambda e, h=h: e.transpose(out=pt[:, 512 + h * 128:512 + h * 128 + T], in_=kg[:T, h * 128:(h + 1) * 128],
                                                identity=self.identb[:T, :T]), r=["kg", "identb"], w=[ptk])
            ptv = pt[:].rearrange("p (k t) -> p k t", k=8)
            P.dve(lambda e: e.tensor_copy(out=qT[:, :, :T], in_=ptv[:, 0:4, :T]), r=[ptk], w=["qT"])
            P.dve(lambda e: e.tensor_copy(out=kT[:, :, :T], in_=ptv[:, 4:8, :T]), r=[ptk], w=["kT"])
            if self.dbg_stop <= 7:
                return
            psa, pka = self.psum()
            for h in range(4):
                P.pe(lambda e, h=h: e.matmul(psa[:T, h * 128:h * 128 + T], lhsT=kT[:, h, :T], rhs=qT[:, h, :T], start=True, stop=True),
                     r=["qT", "kT"], w=[pka])
            attT = self.sb("attT", [128, 4, 128], BF16)
            P.dve(lambda e: e.tensor_tensor(out=attT[:T, :, :T], in0=psa[:T, :].rearrange("p (h t) -> p h t", h=4)[:, :, :T],
                                            in1=bc_mid(M["LE"][:T, :T], 4), op=ALU.mult), r=[pka, f"m_{kind}_LE"], w=["attT"])
            if self.dbg_stop <= 8:
                return
            pso = [self.psum(hold=True) for _ in range(4)]
            for h in range(4):
                P.pe(lambda e, h=h: e.matmul(pso[h][0][:T, :], lhsT=attT[:T, h, :T], rhs=vb[:T, h * 512:(h + 1) * 512], start=True, stop=False),
                     r=["attT", ("vb", h)], w=[pso[h][1]])
        if self.dbg_stop <= 9:
            return
        for si, sq in enumerate(seqs):
            j = sq["j"]
            if sq.get("load") is not None:
                sq["load"]()
            S, Sk, Sb, Sbk = sq["S"], sq["Skey"], sq["Sb"], sq["Sbkey"]
            last = si == nseq - 1
            if not state_only:
                if sq["masked"]:
                    qTm = self.sb("qTm", [128, 4, 128], BF16)
                    self.masked_cols(qTm, "qTm", qT, "qT", si, T)
                    qsrc, qk = qTm, "qTm"
                else:
                    qsrc, qk = qT, "qT"
                for h in range(4):
                    P.pe(lambda e, h=h, qsrc=qsrc, Sb=Sb, last=last: e.matmul(pso[h][0][:T, :], lhsT=qsrc[:, h, :T], rhs=Sb[:, h, :],
                                                                             start=False, stop=last),
                         r=[qk, Sbk], w=[pso[h][1]])
            if sq["masked"]:
                khm = self.sb("khm", [128, 512], BF16)
                P.dve(lambda e, j=j: e.tensor_scalar(out=khm[:T, :], in0=kh[:T, :], scalar1=M["seg"][:T, j:j + 1], scalar2=None, op0=ALU.mult),
                      r=["kh", "m_s_seg"], w=["khm"])
                ksrc, kk = khm, "khm"
            else:
                ksrc, kk = kh, "kh"
            for h in range(4):
                psu, pku = self.psum()
                P.pe(lambda e, h=h, psu=psu, ksrc=ksrc: e.matmul(psu[:, :], lhsT=ksrc[:T, h * 128:(h + 1) * 128], rhs=vb[:T, h * 512:(h + 1) * 512],
                                                                start=True, stop=True), r=[kk, ("vb", h)], w=[pku])
                P.dve(lambda e, h=h, psu=psu, S=S, j=j: e.scalar_tensor_tensor(out=S[:, h, :], in0=S[:, h, :], scalar=elast[:, h, j:j + 1],
                                                                                in1=psu[:, :], op0=ALU.mult, op1=ALU.add),
                      r=[pku, "elast", Sk], w=[Sk])
            if sq.get("done") is not None:
                sq["done"]()
        if state_only:
            return
        if self.dbg_stop <= 10:
            for h in range(4):
                self.psum_release(pso[h][1])
            return
        st = self.sb("ostat", [128, 12])
        junk = self.sb("junk", [128, DI], BF16)
        for h in range(4):
            P.act(lambda e, h=h: e.activation(out=junk[:T, h * 512:(h + 1) * 512], in_=pso[h][0][:T, :], func=AF.Square, accum_out=st[:T, h:h + 1]),
                  r=[pso[h][1]], w=[("junk", h), ("ostat", h)])
        P.act(lambda e: e.activation(out=st[:T, 4:8], in_=st[:T, 0:4], func=AF.Sqrt, scale=1.0 / 512, bias=self.epsc[:T, 0:1]),
              r=["ostat", "epsc"], w=[("ostat", 4)])
        P.dve(lambda e: e.reciprocal(out=st[:T, 8:12], in_=st[:T, 4:8]), r=[("ostat", 4)], w=[("ostat", 8)])
        og = self.sb("og", [128, DI], BF16)
        for h in range(4):
            on = self.sb(f"on{h % 2}", [128, 512])
            onk = f"on{h % 2}"
            P.dve(lambda e, h=h, on=on: e.scalar_tensor_tensor(out=on[:T, :], in0=pso[h][0][:T, :], scalar=st[:T, 8 + h:9 + h],
                                                               in1=self.hnb[:T, :], op0=ALU.mult, op1=ALU.mult),
                  r=[pso[h][1], ("ostat", 8), "hnb"], w=[onk])
            self.psum_release(pso[h][1])
            P.dve(lambda e, h=h, on=on: e.tensor_tensor(out=og[:T, h * 512:(h + 1) * 512], in0=on[:T, :],
                                                        in1=sr[:T, h * 512:(h + 1) * 512], op=ALU.mult),
                  r=[onk, ("sr", h)], w=[("og", h)])
        self.out_proj_residual(og, "og", xt, xkey, T)

    def run_layer_gla(self, l, first, lastl):
        P, d, cfg = self.P, self.d, self.cfg
        NT, NS, TS = cfg.NT, cfg.NS, cfg.TS
        self.gla_setup(l)
        sg_in = d["sg0"] if l == 0 else d["sg3"]
        out_p = d["gla0_p"] if l == 0 else d["gla3_p"]
        out_s = d["gla0_s"] if l == 0 else d["gla3_s"]
        S = [self.sb("gS0", [128, 4, 512])]
        Sb = [self.sb("gSb0", [128, 4, 512], BF16)]
        P.dve(lambda e: e.memset(S[0][:], 0.0), w=["gS0"])
        P.dve(lambda e: e.memset(Sb[0][:], 0.0), w=["gSb0"])
        if cfg.G > 1:
            Dt = self.sb("Dtot", [128, 32])
            P.dve(lambda e: e.memset(Dt[:], 1.0), w=["Dtot"])
            for ti, xt, xkey in self.tiles_iter(first, list(range(NT))):
                seqs = [dict(j=0, S=S[0], Skey="gS0", Sb=Sb[0], Sbkey="gSb0", masked=False)]
                self.gla_tile(l, ti, xt, xkey, 128, "p", True, seqs)
                el = self.bufs["elast"]
                P.dve(lambda e, el=el: e.tensor_tensor(out=Dt[:, 0:4], in0=Dt[:, 0:4], in1=el[:, :, 0], op=ALU.mult), r=["Dtot", "elast"], w=["Dtot"])
            self.state_combine(f"g{l}", S[0][:].rearrange("p a b -> p (a b)"), "gS0", Dt[:, 0:4], "Dtot", 4)
            P.act(lambda e: e.activation(out=Sb[0][:], in_=S[0][:], func=AF.Copy), r=["gS0"], w=["gSb0"])
        ntiles = NT + 1
        for ti, xt, xkey in self.tiles_iter(first, list(range(ntiles))):
            T = 128 if ti < NT else TS
            if ti < NT:
                def done(S0=S[0], Sb0=Sb[0]):
                    P.act(lambda e: e.activation(out=Sb0[:], in_=S0[:], func=AF.Copy), r=["gS0"], w=["gSb0"])
                seqs = [dict(j=0, S=S[0], Skey="gS0", Sb=Sb[0], Sbkey="gSb0", masked=False, done=done)]
                self.gla_tile(l, ti, xt, xkey, T, "p", False, seqs)
                if ti == NT - 1:
                    P.dma("pool", out_p.rearrange("h k v -> k h v"), S[0][:], r=["gS0"], semkey="gst_p", final=True)
            else:
                seqs = []
                for j in range(NS):
                    sl = 0
                    def load(j=j, sl=sl):
                        P.dma("sp", S[sl][:], sg_in[j].rearrange("h k v -> k h v"), w=[f"gS{sl}"], semkey=("gsl", sl))
                        P.act(lambda e: e.activation(out=Sb[sl][:], in_=S[sl][:], func=AF.Copy), r=[f"gS{sl}"], w=[f"gSb{sl}"])
                    def done(j=j, sl=sl):
                        P.dma("pool", out_s[j].rearrange("h k v -> k h v"), S[sl][:], r=[f"gS{sl}"], semkey=("gss", sl), final=True)
                    seqs.append(dict(j=j, S=S[sl], Skey=f"gS{sl}", Sb=Sb[sl], Sbkey=f"gSb{sl}", masked=True, load=load, done=done))
                self.gla_tile(l, ti, xt, xkey, T, "s", False, seqs)
            self.store_x(xt, xkey, ti, T, lastl)


    def ssd_setup(self):
        P, d, cfg = self.P, self.d, self.cfg
        NS, TS = cfg.NS, cfg.TS
        P.dma("pool", d["cwb"][:, 0:4 * 3072], d["l1_conv_w"].rearrange("j c -> (j c)").partition_broadcast(128),
              w=["cwb_d"], semkey="cwbd")
        P.dma("pool", d["cwb"][:, 4 * 3072:5 * 3072], d["l1_conv_b"].partition_broadcast(128), w=["cwb_d2"], semkey="cwbd2")
        cbrow = None
        onesb = self.sb("ones_row_b", [1, 128], BF16)
        P.dve(lambda e: e.memset(onesb[:], 1.0), w=["ones_row_b"])
        dtb = self.sb("dtb", [128, 32])
        P.dma("sp", dtb[:], d["l1_dt_bias"].partition_broadcast(128), w=["dtb"], semkey="dtb")
        aneg = self.sb("aneg", [128, 32])
        P.dma("sp", aneg[:], d["l1_a_log"].partition_broadcast(128), w=["aneg"], semkey="aneg")
        P.act(lambda e: e.activation(out=aneg[:], in_=aneg[:], func=AF.Exp), r=["aneg"], w=["aneg"])
        P.dve(lambda e: e.tensor_scalar(out=aneg[:], in0=aneg[:], scalar1=-1.0, scalar2=None, op0=ALU.mult), r=["aneg"], w=["aneg"])
        dsk = self.sb("dsk", [128, 32])
        P.dma("sp", dsk[:], d["l1_d_skip"].partition_broadcast(128), w=["dsk"], semkey="dsk")
        gcol = self.sb("gcol", [128, 16])
        P.dma("sp", gcol[:], d["l1_gate_norm"].rearrange("(rc p) -> p rc", p=128), w=["gcol"], semkey="gcol",
              allow_slow_non_contiguous=True)
        for rc in range(16):
            P.dve(lambda e, rc=rc: e.tensor_scalar(out=self.wout[:, rc, :], in0=self.wout[:, rc, :], scalar1=gcol[:, rc:rc + 1],
                                                   scalar2=None, op0=ALU.mult), r=[("w_out", rc // 4), "gcol"], w=[("w_out", rc // 4)])
        self.Sh = {}
        for kind, T, K in (("p", 128, 128), ("s", TS, NS * 3)):
            sh = self.sb(f"m_{kind}_Sh", [T, 3, T], BF16)
            P.dma("pool", sh[:], d[f"c_{kind}_Sh"][:, :, :], w=[f"m_{kind}_Sh"], semkey=f"c_{kind}_Sh")
            shp = self.sb(f"m_{kind}_ShP", [128, 3, T], BF16)
            P.dma("pool", shp[:K], d[f"c_{kind}_ShP"][:, :, :], w=[f"m_{kind}_ShP"], semkey=f"c_{kind}_ShP")
            self.Sh[kind] = (sh, shp)
        self.cbrow, self.onesb, self.dtb, self.aneg, self.dsk = cbrow, onesb, dtb, aneg, dsk

    def ssd_tile(self, ti, xt, xkey, T, kind, state_only, seqs, rows, conv_out):
        P, d, cfg = self.P, self.d, self.cfg
        M = self.M[kind]
        sh, shp = self.Sh[kind]
        nseq = len(seqs)
        r0, r1 = rows
        hT = self.norm_transpose(xt, xkey, T)
        if self.dbg_stop <= 20:
            return
        xs = self.sb("vb", [128, DI], BF16)
        Bb = self.sb("qg", [128, 512], BF16)
        Cb = self.sb("kg", [128, 512], BF16)
        xtail = self.sb("xtail", [128, 3072], BF16)
        ytail = self.sb("ytail", [128, 3, 512], BF16)
        cwblk = self.sb("cwblk", [128, 5, 512], BF16)
        Yblk = self.sb("ogT", [128, 16, 128], BF16)[:].rearrange("p a b -> p (a b)").rearrange("p (j c) -> p j c", j=4)
        cwsrc = d["cwb"].rearrange("p (j c) -> p j c", j=5)
        nblk = 5 if state_only else 6
        for blk in range(nblk):
            c0 = 2048 + blk * 512
            ps, pk = self.proj_block(hT, T, c0, c0 + 512)
            P.dma("sp", cwblk[:, :, :], cwsrc[:, :, blk * 512:(blk + 1) * 512], r=["cwb_d", "cwb_d2"], w=["cwblk"], semkey="cwblk")
            psb = AP(ps[:T, :].tensor, ps[:T, :].offset, [list(ps[:T, :].ap[0]), [0, 4], list(ps[:T, :].ap[1])])
            P.dve(lambda e, psb=psb: e.tensor_tensor(out=Yblk[:T, :, :], in0=psb, in1=cwblk[:T, 0:4, :], op=ALU.mult),
                  r=[pk, "cwblk"], w=["ogT"])
            if self.dbg_stop <= 31:
                return
            xtb = xtail[r0:r1, blk * 512:(blk + 1) * 512]
            xtb3 = AP(xtb.tensor, xtb.offset, [list(xtb.ap[0]), [0, 3], list(xtb.ap[1])])
            P.dve(lambda e, xtb3=xtb3: e.tensor_tensor(out=ytail[r0:r1, :, :], in0=xtb3, in1=cwblk[r0:r1, 0:3, :], op=ALU.mult),
                  r=[("xtail", blk), "cwblk"], w=["ytail"])
            if self.dbg_stop <= 32:
                return
            if conv_out is not None:
                stg = self.sb(f"on{blk % 2}", [128, 512])
                P.dve(lambda e, ps=ps, stg=stg: e.tensor_copy(out=stg[:T, :], in_=ps[:T, :]), r=[pk], w=[f"on{blk % 2}"])
                conv_out(stg, f"on{blk % 2}", blk)
            if kind == "p":
                P.dve(lambda e, ps=ps, blk=blk: e.tensor_copy(out=xtail[64:128, blk * 512:(blk + 1) * 512], in_=ps[64:128, :]),
                      r=[pk, "ytail"], w=[("xtail", blk)])
            if self.dbg_stop <= 33:
                return
            pc, pck = self.psum()
            for j in range(3):
                P.pe(lambda e, j=j, pc=pc: e.matmul(pc[:T, :], lhsT=sh[:T, j, :T], rhs=Yblk[:T, j, :], start=(j == 0), stop=False),
                     r=["ogT", f"m_{kind}_Sh"], w=[pck])
            P.pe(lambda e, pc=pc: e.matmul(pc[:T, :], lhsT=self.identb[:T, :T], rhs=Yblk[:T, 3, :], start=False, stop=False),
                 r=["ogT", "identb"], w=[pck])
            if self.dbg_stop <= 34:
                return
            for j in range(3):
                P.pe(lambda e, j=j, pc=pc: e.matmul(pc[:T, :], lhsT=shp[r0:r1, j, :T], rhs=ytail[r0:r1, j, :], start=False, stop=False),
                     r=["ytail", f"m_{kind}_ShP"], w=[pck])
            if self.dbg_stop <= 35:
                return
            P.pe(lambda e, pc=pc, blk=blk: e.matmul(pc[:T, :], lhsT=self.onesb[0:1, :T], rhs=cwblk[0:1, 4, :],
                                                    start=False, stop=True), r=["ones_row_b", "cwblk"], w=[pck])
            if blk < 4:
                P.act(lambda e, pc=pc, blk=blk: e.activation(out=xs[:T, blk * 512:(blk + 1) * 512], in_=pc[:T, :], func=AF.Silu),
                      r=[pck], w=[("vb", blk)])
            elif blk == 4:
                P.act(lambda e, pc=pc: e.activation(out=Bb[:T, :], in_=pc[:T, :], func=AF.Silu), r=[pck], w=["qg"])
            else:
                P.act(lambda e, pc=pc: e.activation(out=Cb[:T, :], in_=pc[:T, :], func=AF.Silu), r=[pck], w=["kg"])
        if self.dbg_stop <= 41:
            return
        sm = self.sb("ssm_small", [128, 8, 32])
        psd, pkd = self.proj_block(hT, T, 5120, 5152)
        P.dve(lambda e: e.tensor_tensor(out=sm[:T, 4, :], in0=psd[:T, 0:32], in1=self.dtb[:T, :], op=ALU.add), r=[pkd, "dtb"], w=[("sm", 4)])
        P.act(lambda e: e.activation(out=sm[:T, 4, :], in_=sm[:T, 4, :], func=AF.Exp), r=[("sm", 4)], w=[("sm", 4)])
        P.act(lambda e: e.activation(out=sm[:T, 0, :], in_=sm[:T, 4, :], func=AF.Ln, bias=self.onec[:T, 0:1]), r=[("sm", 4), "onec"], w=[("sm", 0)])
        P.dve(lambda e: e.tensor_tensor(out=sm[:T, 1, :], in0=sm[:T, 0, :], in1=self.aneg[:T, :], op=ALU.mult), r=[("sm", 0), "aneg"], w=[("sm", 1)])
        la = sm[:T, 1, :]
        psr, pkr = self.psum()
        P.pe(lambda e: e.matmul(psr[:T, 0:32], lhsT=M["GT"][:T, :T], rhs=la, start=True, stop=True), r=[("sm", 1), f"m_{kind}_GT"], w=[pkr])
        P.act(lambda e: e.activation(out=sm[:T, 2, :], in_=psr[:T, 0:32], func=AF.Exp), r=[pkr], w=[("sm", 2)])
        P.dve(lambda e: e.tensor_tensor(out=sm[:T, 2, :], in0=sm[:T, 2, :], in1=sm[:T, 0, :], op=ALU.mult), r=[("sm", 2), ("sm", 0)], w=[("sm", 2)])
        if not state_only:
            psc, pkc = self.psum()
            P.pe(lambda e: e.matmul(psc[:T, 0:32], lhsT=M["LE"][:T, :T], rhs=la, start=True, stop=True), r=[("sm", 1), f"m_{kind}_LE"], w=[pkc])
            P.act(lambda e: e.activation(out=sm[:T, 3, :], in_=psc[:T, 0:32], func=AF.Exp), r=[pkc], w=[("sm", 3)])
        elb = self.sb("g4", [128, 4, 512])[:, 3, :].rearrange("p (j h) -> p j h", h=32)
        pse, pke = self.psum()
        for sq in seqs:
            j = sq["j"]
            sc = M["seg"][:T, j:j + 1]
            scb = AP(sc.tensor, sc.offset, [list(sc.ap[0]), [0, 128]])
            P.pe(lambda e, j=j, scb=scb: e.matmul(pse[:, j * 32:(j + 1) * 32], lhsT=scb, rhs=la, start=True, stop=True),
                 r=[("sm", 1), f"m_{kind}_seg"], w=[pke])
        P.act(lambda e: e.activation(out=elb[:, :nseq, :], in_=pse[:, 0:nseq * 32].rearrange("p (j h) -> p j h", h=32), func=AF.Exp),
              r=[pke], w=["encum"])
        if self.dbg_stop <= 42:
            return
        if not state_only:
            zs = self.sb("sr", [128, DI], BF16)
            for b in range(4):
                psz, pkz = self.proj_block(hT, T, b * 512, (b + 1) * 512)
                P.act(lambda e, b=b, psz=psz: e.activation(out=zs[:T, b * 512:(b + 1) * 512], in_=psz[:T, :], func=AF.Silu), r=[pkz], w=[("sr", b)])
            BT = self.sb("qT", [128, 4, 128], BF16)
            CT = self.sb("kT", [128, 4, 128], BF16)
            pt, ptk = self.psum_t()
            for g in range(4):
                P.pe(lambda e, g=g: e.transpose(out=pt[:, g * 128:g * 128 + T], in_=Bb[:T, g * 128:(g + 1) * 128], identity=self.identb[:T, :T]),
                     r=["qg", "identb"], w=[ptk])
                P.pe(lambda e, g=g: e.transpose(out=pt[:, 512 + g * 128:512 + g * 128 + T], in_=Cb[:T, g * 128:(g + 1) * 128],
                                                identity=self.identb[:T, :T]), r=["kg", "identb"], w=[ptk])
            ptv = pt[:].rearrange("p (k t) -> p k t", k=8)
            P.dve(lambda e: e.tensor_copy(out=BT[:, :, :T], in_=ptv[:, 0:4, :T]), r=[ptk], w=["qT"])
            P.dve(lambda e: e.tensor_copy(out=CT[:, :, :T], in_=ptv[:, 4:8, :T]), r=[ptk], w=["kT"])
            psa, pka = self.psum()
            for g in range(4):
                P.pe(lambda e, g=g: e.matmul(psa[:T, g * 128:g * 128 + T], lhsT=BT[:, g, :T], rhs=CT[:, g, :T], start=True, stop=True),
                     r=["qT", "kT"], w=[pka])
            cbT = self.sb("attT", [128, 4, 128], BF16)
            P.dve(lambda e: e.tensor_tensor(out=cbT[:T, :, :T], in0=psa[:T, :].rearrange("p (g t) -> p g t", g=4)[:, :, :T],
                                            in1=bc_mid(M["LE"][:T, :T], 4), op=ALU.mult), r=[pka, f"m_{kind}_LE"], w=["attT"])
            psy = [self.psum(hold=True) for _ in range(4)]
        if self.dbg_stop <= 43:
            for g in range(4):
                self.psum_release(psy[g][1])
            return
        uu = self.sb("junk", [128, DI], BF16)
        P.dve(lambda e: e.tensor_tensor(out=uu[:T, :].rearrange("p (h q) -> p h q", h=32), in0=xs[:T, :].rearrange("p (h q) -> p h q", h=32),
                                        in1=bc_last(sm[:T, 2, :], 64), op=ALU.mult), r=["vb", ("sm", 2)], w=["junk"])
        if self.dbg_stop <= 43.1:
            for g in range(4):
                self.psum_release(psy[g][1])
            return
        for si, sq in enumerate(seqs):
            j = sq["j"]
            if sq.get("load") is not None:
                sq["load"]()
            if self.dbg_stop <= 43.2:
                for g in range(4):
                    self.psum_release(psy[g][1])
                return
            ST, STk, STb, STbk = sq["S"], sq["Skey"], sq["Sb"], sq["Sbkey"]
            last = si == nseq - 1
            if not state_only:
                if sq["masked"]:
                    CTm = self.sb("qTm", [128, 4, 128], BF16)
                    self.masked_cols(CTm, "qTm", CT, "kT", si, T)
                    csrc, ck = CTm, "qTm"
                else:
                    csrc, ck = CT, "kT"
                for g in range(4):
                    P.pe(lambda e, g=g, csrc=csrc, STb=STb, si=si, last=last: e.matmul(psy[g][0][:T, :], lhsT=csrc[:, g, :T],
                                                                                      rhs=STb[:, g * 512:(g + 1) * 512],
                                                                                      start=(si == 0), stop=last),
                         r=[ck, STbk], w=[psy[g][1]])
            if self.dbg_stop <= 43.4:
                for g in range(4):
                    self.psum_release(psy[g][1])
                return
            if sq["masked"]:
                Bm = self.sb("khm", [128, 512], BF16)
                P.dve(lambda e, j=j: e.tensor_scalar(out=Bm[:T, :], in0=Bb[:T, :], scalar1=M["seg"][:T, j:j + 1], scalar2=None, op0=ALU.mult),
                      r=["qg", "m_s_seg"], w=["khm"])
                bsrc, bk = Bm, "khm"
            else:
                bsrc, bk = Bb, "qg"
            for g in range(4):
                psu, pku = self.psum()
                P.pe(lambda e, g=g, psu=psu, bsrc=bsrc: e.matmul(psu[:, :], lhsT=bsrc[:T, g * 128:(g + 1) * 128], rhs=uu[:T, g * 512:(g + 1) * 512],
                                                                start=True, stop=True), r=[bk, "junk"], w=[pku])
                stv = ST[:, g * 512:(g + 1) * 512].rearrange("p (h q) -> p h q", h=8)
                P.dve(lambda e, g=g, stv=stv, j=j: e.tensor_tensor(out=stv, in0=stv, in1=bc_last(elb[:, j, g * 8:(g + 1) * 8], 64), op=ALU.mult),
                      r=[STk, "encum"], w=[STk])
                P.dve(lambda e, g=g, psu=psu, ST=ST: e.tensor_tensor(out=ST[:, g * 512:(g + 1) * 512], in0=ST[:, g * 512:(g + 1) * 512],
                                                                     in1=psu[:, :], op=ALU.add), r=[pku, STk], w=[STk])
            if self.dbg_stop <= 43.6:
                for g in range(4):
                    self.psum_release(psy[g][1])
                return
            if sq.get("done") is not None:
                sq["done"]()
        if state_only:
            return
        if self.dbg_stop <= 44:
            for g in range(4):
                self.psum_release(psy[g][1])
            return
        og = self.sb("og", [128, DI], BF16)
        for g in range(4):
            P.dve(lambda e, g=g: e.tensor_tensor(out=og[:T, g * 512:(g + 1) * 512].rearrange("p (h q) -> p h q", h=8),
                                                 in0=psy[g][0][:T, :].rearrange("p (h q) -> p h q", h=8),
                                                 in1=bc_last(sm[:T, 3, g * 8:(g + 1) * 8], 64), op=ALU.mult),
                  r=[psy[g][1], ("sm", 3)], w=[("og", g)])
            self.psum_release(psy[g][1])
        if self.dbg_stop <= 45:
            return
        P.dve(lambda e: e.tensor_tensor(out=uu[:T, :].rearrange("p (h q) -> p h q", h=32), in0=xs[:T, :].rearrange("p (h q) -> p h q", h=32),
                                        in1=bc_last(sm[:T, 0, :], 64), op=ALU.mult), r=["vb", ("sm", 0)], w=["junk"])
        g4 = self.sb("g4", [128, 4, 512])
        laexp = g4[:, 0:2, :].rearrange("p a (b t) -> p (a b) t", t=128)
        Eg = self.sb("Eg", [128, 8, 128], BF16)
        Mg = self.sb("Mg", [128, 8, 128], BF16)
        st = self.sb("ostat", [128, 12])
        sq_junk = g4[:, 3, :]
        for g in range(4):
            P.dve(lambda e, g=g: e.tensor_tensor(out=laexp[:T, :, :T], in0=bc_last(sm[:T, 1, g * 8:(g + 1) * 8], T), in1=bc_mid(M["LE"][:T, :T], 8),
                                                 op=ALU.mult), r=[("sm", 1), f"m_{kind}_LE"], w=["spf", "erev"])
            nmm = 2 if T == 128 else 1
            for hb in range(nmm):
                psd2, pkd2 = self.psum()
                e0, e1 = (hb * 4, hb * 4 + 4) if nmm == 2 else (0, 8)
                P.pe(lambda e, psd2=psd2, e0=e0, e1=e1: e.matmul(psd2[:T, 0:(e1 - e0) * T].rearrange("p (a t) -> p a t", t=T),
                                                                 lhsT=M["GT"][:T, :T], rhs=laexp[:T, e0:e1, :T], start=True, stop=True),
                     r=["spf", "erev", f"m_{kind}_GT"], w=[pkd2])
                P.act(lambda e, psd2=psd2, e0=e0, e1=e1: e.activation(out=Eg[:T, e0:e1, :T],
                                                                      in_=psd2[:T, 0:(e1 - e0) * T].rearrange("p (a t) -> p a t", t=T), func=AF.Exp),
                      r=[pkd2], w=[("Eg", hb)])
            P.dve(lambda e, g=g: e.tensor_tensor(out=Mg[:T, :, :T], in0=Eg[:T, :, :T], in1=bc_mid(cbT[:T, g, :T], 8), op=ALU.mult),
                  r=["Eg", "attT"], w=["Mg"])
            pyi, pyik = self.psum()
            for eh in range(8):
                h = g * 8 + eh
                P.pe(lambda e, eh=eh, h=h, pyi=pyi: e.matmul(pyi[:T, eh * 64:(eh + 1) * 64], lhsT=Mg[:T, eh, :T], rhs=uu[:T, h * 64:(h + 1) * 64],
                                                             start=True, stop=True), r=["Mg", "junk"], w=[pyik])
            on = self.sb(f"on{g % 2}", [128, 512])
            onk = f"on{g % 2}"
            on2 = g4[:, 2, :]
            gs = slice(g * 512, (g + 1) * 512)
            P.dve(lambda e, on=on, pyi=pyi, gs=gs: e.tensor_tensor(out=on[:T, :], in0=pyi[:T, :], in1=og[:T, gs], op=ALU.add),
                  r=[pyik, ("og", g)], w=[onk])
            P.dve(lambda e, g=g, gs=gs: e.tensor_tensor(out=on2[:T, :].rearrange("p (h q) -> p h q", h=8),
                                                        in0=xs[:T, gs].rearrange("p (h q) -> p h q", h=8),
                                                        in1=bc_last(self.dsk[:T, g * 8:(g + 1) * 8], 64), op=ALU.mult),
                  r=[("vb", g), "dsk"], w=["ecum"])
            P.dve(lambda e, on=on: e.tensor_tensor(out=on[:T, :], in0=on[:T, :], in1=on2[:T, :], op=ALU.add), r=[onk, "ecum"], w=[onk])
            P.dve(lambda e, on=on, gs=gs: e.tensor_tensor(out=on[:T, :], in0=on[:T, :], in1=zs[:T, gs], op=ALU.mult), r=[onk, ("sr", g)], w=[onk])
            P.act(lambda e, on=on, g=g: e.activation(out=sq_junk[:T, :], in_=on[:T, :], func=AF.Square, accum_out=st[:T, g:g + 1]),
                  r=[onk], w=["encum", ("ostat", g)])
            P.act(lambda e, g=g: e.activation(out=st[:T, 4 + g:5 + g], in_=st[:T, g:g + 1], func=AF.Sqrt, scale=1.0 / 512, bias=self.epsc[:T, 0:1]),
                  r=[("ostat", g), "epsc"], w=[("ostat", 4 + g)])
            P.dve(lambda e, g=g: e.reciprocal(out=st[:T, 8 + g:9 + g], in_=st[:T, 4 + g:5 + g]), r=[("ostat", 4 + g)], w=[("ostat", 8 + g)])
            P.dve(lambda e, on=on, g=g, gs=gs: e.tensor_scalar(out=og[:T, gs], in0=on[:T, :], scalar1=st[:T, 8 + g:9 + g], scalar2=None, op0=ALU.mult),
                  r=[onk, ("ostat", 8 + g)], w=[("og", g)])
        if self.dbg_stop <= 46:
            return
        self.out_proj_residual(og, "og", xt, xkey, T)

    def run_layer_ssd(self, l, first, lastl):
        P, d, cfg = self.P, self.d, self.cfg
        NT, NS, TS = cfg.NT, cfg.NS, cfg.TS
        self.ssd_setup()
        ST = self.sb("gS0", [128, 4, 512])[:].rearrange("p a b -> p (a b)")
        STb = self.sb("gSb0", [128, 4, 512], BF16)[:].rearrange("p a b -> p (a b)")
        xtail = self.sb("xtail", [128, 3072], BF16)
        P.dve(lambda e: e.memset(ST, 0.0), w=["gS0"])
        P.dve(lambda e: e.memset(STb, 0.0), w=["gSb0"])
        P.dve(lambda e: e.memset(xtail[:, :], 0.0), w=["xtail"])
        if cfg.G > 1:
            xt = self.load_x(first, NT - 1, 0)
            hT = self.norm_transpose(xt, "xt0", 128)
            writes = []
            for blk in range(6):
                ps, pk = self.proj_block(hT, 128, 2048 + blk * 512, 2560 + blk * 512)
                so = self.sb(f"on{blk % 2}", [128, 512])
                P.dve(lambda e, ps=ps, so=so: e.tensor_copy(out=so[:, :], in_=ps[:, :]), r=[pk], w=[f"on{blk % 2}"])
                writes.append((blk * 512, (blk + 1) * 512, so[125:128, :], [f"on{blk % 2}"]))
                if blk == 0:
                    xin_cvp = self.dscr("xin_cv", [128, 256])
                    xout_cvp = self.dscr("xout_cv", [cfg.G * 128, 256])
                    xin_cv = xin_cvp.rearrange("p w -> (p w)")[0:9216].rearrange("(r c) -> r c", c=3072)
                    xout_cv = xout_cvp.rearrange("(g p) w -> g (p w)", g=cfg.G)[:, 0:9216].rearrange("g (r c) -> g r c", c=3072)
                P.dma("sp", xin_cv[:, blk * 512:(blk + 1) * 512], so[125:128, :], r=[f"on{blk % 2}"], w=[("xin_cv", blk)], semkey=("xi_cv", blk % 2))
            groups = [list(range(b * cfg.G, (b + 1) * cfg.G)) for b in range(cfg.B)]
            P.op("pool", lambda e: e.collective_compute("AllGather", ALU.bypass, replica_groups=groups, ins=[xin_cvp[:, :]], outs=[xout_cvp[:, :]]),
                 r=["xin_cv"], w=["xout_cv"], dma=True, semkey=("dma", "cc", 3), inc=1)

            def init_tail():
                for blk in range(6):
                    so = self.sb(f"on{blk % 2}", [128, 512])
                    for gg in range(cfg.G):
                        P.dma("sp", so[gg * 3:gg * 3 + 3, :], xout_cv[gg, :, blk * 512:(blk + 1) * 512], r=["xout_cv"], w=[f"on{blk % 2}"],
                              semkey=("xo_cv", blk % 2))
                    ps, pk = self.psum()
                    P.pe(lambda e, ps=ps, so=so: e.matmul(ps[:, :], lhsT=self.sel[:, :], rhs=so[0:cfg.G * 3, :], start=True, stop=True),
                         r=[f"on{blk % 2}", "sel"], w=[pk])
                    P.dve(lambda e, ps=ps, blk=blk: e.tensor_copy(out=xtail[64:128, blk * 512:(blk + 1) * 512], in_=ps[64:128, :]),
                          r=[pk], w=[("xtail", blk)])
            init_tail()
            Dt = self.sb("Dtot", [128, 32])
            P.dve(lambda e: e.memset(Dt[:], 1.0), w=["Dtot"])
            for ti, xt, xkey in self.tiles_iter(first, list(range(NT))):
                seqs = [dict(j=0, S=ST, Skey="gS0", Sb=STb, Sbkey="gSb0", masked=False)]
                self.ssd_tile(ti, xt, xkey, 128, "p", True, seqs, (64, 128), None)
                elb = self.bufs["g4"][:, 3, :].rearrange("p (j h) -> p j h", h=32)
                P.dve(lambda e, elb=elb: e.tensor_tensor(out=Dt[:, :], in0=Dt[:, :], in1=elb[:, 0, :], op=ALU.mult), r=["Dtot", "encum"], w=["Dtot"])
            self.state_combine("s1", ST, "gS0", Dt[:, :], "Dtot", 32)
            P.act(lambda e: e.activation(out=STb, in_=ST, func=AF.Copy), r=["gS0"], w=["gSb0"])
            init_tail()

        def st_load(src_seq):
            srcv = src_seq.rearrange("(c h2) q n -> (h2 q) c n", c=16)
            for cg in range(4):
                stg = self.sb(f"on{cg % 2}", [128, 512])
                P.dma("sp", stg[:].rearrange("p (c n) -> p c n", c=4), srcv[:, cg * 4:(cg + 1) * 4, :], w=[f"on{cg % 2}"], semkey=("stl", cg % 2))
                ps, pk = self.psum()
                for c in range(4):
                    P.pe(lambda e, c=c, ps=ps, stg=stg: e.transpose(out=ps[:, c * 128:(c + 1) * 128], in_=stg[:, c * 128:(c + 1) * 128],
                                                                    identity=self.identf[:, :]), r=[f"on{cg % 2}", "identf"], w=[pk])
                P.dve(lambda e, ps=ps, cg=cg: e.tensor_copy(out=ST[:, cg * 512:(cg + 1) * 512], in_=ps[:, :]), r=[pk], w=["gS0"])
                P.dve(lambda e, ps=ps, cg=cg: e.tensor_copy(out=STb[:, cg * 512:(cg + 1) * 512], in_=ps[:, :]), r=[pk], w=["gSb0"])

        def st_store(dst_seq, semname):
            dstv = dst_seq.rearrange("(c h2) q n -> (h2 q) c n", c=16)
            for cg in range(4):
                ps, pk = self.psum()
                for c in range(4):
                    cc = cg * 4 + c
                    P.pe(lambda e, c=c, cc=cc, ps=ps: e.transpose(out=ps[:, c * 128:(c + 1) * 128], in_=ST[:, cc * 128:(cc + 1) * 128],
                                                                  identity=self.identf[:, :]), r=["gS0", "identf"], w=[pk])
                stg = self.sb(f"on{cg % 2}", [128, 512])
                P.dve(lambda e, ps=ps, stg=stg: e.tensor_copy(out=stg[:, :], in_=ps[:, :]), r=[pk], w=[f"on{cg % 2}"])
                P.dma("pool", dstv[:, cg * 4:(cg + 1) * 4, :], stg[:].rearrange("p (c n) -> p c n", c=4), r=[f"on{cg % 2}"],
                      semkey=(semname, cg % 2), final=True)

        ntiles = NT + 1
        for ti, xt, xkey in self.tiles_iter(first, list(range(ntiles))):
            T = 128 if ti < NT else TS
            if ti < NT:
                def done():
                    P.act(lambda e: e.activation(out=STb, in_=ST, func=AF.Copy), r=["gS0"], w=["gSb0"])
                seqs = [dict(j=0, S=ST, Skey="gS0", Sb=STb, Sbkey="gSb0", masked=False, done=done)]
                conv_out = None
                if ti == NT - 1:
                    def conv_out(stg, skey, blk):
                        P.dma("pool", d["conv1_p"][:, blk * 512:(blk + 1) * 512], stg[125:128, :], r=[skey], semkey=("cvo", skey), final=True)
                        if blk == 0:
                            self.dbg("stg0", stg[:, :], [skey], [128, 512])
                self.ssd_tile(ti, xt, xkey, T, "p", False, seqs, (64, 128), conv_out)
                if ti == NT - 1:
                    st_store(d["ssm1_p"], "sst_p")
            else:
                P.dma("pool", xtail[0:NS * 3, :], d["conv1"].rearrange("s r c -> (s r) c"), w=["xtail"], semkey="xtl_s")
                seqs = []
                for j in range(NS):
                    def load(j=j):
                        st_load(d["ssm1"][j])
                    def done(j=j):
                        st_store(d["ssm1_s"][j], "sst_s")
                    seqs.append(dict(j=j, S=ST, Skey="gS0", Sb=STb, Sbkey="gSb0", masked=True, load=load, done=done))
                def conv_out(stg, skey, blk):
                    P.dma("pool", d["cvs"][:, blk * 512:(blk + 1) * 512], stg[:TS, :], r=[skey], w=[("cvs", blk)], semkey=("cvo", skey))
                    P.dma("pool", d["conv1_s"][:, :, blk * 512:(blk + 1) * 512],
                          d["cvs"][:, blk * 512:(blk + 1) * 512].rearrange("(s t) c -> s t c", t=8)[:, 5:8, :],
                          r=[("cvs", blk)], semkey="fin2", final=True)
                self.ssd_tile(ti, xt, xkey, T, "s", False, seqs, (0, NS * 3), conv_out)
            self.store_x(xt, xkey, ti, T, lastl)


    def swa_setup(self):
        P, d, cfg = self.P, self.d, self.cfg
        NS, TS = cfg.NS, cfg.TS
        esink = self.sb("esink", [128, 32])
        P.dma("sp", esink[:], d["l2_sinks"].partition_broadcast(128), w=["esink"], semkey="esink")
        P.act(lambda e: e.activation(out=esink[:], in_=esink[:], func=AF.Exp), r=["esink"], w=["esink"])
        mas = self.sb("m_s_maskA", [128, TS])
        P.dma("sp", mas[:], d["c_s_maskA"][:, :], w=["m_s_maskA"], semkey="c_s_maskA")
        ma0 = self.sb("m_p_maskA0", [128, 128])
        P.dma("sp", ma0[:], d["c_p_maskA0"][:, :], w=["m_p_maskA0"], semkey="c_p_maskA0")
        zl = self.sb("zeros_b", [128, 128], BF16)
        P.dve(lambda e: e.memset(zl[:], 0.0), w=["zeros_b"])
        self.esink, self.mas, self.ma0, self.zl = esink, mas, ma0, zl

    def swa_views(self):
        cw = self.sb("cwblk", [128, 5, 512], BF16)
        yt = self.sb("ytail", [128, 3, 512], BF16)
        kT2 = [cw[:, i, :].rearrange("p (k t) -> p k t", k=4) for i in range(3)]
        vaug = [yt[:, i, 0:260].rearrange("p (k c) -> p k c", k=4) for i in range(3)]
        return kT2, vaug

    def swa_kv_prep(self, kvf, kvkey, T, slot):
        P = self.P
        kT2, vaug = self.swa_views()
        kd = self.sb("qg", [128, 512], BF16)
        kdv = kd[:T, :].rearrange("p (k r c) -> p k r c", k=4, r=2)
        kin = kvf[:T, 0:256].rearrange("p (k c) -> p k c", k=4)
        for r in range(2):
            P.dve(lambda e, r=r: e.tensor_copy(out=kdv[:, :, r, :], in_=kin), r=[kvkey], w=["qg"])
        P.dve(lambda e: e.tensor_copy(out=vaug[slot][:T, :, 0:64], in_=kvf[:T, 256:512].rearrange("p (k c) -> p k c", k=4)),
              r=[kvkey], w=[("ytail", slot)])
        P.dve(lambda e: e.memset(vaug[slot][:T, :, 64:65], 1.0), w=[("ytail", slot)])
        pt, ptk = self.psum_t()
        for k in range(4):
            P.pe(lambda e, k=k: e.transpose(out=pt[:, k * 128:k * 128 + T], in_=kd[:T, k * 128:(k + 1) * 128], identity=self.identb[:T, :T]),
                 r=["qg", "identb"], w=[ptk])
        P.dve(lambda e: e.tensor_copy(out=kT2[slot][:, :, :T], in_=pt[:, 0:512].rearrange("p (k t) -> p k t", k=4)[:, :, :T]),
              r=[ptk], w=[("cwblk", slot)])

    def swa_tile(self, ti, xt, xkey, T, kind, cur, prev, maskA, maskAkey, seqs_cache, kv_out):
        P, d, cfg = self.P, self.d, self.cfg
        M = self.M[kind]
        kT2, vaug = self.swa_views()
        hT = self.norm_transpose(xt, xkey, T)
        qb = self.sb("vb", [128, DI], BF16)
        for b in range(4):
            ps, pk = self.proj_block(hT, T, b * 512, (b + 1) * 512)
            P.act(lambda e, b=b, ps=ps: e.activation(out=qb[:T, b * 512:(b + 1) * 512], in_=ps[:T, :], func=AF.Copy, scale=0.125),
                  r=[pk], w=[("vb", b)])
        qT = self.sb("ogT", [128, 16, 128], BF16)
        for half in range(2):
            pt, ptk = self.psum_t()
            for jj in range(8):
                c = half * 8 + jj
                P.pe(lambda e, jj=jj, c=c, pt=pt: e.transpose(out=pt[:, jj * 128:jj * 128 + T], in_=qb[:T, c * 128:(c + 1) * 128],
                                                              identity=self.identb[:T, :T]), r=[("vb", c // 4), "identb"], w=[ptk])
            P.dve(lambda e, half=half, pt=pt: e.tensor_copy(out=qT[:, half * 8:half * 8 + 8, :T],
                                                            in_=pt[:].rearrange("p (k t) -> p k t", k=8)[:, :, :T]), r=[ptk], w=[("ogT", half)])
        psk, pkk = self.proj_block(hT, T, 2048, 2560)
        kvf = self.sb("on0", [128, 512])
        P.dve(lambda e: e.tensor_copy(out=kvf[:T, :], in_=psk[:T, :]), r=[pkk], w=["on0"])
        self.swa_kv_prep(kvf, "on0", T, cur)
        if kv_out is not None:
            kv_out(kvf, "on0")
        sg = self.sb("sr", [128, DI], BF16)
        for b in range(4):
            ps, pk = self.proj_block(hT, T, 2560 + b * 512, 3072 + b * 512)
            P.act(lambda e, b=b, ps=ps: e.activation(out=sg[:T, b * 512:(b + 1) * 512], in_=ps[:T, :], func=AF.Silu), r=[pk], w=[("sr", b)])
        og = self.sb("og", [128, DI], BF16)
        PA = self.sb("qT", [128, 4, 128], BF16)
        PB = self.sb("kT", [128, 4, 128], BF16)
        PAm = self.sb("qTm", [128, 4, 128], BF16)
        st = self.sb("swstat", [128, 8])
        for par in range(2):
            pbase = par * 64
            if seqs_cache is None:
                combos = [[kvh] for kvh in range(4)]
            else:
                combos = [[0, 1, 2, 3]]
            for cb in combos:
                pv = {}
                for kvh in cb:
                    pv[kvh] = self.psum(hold=True)
                    P.pe(lambda e, pv=pv, kvh=kvh: e.matmul(pv[kvh][0][:T, 0:260], lhsT=self.zl[:T, :T], rhs=vaug[cur][:T, :, :].rearrange("p k c -> p (k c)"),
                                                    start=True, stop=False), r=["zeros_b", ("ytail", cur)], w=[pv[kvh][1]])
                for kvh in cb:
                    qrhs = qT[pbase:pbase + 64, kvh * 4:kvh * 4 + 4, :T]
                    psb, pkb = self.psum()
                    klhs = kT2[cur][pbase:pbase + 64, kvh, :T]
                    P.pe(lambda e, kvh=kvh, psb=psb, qrhs=qrhs, klhs=klhs: e.matmul(psb[:T, 0:4 * T].rearrange("p (a t) -> p a t", a=4),
                                                                        lhsT=klhs, rhs=qrhs, start=True, stop=True),
                         r=[("cwblk", cur), "ogT"], w=[pkb])
                    P.act(lambda e, psb=psb: e.activation(out=PB[:T, :, :T], in_=psb[:T, 0:4 * T].rearrange("p (a t) -> p a t", a=4), func=AF.Exp),
                          r=[pkb], w=["kT"])
                    P.dve(lambda e: e.tensor_tensor(out=PB[:T, :, :T], in0=PB[:T, :, :T], in1=bc_mid(M["LE"][:T, :T], 4), op=ALU.mult),
                          r=["kT", f"m_{kind}_LE"], w=["kT"])
                    for i in range(4):
                        P.pe(lambda e, pv=pv, kvh=kvh, i=i: e.matmul(pv[kvh][0][:T, i * 65:(i + 1) * 65], lhsT=PB[:T, i, :T], rhs=vaug[cur][:T, kvh, :],
                                                             start=False, stop=False), r=["kT", ("ytail", cur)], w=[pv[kvh][1]])
                if seqs_cache is None:
                    kvh = cb[0]
                    qrhs = qT[pbase:pbase + 64, kvh * 4:kvh * 4 + 4, :T]
                    psa, pka = self.psum()
                    klhs = kT2[prev][pbase:pbase + 64, kvh, :]
                    P.pe(lambda e, kvh=kvh, psa=psa, qrhs=qrhs, klhs=klhs: e.matmul(psa[:, 0:4 * T].rearrange("p (a t) -> p a t", a=4),
                                                                        lhsT=klhs, rhs=qrhs, start=True, stop=True),
                         r=[("cwblk", prev), "ogT"], w=[pka])
                    P.act(lambda e, psa=psa: e.activation(out=PA[:, :, :T], in_=psa[:, 0:4 * T].rearrange("p (a t) -> p a t", a=4), func=AF.Exp),
                          r=[pka], w=["qT"])
                    P.dve(lambda e: e.tensor_tensor(out=PA[:, :, :T], in0=PA[:, :, :T], in1=bc_mid(maskA[:, :T], 4), op=ALU.mult),
                          r=["qT", maskAkey], w=["qT"])
                    for i in range(4):
                        P.pe(lambda e, pv=pv, kvh=kvh, i=i: e.matmul(pv[kvh][0][:T, i * 65:(i + 1) * 65], lhsT=PA[:, i, :T], rhs=vaug[prev][:, kvh, :],
                                                             start=False, stop=False), r=["qT", ("ytail", prev)], w=[pv[kvh][1]])
                else:
                    for si, sq in enumerate(seqs_cache):
                        sq["load"]()
                        for kvh in cb:
                            qrhs = qT[pbase:pbase + 64, kvh * 4:kvh * 4 + 4, 8 * si:8 * si + 8]
                            psa, pka = self.psum()
                            klhs = kT2[2][pbase:pbase + 64, kvh, :]
                            P.pe(lambda e, kvh=kvh, psa=psa, qrhs=qrhs, klhs=klhs: e.matmul(psa[:, 0:32].rearrange("p (a t) -> p a t", a=4),
                                                                                lhsT=klhs, rhs=qrhs, start=True, stop=True),
                                 r=[("cwblk", 2), "ogT"], w=[pka])
                            P.act(lambda e, psa=psa: e.activation(out=PA[:, :, 0:8], in_=psa[:, 0:32].rearrange("p (a t) -> p a t", a=4), func=AF.Exp),
                                  r=[pka], w=["qT"])
                            first_use = (si == 0 and kvh == cb[0])
                            if first_use:
                                P.dve(lambda e: e.memset(PAm[:, :, :T], 0.0), w=["qTm"])
                            elif si > 0 and kvh == cb[0]:
                                P.dve(lambda e, si=si: e.memset(PAm[:, :, 8 * (si - 1):8 * si], 0.0), w=["qTm"])
                            P.dve(lambda e, si=si: e.tensor_tensor(out=PAm[:, :, 8 * si:8 * si + 8], in0=PA[:, :, 0:8],
                                                                   in1=bc_mid(self.mas[:, 8 * si:8 * si + 8], 4), op=ALU.mult),
                                  r=["qT", "m_s_maskA"], w=["qTm"])
                            for i in range(4):
                                P.pe(lambda e, pv=pv, kvh=kvh, i=i: e.matmul(pv[kvh][0][:T, i * 65:(i + 1) * 65], lhsT=PAm[:, i, :T], rhs=vaug[2][:, kvh, :],
                                                                     start=False, stop=False), r=["qTm", ("ytail", 2)], w=[pv[kvh][1]])
                for kvh in cb:
                    P.pe(lambda e, pv=pv, kvh=kvh: e.matmul(pv[kvh][0][:T, 0:260], lhsT=self.zl[:T, :T], rhs=vaug[cur][:T, :, :].rearrange("p k c -> p (k c)"),
                                                    start=False, stop=True), r=["zeros_b", ("ytail", cur)], w=[pv[kvh][1]])
                    pvv = pv[kvh][0][:T, 0:260].rearrange("p (a c) -> p a c", a=4)
                    h0 = kvh * 8 + par
                    es = self.esink[:T, h0:h0 + 7:2]
                    P.dve(lambda e, pvv=pvv, es=es: e.tensor_tensor(out=st[:T, 0:4], in0=pvv[:, :, 64], in1=es, op=ALU.add),
                          r=[pv[kvh][1], "esink"], w=[("swstat", 0)])
                    P.dve(lambda e: e.reciprocal(out=st[:T, 4:8], in_=st[:T, 0:4]), r=[("swstat", 0)], w=[("swstat", 4)])
                    on = self.sb("on1", [128, 512])
                    onv = on[:T, 0:256].rearrange("p (a c) -> p a c", a=4)
                    P.dve(lambda e, pvv=pvv, onv=onv: e.tensor_tensor(out=onv, in0=pvv[:, :, 0:64], in1=bc_last(st[:T, 4:8], 64), op=ALU.mult),
                          r=[pv[kvh][1], ("swstat", 4)], w=["on1"])
                    self.psum_release(pv[kvh][1])
                    c0 = (kvh * 8 + par) * 64
                    ogv = AP(og[:T, c0:c0 + 64].tensor, og[:T, c0:c0 + 64].offset, [list(og[:T, c0:c0 + 64].ap[0]), [128, 4], [1, 64]])
                    sgv = AP(sg[:T, c0:c0 + 64].tensor, sg[:T, c0:c0 + 64].offset, [list(sg[:T, c0:c0 + 64].ap[0]), [128, 4], [1, 64]])
                    P.dve(lambda e, ogv=ogv, sgv=sgv, onv=onv: e.tensor_tensor(out=ogv, in0=onv, in1=sgv, op=ALU.mult),
                          r=["on1", "sr"], w=["og"])
        self.out_proj_residual(og, "og", xt, xkey, T)

    def run_layer_swa(self, l, first, lastl):
        P, d, cfg = self.P, self.d, self.cfg
        NT, NS, TS = cfg.NT, cfg.NS, cfg.TS
        self.swa_setup()
        kT2, vaug = self.swa_views()
        cw = self.sb("cwblk", [128, 5, 512], BF16)
        yt = self.sb("ytail", [128, 3, 512], BF16)
        P.dve(lambda e: e.memset(cw[:], 0.0), w=["cwblk"])
        P.dve(lambda e: e.memset(yt[:], 0.0), w=["ytail"])
        if cfg.G > 1:
            xt = self.load_x(first, NT - 1, 0)
            hT = self.norm_transpose(xt, "xt0", 128)
            psk, pkk = self.proj_block(hT, 128, 2048, 2560)
            kvf = self.sb("on0", [128, 512])
            P.dve(lambda e: e.tensor_copy(out=kvf[:, :], in_=psk[:, :]), r=[pkk], w=["on0"])
            xout, xk = self.allgather("kv", 128, 512, [(0, 512, kvf[:, :], ["on0"])], cls=2)
            P.dve(lambda e: e.memset(kvf[:, :], 0.0), r=[("xin_kv", 0)], w=["on0"])
            cand = self.sb("on1", [128, 512])
            for j in range(cfg.G):
                P.dma("sp", cand[:, :], xout[j * 128:(j + 1) * 128, :], r=[xk], w=["on1"], semkey="xkv")
                P.dve(lambda e, j=j: e.scalar_tensor_tensor(out=kvf[:, :], in0=cand[:, :], scalar=self.oh[:, j:j + 1], in1=kvf[:, :],
                                                            op0=ALU.mult, op1=ALU.add), r=["on1", "on0", "oh"], w=["on0"])
            self.swa_kv_prep(kvf, "on0", 128, 1)
        ntiles = NT + 1
        for ti, xt, xkey in self.tiles_iter(first, list(range(ntiles))):
            T = 128 if ti < NT else TS
            if ti < NT:
                cur, prev = ti % 2, 1 - ti % 2
                if ti == 0:
                    maskA, mk = self.ma0, "m_p_maskA0"
                else:
                    maskA, mk = self.M["p"]["GT"], "m_p_GT"
                kv_out = None
                if ti == NT - 1:
                    def kv_out(kvf, key):
                        P.dma("pool", d["k2_p"][:, :], kvf[:, 0:256], r=[key], semkey="k2p", final=True)
                        P.dma("pool", d["v2_p"][:, :], kvf[:, 256:512], r=[key], semkey="v2p", final=True)
                self.swa_tile(ti, xt, xkey, T, "p", cur, prev, maskA, mk, None, kv_out)
            else:
                seqs = []
                for j in range(NS):
                    def load(j=j):
                        cst = self.sb("on1", [128, 512])
                        P.dma("sp", cst[:, 0:256], d["kc2"][j], w=["on1"], semkey="kc2l")
                        P.dma("sp", cst[:, 256:512], d["vc2"][j], w=["on1"], semkey="vc2l")
                        self.swa_kv_prep(cst, "on1", 128, 2)
                    seqs.append(dict(j=j, load=load))
                def kv_out(kvf, key):
                    P.dma("pool", d["kvs"][:, :], kvf[:TS, :], r=[key], w=["kvs"], semkey="kvs")
                    kvv = d["kvs"].rearrange("(s t) c -> s t c", t=8)
                    P.dma("pool", d["k2_s"][:, 120:128, :], kvv[:, :, 0:256], r=["kvs"], semkey="fin2", final=True)
                    P.dma("pool", d["v2_s"][:, 120:128, :], kvv[:, :, 256:512], r=["kvs"], semkey="fin2", final=True)
                    P.dma("pool", d["k2_s"][:, 0:120, :], d["kc2"][:, 8:128, :], semkey="fin2", final=True)
                    P.dma("pool", d["v2_s"][:, 0:120, :], d["vc2"][:, 8:128, :], semkey="fin2", final=True)
                self.swa_tile(ti, xt, xkey, T, "s", 0, 1, None, None, seqs, kv_out)
            self.store_x(xt, xkey, ti, T, lastl)

    def build(self):
        cfg, P, d = self.cfg, self.P, self.d
        self.declare()
        self.setup_consts()
        self.epsc = self.sb("epsc", [128, 1])
        P.dve(lambda e: e.memset(self.epsc[:], EPS), w=["epsc"])
        self.onec = self.sb("onec", [128, 1])
        P.dve(lambda e: e.memset(self.onec[:], 1.0), w=["onec"])
        if cfg.G > 1:
            self.rank_consts()
        layers = cfg.layers
        for li, l in enumerate(layers):
            first = li == 0
            lastl = li == len(layers) - 1
            self.load_layer_weights(l)
            kind = LAYER_KIND[l]
            if kind == "gla":
                self.run_layer_gla(l, first, lastl)
            elif kind == "ssd":
                self.run_layer_ssd(l, first, lastl)
            elif kind == "swa":
                self.run_layer_swa(l, first, lastl)
            else:
                raise NotImplementedError(kind)
        P.emit(self.stack)
        return self.nc


_PROG_CACHE = {}


def _run(inputs, layers=(0, 1, 2, 3)):
    xp = np.asarray(inputs["x_prompt"], dtype=np.float32)
    xs = np.asarray(inputs["x_sample"], dtype=np.float32)
    B, SEQ, _ = xp.shape
    DB = xs.shape[0]
    G = NCORES // B if (NCORES % B == 0 and SEQ % ((NCORES // B) * 128) == 0) else 1
    if FORCE_G is not None:
        G = FORCE_G
    cfg = Cfg(B, SEQ, DB, layers, G=G)
    G, NT, NS, TS = cfg.G, cfg.NT, cfg.NS, cfg.TS
    key = (B, SEQ, DB, tuple(layers))
    if key not in _PROG_CACHE:
        bld = Builder(cfg)
        nc = bld.build()
        _PROG_CACHE[key] = (bld, nc)
    bld, nc = _PROG_CACHE[key]
    mp = make_masks(128, 128)
    ms = make_masks(TS, 8)
    colmask = np.ascontiguousarray(ms["seg"].T)
    shared = {}
    for k, v in inputs.items():
        if k.startswith("l") or k == "final_norm":
            shared[k] = np.ascontiguousarray(np.asarray(v, dtype=np.float32))
    shared["c_ident"] = np.eye(128, dtype=np.float32)
    shared["c_p_LE"], shared["c_p_GT"] = mp["LE"], mp["GT"]
    shared["c_s_LE"], shared["c_s_GT"] = ms["LE"], ms["GT"]
    shared["c_s_seg"] = ms["seg"]
    shared["c_p_Sh"], shared["c_s_Sh"] = mp["Sh"], ms["Sh"]
    shared["c_s_maskA"] = (np.arange(128)[:, None] > (np.arange(TS)[None, :] % 8)).astype(np.float32)
    shared["c_p_ShP"], shared["c_s_ShP"] = mp["ShP"], ms["ShP"]
    in_maps = []
    NP = NT * 128
    for c in range(NCORES):
        b, g = (c // G, c % G) if c < B * G else (0, 0)
        m = dict(shared)
        m["xp"] = np.ascontiguousarray(xp[b, g * NP:(g + 1) * NP, :])
        sl = slice(c * NS, (c + 1) * NS)
        m["xsamp"] = np.ascontiguousarray(xs[sl].reshape(TS, D))
        m["sg0"] = np.ascontiguousarray(inputs["state_gla_0"][sl])
        m["ssm1"] = np.ascontiguousarray(inputs["state_ssm_1"][sl])
        m["conv1"] = np.ascontiguousarray(inputs["state_conv_1"][sl])
        m["kc2"] = np.ascontiguousarray(np.asarray(inputs["cache_swa_k_2"][sl]).reshape(NS, 128, 256))
        m["vc2"] = np.ascontiguousarray(np.asarray(inputs["cache_swa_v_2"][sl]).reshape(NS, 128, 256))
        m["sg3"] = np.ascontiguousarray(inputs["state_gla_3"][sl])
        pm = np.zeros((1, 2 * G), np.float32)
        for j in range(G):
            pm[0, j] = 1.0 if j < g else 0.0
            pm[0, G + j] = 1.0 - pm[0, j]
        m["c_pm"] = pm
        ohv = np.zeros((1, G), np.float32)
        if g > 0:
            ohv[0, g - 1] = 1.0
        m["c_oh"] = ohv
        selv = np.zeros((G * 3, 128), np.float32)
        if g > 0:
            for r in range(3):
                selv[(g - 1) * 3 + r, 125 + r] = 1.0
        m["c_sel"] = selv
        m["c_p_maskA0"] = (mp["GT"] * (1.0 if g > 0 else 0.0)).astype(np.float32)
        in_maps.append(m)
    res = run_bass_kernel_spmd(nc, in_maps, core_ids=list(range(NCORES)))
    R = res.results
    last = [b * G + G - 1 for b in range(B)]
    y_prompt = np.stack([np.concatenate([R[b * G + g]["yp"] for g in range(G)], axis=0) for b in range(B)])
    y_sample = np.concatenate([R[c]["ysamp"].reshape(NS, 8, D) for c in range(NCORES)], axis=0)

    def pst(name, shape):
        return np.stack([R[c][name].reshape(shape) for c in last])

    def sst(name, shape):
        return np.concatenate([R[c][name].reshape((NS,) + shape) for c in range(NCORES)], axis=0)

    outs = (y_prompt, y_sample,
            pst("gla0_p", (4, 128, 512)), sst("gla0_s", (4, 128, 512)),
            pst("ssm1_p", (32, 64, 128)), sst("ssm1_s", (32, 64, 128)),
            pst("conv1_p", (3, 3072)), sst("conv1_s", (3, 3072)),
            pst("k2_p", (128, 4, 64)), sst("k2_s", (128, 4, 64)),
            pst("v2_p", (128, 4, 64)), sst("v2_s", (128, 4, 64)),
            pst("gla3_p", (4, 128, 512)), sst("gla3_s", (4, 128, 512)))
    return tuple(np.ascontiguousarray(o, dtype=np.float32) for o in outs)


def kernel(**inputs):
    return _run(inputs)
```

```python
import numpy as np
from contextlib import ExitStack
import concourse.bass as bass
import concourse.mybir as mybir
from concourse.ap import AP
from concourse.bass_utils import run_bass_kernel_spmd

F32 = mybir.dt.float32
BF16 = mybir.dt.bfloat16
AF = mybir.ActivationFunctionType
ALU = mybir.AluOpType

D = 1024
DI = 2048
EPS = 1e-6
GLA_IN = 5136
SSD_IN = 5152
SWA_IN = 4608
NCORES = 8
FORCE_G = None

ENGS = ("pe", "act", "dve", "pool", "sp")


def _conflict(a, b):
    n = min(len(a), len(b))
    return a[:n] == b[:n]


class Op:
    __slots__ = ("eng", "fn", "reads", "writes", "dma", "semkey", "inc", "deps",
                 "sem", "semval", "need_inc", "idx")


class Prog:
    def __init__(self, nc):
        self.nc = nc
        self.ops = []
        self.state = {}
        self.final_waits = []

    @staticmethod
    def _norm(keys):
        out = []
        for k in keys:
            if k is None:
                continue
            if not isinstance(k, tuple):
                k = (k,)
            out.append(k)
        return out

    def op(self, eng, fn, r=(), w=(), dma=False, semkey=None, inc=None, final=False):
        o = Op()
        o.eng = eng
        o.fn = fn
        o.reads = self._norm(r)
        o.writes = self._norm(w)
        o.dma = dma
        o.semkey = semkey
        o.inc = inc if inc is not None else (16 if dma else 1)
        o.idx = len(self.ops)
        o.need_inc = False
        deps = set()
        for k in o.reads:
            tab = self.state.setdefault(k[0], {})
            for kk, st in tab.items():
                if _conflict(k, kk):
                    if st[0] is not None:
                        deps.add(st[0])
                    if k[0] in ("ps", "pst"):
                        deps.update(r for r in st[1] if self.ops[r].eng != eng)
        for k in o.writes:
            tab = self.state.setdefault(k[0], {})
            for kk, st in tab.items():
                if _conflict(k, kk):
                    if st[0] is not None:
                        deps.add(st[0])
                    deps.update(st[1])
        for k in o.reads:
            tab = self.state[k[0]]
            if k not in tab:
                tab[k] = [None, []]
            tab[k][1].append(o.idx)
        for k in o.writes:
            tab = self.state[k[0]]
            for kk in [kk for kk in tab if len(kk) > len(k) and kk[:len(k)] == k]:
                del tab[kk]
            tab[k] = [o.idx, []]
        deps.discard(o.idx)
        keep = set()
        for d in deps:
            dop = self.ops[d]
            if (not dop.dma) and (not o.dma) and dop.eng == eng:
                raw = False
                for k in o.reads:
                    for kk in dop.writes:
                        if _conflict(k, kk):
                            raw = True
                if not raw:
                    continue
            keep.add(d)
        o.deps = sorted(keep)
        self.ops.append(o)
        if final:
            self.final_waits.append(o.idx)
        return o

    def pe(self, fn, r=(), w=(), **kw):
        return self.op("pe", fn, r, w, **kw)

    def act(self, fn, r=(), w=(), **kw):
        return self.op("act", fn, r, w, **kw)

    def dve(self, fn, r=(), w=(), **kw):
        return self.op("dve", fn, r, w, **kw)

    def pool(self, fn, r=(), w=(), **kw):
        return self.op("pool", fn, r, w, **kw)

    def dma(self, q, out, in_, r=(), w=(), semkey=None, final=False, **dkw):
        assert semkey is not None
        sk = ("dma",) + (tuple(semkey) if isinstance(semkey, tuple) else (semkey,))
        return self.op(q, lambda e: e.dma_start(out=out, in_=in_, **dkw), r, w,
                       dma=True, semkey=sk, final=final)

    def emit(self, stack):
        nc = self.nc
        ops = self.ops
        for o in ops:
            for d in o.deps:
                ops[d].need_inc = True
        for i in self.final_waits:
            ops[i].need_inc = True
        engsem = {}
        for e in ("pe", "act", "dve", "pool"):
            engsem[e] = stack.enter_context(nc.semaphore("sem_" + e))
        dmasem = {}
        cnt = {e: 0 for e in engsem}
        dcnt = {}
        for o in ops:
            if o.dma:
                if o.semkey not in dmasem:
                    dmasem[o.semkey] = stack.enter_context(
                        nc.semaphore("sd_" + "_".join(str(x) for x in o.semkey[1:])))
                    dcnt[o.semkey] = 0
                dcnt[o.semkey] += o.inc
                o.sem = dmasem[o.semkey]
                o.semval = dcnt[o.semkey]
                o.need_inc = True
            elif o.need_inc:
                cnt[o.eng] += 1
                o.sem = engsem[o.eng]
                o.semval = cnt[o.eng]
        self.nsems = len(engsem) + len(dmasem)
        self.counts = dict(cnt)
        streams = {e: [o for o in ops if o.eng == e] for e in ENGS}
        block = stack.enter_context(nc.Block())
        final_waits = self.final_waits

        def run_stream(e, eng):
            waited = {}
            issued = []
            for o in streams[e]:
                need = {}
                if e == "pool" and o.dma:
                    if len(issued) >= 2:
                        po = issued[-2]
                        need[po.sem.num] = (po.sem, po.semval)
                    issued.append(o)
                for d in o.deps:
                    dop = ops[d]
                    key = dop.sem.num
                    if key not in need or need[key][1] < dop.semval:
                        need[key] = (dop.sem, dop.semval)
                for key, (sem, val) in need.items():
                    if waited.get(key, 0) >= val:
                        continue
                    eng.wait_ge(sem, val)
                    waited[key] = val
                ins = o.fn(eng)
                if o.need_inc:
                    ins.then_inc(o.sem, o.inc)
            if e == "sp":
                need = {}
                for i in final_waits:
                    dop = ops[i]
                    key = dop.sem.num
                    if key not in need or need[key][1] < dop.semval:
                        need[key] = (dop.sem, dop.semval)
                for key, (sem, val) in need.items():
                    if waited.get(key, 0) >= val:
                        continue
                    eng.wait_ge(sem, val)

        @block.sync
        def _(eng):
            run_stream("sp", eng)

        @block.gpsimd
        def _(eng):
            run_stream("pool", eng)

        @block.scalar
        def _(eng):
            run_stream("act", eng)

        @block.vector
        def _(eng):
            run_stream("dve", eng)

        @block.tensor
        def _(eng):
            run_stream("pe", eng)


def bc_mid(ap2d, n):
    a = ap2d.ap
    return AP(ap2d.tensor, ap2d.offset, [list(a[0]), [0, n], list(a[1])])


def bc_last(ap2d, n):
    a = ap2d.ap
    return AP(ap2d.tensor, ap2d.offset, [list(a[0]), list(a[1]), [0, n]])


def make_masks(T, L):
    idx = np.arange(T)
    seq = idx // L
    same = seq[:, None] == seq[None, :]
    s = idx[:, None]
    t = idx[None, :]
    m = {}
    m["LE"] = (same & (s <= t)).astype(np.float32)
    m["GT"] = (same & (s > t)).astype(np.float32)
    nseq = T // L
    seg = (seq[:, None] == np.arange(nseq)[None, :]).astype(np.float32)
    m["seg"] = seg
    sh = np.zeros((T, 3, T), np.float32)
    for j in range(3):
        sh[:, j, :] = (same & (s == t + j - 3)).astype(np.float32)
    m["Sh"] = sh
    if L == 128:
        shp = np.zeros((128, 3, 128), np.float32)
        for j in range(3):
            for tt in range(3):
                if tt + j < 3:
                    shp[125 + tt + j, j, tt] = 1.0
        m["ShP"] = shp
    else:
        shp = np.zeros((nseq * 3, 3, T), np.float32)
        for j in range(3):
            for tt in range(T):
                q = tt % L
                if q + j < 3:
                    shp[(tt // L) * 3 + q + j, j, tt] = 1.0
        m["ShP"] = shp
    return m


class Cfg:
    def __init__(self, B, SEQ, DB, layers=(0, 1, 2, 3), G=1):
        assert B * G <= NCORES
        self.B = B
        self.G = G
        assert SEQ % (self.G * 128) == 0
        self.NT = SEQ // self.G // 128
        assert DB % NCORES == 0
        self.NS = DB // NCORES
        self.TS = self.NS * 8
        assert self.TS <= 128
        self.layers = tuple(layers)
        self.SEQ = SEQ
        self.DB = DB


LAYER_KIND = {0: "gla", 1: "ssd", 2: "swa", 3: "gla"}
LAYER_NIN = {0: GLA_IN, 1: SSD_IN, 2: SWA_IN, 3: GLA_IN}


class Builder:
    def __init__(self, cfg):
        self.cfg = cfg
        self.nc = bass.Bass("TRN2", target_bir_lowering=False)
        self.P = Prog(self.nc)
        self.stack = ExitStack()
        self.d = {}
        self.bufs = {}
        self.psrr = 0
        self.dbg_stop = 99
        self.dbg_on = False
        self.dbg_names = []
        self.held = set()

    def din(self, name, shape, dt=F32):
        self.d[name] = self.nc.dram_tensor(name, list(shape), dt, kind="ExternalInput").ap()
        return self.d[name]

    def dout(self, name, shape, dt=F32):
        self.d[name] = self.nc.dram_tensor(name, list(shape), dt, kind="ExternalOutput").ap()
        return self.d[name]

    def dscr(self, name, shape, dt=F32):
        self.d[name] = self.nc.dram_tensor(name, list(shape), dt).ap()
        return self.d[name]

    def sb(self, name, shape, dt=F32):
        if name in self.bufs:
            return self.bufs[name]
        t = self.stack.enter_context(self.nc.sbuf_tensor(name, list(shape), dt))
        self.bufs[name] = t
        return t

    def dbg(self, name, ap, rkeys, shape):
        if not getattr(self, "dbg_on", False):
            return
        o = self.dout("dbg_" + name, shape)
        self.P.dma("sp", o, ap, r=rkeys, semkey=("dbg", name), final=True)
        self.dbg_names.append("dbg_" + name)

    def psum(self, hold=False):
        for _ in range(len(self.psb)):
            i = self.psrr
            self.psrr = (self.psrr + 1) % len(self.psb)
            if i not in self.held:
                if hold:
                    self.held.add(i)
                return self.psb[i], ("ps", i)
        raise RuntimeError("no free PSUM bank")

    def psum_release(self, key):
        self.held.discard(key[1])

    def psum_t(self):
        i = self.pstrr
        self.pstrr = (self.pstrr + 1) % len(self.pst)
        return self.pst[i], ("pst", i)

    def declare(self):
        cfg = self.cfg
        NT, NS, TS, G = cfg.NT, cfg.NS, cfg.TS, cfg.G
        NP = NT * 128
        self.din("xp", [NP, D])
        self.din("xsamp", [TS, D])
        self.din("sg0", [NS, 4, 128, 512])
        self.din("ssm1", [NS, 32, 64, 128])
        self.din("conv1", [NS, 3, 3072])
        self.din("kc2", [NS, 128, 256])
        self.din("vc2", [NS, 128, 256])
        self.din("sg3", [NS, 4, 128, 512])
        for l in (0, 3):
            self.din(f"l{l}_norm", [D])
            self.din(f"l{l}_w_in", [D, GLA_IN])
            self.din(f"l{l}_w_gk2", [16, 512])
            self.din(f"l{l}_b_gk", [512])
            self.din(f"l{l}_head_norm", [512])
            self.din(f"l{l}_w_out", [DI, D])
        self.din("l1_norm", [D])
        self.din("l1_w_in", [D, SSD_IN])
        self.din("l1_conv_w", [4, 3072])
        self.din("l1_conv_b", [3072])
        self.din("l1_dt_bias", [32])
        self.din("l1_a_log", [32])
        self.din("l1_d_skip", [32])
        self.din("l1_gate_norm", [DI])
        self.din("l1_w_out", [DI, D])
        self.din("l2_norm", [D])
        self.din("l2_w_in", [D, SWA_IN])
        self.din("l2_sinks", [32])
        self.din("l2_w_out", [DI, D])
        self.din("final_norm", [D])
        self.din("c_ident", [128, 128])
        for kind, T in (("p", 128), ("s", TS)):
            self.din(f"c_{kind}_LE", [T, T])
            self.din(f"c_{kind}_GT", [T, T])
        self.din("c_s_seg", [TS, NS])
        self.din("c_p_Sh", [128, 3, 128])
        self.din("c_s_maskA", [128, TS])
        self.din("c_p_maskA0", [128, 128])
        self.din("c_s_Sh", [TS, 3, TS])
        self.din("c_p_ShP", [128, 3, 128])
        self.din("c_s_ShP", [NS * 3, 3, TS])
        self.din("c_pm", [1, 2 * G])
        self.din("c_oh", [1, G])
        self.din("c_sel", [G * 3, 128])
        self.dout("yp", [NP, D])
        self.dout("ysamp", [TS, D])
        self.dout("gla0_p", [4, 128, 512])
        self.dout("gla0_s", [NS, 4, 128, 512])
        self.dout("ssm1_p", [32, 64, 128])
        self.dout("ssm1_s", [NS, 32, 64, 128])
        self.dout("conv1_p", [3, 3072])
        self.dout("conv1_s", [NS, 3, 3072])
        self.dout("k2_p", [128, 256])
        self.dout("k2_s", [NS, 128, 256])
        self.dout("v2_p", [128, 256])
        self.dout("v2_s", [NS, 128, 256])
        self.dout("gla3_p", [4, 128, 512])
        self.dout("gla3_s", [NS, 4, 128, 512])
        self.dscr("xres", [NP + 128, D])
        self.dscr("cwb", [128, 5 * 3072], BF16)
        self.dscr("cvs", [TS, 3072])
        self.dscr("kvs", [TS, 512])

    def setup_consts(self):
        P, d, cfg = self.P, self.d, self.cfg
        NS, TS = cfg.NS, cfg.TS
        nc = self.nc
        self.psb = [self.stack.enter_context(nc.psum_tensor(f"ps{i}", [128, 512], F32)) for i in range(6)]
        self.pst = [self.stack.enter_context(nc.psum_tensor(f"pst{i}", [128, 1024], BF16)) for i in range(2)]
        self.pstrr = 0
        self.identf = self.sb("identf", [128, 128])
        self.identb = self.sb("identb", [128, 128], BF16)
        P.dma("sp", self.identf[:], d["c_ident"][:, :], w=["identf"], semkey="c0")
        P.dma("pool", self.identb[:], d["c_ident"][:, :], w=["identb"], semkey="c1")
        self.ones_row = self.sb("ones_row", [1, 128])
        P.dve(lambda e: e.memset(self.ones_row[:], 1.0), w=["ones_row"])
        self.M = {}
        for kind, T in (("p", 128), ("s", TS)):
            m = {}
            for nm in ("LE", "GT"):
                t = self.sb(f"m_{kind}_{nm}", [T, T])
                P.dma("sp", t[:], d[f"c_{kind}_{nm}"][:, :], w=[f"m_{kind}_{nm}"], semkey=f"c_{kind}_{nm}")
                m[nm] = t
            self.M[kind] = m
        seg = self.sb("m_s_seg", [TS, NS])
        P.dma("sp", seg[:], d["c_s_seg"][:, :], w=["m_s_seg"], semkey="c_seg")
        self.M["s"]["seg"] = seg
        segp = self.sb("m_p_seg", [128, 1])
        P.dve(lambda e: e.memset(segp[:], 1.0), w=["m_p_seg"])
        self.M["p"]["seg"] = segp

    def load_layer_weights(self, l):
        P, d = self.P, self.d
        nin = LAYER_NIN[l]
        win = self.sb("w_in", [128, 8, SSD_IN], BF16)
        wsrc = d[f"l{l}_w_in"].rearrange("(kc p) n -> p kc n", p=128)
        nblk = (nin + 511) // 512
        for b in range(nblk):
            c0, c1 = b * 512, min(nin, (b + 1) * 512)
            P.dma("pool", win[:, :, c0:c1], wsrc[:, :, c0:c1], w=[("w_in", b)], semkey=("win", b))
        wout = self.sb("w_out", [128, 16, D], BF16)
        wosrc = d[f"l{l}_w_out"].rearrange("(rc p) n -> p rc n", p=128)
        for b in range(4):
            P.dma("pool", wout[:, b * 4:(b + 1) * 4, :], wosrc[:, b * 4:(b + 1) * 4, :], w=[("w_out", b)], semkey=("wout", b))
        ncol = self.sb("normcol", [128, 8])
        P.dma("sp", ncol[:], d[f"l{l}_norm"].rearrange("(kc p) -> p kc", p=128), w=["normcol"], semkey="nrm",
              allow_slow_non_contiguous=True)
        self.win, self.wout, self.ncol = win, wout, ncol

    def tile_src(self, l_first, ti):
        cfg, d = self.cfg, self.d
        NT, TS = cfg.NT, cfg.TS
        if ti < NT:
            if l_first:
                return d["xp"][ti * 128:(ti + 1) * 128, :], ("xp", ti)
            return d["xres"][ti * 128:(ti + 1) * 128, :], ("xres", ti)
        if l_first:
            return d["xsamp"][:, :], ("xsamp",)
        return d["xres"][NT * 128:NT * 128 + TS, :], ("xres", NT)

    def load_x(self, l_first, ti, slot):
        T = 128 if ti < self.cfg.NT else self.cfg.TS
        xt = self.sb(f"xt{slot}", [128, D])
        src, key = self.tile_src(l_first, ti)
        self.P.dma("sp", xt[:T, :], src, r=[key], w=[f"xt{slot}"], semkey=("xt", slot))
        return xt

    def tiles_iter(self, first, tile_ids):
        slot = 0
        xt = self.load_x(first, tile_ids[0], slot)
        for i, ti in enumerate(tile_ids):
            nxt = None
            if i + 1 < len(tile_ids):
                nxt = self.load_x(first, tile_ids[i + 1], 1 - slot)
            yield ti, xt, f"xt{slot}"
            xt = nxt
            slot = 1 - slot

    def norm_transpose(self, xt, xkey, T):
        P = self.P
        junk = self.sb("junk", [128, DI], BF16)
        st = self.sb("nstat", [128, 4])
        hn = junk[:, D:2 * D]
        hT = self.sb("hT", [128, 8, 128], BF16)
        P.act(lambda e: e.activation(out=junk[:T, 0:D], in_=xt[:T, :], func=AF.Square, accum_out=st[:T, 0:1]),
              r=[xkey], w=["junk", ("nstat", 0)])
        P.act(lambda e: e.activation(out=st[:T, 1:2], in_=st[:T, 0:1], func=AF.Sqrt, scale=1.0 / D, bias=self.epsc[:T, 0:1]),
              r=[("nstat", 0), "epsc"], w=[("nstat", 1)])
        P.dve(lambda e: e.reciprocal(out=st[:T, 2:3], in_=st[:T, 1:2]), r=[("nstat", 1)], w=[("nstat", 2)])
        P.act(lambda e: e.activation(out=hn[:T, :], in_=xt[:T, :], func=AF.Copy, scale=st[:T, 2:3]),
              r=[xkey, ("nstat", 2)], w=["junk"])
        pt, pk = self.psum_t()
        for kc in range(8):
            P.pe(lambda e, kc=kc: e.transpose(out=pt[:, kc * 128:kc * 128 + T], in_=hn[:T, kc * 128:(kc + 1) * 128],
                                              identity=self.identb[:T, :T]),
                 r=["junk", "identb"], w=[pk])
        ptv = pt[:].rearrange("p (k t) -> p k t", k=8)[:, :, :T]
        P.dve(lambda e: e.tensor_tensor(out=hT[:, :, :T], in0=ptv, in1=bc_last(self.ncol[:, :], T), op=ALU.mult),
              r=[pk, "normcol"], w=["hT"])
        return hT

    def masked_cols(self, dst, dkey, src, skey, si, T):
        P = self.P
        if si == 0:
            P.dve(lambda e: e.memset(dst[:, :, :T], 0.0), w=[dkey])
        else:
            P.dve(lambda e: e.memset(dst[:, :, 8 * (si - 1):8 * si], 0.0), w=[dkey])
        P.dve(lambda e: e.tensor_copy(out=dst[:, :, 8 * si:8 * si + 8], in_=src[:, :, 8 * si:8 * si + 8]), r=[skey], w=[dkey])

    def proj_block(self, hT, T, c0, c1):
        P = self.P
        ps, pk = self.psum()
        b = c0 // 512
        assert (c1 - 1) // 512 == b
        for kc in range(8):
            P.pe(lambda e, kc=kc: e.matmul(ps[:T, 0:c1 - c0], lhsT=hT[:, kc, :T], rhs=self.win[:, kc, c0:c1],
                                           start=(kc == 0), stop=(kc == 7)),
                 r=["hT", ("w_in", b)], w=[pk])
        return ps, pk

    def out_proj_residual(self, og, ogkey, xt, xkey, T):
        P = self.P
        ogT = self.sb("ogT", [128, 16, 128], BF16)
        for half in range(2):
            pt, pk = self.psum_t()
            for j in range(8):
                vc = half * 8 + j
                P.pe(lambda e, j=j, vc=vc, pt=pt: e.transpose(out=pt[:, j * 128:j * 128 + T], in_=og[:T, vc * 128:(vc + 1) * 128],
                                                              identity=self.identb[:T, :T]),
                     r=[ogkey, "identb"], w=[pk])
            ptv = pt[:].rearrange("p (k t) -> p k t", k=8)[:, :, :T]
            P.dve(lambda e, ptv=ptv, half=half: e.tensor_copy(out=ogT[:, half * 8:half * 8 + 8, :T], in_=ptv), r=[pk], w=[("ogT", half)])
        if self.dbg_stop <= 47:
            return
        for nb in range(2):
            if self.dbg_stop <= 48 and nb == 1:
                return
            ps, pk = self.psum()
            for vc in range(16):
                P.pe(lambda e, vc=vc, nb=nb, ps=ps: e.matmul(ps[:T, :], lhsT=ogT[:, vc, :T], rhs=self.wout[:, vc, nb * 512:(nb + 1) * 512],
                                                             start=(vc == 0), stop=(vc == 15)),
                     r=[("ogT", vc // 8), ("w_out", vc // 4)], w=[pk])
            P.dve(lambda e, nb=nb, ps=ps: e.tensor_tensor(out=xt[:T, nb * 512:(nb + 1) * 512], in0=xt[:T, nb * 512:(nb + 1) * 512],
                                                          in1=ps[:T, :], op=ALU.add),
                  r=[pk, xkey], w=[xkey])

    def store_x(self, xt, xkey, ti, T, last_layer):
        P, d, cfg = self.P, self.d, self.cfg
        NT = cfg.NT
        if not last_layer:
            dst = d["xres"][ti * 128:ti * 128 + T, :]
            P.dma("pool", dst, xt[:T, :], r=[xkey], w=[("xres", ti)], semkey=("xst", xkey))
            return
        junk = self.sb("junk", [128, DI], BF16)
        st = self.sb("nstat", [128, 4])
        P.act(lambda e: e.activation(out=junk[:T, 0:D], in_=xt[:T, :], func=AF.Square, accum_out=st[:T, 0:1]),
              r=[xkey], w=["junk", ("nstat", 0)])
        P.act(lambda e: e.activation(out=st[:T, 1:2], in_=st[:T, 0:1], func=AF.Sqrt, scale=1.0 / D, bias=self.epsc[:T, 0:1]),
              r=[("nstat", 0), "epsc"], w=[("nstat", 1)])
        P.dve(lambda e: e.reciprocal(out=st[:T, 2:3], in_=st[:T, 1:2]), r=[("nstat", 1)], w=[("nstat", 2)])
        for hf in range(2):
            fb = self.sb(f"on{hf}", [128, 512])
            P.dma("sp", fb[:], d["final_norm"][hf * 512:(hf + 1) * 512].partition_broadcast(128), w=[f"on{hf}"], semkey=("fnb", hf))
            P.dve(lambda e, hf=hf, fb=fb: e.scalar_tensor_tensor(out=xt[:T, hf * 512:(hf + 1) * 512], in0=xt[:T, hf * 512:(hf + 1) * 512],
                                                                 scalar=st[:T, 2:3], in1=fb[:T, :], op0=ALU.mult, op1=ALU.mult),
                  r=[xkey, ("nstat", 2), f"on{hf}"], w=[xkey])
        dst = d["yp"][ti * 128:(ti + 1) * 128, :] if ti < NT else d["ysamp"][:, :]
        P.dma("pool", dst, xt[:T, :], r=[xkey], semkey=("yst", xkey), final=True)


    def rank_consts(self):
        P, d, G = self.P, self.d, self.cfg.G
        pm = self.sb("pm", [128, 2 * G])
        P.dma("sp", pm[:], d["c_pm"].rearrange("o n -> (o n)").partition_broadcast(128), w=["pm"], semkey="c_pm")
        oh = self.sb("oh", [128, G])
        P.dma("sp", oh[:], d["c_oh"].rearrange("o n -> (o n)").partition_broadcast(128), w=["oh"], semkey="c_oh")
        sel = self.sb("sel", [G * 3, 128])
        P.dma("sp", sel[:], d["c_sel"][:, :], w=["sel"], semkey="c_sel")
        self.pm, self.oh, self.sel = pm, oh, sel

    def allgather(self, tag, rows, W, writes, cls=0):
        P, G = self.P, self.cfg.G
        xin = self.dscr(f"xin_{tag}", [rows, W])
        xout = self.dscr(f"xout_{tag}", [G * rows, W])
        for i, (c0, c1, src, rk) in enumerate(writes):
            P.dma("sp", xin[:, c0:c1], src, r=rk, w=[(f"xin_{tag}", i)], semkey=("xi", cls, i))
        groups = [list(range(b * G, (b + 1) * G)) for b in range(self.cfg.B)]
        P.op("pool", lambda e: e.collective_compute("AllGather", ALU.bypass, replica_groups=groups, ins=[xin[:, :]], outs=[xout[:, :]]),
             r=[f"xin_{tag}"], w=[f"xout_{tag}"], dma=True, semkey=("dma", "cc", cls), inc=1)
        return xout, f"xout_{tag}"

    def state_combine(self, tag, Sview, Skey, Dview, Dkey, nd):
        P, G = self.P, self.cfg.G
        xout, xk = self.allgather(tag, 128, 2048, [(0, 2048, Sview, [Skey])])
        xoutd, xkd = self.allgather(tag + "d", 128, 256, [(0, nd, Dview, [Dkey])], cls=1)
        P.dve(lambda e: e.memset(Sview, 0.0), r=[(f"xin_{tag}", 0)], w=[Skey])
        g4 = self.sb("g4", [128, 4, 512])
        cand = g4[:].rearrange("p a b -> p (a b)")
        ck = ["spf", "erev", "ecum", "encum"]
        dj = self.sb("xD", [128, 32])
        S3 = Sview.rearrange("p (a b) -> p a b", a=nd)
        for j in range(G):
            P.dma("sp", cand, xout[j * 128:(j + 1) * 128, 0:2048], r=[xk], w=ck, semkey="xc")
            P.dma("sp", dj[:, 0:nd], xoutd[j * 128:(j + 1) * 128, 0:nd], r=[xkd], w=["xD"], semkey="xd")
            P.dve(lambda e, j=j: e.tensor_scalar(out=dj[:, 0:nd], in0=dj[:, 0:nd], scalar1=self.pm[:, j:j + 1], scalar2=self.pm[:, G + j:G + j + 1],
                                                 op0=ALU.mult, op1=ALU.add), r=["xD", "pm"], w=["xD"])
            P.dve(lambda e: e.tensor_tensor(out=S3, in0=S3, in1=bc_last(dj[:, 0:nd], 2048 // nd), op=ALU.mult), r=[Skey, "xD"], w=[Skey])
            P.dve(lambda e, j=j: e.scalar_tensor_tensor(out=Sview, in0=cand, scalar=self.pm[:, j:j + 1], in1=Sview, op0=ALU.mult, op1=ALU.add),
                  r=ck + [Skey, "pm"], w=[Skey])

    def gla_setup(self, l):
        P, d = self.P, self.d
        wgk = self.sb("wgk2", [17, 512])
        P.dma("sp", wgk[0:16, :], d[f"l{l}_w_gk2"][:, :], w=["wgk2"], semkey="gs0")
        P.dma("sp", wgk[16:17, :], d[f"l{l}_b_gk"].rearrange("(o n) -> o n", o=1), w=["wgk2"], semkey="gs1")
        lrT = self.sb("lrT", [17, 128])
        P.dve(lambda e: e.memset(lrT[:, :], 1.0), w=["lrT"])
        bgk = None
        hnb = self.sb("hnb", [128, 512])
        P.dma("sp", hnb[:], d[f"l{l}_head_norm"].partition_broadcast(128), w=["hnb"], semkey="gs2")
        self.wgk, self.bgk, self.hnb = wgk, bgk, hnb

    def gla_tile(self, l, ti, xt, xkey, T, kind, state_only, seqs):
        P, d, cfg = self.P, self.d, self.cfg
        M = self.M[kind]
        nseq = len(seqs)
        hT = self.norm_transpose(xt, xkey, T)
        vb = self.sb("vb", [128, DI], BF16)
        sr = self.sb("sr", [128, DI], BF16)

        def emit_v(b):
            psv, pkv = self.proj_block(hT, T, 1024 + b * 512, 1536 + b * 512)
            P.act(lambda e, b=b, psv=psv: e.activation(out=vb[:T, b * 512:(b + 1) * 512], in_=psv[:T, :], func=AF.Copy),
                  r=[pkv], w=[("vb", b)])

        def emit_r(b):
            psr2, pkr2 = self.proj_block(hT, T, 3072 + b * 512, 3584 + b * 512)
            P.act(lambda e, b=b, psr2=psr2: e.activation(out=sr[:T, b * 512:(b + 1) * 512], in_=psr2[:T, :], func=AF.Silu),
                  r=[pkr2], w=[("sr", b)])

        ps, pk = self.proj_block(hT, T, 5120, 5136)
        lrf = self.sb("lrf", [128, 16])
        P.dve(lambda e: e.tensor_copy(out=lrf[:T, :], in_=ps[:T, 0:16]), r=[pk], w=["lrf"])
        emit_v(0)
        emit_v(1)
        ps2, pk2 = self.psum()
        P.pe(lambda e: e.transpose(out=ps2[:16, :T], in_=lrf[:T, :], identity=self.identf[:T, :T]), r=["lrf", "identf"], w=[pk2])
        lrT = self.sb("lrT", [17, 128])
        P.dve(lambda e: e.tensor_copy(out=lrT[0:16, :T], in_=ps2[:16, :T]), r=[pk2], w=["lrT"])
        psz, pkz = self.psum()
        P.pe(lambda e: e.matmul(psz[:T, :], lhsT=lrT[:, :T], rhs=self.wgk[:, :], start=True, stop=True), r=["lrT", "wgk2"], w=[pkz])
        if self.dbg_stop <= 1:
            return
        g4 = self.sb("g4", [128, 4, 512])
        spf = g4[:, 0, :]
        P.act(lambda e: e.activation(out=spf[:T, :], in_=psz[:T, :], func=AF.Exp, scale=-1.0), r=[pkz], w=["spf"])
        P.act(lambda e: e.activation(out=spf[:T, :], in_=spf[:T, :], func=AF.Ln, bias=self.onec[:T, 0:1]), r=["spf", "onec"], w=["spf"])
        emit_v(2)
        emit_v(3)
        if self.dbg_stop <= 2:
            return
        erev = g4[:, 1, :]
        psr, pkr = self.psum()
        P.pe(lambda e: e.matmul(psr[:T, :], lhsT=M["GT"][:T, :T], rhs=spf[:T, :], start=True, stop=True), r=["spf", f"m_{kind}_GT"], w=[pkr])
        P.act(lambda e: e.activation(out=erev[:T, :], in_=psr[:T, :], func=AF.Exp, scale=-1.0 / 16.0), r=[pkr], w=["erev"])
        if not state_only:
            ecum = g4[:, 2, :]
            encum = g4[:, 3, :]
            psc, pkc = self.psum()
            P.pe(lambda e: e.matmul(psc[:T, :], lhsT=M["LE"][:T, :T], rhs=spf[:T, :], start=True, stop=True), r=["spf", f"m_{kind}_LE"], w=[pkc])
            P.act(lambda e: e.activation(out=ecum[:T, :], in_=psc[:T, :], func=AF.Exp, scale=-1.0 / 16.0), r=[pkc], w=["ecum"])
            P.act(lambda e: e.activation(out=encum[:T, :], in_=psc[:T, :], func=AF.Exp, scale=1.0 / 16.0), r=[pkc], w=["encum"])
        if self.dbg_stop <= 3:
            return
        elast = self.sb("elast", [128, 4, 16])
        psl, pkl = self.psum()
        for h in range(4):
            P.pe(lambda e, h=h: e.matmul(psl[:, h * 16:h * 16 + nseq], lhsT=spf[:T, h * 128:(h + 1) * 128], rhs=M["seg"][:T, :nseq],
                                         start=True, stop=True), r=["spf", "m_s_seg", "m_p_seg"], w=[pkl])
        P.act(lambda e: e.activation(out=elast[:, :, :nseq], in_=psl[:, 0:64].rearrange("p (h j) -> p h j", h=4)[:, :, :nseq], func=AF.Exp, scale=-1.0 / 16.0),
              r=[pkl], w=["elast"])
        if self.dbg_stop <= 4:
            return
        if kind == "s":
            self.dbg("spf", spf[:T, :], ["spf"], [T, 512])
            self.dbg("erev", erev[:T, :], ["erev"], [T, 512])
            self.dbg("elast", elast[:, :, :nseq], ["elast"], [128, 4, nseq])
        if not state_only:
            for b in range(4):
                emit_r(b)
        if not state_only:
            psq, pkq = self.proj_block(hT, T, 0, 512)
            qg = self.sb("qg", [128, 512], BF16)
            P.dve(lambda e: e.scalar_tensor_tensor(out=qg[:T, :], in0=psq[:T, :], scalar=float(128 ** -0.5), in1=ecum[:T, :],
                                                   op0=ALU.mult, op1=ALU.mult), r=[pkq, "ecum"], w=["qg"])
        psk, pkk = self.proj_block(hT, T, 512, 1024)
        kh = self.sb("kh", [128, 512], BF16)
        P.dve(lambda e: e.tensor_tensor(out=kh[:T, :], in0=psk[:T, :], in1=erev[:T, :], op=ALU.mult), r=[pkk, "erev"], w=["kh"])
        if not state_only:
            kg = self.sb("kg", [128, 512], BF16)
            P.dve(lambda e: e.tensor_tensor(out=kg[:T, :], in0=psk[:T, :], in1=encum[:T, :], op=ALU.mult), r=[pkk, "encum"], w=["kg"])
        if self.dbg_stop <= 5:
            return
        if not state_only:
            if self.dbg_stop <= 6:
                return
            qT = self.sb("qT", [128, 4, 128], BF16)
            kT = self.sb("kT", [128, 4, 128], BF16)
            pt, ptk = self.psum_t()
            for h in range(4):
                P.pe(lambda e, h=h: e.transpose(out=pt[:, h * 128:h * 128 + T], in_=qg[:T, h * 128:(h + 1) * 128], identity=self.identb[:T, :T]),
                     r=["qg", "identb"], w=[ptk])
                P.pe(lambda e, h=h: e.transpose(out=pt[:, 512 + h * 128:512 + h * 128 + T], in_=kg[:T, h * 128:(h + 1) * 128],
                                                identity=self.identb[:T, :T]), r=["kg", "identb"], w=[ptk])
            ptv = pt[:].rearrange("p (k t) -> p k t", k=8)
            P.dve(lambda e: e.tensor_copy(out=qT[:, :, :T], in_=ptv[:, 0:4, :T]), r=[ptk], w=["qT"])
            P.dve(lambda e: e.tensor_copy(out=kT[:, :, :T], in_=ptv[:, 4:8, :T]), r=[ptk], w=["kT"])
            if self.dbg_stop <= 7:
                return
            psa, pka = self.psum()
            for h in range(4):
                P.pe(lambda e, h=h: e.matmul(psa[:T, h * 128:h * 128 + T], lhsT=kT[:, h, :T], rhs=qT[:, h, :T], start=True, stop=True),
                     r=["qT", "kT"], w=[pka])
            attT = self.sb("attT", [128, 4, 128], BF16)
            P.dve(lambda e: e.tensor_tensor(out=attT[:T, :, :T], in0=psa[:T, :].rearrange("p (h t) -> p h t", h=4)[:, :, :T],
                                            in1=bc_mid(M["LE"][:T, :T], 4), op=ALU.mult), r=[pka, f"m_{kind}_LE"], w=["attT"])
            if self.dbg_stop <= 8:
                return
            pso = [self.psum(hold=True) for _ in range(4)]
            for h in range(4):
                P.pe(lambda e, h=h: e.matmul(pso[h][0][:T, :], lhsT=attT[:T, h, :T], rhs=vb[:T, h * 512:(h + 1) * 512], start=True, stop=False),
                     r=["attT", ("vb", h)], w=[pso[h][1]])
        if self.dbg_stop <= 9:
            return
        for si, sq in enumerate(seqs):
            j = sq["j"]
            if sq.get("load") is not None:
                sq["load"]()
            S, Sk, Sb, Sbk = sq["S"], sq["Skey"], sq["Sb"], sq["Sbkey"]
            last = si == nseq - 1
            if not state_only:
                if sq["masked"]:
                    qTm = self.sb("qTm", [128, 4, 128], BF16)
                    self.masked_cols(qTm, "qTm", qT, "qT", si, T)
                    qsrc, qk = qTm, "qTm"
                else:
                    qsrc, qk = qT, "qT"
                for h in range(4):
                    P.pe(lambda e, h=h, qsrc=qsrc, Sb=Sb, last=last: e.matmul(pso[h][0][:T, :], lhsT=qsrc[:, h, :T], rhs=Sb[:, h, :],
                                                                             start=False, stop=last),
                         r=[qk, Sbk], w=[pso[h][1]])
            if sq["masked"]:
                khm = self.sb("khm", [128, 512], BF16)
                P.dve(lambda e, j=j: e.tensor_scalar(out=khm[:T, :], in0=kh[:T, :], scalar1=M["seg"][:T, j:j + 1], scalar2=None, op0=ALU.mult),
                      r=["kh", "m_s_seg"], w=["khm"])
                ksrc, kk = khm, "khm"
            else:
                ksrc, kk = kh, "kh"
            for h in range(4):
                psu, pku = self.psum()
                P.pe(lambda e, h=h, psu=psu, ksrc=ksrc: e.matmul(psu[:, :], lhsT=ksrc[:T, h * 128:(h + 1) * 128], rhs=vb[:T, h * 512:(h + 1) * 512],
                                                                start=True, stop=True), r=[kk, ("vb", h)], w=[pku])
                P.dve(lambda e, h=h, psu=psu, S=S, j=j: e.scalar_tensor_tensor(out=S[:, h, :], in0=S[:, h, :], scalar=elast[:, h, j:j + 1],
                                                                                in1=psu[:, :], op0=ALU.mult, op1=ALU.add),
                      r=[pku, "elast", Sk], w=[Sk])
            if sq.get("done") is not None:
                sq["done"]()
        if state_only:
            return
        if self.dbg_stop <= 10:
            for h in range(4):
                self.psum_release(pso[h][1])
            return
        st = self.sb("ostat", [128, 12])
        junk = self.sb("junk", [128, DI], BF16)
        for h in range(4):
            P.act(lambda e, h=h: e.activation(out=junk[:T, h * 512:(h + 1) * 512], in_=pso[h][0][:T, :], func=AF.Square, accum_out=st[:T, h:h + 1]),
                  r=[pso[h][1]], w=[("junk", h), ("ostat", h)])
        P.act(lambda e: e.activation(out=st[:T, 4:8], in_=st[:T, 0:4], func=AF.Sqrt, scale=1.0 / 512, bias=self.epsc[:T, 0:1]),
              r=["ostat", "epsc"], w=[("ostat", 4)])
        P.dve(lambda e: e.reciprocal(out=st[:T, 8:12], in_=st[:T, 4:8]), r=[("ostat", 4)], w=[("ostat", 8)])
        og = self.sb("og", [128, DI], BF16)
        for h in range(4):
            on = self.sb(f"on{h % 2}", [128, 512])
            onk = f"on{h % 2}"
            P.dve(lambda e, h=h, on=on: e.scalar_tensor_tensor(out=on[:T, :], in0=pso[h][0][:T, :], scalar=st[:T, 8 + h:9 + h],
                                                               in1=self.hnb[:T, :], op0=ALU.mult, op1=ALU.mult),
                  r=[pso[h][1], ("ostat", 8), "hnb"], w=[onk])
            self.psum_release(pso[h][1])
            P.dve(lambda e, h=h, on=on: e.tensor_tensor(out=og[:T, h * 512:(h + 1) * 512], in0=on[:T, :],
                                                        in1=sr[:T, h * 512:(h + 1) * 512], op=ALU.mult),
                  r=[onk, ("sr", h)], w=[("og", h)])
        self.out_proj_residual(og, "og", xt, xkey, T)

    def run_layer_gla(self, l, first, lastl):
        P, d, cfg = self.P, self.d, self.cfg
        NT, NS, TS = cfg.NT, cfg.NS, cfg.TS
        self.gla_setup(l)
        sg_in = d["sg0"] if l == 0 else d["sg3"]
        out_p = d["gla0_p"] if l == 0 else d["gla3_p"]
        out_s = d["gla0_s"] if l == 0 else d["gla3_s"]
        S = [self.sb("gS0", [128, 4, 512])]
        Sb = [self.sb("gSb0", [128, 4, 512], BF16)]
        P.dve(lambda e: e.memset(S[0][:], 0.0), w=["gS0"])
        P.dve(lambda e: e.memset(Sb[0][:], 0.0), w=["gSb0"])
        if cfg.G > 1:
            Dt = self.sb("Dtot", [128, 32])
            P.dve(lambda e: e.memset(Dt[:], 1.0), w=["Dtot"])
            for ti, xt, xkey in self.tiles_iter(first, list(range(NT))):
                seqs = [dict(j=0, S=S[0], Skey="gS0", Sb=Sb[0], Sbkey="gSb0", masked=False)]
                self.gla_tile(l, ti, xt, xkey, 128, "p", True, seqs)
                el = self.bufs["elast"]
                P.dve(lambda e, el=el: e.tensor_tensor(out=Dt[:, 0:4], in0=Dt[:, 0:4], in1=el[:, :, 0], op=ALU.mult), r=["Dtot", "elast"], w=["Dtot"])
            self.state_combine(f"g{l}", S[0][:].rearrange("p a b -> p (a b)"), "gS0", Dt[:, 0:4], "Dtot", 4)
            P.act(lambda e: e.activation(out=Sb[0][:], in_=S[0][:], func=AF.Copy), r=["gS0"], w=["gSb0"])
        ntiles = NT + 1
        for ti, xt, xkey in self.tiles_iter(first, list(range(ntiles))):
            T = 128 if ti < NT else TS
            if ti < NT:
                def done(S0=S[0], Sb0=Sb[0]):
                    P.act(lambda e: e.activation(out=Sb0[:], in_=S0[:], func=AF.Copy), r=["gS0"], w=["gSb0"])
                seqs = [dict(j=0, S=S[0], Skey="gS0", Sb=Sb[0], Sbkey="gSb0", masked=False, done=done)]
                self.gla_tile(l, ti, xt, xkey, T, "p", False, seqs)
                if ti == NT - 1:
                    P.dma("pool", out_p.rearrange("h k v -> k h v"), S[0][:], r=["gS0"], semkey="gst_p", final=True)
            else:
                seqs = []
                for j in range(NS):
                    sl = 0
                    def load(j=j, sl=sl):
                        P.dma("sp", S[sl][:], sg_in[j].rearrange("h k v -> k h v"), w=[f"gS{sl}"], semkey=("gsl", sl))
                        P.act(lambda e: e.activation(out=Sb[sl][:], in_=S[sl][:], func=AF.Copy), r=[f"gS{sl}"], w=[f"gSb{sl}"])
                    def done(j=j, sl=sl):
                        P.dma("pool", out_s[j].rearrange("h k v -> k h v"), S[sl][:], r=[f"gS{sl}"], semkey=("gss", sl), final=True)
                    seqs.append(dict(j=j, S=S[sl], Skey=f"gS{sl}", Sb=Sb[sl], Sbkey=f"gSb{sl}", masked=True, load=load, done=done))
                self.gla_tile(l, ti, xt, xkey, T, "s", False, seqs)
            self.store_x(xt, xkey, ti, T, lastl)


    def ssd_setup(self):
        P, d, cfg = self.P, self.d, self.cfg
        NS, TS = cfg.NS, cfg.TS
        P.dma("pool", d["cwb"][:, 0:4 * 3072], d["l1_conv_w"].rearrange("j c -> (j c)").partition_broadcast(128),
              w=["cwb_d"], semkey="cwbd")
        P.dma("pool", d["cwb"][:, 4 * 3072:5 * 3072], d["l1_conv_b"].partition_broadcast(128), w=["cwb_d2"], semkey="cwbd2")
        cbrow = None
        onesb = self.sb("ones_row_b", [1, 128], BF16)
        P.dve(lambda e: e.memset(onesb[:], 1.0), w=["ones_row_b"])
        dtb = self.sb("dtb", [128, 32])
        P.dma("sp", dtb[:], d["l1_dt_bias"].partition_broadcast(128), w=["dtb"], semkey="dtb")
        aneg = self.sb("aneg", [128, 32])
        P.dma("sp", aneg[:], d["l1_a_log"].partition_broadcast(128), w=["aneg"], semkey="aneg")
        P.act(lambda e: e.activation(out=aneg[:], in_=aneg[:], func=AF.Exp), r=["aneg"], w=["aneg"])
        P.dve(lambda e: e.tensor_scalar(out=aneg[:], in0=aneg[:], scalar1=-1.0, scalar2=None, op0=ALU.mult), r=["aneg"], w=["aneg"])
        dsk = self.sb("dsk", [128, 32])
        P.dma("sp", dsk[:], d["l1_d_skip"].partition_broadcast(128), w=["dsk"], semkey="dsk")
        gcol = self.sb("gcol", [128, 16])
        P.dma("sp", gcol[:], d["l1_gate_norm"].rearrange("(rc p) -> p rc", p=128), w=["gcol"], semkey="gcol",
              allow_slow_non_contiguous=True)
        for rc in range(16):
            P.dve(lambda e, rc=rc: e.tensor_scalar(out=self.wout[:, rc, :], in0=self.wout[:, rc, :], scalar1=gcol[:, rc:rc + 1],
                                                   scalar2=None, op0=ALU.mult), r=[("w_out", rc // 4), "gcol"], w=[("w_out", rc // 4)])
        self.Sh = {}
        for kind, T, K in (("p", 128, 128), ("s", TS, NS * 3)):
            sh = self.sb(f"m_{kind}_Sh", [T, 3, T], BF16)
            P.dma("pool", sh[:], d[f"c_{kind}_Sh"][:, :, :], w=[f"m_{kind}_Sh"], semkey=f"c_{kind}_Sh")
            shp = self.sb(f"m_{kind}_ShP", [128, 3, T], BF16)
            P.dma("pool", shp[:K], d[f"c_{kind}_ShP"][:, :, :], w=[f"m_{kind}_ShP"], semkey=f"c_{kind}_ShP")
            self.Sh[kind] = (sh, shp)
        self.cbrow, self.onesb, self.dtb, self.aneg, self.dsk = cbrow, onesb, dtb, aneg, dsk

    def ssd_tile(self, ti, xt, xkey, T, kind, state_only, seqs, rows, conv_out):
        P, d, cfg = self.P, self.d, self.cfg
        M = self.M[kind]
        sh, shp = self.Sh[kind]
        nseq = len(seqs)
        r0, r1 = rows
        hT = self.norm_transpose(xt, xkey, T)
        if self.dbg_stop <= 20:
            return
        xs = self.sb("vb", [128, DI], BF16)
        Bb = self.sb("qg", [128, 512], BF16)
        Cb = self.sb("kg", [128, 512], BF16)
        xtail = self.sb("xtail", [128, 3072], BF16)
        ytail = self.sb("ytail", [128, 3, 512], BF16)
        cwblk = self.sb("cwblk", [128, 5, 512], BF16)
        Yblk = self.sb("ogT", [128, 16, 128], BF16)[:].rearrange("p a b -> p (a b)").rearrange("p (j c) -> p j c", j=4)
        cwsrc = d["cwb"].rearrange("p (j c) -> p j c", j=5)
        nblk = 5 if state_only else 6
        for blk in range(nblk):
            c0 = 2048 + blk * 512
            ps, pk = self.proj_block(hT, T, c0, c0 + 512)
            P.dma("sp", cwblk[:, :, :], cwsrc[:, :, blk * 512:(blk + 1) * 512], r=["cwb_d", "cwb_d2"], w=["cwblk"], semkey="cwblk")
            psb = AP(ps[:T, :].tensor, ps[:T, :].offset, [list(ps[:T, :].ap[0]), [0, 4], list(ps[:T, :].ap[1])])
            P.dve(lambda e, psb=psb: e.tensor_tensor(out=Yblk[:T, :, :], in0=psb, in1=cwblk[:T, 0:4, :], op=ALU.mult),
                  r=[pk, "cwblk"], w=["ogT"])
            if self.dbg_stop <= 31:
                return
            xtb = xtail[r0:r1, blk * 512:(blk + 1) * 512]
            xtb3 = AP(xtb.tensor, xtb.offset, [list(xtb.ap[0]), [0, 3], list(xtb.ap[1])])
            P.dve(lambda e, xtb3=xtb3: e.tensor_tensor(out=ytail[r0:r1, :, :], in0=xtb3, in1=cwblk[r0:r1, 0:3, :], op=ALU.mult),
                  r=[("xtail", blk), "cwblk"], w=["ytail"])
            if self.dbg_stop <= 32:
                return
            if conv_out is not None:
                stg = self.sb(f"on{blk % 2}", [128, 512])
                P.dve(lambda e, ps=ps, stg=stg: e.tensor_copy(out=stg[:T, :], in_=ps[:T, :]), r=[pk], w=[f"on{blk % 2}"])
                conv_out(stg, f"on{blk % 2}", blk)
            if kind == "p":
                P.dve(lambda e, ps=ps, blk=blk: e.tensor_copy(out=xtail[64:128, blk * 512:(blk + 1) * 512], in_=ps[64:128, :]),
                      r=[pk, "ytail"], w=[("xtail", blk)])
            if self.dbg_stop <= 33:
                return
            pc, pck = self.psum()
            for j in range(3):
                P.pe(lambda e, j=j, pc=pc: e.matmul(pc[:T, :], lhsT=sh[:T, j, :T], rhs=Yblk[:T, j, :], start=(j == 0), stop=False),
                     r=["ogT", f"m_{kind}_Sh"], w=[pck])
            P.pe(lambda e, pc=pc: e.matmul(pc[:T, :], lhsT=self.identb[:T, :T], rhs=Yblk[:T, 3, :], start=False, stop=False),
                 r=["ogT", "identb"], w=[pck])
            if self.dbg_stop <= 34:
                return
            for j in range(3):
                P.pe(lambda e, j=j, pc=pc: e.matmul(pc[:T, :], lhsT=shp[r0:r1, j, :T], rhs=ytail[r0:r1, j, :], start=False, stop=False),
                     r=["ytail", f"m_{kind}_ShP"], w=[pck])
            if self.dbg_stop <= 35:
                return
            P.pe(lambda e, pc=pc, blk=blk: e.matmul(pc[:T, :], lhsT=self.onesb[0:1, :T], rhs=cwblk[0:1, 4, :],
                                                    start=False, stop=True), r=["ones_row_b", "cwblk"], w=[pck])
            if blk < 4:
                P.act(lambda e, pc=pc, blk=blk: e.activation(out=xs[:T, blk * 512:(blk + 1) * 512], in_=pc[:T, :], func=AF.Silu),
                      r=[pck], w=[("vb", blk)])
            elif blk == 4:
                P.act(lambda e, pc=pc: e.activation(out=Bb[:T, :], in_=pc[:T, :], func=AF.Silu), r=[pck], w=["qg"])
            else:
                P.act(lambda e, pc=pc: e.activation(out=Cb[:T, :], in_=pc[:T, :], func=AF.Silu), r=[pck], w=["kg"])
        if self.dbg_stop <= 41:
            return
        sm = self.sb("ssm_small", [128, 8, 32])
        psd, pkd = self.proj_block(hT, T, 5120, 5152)
        P.dve(lambda e: e.tensor_tensor(out=sm[:T, 4, :], in0=psd[:T, 0:32], in1=self.dtb[:T, :], op=ALU.add), r=[pkd, "dtb"], w=[("sm", 4)])
        P.act(lambda e: e.activation(out=sm[:T, 4, :], in_=sm[:T, 4, :], func=AF.Exp), r=[("sm", 4)], w=[("sm", 4)])
        P.act(lambda e: e.activation(out=sm[:T, 0, :], in_=sm[:T, 4, :], func=AF.Ln, bias=self.onec[:T, 0:1]), r=[("sm", 4), "onec"], w=[("sm", 0)])
        P.dve(lambda e: e.tensor_tensor(out=sm[:T, 1, :], in0=sm[:T, 0, :], in1=self.aneg[:T, :], op=ALU.mult), r=[("sm", 0), "aneg"], w=[("sm", 1)])
        la = sm[:T, 1, :]
        psr, pkr = self.psum()
        P.pe(lambda e: e.matmul(psr[:T, 0:32], lhsT=M["GT"][:T, :T], rhs=la, start=True, stop=True), r=[("sm", 1), f"m_{kind}_GT"], w=[pkr])
        P.act(lambda e: e.activation(out=sm[:T, 2, :], in_=psr[:T, 0:32], func=AF.Exp), r=[pkr], w=[("sm", 2)])
        P.dve(lambda e: e.tensor_tensor(out=sm[:T, 2, :], in0=sm[:T, 2, :], in1=sm[:T, 0, :], op=ALU.mult), r=[("sm", 2), ("sm", 0)], w=[("sm", 2)])
        if not state_only:
            psc, pkc = self.psum()
            P.pe(lambda e: e.matmul(psc[:T, 0:32], lhsT=M["LE"][:T, :T], rhs=la, start=True, stop=True), r=[("sm", 1), f"m_{kind}_LE"], w=[pkc])
            P.act(lambda e: e.activation(out=sm[:T, 3, :], in_=psc[:T, 0:32], func=AF.Exp), r=[pkc], w=[("sm", 3)])
        elb = self.sb("g4", [128, 4, 512])[:, 3, :].rearrange("p (j h) -> p j h", h=32)
        pse, pke = self.psum()
        for sq in seqs:
            j = sq["j"]
            sc = M["seg"][:T, j:j + 1]
            scb = AP(sc.tensor, sc.offset, [list(sc.ap[0]), [0, 128]])
            P.pe(lambda e, j=j, scb=scb: e.matmul(pse[:, j * 32:(j + 1) * 32], lhsT=scb, rhs=la, start=True, stop=True),
                 r=[("sm", 1), f"m_{kind}_seg"], w=[pke])
        P.act(lambda e: e.activation(out=elb[:, :nseq, :], in_=pse[:, 0:nseq * 32].rearrange("p (j h) -> p j h", h=32), func=AF.Exp),
              r=[pke], w=["encum"])
        if self.dbg_stop <= 42:
            return
        if not state_only:
            zs = self.sb("sr", [128, DI], BF16)
            for b in range(4):
                psz, pkz = self.proj_block(hT, T, b * 512, (b + 1) * 512)
                P.act(lambda e, b=b, psz=psz: e.activation(out=zs[:T, b * 512:(b + 1) * 512], in_=psz[:T, :], func=AF.Silu), r=[pkz], w=[("sr", b)])
            BT = self.sb("qT", [128, 4, 128], BF16)
            CT = self.sb("kT", [128, 4, 128], BF16)
            pt, ptk = self.psum_t()
            for g in range(4):
                P.pe(lambda e, g=g: e.transpose(out=pt[:, g * 128:g * 128 + T], in_=Bb[:T, g * 128:(g + 1) * 128], identity=self.identb[:T, :T]),
                     r=["qg", "identb"], w=[ptk])
                P.pe(lambda e, g=g: e.transpose(out=pt[:, 512 + g * 128:512 + g * 128 + T], in_=Cb[:T, g * 128:(g + 1) * 128],
                                                identity=self.identb[:T, :T]), r=["kg", "identb"], w=[ptk])
            ptv = pt[:].rearrange("p (k t) -> p k t", k=8)
            P.dve(lambda e: e.tensor_copy(out=BT[:, :, :T], in_=ptv[:, 0:4, :T]), r=[ptk], w=["qT"])
            P.dve(lambda e: e.tensor_copy(out=CT[:, :, :T], in_=ptv[:, 4:8, :T]), r=[ptk], w=["kT"])
            psa, pka = self.psum()
            for g in range(4):
                P.pe(lambda e, g=g: e.matmul(psa[:T, g * 128:g * 128 + T], lhsT=BT[:, g, :T], rhs=CT[:, g, :T], start=True, stop=True),
                     r=["qT", "kT"], w=[pka])
            cbT = self.sb("attT", [128, 4, 128], BF16)
            P.dve(lambda e: e.tensor_tensor(out=cbT[:T, :, :T], in0=psa[:T, :].rearrange("p (g t) -> p g t", g=4)[:, :, :T],
                                            in1=bc_mid(M["LE"][:T, :T], 4), op=ALU.mult), r=[pka, f"m_{kind}_LE"], w=["attT"])
            psy = [self.psum(hold=True) for _ in range(4)]
        if self.dbg_stop <= 43:
            for g in range(4):
                self.psum_release(psy[g][1])
            return
        uu = self.sb("junk", [128, DI], BF16)
        P.dve(lambda e: e.tensor_tensor(out=uu[:T, :].rearrange("p (h q) -> p h q", h=32), in0=xs[:T, :].rearrange("p (h q) -> p h q", h=32),
                                        in1=bc_last(sm[:T, 2, :], 64), op=ALU.mult), r=["vb", ("sm", 2)], w=["junk"])
        if self.dbg_stop <= 43.1:
            for g in range(4):
                self.psum_release(psy[g][1])
            return
        for si, sq in enumerate(seqs):
            j = sq["j"]
            if sq.get("load") is not None:
                sq["load"]()
            if self.dbg_stop <= 43.2:
                for g in range(4):
                    self.psum_release(psy[g][1])
                return
            ST, STk, STb, STbk = sq["S"], sq["Skey"], sq["Sb"], sq["Sbkey"]
            last = si == nseq - 1
            if not state_only:
                if sq["masked"]:
                    CTm = self.sb("qTm", [128, 4, 128], BF16)
                    self.masked_cols(CTm, "qTm", CT, "kT", si, T)
                    csrc, ck = CTm, "qTm"
                else:
                    csrc, ck = CT, "kT"
                for g in range(4):
                    P.pe(lambda e, g=g, csrc=csrc, STb=STb, si=si, last=last: e.matmul(psy[g][0][:T, :], lhsT=csrc[:, g, :T],
                                                                                      rhs=STb[:, g * 512:(g + 1) * 512],
                                                                                      start=(si == 0), stop=last),
                         r=[ck, STbk], w=[psy[g][1]])
            if self.dbg_stop <= 43.4:
                for g in range(4):
                    self.psum_release(psy[g][1])
                return
            if sq["masked"]:
                Bm = self.sb("khm", [128, 512], BF16)
                P.dve(lambda e, j=j: e.tensor_scalar(out=Bm[:T, :], in0=Bb[:T, :], scalar1=M["seg"][:T, j:j + 1], scalar2=None, op0=ALU.mult),
                      r=["qg", "m_s_seg"], w=["khm"])
                bsrc, bk = Bm, "khm"
            else:
                bsrc, bk = Bb, "qg"
            for g in range(4):
                psu, pku = self.psum()
                P.pe(lambda e, g=g, psu=psu, bsrc=bsrc: e.matmul(psu[:, :], lhsT=bsrc[:T, g * 128:(g + 1) * 128], rhs=uu[:T, g * 512:(g + 1) * 512],
                                                                start=True, stop=True), r=[bk, "junk"], w=[pku])
                stv = ST[:, g * 512:(g + 1) * 512].rearrange("p (h q) -> p h q", h=8)
                P.dve(lambda e, g=g, stv=stv, j=j: e.tensor_tensor(out=stv, in0=stv, in1=bc_last(elb[:, j, g * 8:(g + 1) * 8], 64), op=ALU.mult),
                      r=[STk, "encum"], w=[STk])
                P.dve(lambda e, g=g, psu=psu, ST=ST: e.tensor_tensor(out=ST[:, g * 512:(g + 1) * 512], in0=ST[:, g * 512:(g + 1) * 512],
                                                                     in1=psu[:, :], op=ALU.add), r=[pku, STk], w=[STk])
            if self.dbg_stop <= 43.6:
                for g in range(4):
                    self.psum_release(psy[g][1])
                return
            if sq.get("done") is not None:
                sq["done"]()
        if state_only:
            return
        if self.dbg_stop <= 44:
            for g in range(4):
                self.psum_release(psy[g][1])
            return
        og = self.sb("og", [128, DI], BF16)
        for g in range(4):
            P.dve(lambda e, g=g: e.tensor_tensor(out=og[:T, g * 512:(g + 1) * 512].rearrange("p (h q) -> p h q", h=8),
                                                 in0=psy[g][0][:T, :].rearrange("p (h q) -> p h q", h=8),
                                                 in1=bc_last(sm[:T, 3, g * 8:(g + 1) * 8], 64), op=ALU.mult),
                  r=[psy[g][1], ("sm", 3)], w=[("og", g)])
            self.psum_release(psy[g][1])
        if self.dbg_stop <= 45:
            return
        P.dve(lambda e: e.tensor_tensor(out=uu[:T, :].rearrange("p (h q) -> p h q", h=32), in0=xs[:T, :].rearrange("p (h q) -> p h q", h=32),
                                        in1=bc_last(sm[:T, 0, :], 64), op=ALU.mult), r=["vb", ("sm", 0)], w=["junk"])
        g4 = self.sb("g4", [128, 4, 512])
        laexp = g4[:, 0:2, :].rearrange("p a (b t) -> p (a b) t", t=128)
        Eg = self.sb("Eg", [128, 8, 128], BF16)
        Mg = self.sb("Mg", [128, 8, 128], BF16)
        st = self.sb("ostat", [128, 12])
        sq_junk = g4[:, 3, :]
        for g in range(4):
            P.dve(lambda e, g=g: e.tensor_tensor(out=laexp[:T, :, :T], in0=bc_last(sm[:T, 1, g * 8:(g + 1) * 8], T), in1=bc_mid(M["LE"][:T, :T], 8),
                                                 op=ALU.mult), r=[("sm", 1), f"m_{kind}_LE"], w=["spf", "erev"])
            nmm = 2 if T == 128 else 1
            for hb in range(nmm):
                psd2, pkd2 = self.psum()
                e0, e1 = (hb * 4, hb * 4 + 4) if nmm == 2 else (0, 8)
                P.pe(lambda e, psd2=psd2, e0=e0, e1=e1: e.matmul(psd2[:T, 0:(e1 - e0) * T].rearrange("p (a t) -> p a t", t=T),
                                                                 lhsT=M["GT"][:T, :T], rhs=laexp[:T, e0:e1, :T], start=True, stop=True),
                     r=["spf", "erev", f"m_{kind}_GT"], w=[pkd2])
                P.act(lambda e, psd2=psd2, e0=e0, e1=e1: e.activation(out=Eg[:T, e0:e1, :T],
                                                                      in_=psd2[:T, 0:(e1 - e0) * T].rearrange("p (a t) -> p a t", t=T), func=AF.Exp),
                      r=[pkd2], w=[("Eg", hb)])
            P.dve(lambda e, g=g: e.tensor_tensor(out=Mg[:T, :, :T], in0=Eg[:T, :, :T], in1=bc_mid(cbT[:T, g, :T], 8), op=ALU.mult),
                  r=["Eg", "attT"], w=["Mg"])
            pyi, pyik = self.psum()
            for eh in range(8):
                h = g * 8 + eh
                P.pe(lambda e, eh=eh, h=h, pyi=pyi: e.matmul(pyi[:T, eh * 64:(eh + 1) * 64], lhsT=Mg[:T, eh, :T], rhs=uu[:T, h * 64:(h + 1) * 64],
                                                             start=True, stop=True), r=["Mg", "junk"], w=[pyik])
            on = self.sb(f"on{g % 2}", [128, 512])
            onk = f"on{g % 2}"
            on2 = g4[:, 2, :]
            gs = slice(g * 512, (g + 1) * 512)
            P.dve(lambda e, on=on, pyi=pyi, gs=gs: e.tensor_tensor(out=on[:T, :], in0=pyi[:T, :], in1=og[:T, gs], op=ALU.add),
                  r=[pyik, ("og", g)], w=[onk])
            P.dve(lambda e, g=g, gs=gs: e.tensor_tensor(out=on2[:T, :].rearrange("p (h q) -> p h q", h=8),
                                                        in0=xs[:T, gs].rearrange("p (h q) -> p h q", h=8),
                                                        in1=bc_last(self.dsk[:T, g * 8:(g + 1) * 8], 64), op=ALU.mult),
                  r=[("vb", g), "dsk"], w=["ecum"])
            P.dve(lambda e, on=on: e.tensor_tensor(out=on[:T, :], in0=on[:T, :], in1=on2[:T, :], op=ALU.add), r=[onk, "ecum"], w=[onk])
            P.dve(lambda e, on=on, gs=gs: e.tensor_tensor(out=on[:T, :], in0=on[:T, :], in1=zs[:T, gs], op=ALU.mult), r=[onk, ("sr", g)], w=[onk])
            P.act(lambda e, on=on, g=g: e.activation(out=sq_junk[:T, :], in_=on[:T, :], func=AF.Square, accum_out=st[:T, g:g + 1]),
                  r=[onk], w=["encum", ("ostat", g)])
            P.act(lambda e, g=g: e.activation(out=st[:T, 4 + g:5 + g], in_=st[:T, g:g + 1], func=AF.Sqrt, scale=1.0 / 512, bias=self.epsc[:T, 0:1]),
                  r=[("ostat", g), "epsc"], w=[("ostat", 4 + g)])
            P.dve(lambda e, g=g: e.reciprocal(out=st[:T, 8 + g:9 + g], in_=st[:T, 4 + g:5 + g]), r=[("ostat", 4 + g)], w=[("ostat", 8 + g)])
            P.dve(lambda e, on=on, g=g, gs=gs: e.tensor_scalar(out=og[:T, gs], in0=on[:T, :], scalar1=st[:T, 8 + g:9 + g], scalar2=None, op0=ALU.mult),
                  r=[onk, ("ostat", 8 + g)], w=[("og", g)])
        if self.dbg_stop <= 46:
            return
        self.out_proj_residual(og, "og", xt, xkey, T)

    def run_layer_ssd(self, l, first, lastl):
        P, d, cfg = self.P, self.d, self.cfg
        NT, NS, TS = cfg.NT, cfg.NS, cfg.TS
        self.ssd_setup()
        ST = self.sb("gS0", [128, 4, 512])[:].rearrange("p a b -> p (a b)")
        STb = self.sb("gSb0", [128, 4, 512], BF16)[:].rearrange("p a b -> p (a b)")
        xtail = self.sb("xtail", [128, 3072], BF16)
        P.dve(lambda e: e.memset(ST, 0.0), w=["gS0"])
        P.dve(lambda e: e.memset(STb, 0.0), w=["gSb0"])
        P.dve(lambda e: e.memset(xtail[:, :], 0.0), w=["xtail"])
        if cfg.G > 1:
            xt = self.load_x(first, NT - 1, 0)
            hT = self.norm_transpose(xt, "xt0", 128)
            writes = []
            for blk in range(6):
                ps, pk = self.proj_block(hT, 128, 2048 + blk * 512, 2560 + blk * 512)
                so = self.sb(f"on{blk % 2}", [128, 512])
                P.dve(lambda e, ps=ps, so=so: e.tensor_copy(out=so[:, :], in_=ps[:, :]), r=[pk], w=[f"on{blk % 2}"])
                writes.append((blk * 512, (blk + 1) * 512, so[125:128, :], [f"on{blk % 2}"]))
                if blk == 0:
                    xin_cvp = self.dscr("xin_cv", [128, 256])
                    xout_cvp = self.dscr("xout_cv", [cfg.G * 128, 256])
                    xin_cv = xin_cvp.rearrange("p w -> (p w)")[0:9216].rearrange("(r c) -> r c", c=3072)
                    xout_cv = xout_cvp.rearrange("(g p) w -> g (p w)", g=cfg.G)[:, 0:9216].rearrange("g (r c) -> g r c", c=3072)
                P.dma("sp", xin_cv[:, blk * 512:(blk + 1) * 512], so[125:128, :], r=[f"on{blk % 2}"], w=[("xin_cv", blk)], semkey=("xi_cv", blk % 2))
            groups = [list(range(b * cfg.G, (b + 1) * cfg.G)) for b in range(cfg.B)]
            P.op("pool", lambda e: e.collective_compute("AllGather", ALU.bypass, replica_groups=groups, ins=[xin_cvp[:, :]], outs=[xout_cvp[:, :]]),
                 r=["xin_cv"], w=["xout_cv"], dma=True, semkey=("dma", "cc", 3), inc=1)

            def init_tail():
                for blk in range(6):
                    so = self.sb(f"on{blk % 2}", [128, 512])
                    for gg in range(cfg.G):
                        P.dma("sp", so[gg * 3:gg * 3 + 3, :], xout_cv[gg, :, blk * 512:(blk + 1) * 512], r=["xout_cv"], w=[f"on{blk % 2}"],
                              semkey=("xo_cv", blk % 2))
                    ps, pk = self.psum()
                    P.pe(lambda e, ps=ps, so=so: e.matmul(ps[:, :], lhsT=self.sel[:, :], rhs=so[0:cfg.G * 3, :], start=True, stop=True),
                         r=[f"on{blk % 2}", "sel"], w=[pk])
                    P.dve(lambda e, ps=ps, blk=blk: e.tensor_copy(out=xtail[64:128, blk * 512:(blk + 1) * 512], in_=ps[64:128, :]),
                          r=[pk], w=[("xtail", blk)])
            init_tail()
            Dt = self.sb("Dtot", [128, 32])
            P.dve(lambda e: e.memset(Dt[:], 1.0), w=["Dtot"])
            for ti, xt, xkey in self.tiles_iter(first, list(range(NT))):
                seqs = [dict(j=0, S=ST, Skey="gS0", Sb=STb, Sbkey="gSb0", masked=False)]
                self.ssd_tile(ti, xt, xkey, 128, "p", True, seqs, (64, 128), None)
                elb = self.bufs["g4"][:, 3, :].rearrange("p (j h) -> p j h", h=32)
                P.dve(lambda e, elb=elb: e.tensor_tensor(out=Dt[:, :], in0=Dt[:, :], in1=elb[:, 0, :], op=ALU.mult), r=["Dtot", "encum"], w=["Dtot"])
            self.state_combine("s1", ST, "gS0", Dt[:, :], "Dtot", 32)
            P.act(lambda e: e.activation(out=STb, in_=ST, func=AF.Copy), r=["gS0"], w=["gSb0"])
            init_tail()

        def st_load(src_seq):
            srcv = src_seq.rearrange("(c h2) q n -> (h2 q) c n", c=16)
            for cg in range(4):
                stg = self.sb(f"on{cg % 2}", [128, 512])
                P.dma("sp", stg[:].rearrange("p (c n) -> p c n", c=4), srcv[:, cg * 4:(cg + 1) * 4, :], w=[f"on{cg % 2}"], semkey=("stl", cg % 2))
                ps, pk = self.psum()
                for c in range(4):
                    P.pe(lambda e, c=c, ps=ps, stg=stg: e.transpose(out=ps[:, c * 128:(c + 1) * 128], in_=stg[:, c * 128:(c + 1) * 128],
                                                                    identity=self.identf[:, :]), r=[f"on{cg % 2}", "identf"], w=[pk])
                P.dve(lambda e, ps=ps, cg=cg: e.tensor_copy(out=ST[:, cg * 512:(cg + 1) * 512], in_=ps[:, :]), r=[pk], w=["gS0"])
                P.dve(lambda e, ps=ps, cg=cg: e.tensor_copy(out=STb[:, cg * 512:(cg + 1) * 512], in_=ps[:, :]), r=[pk], w=["gSb0"])

        def st_store(dst_seq, semname):
            dstv = dst_seq.rearrange("(c h2) q n -> (h2 q) c n", c=16)
            for cg in range(4):
                ps, pk = self.psum()
                for c in range(4):
                    cc = cg * 4 + c
                    P.pe(lambda e, c=c, cc=cc, ps=ps: e.transpose(out=ps[:, c * 128:(c + 1) * 128], in_=ST[:, cc * 128:(cc + 1) * 128],
                                                                  identity=self.identf[:, :]), r=["gS0", "identf"], w=[pk])
                stg = self.sb(f"on{cg % 2}", [128, 512])
                P.dve(lambda e, ps=ps, stg=stg: e.tensor_copy(out=stg[:, :], in_=ps[:, :]), r=[pk], w=[f"on{cg % 2}"])
                P.dma("pool", dstv[:, cg * 4:(cg + 1) * 4, :], stg[:].rearrange("p (c n) -> p c n", c=4), r=[f"on{cg % 2}"],
                      semkey=(semname, cg % 2), final=True)

        ntiles = NT + 1
        for ti, xt, xkey in self.tiles_iter(first, list(range(ntiles))):
            T = 128 if ti < NT else TS
            if ti < NT:
                def done():
                    P.act(lambda e: e.activation(out=STb, in_=ST, func=AF.Copy), r=["gS0"], w=["gSb0"])
                seqs = [dict(j=0, S=ST, Skey="gS0", Sb=STb, Sbkey="gSb0", masked=False, done=done)]
                conv_out = None
                if ti == NT - 1:
                    def conv_out(stg, skey, blk):
                        P.dma("pool", d["conv1_p"][:, blk * 512:(blk + 1) * 512], stg[125:128, :], r=[skey], semkey=("cvo", skey), final=True)
                        if blk == 0:
                            self.dbg("stg0", stg[:, :], [skey], [128, 512])
                self.ssd_tile(ti, xt, xkey, T, "p", False, seqs, (64, 128), conv_out)
                if ti == NT - 1:
                    st_store(d["ssm1_p"], "sst_p")
            else:
                P.dma("pool", xtail[0:NS * 3, :], d["conv1"].rearrange("s r c -> (s r) c"), w=["xtail"], semkey="xtl_s")
                seqs = []
                for j in range(NS):
                    def load(j=j):
                        st_load(d["ssm1"][j])
                    def done(j=j):
                        st_store(d["ssm1_s"][j], "sst_s")
                    seqs.append(dict(j=j, S=ST, Skey="gS0", Sb=STb, Sbkey="gSb0", masked=True, load=load, done=done))
                def conv_out(stg, skey, blk):
                    P.dma("pool", d["cvs"][:, blk * 512:(blk + 1) * 512], stg[:TS, :], r=[skey], w=[("cvs", blk)], semkey=("cvo", skey))
                    P.dma("pool", d["conv1_s"][:, :, blk * 512:(blk + 1) * 512],
                          d["cvs"][:, blk * 512:(blk + 1) * 512].rearrange("(s t) c -> s t c", t=8)[:, 5:8, :],
                          r=[("cvs", blk)], semkey="fin2", final=True)
                self.ssd_tile(ti, xt, xkey, T, "s", False, seqs, (0, NS * 3), conv_out)
            self.store_x(xt, xkey, ti, T, lastl)


    def swa_setup(self):
        P, d, cfg = self.P, self.d, self.cfg
        NS, TS = cfg.NS, cfg.TS
        esink = self.sb("esink", [128, 32])
        P.dma("sp", esink[:], d["l2_sinks"].partition_broadcast(128), w=["esink"], semkey="esink")
        P.act(lambda e: e.activation(out=esink[:], in_=esink[:], func=AF.Exp), r=["esink"], w=["esink"])
        mas = self.sb("m_s_maskA", [128, TS])
        P.dma("sp", mas[:], d["c_s_maskA"][:, :], w=["m_s_maskA"], semkey="c_s_maskA")
        ma0 = self.sb("m_p_maskA0", [128, 128])
        P.dma("sp", ma0[:], d["c_p_maskA0"][:, :], w=["m_p_maskA0"], semkey="c_p_maskA0")
        zl = self.sb("zeros_b", [128, 128], BF16)
        P.dve(lambda e: e.memset(zl[:], 0.0), w=["zeros_b"])
        self.esink, self.mas, self.ma0, self.zl = esink, mas, ma0, zl

    def swa_views(self):
        cw = self.sb("cwblk", [128, 5, 512], BF16)
        yt = self.sb("ytail", [128, 3, 512], BF16)
        kT2 = [cw[:, i, :].rearrange("p (k t) -> p k t", k=4) for i in range(3)]
        vaug = [yt[:, i, 0:260].rearrange("p (k c) -> p k c", k=4) for i in range(3)]
        return kT2, vaug

    def swa_kv_prep(self, kvf, kvkey, T, slot):
        P = self.P
        kT2, vaug = self.swa_views()
        kd = self.sb("qg", [128, 512], BF16)
        kdv = kd[:T, :].rearrange("p (k r c) -> p k r c", k=4, r=2)
        kin = kvf[:T, 0:256].rearrange("p (k c) -> p k c", k=4)
        for r in range(2):
            P.dve(lambda e, r=r: e.tensor_copy(out=kdv[:, :, r, :], in_=kin), r=[kvkey], w=["qg"])
        P.dve(lambda e: e.tensor_copy(out=vaug[slot][:T, :, 0:64], in_=kvf[:T, 256:512].rearrange("p (k c) -> p k c", k=4)),
              r=[kvkey], w=[("ytail", slot)])
        P.dve(lambda e: e.memset(vaug[slot][:T, :, 64:65], 1.0), w=[("ytail", slot)])
        pt, ptk = self.psum_t()
        for k in range(4):
            P.pe(lambda e, k=k: e.transpose(out=pt[:, k * 128:k * 128 + T], in_=kd[:T, k * 128:(k + 1) * 128], identity=self.identb[:T, :T]),
                 r=["qg", "identb"], w=[ptk])
        P.dve(lambda e: e.tensor_copy(out=kT2[slot][:, :, :T], in_=pt[:, 0:512].rearrange("p (k t) -> p k t", k=4)[:, :, :T]),
              r=[ptk], w=[("cwblk", slot)])

    def swa_tile(self, ti, xt, xkey, T, kind, cur, prev, maskA, maskAkey, seqs_cache, kv_out):
        P, d, cfg = self.P, self.d, self.cfg
        M = self.M[kind]
        kT2, vaug = self.swa_views()
        hT = self.norm_transpose(xt, xkey, T)
        qb = self.sb("vb", [128, DI], BF16)
        for b in range(4):
            ps, pk = self.proj_block(hT, T, b * 512, (b + 1) * 512)
            P.act(lambda e, b=b, ps=ps: e.activation(out=qb[:T, b * 512:(b + 1) * 512], in_=ps[:T, :], func=AF.Copy, scale=0.125),
                  r=[pk], w=[("vb", b)])
        qT = self.sb("ogT", [128, 16, 128], BF16)
        for half in range(2):
            pt, ptk = self.psum_t()
            for jj in range(8):
                c = half * 8 + jj
                P.pe(lambda e, jj=jj, c=c, pt=pt: e.transpose(out=pt[:, jj * 128:jj * 128 + T], in_=qb[:T, c * 128:(c + 1) * 128],
                                                              identity=self.identb[:T, :T]), r=[("vb", c // 4), "identb"], w=[ptk])
            P.dve(lambda e, half=half, pt=pt: e.tensor_copy(out=qT[:, half * 8:half * 8 + 8, :T],
                                                            in_=pt[:].rearrange("p (k t) -> p k t", k=8)[:, :, :T]), r=[ptk], w=[("ogT", half)])
        psk, pkk = self.proj_block(hT, T, 2048, 2560)
        kvf = self.sb("on0", [128, 512])
        P.dve(lambda e: e.tensor_copy(out=kvf[:T, :], in_=psk[:T, :]), r=[pkk], w=["on0"])
        self.swa_kv_prep(kvf, "on0", T, cur)
        if kv_out is not None:
            kv_out(kvf, "on0")
        sg = self.sb("sr", [128, DI], BF16)
        for b in range(4):
            ps, pk = self.proj_block(hT, T, 2560 + b * 512, 3072 + b * 512)
            P.act(lambda e, b=b, ps=ps: e.activation(out=sg[:T, b * 512:(b + 1) * 512], in_=ps[:T, :], func=AF.Silu), r=[pk], w=[("sr", b)])
        og = self.sb("og", [128, DI], BF16)
        PA = self.sb("qT", [128, 4, 128], BF16)
        PB = self.sb("kT", [128, 4, 128], BF16)
        PAm = self.sb("qTm", [128, 4, 128], BF16)
        st = self.sb("swstat", [128, 8])
        for par in range(2):
            pbase = par * 64
            if seqs_cache is None:
                combos = [[kvh] for kvh in range(4)]
            else:
                combos = [[0, 1, 2, 3]]
            for cb in combos:
                pv = {}
                for kvh in cb:
                    pv[kvh] = self.psum(hold=True)
                    P.pe(lambda e, pv=pv, kvh=kvh: e.matmul(pv[kvh][0][:T, 0:260], lhsT=self.zl[:T, :T], rhs=vaug[cur][:T, :, :].rearrange("p k c -> p (k c)"),
                                                    start=True, stop=False), r=["zeros_b", ("ytail", cur)], w=[pv[kvh][1]])
                for kvh in cb:
                    qrhs = qT[pbase:pbase + 64, kvh * 4:kvh * 4 + 4, :T]
                    psb, pkb = self.psum()
                    klhs = kT2[cur][pbase:pbase + 64, kvh, :T]
                    P.pe(lambda e, kvh=kvh, psb=psb, qrhs=qrhs, klhs=klhs: e.matmul(psb[:T, 0:4 * T].rearrange("p (a t) -> p a t", a=4),
                                                                        lhsT=klhs, rhs=qrhs, start=True, stop=True),
                         r=[("cwblk", cur), "ogT"], w=[pkb])
                    P.act(lambda e, psb=psb: e.activation(out=PB[:T, :, :T], in_=psb[:T, 0:4 * T].rearrange("p (a t) -> p a t", a=4), func=AF.Exp),
                          r=[pkb], w=["kT"])
                    P.dve(lambda e: e.tensor_tensor(out=PB[:T, :, :T], in0=PB[:T, :, :T], in1=bc_mid(M["LE"][:T, :T], 4), op=ALU.mult),
                          r=["kT", f"m_{kind}_LE"], w=["kT"])
                    for i in range(4):
                        P.pe(lambda e, pv=pv, kvh=kvh, i=i: e.matmul(pv[kvh][0][:T, i * 65:(i + 1) * 65], lhsT=PB[:T, i, :T], rhs=vaug[cur][:T, kvh, :],
                                                             start=False, stop=False), r=["kT", ("ytail", cur)], w=[pv[kvh][1]])
                if seqs_cache is None:
                    kvh = cb[0]
                    qrhs = qT[pbase:pbase + 64, kvh * 4:kvh * 4 + 4, :T]
                    psa, pka = self.psum()
                    klhs = kT2[prev][pbase:pbase + 64, kvh, :]
                    P.pe(lambda e, kvh=kvh, psa=psa, qrhs=qrhs, klhs=klhs: e.matmul(psa[:, 0:4 * T].rearrange("p (a t) -> p a t", a=4),
                                                                        lhsT=klhs, rhs=qrhs, start=True, stop=True),
                         r=[("cwblk", prev), "ogT"], w=[pka])
                    P.act(lambda e, psa=psa: e.activation(out=PA[:, :, :T], in_=psa[:, 0:4 * T].rearrange("p (a t) -> p a t", a=4), func=AF.Exp),
                          r=[pka], w=["qT"])
                    P.dve(lambda e: e.tensor_tensor(out=PA[:, :, :T], in0=PA[:, :, :T], in1=bc_mid(maskA[:, :T], 4), op=ALU.mult),
                          r=["qT", maskAkey], w=["qT"])
                    for i in range(4):
                        P.pe(lambda e, pv=pv, kvh=kvh, i=i: e.matmul(pv[kvh][0][:T, i * 65:(i + 1) * 65], lhsT=PA[:, i, :T], rhs=vaug[prev][:, kvh, :],
                                                             start=False, stop=False), r=["qT", ("ytail", prev)], w=[pv[kvh][1]])
                else:
                    for si, sq in enumerate(seqs_cache):
                        sq["load"]()
                        for kvh in cb:
                            qrhs = qT[pbase:pbase + 64, kvh * 4:kvh * 4 + 4, 8 * si:8 * si + 8]
                            psa, pka = self.psum()
                            klhs = kT2[2][pbase:pbase + 64, kvh, :]
                            P.pe(lambda e, kvh=kvh, psa=psa, qrhs=qrhs, klhs=klhs: e.matmul(psa[:, 0:32].rearrange("p (a t) -> p a t", a=4),
                                                                                lhsT=klhs, rhs=qrhs, start=True, stop=True),
                                 r=[("cwblk", 2), "ogT"], w=[pka])
                            P.act(lambda e, psa=psa: e.activation(out=PA[:, :, 0:8], in_=psa[:, 0:32].rearrange("p (a t) -> p a t", a=4), func=AF.Exp),
                                  r=[pka], w=["qT"])
                            first_use = (si == 0 and kvh == cb[0])
                            if first_use:
                                P.dve(lambda e: e.memset(PAm[:, :, :T], 0.0), w=["qTm"])
                            elif si > 0 and kvh == cb[0]:
                                P.dve(lambda e, si=si: e.memset(PAm[:, :, 8 * (si - 1):8 * si], 0.0), w=["qTm"])
                            P.dve(lambda e, si=si: e.tensor_tensor(out=PAm[:, :, 8 * si:8 * si + 8], in0=PA[:, :, 0:8],
                                                                   in1=bc_mid(self.mas[:, 8 * si:8 * si + 8], 4), op=ALU.mult),
                                  r=["qT", "m_s_maskA"], w=["qTm"])
                            for i in range(4):
                                P.pe(lambda e, pv=pv, kvh=kvh, i=i: e.matmul(pv[kvh][0][:T, i * 65:(i + 1) * 65], lhsT=PAm[:, i, :T], rhs=vaug[2][:, kvh, :],
                                                                     start=False, stop=False), r=["qTm", ("ytail", 2)], w=[pv[kvh][1]])
                for kvh in cb:
                    P.pe(lambda e, pv=pv, kvh=kvh: e.matmul(pv[kvh][0][:T, 0:260], lhsT=self.zl[:T, :T], rhs=vaug[cur][:T, :, :].rearrange("p k c -> p (k c)"),
                                                    start=False, stop=True), r=["zeros_b", ("ytail", cur)], w=[pv[kvh][1]])
                    pvv = pv[kvh][0][:T, 0:260].rearrange("p (a c) -> p a c", a=4)
                    h0 = kvh * 8 + par
                    es = self.esink[:T, h0:h0 + 7:2]
                    P.dve(lambda e, pvv=pvv, es=es: e.tensor_tensor(out=st[:T, 0:4], in0=pvv[:, :, 64], in1=es, op=ALU.add),
                          r=[pv[kvh][1], "esink"], w=[("swstat", 0)])
                    P.dve(lambda e: e.reciprocal(out=st[:T, 4:8], in_=st[:T, 0:4]), r=[("swstat", 0)], w=[("swstat", 4)])
                    on = self.sb("on1", [128, 512])
                    onv = on[:T, 0:256].rearrange("p (a c) -> p a c", a=4)
                    P.dve(lambda e, pvv=pvv, onv=onv: e.tensor_tensor(out=onv, in0=pvv[:, :, 0:64], in1=bc_last(st[:T, 4:8], 64), op=ALU.mult),
                          r=[pv[kvh][1], ("swstat", 4)], w=["on1"])
                    self.psum_release(pv[kvh][1])
                    c0 = (kvh * 8 + par) * 64
                    ogv = AP(og[:T, c0:c0 + 64].tensor, og[:T, c0:c0 + 64].offset, [list(og[:T, c0:c0 + 64].ap[0]), [128, 4], [1, 64]])
                    sgv = AP(sg[:T, c0:c0 + 64].tensor, sg[:T, c0:c0 + 64].offset, [list(sg[:T, c0:c0 + 64].ap[0]), [128, 4], [1, 64]])
                    P.dve(lambda e, ogv=ogv, sgv=sgv, onv=onv: e.tensor_tensor(out=ogv, in0=onv, in1=sgv, op=ALU.mult),
                          r=["on1", "sr"], w=["og"])
        self.out_proj_residual(og, "og", xt, xkey, T)

    def run_layer_swa(self, l, first, lastl):
        P, d, cfg = self.P, self.d, self.cfg
        NT, NS, TS = cfg.NT, cfg.NS, cfg.TS
        self.swa_setup()
        kT2, vaug = self.swa_views()
        cw = self.sb("cwblk", [128, 5, 512], BF16)
        yt = self.sb("ytail", [128, 3, 512], BF16)
        P.dve(lambda e: e.memset(cw[:], 0.0), w=["cwblk"])
        P.dve(lambda e: e.memset(yt[:], 0.0), w=["ytail"])
        if cfg.G > 1:
            xt = self.load_x(first, NT - 1, 0)
            hT = self.norm_transpose(xt, "xt0", 128)
            psk, pkk = self.proj_block(hT, 128, 2048, 2560)
            kvf = self.sb("on0", [128, 512])
            P.dve(lambda e: e.tensor_copy(out=kvf[:, :], in_=psk[:, :]), r=[pkk], w=["on0"])
            xout, xk = self.allgather("kv", 128, 512, [(0, 512, kvf[:, :], ["on0"])], cls=2)
            P.dve(lambda e: e.memset(kvf[:, :], 0.0), r=[("xin_kv", 0)], w=["on0"])
            cand = self.sb("on1", [128, 512])
            for j in range(cfg.G):
                P.dma("sp", cand[:, :], xout[j * 128:(j + 1) * 128, :], r=[xk], w=["on1"], semkey="xkv")
                P.dve(lambda e, j=j: e.scalar_tensor_tensor(out=kvf[:, :], in0=cand[:, :], scalar=self.oh[:, j:j + 1], in1=kvf[:, :],
                                                            op0=ALU.mult, op1=ALU.add), r=["on1", "on0", "oh"], w=["on0"])
            self.swa_kv_prep(kvf, "on0", 128, 1)
        ntiles = NT + 1
        for ti, xt, xkey in self.tiles_iter(first, list(range(ntiles))):
            T = 128 if ti < NT else TS
            if ti < NT:
                cur, prev = ti % 2, 1 - ti % 2
                if ti == 0:
                    maskA, mk = self.ma0, "m_p_maskA0"
                else:
                    maskA, mk = self.M["p"]["GT"], "m_p_GT"
                kv_out = None
                if ti == NT - 1:
                    def kv_out(kvf, key):
                        P.dma("pool", d["k2_p"][:, :], kvf[:, 0:256], r=[key], semkey="k2p", final=True)
                        P.dma("pool", d["v2_p"][:, :], kvf[:, 256:512], r=[key], semkey="v2p", final=True)
                self.swa_tile(ti, xt, xkey, T, "p", cur, prev, maskA, mk, None, kv_out)
            else:
                seqs = []
                for j in range(NS):
                    def load(j=j):
                        cst = self.sb("on1", [128, 512])
                        P.dma("sp", cst[:, 0:256], d["kc2"][j], w=["on1"], semkey="kc2l")
                        P.dma("sp", cst[:, 256:512], d["vc2"][j], w=["on1"], semkey="vc2l")
                        self.swa_kv_prep(cst, "on1", 128, 2)
                    seqs.append(dict(j=j, load=load))
                def kv_out(kvf, key):
                    P.dma("pool", d["kvs"][:, :], kvf[:TS, :], r=[key], w=["kvs"], semkey="kvs")
                    kvv = d["kvs"].rearrange("(s t) c -> s t c", t=8)
                    P.dma("pool", d["k2_s"][:, 120:128, :], kvv[:, :, 0:256], r=["kvs"], semkey="fin2", final=True)
                    P.dma("pool", d["v2_s"][:, 120:128, :], kvv[:, :, 256:512], r=["kvs"], semkey="fin2", final=True)
                    P.dma("pool", d["k2_s"][:, 0:120, :], d["kc2"][:, 8:128, :], semkey="fin2", final=True)
                    P.dma("pool", d["v2_s"][:, 0:120, :], d["vc2"][:, 8:128, :], semkey="fin2", final=True)
                self.swa_tile(ti, xt, xkey, T, "s", 0, 1, None, None, seqs, kv_out)
            self.store_x(xt, xkey, ti, T, lastl)

    def build(self):
        cfg, P, d = self.cfg, self.P, self.d
        self.declare()
        self.setup_consts()
        self.epsc = self.sb("epsc", [128, 1])
        P.dve(lambda e: e.memset(self.epsc[:], EPS), w=["epsc"])
        self.onec = self.sb("onec", [128, 1])
        P.dve(lambda e: e.memset(self.onec[:], 1.0), w=["onec"])
        if cfg.G > 1:
            self.rank_consts()
        layers = cfg.layers
        for li, l in enumerate(layers):
            first = li == 0
            lastl = li == len(layers) - 1
            self.load_layer_weights(l)
            kind = LAYER_KIND[l]
            if kind == "gla":
                self.run_layer_gla(l, first, lastl)
            elif kind == "ssd":
                self.run_layer_ssd(l, first, lastl)
            elif kind == "swa":
                self.run_layer_swa(l, first, lastl)
            else:
                raise NotImplementedError(kind)
        P.emit(self.stack)
        return self.nc


_PROG_CACHE = {}


def _run(inputs, layers=(0, 1, 2, 3)):
    xp = np.asarray(inputs["x_prompt"], dtype=np.float32)
    xs = np.asarray(inputs["x_sample"], dtype=np.float32)
    B, SEQ, _ = xp.shape
    DB = xs.shape[0]
    G = NCORES // B if (NCORES % B == 0 and SEQ % ((NCORES // B) * 128) == 0) else 1
    if FORCE_G is not None:
        G = FORCE_G
    cfg = Cfg(B, SEQ, DB, layers, G=G)
    G, NT, NS, TS = cfg.G, cfg.NT, cfg.NS, cfg.TS
    key = (B, SEQ, DB, tuple(layers))
    if key not in _PROG_CACHE:
        bld = Builder(cfg)
        nc = bld.build()
        _PROG_CACHE[key] = (bld, nc)
    bld, nc = _PROG_CACHE[key]
    mp = make_masks(128, 128)
    ms = make_masks(TS, 8)
    colmask = np.ascontiguousarray(ms["seg"].T)
    shared = {}
    for k, v in inputs.items():
        if k.startswith("l") or k == "final_norm":
            shared[k] = np.ascontiguousarray(np.asarray(v, dtype=np.float32))
    shared["c_ident"] = np.eye(128, dtype=np.float32)
    shared["c_p_LE"], shared["c_p_GT"] = mp["LE"], mp["GT"]
    shared["c_s_LE"], shared["c_s_GT"] = ms["LE"], ms["GT"]
    shared["c_s_seg"] = ms["seg"]
    shared["c_p_Sh"], shared["c_s_Sh"] = mp["Sh"], ms["Sh"]
    shared["c_s_maskA"] = (np.arange(128)[:, None] > (np.arange(TS)[None, :] % 8)).astype(np.float32)
    shared["c_p_ShP"], shared["c_s_ShP"] = mp["ShP"], ms["ShP"]
    in_maps = []
    NP = NT * 128
    for c in range(NCORES):
        b, g = (c // G, c % G) if c < B * G else (0, 0)
        m = dict(shared)
        m["xp"] = np.ascontiguousarray(xp[b, g * NP:(g + 1) * NP, :])
        sl = slice(c * NS, (c + 1) * NS)
        m["xsamp"] = np.ascontiguousarray(xs[sl].reshape(TS, D))
        m["sg0"] = np.ascontiguousarray(inputs["state_gla_0"][sl])
        m["ssm1"] = np.ascontiguousarray(inputs["state_ssm_1"][sl])
        m["conv1"] = np.ascontiguousarray(inputs["state_conv_1"][sl])
        m["kc2"] = np.ascontiguousarray(np.asarray(inputs["cache_swa_k_2"][sl]).reshape(NS, 128, 256))
        m["vc2"] = np.ascontiguousarray(np.asarray(inputs["cache_swa_v_2"][sl]).reshape(NS, 128, 256))
        m["sg3"] = np.ascontiguousarray(inputs["state_gla_3"][sl])
        pm = np.zeros((1, 2 * G), np.float32)
        for j in range(G):
            pm[0, j] = 1.0 if j < g else 0.0
            pm[0, G + j] = 1.0 - pm[0, j]
        m["c_pm"] = pm
        ohv = np.zeros((1, G), np.float32)
        if g > 0:
            ohv[0, g - 1] = 1.0
        m["c_oh"] = ohv
        selv = np.zeros((G * 3, 128), np.float32)
        if g > 0:
            for r in range(3):
                selv[(g - 1) * 3 + r, 125 + r] = 1.0
        m["c_sel"] = selv
        m["c_p_maskA0"] = (mp["GT"] * (1.0 if g > 0 else 0.0)).astype(np.float32)
        in_maps.append(m)
    res = run_bass_kernel_spmd(nc, in_maps, core_ids=list(range(NCORES)))
    R = res.results
    last = [b * G + G - 1 for b in range(B)]
    y_prompt = np.stack([np.concatenate([R[b * G + g]["yp"] for g in range(G)], axis=0) for b in range(B)])
    y_sample = np.concatenate([R[c]["ysamp"].reshape(NS, 8, D) for c in range(NCORES)], axis=0)

    def pst(name, shape):
        return np.stack([R[c][name].reshape(shape) for c in last])

    def sst(name, shape):
        return np.concatenate([R[c][name].reshape((NS,) + shape) for c in range(NCORES)], axis=0)

    outs = (y_prompt, y_sample,
            pst("gla0_p", (4, 128, 512)), sst("gla0_s", (4, 128, 512)),
            pst("ssm1_p", (32, 64, 128)), sst("ssm1_s", (32, 64, 128)),
            pst("conv1_p", (3, 3072)), sst("conv1_s", (3, 3072)),
            pst("k2_p", (128, 4, 64)), sst("k2_s", (128, 4, 64)),
            pst("v2_p", (128, 4, 64)), sst("v2_s", (128, 4, 64)),
            pst("gla3_p", (4, 128, 512)), sst("gla3_s", (4, 128, 512)))
    return tuple(np.ascontiguousarray(o, dtype=np.float32) for o in outs)


def kernel(**inputs):
    return _run(inputs)
```

```python
import numpy as np
from contextlib import ExitStack
import concourse.bass as bass
import concourse.mybir as mybir
from concourse.ap import AP
from concourse.bass_utils import run_bass_kernel_spmd

F32 = mybir.dt.float32
BF16 = mybir.dt.bfloat16
AF = mybir.ActivationFunctionType
ALU = mybir.AluOpType

D = 1024
DI = 2048
EPS = 1e-6
GLA_IN = 5136
SSD_IN = 5152
SWA_IN = 4608
NCORES = 8
FORCE_G = None

ENGS = ("pe", "act", "dve", "pool", "sp")


def _conflict(a, b):
    n = min(len(a), len(b))
    return a[:n] == b[:n]


class Op:
    __slots__ = ("eng", "fn", "reads", "writes", "dma", "semkey", "inc", "deps",
                 "sem", "semval", "need_inc", "idx")


class Prog:
    def __init__(self, nc):
        self.nc = nc
        self.ops = []
        self.state = {}
        self.final_waits = []

    @staticmethod
    def _norm(keys):
        out = []
        for k in keys:
            if k is None:
                continue
            if not isinstance(k, tuple):
                k = (k,)
            out.append(k)
        return out

    def op(self, eng, fn, r=(), w=(), dma=False, semkey=None, inc=None, final=False):
        o = Op()
        o.eng = eng
        o.fn = fn
        o.reads = self._norm(r)
        o.writes = self._norm(w)
        o.dma = dma
        o.semkey = semkey
        o.inc = inc if inc is not None else (16 if dma else 1)
        o.idx = len(self.ops)
        o.need_inc = False
        deps = set()
        for k in o.reads:
            tab = self.state.setdefault(k[0], {})
            for kk, st in tab.items():
                if _conflict(k, kk):
                    if st[0] is not None:
                        deps.add(st[0])
                    if k[0] in ("ps", "pst"):
                        deps.update(r for r in st[1] if self.ops[r].eng != eng)
        for k in o.writes:
            tab = self.state.setdefault(k[0], {})
            for kk, st in tab.items():
                if _conflict(k, kk):
                    if st[0] is not None:
                        deps.add(st[0])
                    deps.update(st[1])
        for k in o.reads:
            tab = self.state[k[0]]
            if k not in tab:
                tab[k] = [None, []]
            tab[k][1].append(o.idx)
        for k in o.writes:
            tab = self.state[k[0]]
            for kk in [kk for kk in tab if len(kk) > len(k) and kk[:len(k)] == k]:
                del tab[kk]
            tab[k] = [o.idx, []]
        deps.discard(o.idx)
        keep = set()
        for d in deps:
            dop = self.ops[d]
            if (not dop.dma) and (not o.dma) and dop.eng == eng:
                raw = False
                for k in o.reads:
                    for kk in dop.writes:
                        if _conflict(k, kk):
                            raw = True
                if not raw:
                    continue
            keep.add(d)
        o.deps = sorted(keep)
        self.ops.append(o)
        if final:
            self.final_waits.append(o.idx)
        return o

    def pe(self, fn, r=(), w=(), **kw):
        return self.op("pe", fn, r, w, **kw)

    def act(self, fn, r=(), w=(), **kw):
        return self.op("act", fn, r, w, **kw)

    def dve(self, fn, r=(), w=(), **kw):
        return self.op("dve", fn, r, w, **kw)

    def pool(self, fn, r=(), w=(), **kw):
        return self.op("pool", fn, r, w, **kw)

    def dma(self, q, out, in_, r=(), w=(), semkey=None, final=False, **dkw):
        assert semkey is not None
        sk = ("dma",) + (tuple(semkey) if isinstance(semkey, tuple) else (semkey,))
        return self.op(q, lambda e: e.dma_start(out=out, in_=in_, **dkw), r, w,
                       dma=True, semkey=sk, final=final)

    def emit(self, stack):
        nc = self.nc
        ops = self.ops
        for o in ops:
            for d in o.deps:
                ops[d].need_inc = True
        for i in self.final_waits:
            ops[i].need_inc = True
        engsem = {}
        for e in ("pe", "act", "dve", "pool"):
            engsem[e] = stack.enter_context(nc.semaphore("sem_" + e))
        dmasem = {}
        cnt = {e: 0 for e in engsem}
        dcnt = {}
        for o in ops:
            if o.dma:
                if o.semkey not in dmasem:
                    dmasem[o.semkey] = stack.enter_context(
                        nc.semaphore("sd_" + "_".join(str(x) for x in o.semkey[1:])))
                    dcnt[o.semkey] = 0
                dcnt[o.semkey] += o.inc
                o.sem = dmasem[o.semkey]
                o.semval = dcnt[o.semkey]
                o.need_inc = True
            elif o.need_inc:
                cnt[o.eng] += 1
                o.sem = engsem[o.eng]
                o.semval = cnt[o.eng]
        self.nsems = len(engsem) + len(dmasem)
        self.counts = dict(cnt)
        streams = {e: [o for o in ops if o.eng == e] for e in ENGS}
        block = stack.enter_context(nc.Block())
        final_waits = self.final_waits

        def run_stream(e, eng):
            waited = {}
            issued = []
            for o in streams[e]:
                need = {}
                if e == "pool" and o.dma:
                    if len(issued) >= 2:
                        po = issued[-2]
                        need[po.sem.num] = (po.sem, po.semval)
                    issued.append(o)
                for d in o.deps:
                    dop = ops[d]
                    key = dop.sem.num
                    if key not in need or need[key][1] < dop.semval:
                        need[key] = (dop.sem, dop.semval)
                for key, (sem, val) in need.items():
                    if waited.get(key, 0) >= val:
                        continue
                    eng.wait_ge(sem, val)
                    waited[key] = val
                ins = o.fn(eng)
                if o.need_inc:
                    ins.then_inc(o.sem, o.inc)
            if e == "sp":
                need = {}
                for i in final_waits:
                    dop = ops[i]
                    key = dop.sem.num
                    if key not in need or need[key][1] < dop.semval:
                        need[key] = (dop.sem, dop.semval)
                for key, (sem, val) in need.items():
                    if waited.get(key, 0) >= val:
                        continue
                    eng.wait_ge(sem, val)

        @block.sync
        def _(eng):
            run_stream("sp", eng)

        @block.gpsimd
        def _(eng):
            run_stream("pool", eng)

        @block.scalar
        def _(eng):
            run_stream("act", eng)

        @block.vector
        def _(eng):
            run_stream("dve", eng)

        @block.tensor
        def _(eng):
            run_stream("pe", eng)


def bc_mid(ap2d, n):
    a = ap2d.ap
    return AP(ap2d.tensor, ap2d.offset, [list(a[0]), [0, n], list(a[1])])


def bc_last(ap2d, n):
    a = ap2d.ap
    return AP(ap2d.tensor, ap2d.offset, [list(a[0]), list(a[1]), [0, n]])


def make_masks(T, L):
    idx = np.arange(T)
    seq = idx // L
    same = seq[:, None] == seq[None, :]
    s = idx[:, None]
    t = idx[None, :]
    m = {}
    m["LE"] = (same & (s <= t)).astype(np.float32)
    m["GT"] = (same & (s > t)).astype(np.float32)
    nseq = T // L
    seg = (seq[:, None] == np.arange(nseq)[None, :]).astype(np.float32)
    m["seg"] = seg
    sh = np.zeros((T, 3, T), np.float32)
    for j in range(3):
        sh[:, j, :] = (same & (s == t + j - 3)).astype(np.float32)
    m["Sh"] = sh
    if L == 128:
        shp = np.zeros((128, 3, 128), np.float32)
        for j in range(3):
            for tt in range(3):
                if tt + j < 3:
                    shp[125 + tt + j, j, tt] = 1.0
        m["ShP"] = shp
    else:
        shp = np.zeros((nseq * 3, 3, T), np.float32)
        for j in range(3):
            for tt in range(T):
                q = tt % L
                if q + j < 3:
                    shp[(tt // L) * 3 + q + j, j, tt] = 1.0
        m["ShP"] = shp
    return m


class Cfg:
    def __init__(self, B, SEQ, DB, layers=(0, 1, 2, 3), G=1):
        assert B * G <= NCORES
        self.B = B
        self.G = G
        assert SEQ % (self.G * 128) == 0
        self.NT = SEQ // self.G // 128
        assert DB % NCORES == 0
        self.NS = DB // NCORES
        self.TS = self.NS * 8
        assert self.TS <= 128
        self.layers = tuple(layers)
        self.SEQ = SEQ
        self.DB = DB


LAYER_KIND = {0: "gla", 1: "ssd", 2: "swa", 3: "gla"}
LAYER_NIN = {0: GLA_IN, 1: SSD_IN, 2: SWA_IN, 3: GLA_IN}


class Builder:
    def __init__(self, cfg):
        self.cfg = cfg
        self.nc = bass.Bass("TRN2", target_bir_lowering=False)
        self.P = Prog(self.nc)
        self.stack = ExitStack()
        self.d = {}
        self.bufs = {}
        self.psrr = 0
        self.dbg_stop = 99
        self.dbg_on = False
        self.dbg_names = []
        self.held = set()

    def din(self, name, shape, dt=F32):
        self.d[name] = self.nc.dram_tensor(name, list(shape), dt, kind="ExternalInput").ap()
        return self.d[name]

    def dout(self, name, shape, dt=F32):
        self.d[name] = self.nc.dram_tensor(name, list(shape), dt, kind="ExternalOutput").ap()
        return self.d[name]

    def dscr(self, name, shape, dt=F32):
        self.d[name] = self.nc.dram_tensor(name, list(shape), dt).ap()
        return self.d[name]

    def sb(self, name, shape, dt=F32):
        if name in self.bufs:
            return self.bufs[name]
        t = self.stack.enter_context(self.nc.sbuf_tensor(name, list(shape), dt))
        self.bufs[name] = t
        return t

    def dbg(self, name, ap, rkeys, shape):
        if not getattr(self, "dbg_on", False):
            return
        o = self.dout("dbg_" + name, shape)
        self.P.dma("sp", o, ap, r=rkeys, semkey=("dbg", name), final=True)
        self.dbg_names.append("dbg_" + name)

    def psum(self, hold=False):
        for _ in range(len(self.psb)):
            i = self.psrr
            self.psrr = (self.psrr + 1) % len(self.psb)
            if i not in self.held:
                if hold:
                    self.held.add(i)
                return self.psb[i], ("ps", i)
        raise RuntimeError("no free PSUM bank")

    def psum_release(self, key):
        self.held.discard(key[1])

    def psum_t(self):
        i = self.pstrr
        self.pstrr = (self.pstrr + 1) % len(self.pst)
        return self.pst[i], ("pst", i)

    def declare(self):
        cfg = self.cfg
        NT, NS, TS, G = cfg.NT, cfg.NS, cfg.TS, cfg.G
        NP = NT * 128
        self.din("xp", [NP, D])
        self.din("xsamp", [TS, D])
        self.din("sg0", [NS, 4, 128, 512])
        self.din("ssm1", [NS, 32, 64, 128])
        self.din("conv1", [NS, 3, 3072])
        self.din("kc2", [NS, 128, 256])
        self.din("vc2", [NS, 128, 256])
        self.din("sg3", [NS, 4, 128, 512])
        for l in (0, 3):
            self.din(f"l{l}_norm", [D])
            self.din(f"l{l}_w_in", [D, GLA_IN])
            self.din(f"l{l}_w_gk2", [16, 512])
            self.din(f"l{l}_b_gk", [512])
            self.din(f"l{l}_head_norm", [512])
            self.din(f"l{l}_w_out", [DI, D])
        self.din("l1_norm", [D])
        self.din("l1_w_in", [D, SSD_IN])
        self.din("l1_conv_w", [4, 3072])
        self.din("l1_conv_b", [3072])
        self.din("l1_dt_bias", [32])
        self.din("l1_a_log", [32])
        self.din("l1_d_skip", [32])
        self.din("l1_gate_norm", [DI])
        self.din("l1_w_out", [DI, D])
        self.din("l2_norm", [D])
        self.din("l2_w_in", [D, SWA_IN])
        self.din("l2_sinks", [32])
        self.din("l2_w_out", [DI, D])
        self.din("final_norm", [D])
        self.din("c_ident", [128, 128])
        for kind, T in (("p", 128), ("s", TS)):
            self.din(f"c_{kind}_LE", [T, T])
            self.din(f"c_{kind}_GT", [T, T])
        self.din("c_s_seg", [TS, NS])
        self.din("c_p_Sh", [128, 3, 128])
        self.din("c_s_maskA", [128, TS])
        self.din("c_p_maskA0", [128, 128])
        self.din("c_s_Sh", [TS, 3, TS])
        self.din("c_p_ShP", [128, 3, 128])
        self.din("c_s_ShP", [NS * 3, 3, TS])
        self.din("c_pm", [1, 2 * G])
        self.din("c_oh", [1, G])
        self.din("c_sel", [G * 3, 128])
        self.dout("yp", [NP, D])
        self.dout("ysamp", [TS, D])
        self.dout("gla0_p", [4, 128, 512])
        self.dout("gla0_s", [NS, 4, 128, 512])
        self.dout("ssm1_p", [32, 64, 128])
        self.dout("ssm1_s", [NS, 32, 64, 128])
        self.dout("conv1_p", [3, 3072])
        self.dout("conv1_s", [NS, 3, 3072])
        self.dout("k2_p", [128, 256])
        self.dout("k2_s", [NS, 128, 256])
        self.dout("v2_p", [128, 256])
        self.dout("v2_s", [NS, 128, 256])
        self.dout("gla3_p", [4, 128, 512])
        self.dout("gla3_s", [NS, 4, 128, 512])
        self.dscr("xres", [NP + 128, D])
        self.dscr("cwb", [128, 5 * 3072], BF16)
        self.dscr("cvs", [TS, 3072])
        self.dscr("kvs", [TS, 512])

    def setup_consts(self):
        P, d, cfg = self.P, self.d, self.cfg
        NS, TS = cfg.NS, cfg.TS
        nc = self.nc
        self.psb = [self.stack.enter_context(nc.psum_tensor(f"ps{i}", [128, 512], F32)) for i in range(6)]
        self.pst = [self.stack.enter_context(nc.psum_tensor(f"pst{i}", [128, 1024], BF16)) for i in range(2)]
        self.pstrr = 0
        self.identf = self.sb("identf", [128, 128])
        self.identb = self.sb("identb", [128, 128], BF16)
        P.dma("sp", self.identf[:], d["c_ident"][:, :], w=["identf"], semkey="c0")
        P.dma("pool", self.identb[:], d["c_ident"][:, :], w=["identb"], semkey="c1")
        self.ones_row = self.sb("ones_row", [1, 128])
        P.dve(lambda e: e.memset(self.ones_row[:], 1.0), w=["ones_row"])
        self.M = {}
        for kind, T in (("p", 128), ("s", TS)):
            m = {}
            for nm in ("LE", "GT"):
                t = self.sb(f"m_{kind}_{nm}", [T, T])
                P.dma("sp", t[:], d[f"c_{kind}_{nm}"][:, :], w=[f"m_{kind}_{nm}"], semkey=f"c_{kind}_{nm}")
                m[nm] = t
            self.M[kind] = m
        seg = self.sb("m_s_seg", [TS, NS])
        P.dma("sp", seg[:], d["c_s_seg"][:, :], w=["m_s_seg"], semkey="c_seg")
        self.M["s"]["seg"] = seg
        segp = self.sb("m_p_seg", [128, 1])
        P.dve(lambda e: e.memset(segp[:], 1.0), w=["m_p_seg"])
        self.M["p"]["seg"] = segp

    def load_layer_weights(self, l):
        P, d = self.P, self.d
        nin = LAYER_NIN[l]
        win = self.sb("w_in", [128, 8, SSD_IN], BF16)
        wsrc = d[f"l{l}_w_in"].rearrange("(kc p) n -> p kc n", p=128)
        nblk = (nin + 511) // 512
        for b in range(nblk):
            c0, c1 = b * 512, min(nin, (b + 1) * 512)
            P.dma("pool", win[:, :, c0:c1], wsrc[:, :, c0:c1], w=[("w_in", b)], semkey=("win", b))
        wout = self.sb("w_out", [128, 16, D], BF16)
        wosrc = d[f"l{l}_w_out"].rearrange("(rc p) n -> p rc n", p=128)
        for b in range(4):
            P.dma("pool", wout[:, b * 4:(b + 1) * 4, :], wosrc[:, b * 4:(b + 1) * 4, :], w=[("w_out", b)], semkey=("wout", b))
        ncol = self.sb("normcol", [128, 8])
        P.dma("sp", ncol[:], d[f"l{l}_norm"].rearrange("(kc p) -> p kc", p=128), w=["normcol"], semkey="nrm",
              allow_slow_non_contiguous=True)
        self.win, self.wout, self.ncol = win, wout, ncol

    def tile_src(self, l_first, ti):
        cfg, d = self.cfg, self.d
        NT, TS = cfg.NT, cfg.TS
        if ti < NT:
            if l_first:
                return d["xp"][ti * 128:(ti + 1) * 128, :], ("xp", ti)
            return d["xres"][ti * 128:(ti + 1) * 128, :], ("xres", ti)
        if l_first:
            return d["xsamp"][:, :], ("xsamp",)
        return d["xres"][NT * 128:NT * 128 + TS, :], ("xres", NT)

    def load_x(self, l_first, ti, slot):
        T = 128 if ti < self.cfg.NT else self.cfg.TS
        xt = self.sb(f"xt{slot}", [128, D])
        src, key = self.tile_src(l_first, ti)
        self.P.dma("sp", xt[:T, :], src, r=[key], w=[f"xt{slot}"], semkey=("xt", slot))
        return xt

    def tiles_iter(self, first, tile_ids):
        slot = 0
        xt = self.load_x(first, tile_ids[0], slot)
        for i, ti in enumerate(tile_ids):
            nxt = None
            if i + 1 < len(tile_ids):
                nxt = self.load_x(first, tile_ids[i + 1], 1 - slot)
            yield ti, xt, f"xt{slot}"
            xt = nxt
            slot = 1 - slot

    def norm_transpose(self, xt, xkey, T):
        P = self.P
        junk = self.sb("junk", [128, DI], BF16)
        st = self.sb("nstat", [128, 4])
        hn = junk[:, D:2 * D]
        hT = self.sb("hT", [128, 8, 128], BF16)
        P.act(lambda e: e.activation(out=junk[:T, 0:D], in_=xt[:T, :], func=AF.Square, accum_out=st[:T, 0:1]),
              r=[xkey], w=["junk", ("nstat", 0)])
        P.act(lambda e: e.activation(out=st[:T, 1:2], in_=st[:T, 0:1], func=AF.Sqrt, scale=1.0 / D, bias=self.epsc[:T, 0:1]),
              r=[("nstat", 0), "epsc"], w=[("nstat", 1)])
        P.dve(lambda e: e.reciprocal(out=st[:T, 2:3], in_=st[:T, 1:2]), r=[("nstat", 1)], w=[("nstat", 2)])
        P.act(lambda e: e.activation(out=hn[:T, :], in_=xt[:T, :], func=AF.Copy, scale=st[:T, 2:3]),
              r=[xkey, ("nstat", 2)], w=["junk"])
        pt, pk = self.psum_t()
        for kc in range(8):
            P.pe(lambda e, kc=kc: e.transpose(out=pt[:, kc * 128:kc * 128 + T], in_=hn[:T, kc * 128:(kc + 1) * 128],
                                              identity=self.identb[:T, :T]),
                 r=["junk", "identb"], w=[pk])
        ptv = pt[:].rearrange("p (k t) -> p k t", k=8)[:, :, :T]
        P.dve(lambda e: e.tensor_tensor(out=hT[:, :, :T], in0=ptv, in1=bc_last(self.ncol[:, :], T), op=ALU.mult),
              r=[pk, "normcol"], w=["hT"])
        return hT

    def masked_cols(self, dst, dkey, src, skey, si, T):
        P = self.P
        if si == 0:
            P.dve(lambda e: e.memset(dst[:, :, :T], 0.0), w=[dkey])
        else:
            P.dve(lambda e: e.memset(dst[:, :, 8 * (si - 1):8 * si], 0.0), w=[dkey])
        P.dve(lambda e: e.tensor_copy(out=dst[:, :, 8 * si:8 * si + 8], in_=src[:, :, 8 * si:8 * si + 8]), r=[skey], w=[dkey])

    def proj_block(self, hT, T, c0, c1):
        P = self.P
        ps, pk = self.psum()
        b = c0 // 512
        assert (c1 - 1) // 512 == b
        for kc in range(8):
            P.pe(lambda e, kc=kc: e.matmul(ps[:T, 0:c1 - c0], lhsT=hT[:, kc, :T], rhs=self.win[:, kc, c0:c1],
                                           start=(kc == 0), stop=(kc == 7)),
                 r=["hT", ("w_in", b)], w=[pk])
        return ps, pk

    def out_proj_residual(self, og, ogkey, xt, xkey, T):
        P = self.P
        ogT = self.sb("ogT", [128, 16, 128], BF16)
        for half in range(2):
            pt, pk = self.psum_t()
            for j in range(8):
                vc = half * 8 + j
                P.pe(lambda e, j=j, vc=vc, pt=pt: e.transpose(out=pt[:, j * 128:j * 128 + T], in_=og[:T, vc * 128:(vc + 1) * 128],
                                                              identity=self.identb[:T, :T]),
                     r=[ogkey, "identb"], w=[pk])
            ptv = pt[:].rearrange("p (k t) -> p k t", k=8)[:, :, :T]
            P.dve(lambda e, ptv=ptv, half=half: e.tensor_copy(out=ogT[:, half * 8:half * 8 + 8, :T], in_=ptv), r=[pk], w=[("ogT", half)])
        if self.dbg_stop <= 47:
            return
        for nb in range(2):
            if self.dbg_stop <= 48 and nb == 1:
                return
            ps, pk = self.psum()
            for vc in range(16):
                P.pe(lambda e, vc=vc, nb=nb, ps=ps: e.matmul(ps[:T, :], lhsT=ogT[:, vc, :T], rhs=self.wout[:, vc, nb * 512:(nb + 1) * 512],
                                                             start=(vc == 0), stop=(vc == 15)),
                     r=[("ogT", vc // 8), ("w_out", vc // 4)], w=[pk])
            P.dve(lambda e, nb=nb, ps=ps: e.tensor_tensor(out=xt[:T, nb * 512:(nb + 1) * 512], in0=xt[:T, nb * 512:(nb + 1) * 512],
                                                          in1=ps[:T, :], op=ALU.add),
                  r=[pk, xkey], w=[xkey])

    def store_x(self, xt, xkey, ti, T, last_layer):
        P, d, cfg = self.P, self.d, self.cfg
        NT = cfg.NT
        if not last_layer:
            dst = d["xres"][ti * 128:ti * 128 + T, :]
            P.dma("pool", dst, xt[:T, :], r=[xkey], w=[("xres", ti)], semkey=("xst", xkey))
            return
        junk = self.sb("junk", [128, DI], BF16)
        st = self.sb("nstat", [128, 4])
        P.act(lambda e: e.activation(out=junk[:T, 0:D], in_=xt[:T, :], func=AF.Square, accum_out=st[:T, 0:1]),
              r=[xkey], w=["junk", ("nstat", 0)])
        P.act(lambda e: e.activation(out=st[:T, 1:2], in_=st[:T, 0:1], func=AF.Sqrt, scale=1.0 / D, bias=self.epsc[:T, 0:1]),
              r=[("nstat", 0), "epsc"], w=[("nstat", 1)])
        P.dve(lambda e: e.reciprocal(out=st[:T, 2:3], in_=st[:T, 1:2]), r=[("nstat", 1)], w=[("nstat", 2)])
        for hf in range(2):
            fb = self.sb(f"on{hf}", [128, 512])
            P.dma("sp", fb[:], d["final_norm"][hf * 512:(hf + 1) * 512].partition_broadcast(128), w=[f"on{hf}"], semkey=("fnb", hf))
            P.dve(lambda e, hf=hf, fb=fb: e.scalar_tensor_tensor(out=xt[:T, hf * 512:(hf + 1) * 512], in0=xt[:T, hf * 512:(hf + 1) * 512],
                                                                 scalar=st[:T, 2:3], in1=fb[:T, :], op0=ALU.mult, op1=ALU.mult),
                  r=[xkey, ("nstat", 2), f"on{hf}"], w=[xkey])
        dst = d["yp"][ti * 128:(ti + 1) * 128, :] if ti < NT else d["ysamp"][:, :]
        P.dma("pool", dst, xt[:T, :], r=[xkey], semkey=("yst", xkey), final=True)


    def rank_consts(self):
        P, d, G = self.P, self.d, self.cfg.G
        pm = self.sb("pm", [128, 2 * G])
        P.dma("sp", pm[:], d["c_pm"].rearrange("o n -> (o n)").partition_broadcast(128), w=["pm"], semkey="c_pm")
        oh = self.sb("oh", [128, G])
        P.dma("sp", oh[:], d["c_oh"].rearrange("o n -> (o n)").partition_broadcast(128), w=["oh"], semkey="c_oh")
        sel = self.sb("sel", [G * 3, 128])
        P.dma("sp", sel[:], d["c_sel"][:, :], w=["sel"], semkey="c_sel")
        self.pm, self.oh, self.sel = pm, oh, sel

    def allgather(self, tag, rows, W, writes, cls=0):
        P, G = self.P, self.cfg.G
        xin = self.dscr(f"xin_{tag}", [rows, W])
        xout = self.dscr(f"xout_{tag}", [G * rows, W])
        for i, (c0, c1, src, rk) in enumerate(writes):
            P.dma("sp", xin[:, c0:c1], src, r=rk, w=[(f"xin_{tag}", i)], semkey=("xi", cls, i))
        groups = [list(range(b * G, (b + 1) * G)) for b in range(self.cfg.B)]
        P.op("pool", lambda e: e.collective_compute("AllGather", ALU.bypass, replica_groups=groups, ins=[xin[:, :]], outs=[xout[:, :]]),
             r=[f"xin_{tag}"], w=[f"xout_{tag}"], dma=True, semkey=("dma", "cc", cls), inc=1)
        return xout, f"xout_{tag}"

    def state_combine(self, tag, Sview, Skey, Dview, Dkey, nd):
        P, G = self.P, self.cfg.G
        xout, xk = self.allgather(tag, 128, 2048, [(0, 2048, Sview, [Skey])])
        idd = self.d["c_ident"]
        xoutd, xkd = self.allgather(tag + "d", 128, 256,
                                    [(0, 128, idd[:, :], []), (128, 256, idd[:, :], []),
                                     (0, nd, Dview, [Dkey, (f"xin_{tag}d", 0), (f"xin_{tag}d", 1)])], cls=1)
        P.dve(lambda e: e.memset(Sview, 0.0), r=[(f"xin_{tag}", 0)], w=[Skey])
        g4 = self.sb("g4", [128, 4, 512])
        cand = g4[:].rearrange("p a b -> p (a b)")
        ck = ["spf", "erev", "ecum", "encum"]
        dj = self.sb("xD", [128, 32])
        S3 = Sview.rearrange("p (a b) -> p a b", a=nd)
        for j in range(G):
            P.dma("sp", cand, xout[j * 128:(j + 1) * 128, 0:2048], r=[xk], w=ck, semkey="xc")
            P.dma("sp", dj[:, 0:nd], xoutd[j * 128:(j + 1) * 128, 0:nd], r=[xkd], w=["xD"], semkey="xd")
            P.dve(lambda e, j=j: e.tensor_scalar(out=dj[:, 0:nd], in0=dj[:, 0:nd], scalar1=self.pm[:, j:j + 1], scalar2=self.pm[:, G + j:G + j + 1],
                                                 op0=ALU.mult, op1=ALU.add), r=["xD", "pm"], w=["xD"])
            P.dve(lambda e: e.tensor_tensor(out=S3, in0=S3, in1=bc_last(dj[:, 0:nd], 2048 // nd), op=ALU.mult), r=[Skey, "xD"], w=[Skey])
            P.dve(lambda e, j=j: e.scalar_tensor_tensor(out=Sview, in0=cand, scalar=self.pm[:, j:j + 1], in1=Sview, op0=ALU.mult, op1=ALU.add),
                  r=ck + [Skey, "pm"], w=[Skey])

    def gla_setup(self, l):
        P, d = self.P, self.d
        wgk = self.sb("wgk2", [17, 512])
        P.dma("sp", wgk[0:16, :], d[f"l{l}_w_gk2"][:, :], w=["wgk2"], semkey="gs0")
        P.dma("sp", wgk[16:17, :], d[f"l{l}_b_gk"].rearrange("(o n) -> o n", o=1), w=["wgk2"], semkey="gs1")
        lrT = self.sb("lrT", [17, 128])
        P.dve(lambda e: e.memset(lrT[:, :], 1.0), w=["lrT"])
        bgk = None
        hnb = self.sb("hnb", [128, 512])
        P.dma("sp", hnb[:], d[f"l{l}_head_norm"].partition_broadcast(128), w=["hnb"], semkey="gs2")
        self.wgk, self.bgk, self.hnb = wgk, bgk, hnb

    def gla_tile(self, l, ti, xt, xkey, T, kind, state_only, seqs):
        P, d, cfg = self.P, self.d, self.cfg
        M = self.M[kind]
        nseq = len(seqs)
        hT = self.norm_transpose(xt, xkey, T)
        vb = self.sb("vb", [128, DI], BF16)
        sr = self.sb("sr", [128, DI], BF16)

        def emit_v(b):
            psv, pkv = self.proj_block(hT, T, 1024 + b * 512, 1536 + b * 512)
            P.act(lambda e, b=b, psv=psv: e.activation(out=vb[:T, b * 512:(b + 1) * 512], in_=psv[:T, :], func=AF.Copy),
                  r=[pkv], w=[("vb", b)])

        def emit_r(b):
            psr2, pkr2 = self.proj_block(hT, T, 3072 + b * 512, 3584 + b * 512)
            P.act(lambda e, b=b, psr2=psr2: e.activation(out=sr[:T, b * 512:(b + 1) * 512], in_=psr2[:T, :], func=AF.Silu),
                  r=[pkr2], w=[("sr", b)])

        ps, pk = self.proj_block(hT, T, 5120, 5136)
        lrf = self.sb("lrf", [128, 16])
        P.dve(lambda e: e.tensor_copy(out=lrf[:T, :], in_=ps[:T, 0:16]), r=[pk], w=["lrf"])
        emit_v(0)
        emit_v(1)
        ps2, pk2 = self.psum()
        P.pe(lambda e: e.transpose(out=ps2[:16, :T], in_=lrf[:T, :], identity=self.identf[:T, :T]), r=["lrf", "identf"], w=[pk2])
        lrT = self.sb("lrT", [17, 128])
        P.dve(lambda e: e.tensor_copy(out=lrT[0:16, :T], in_=ps2[:16, :T]), r=[pk2], w=["lrT"])
        psz, pkz = self.psum()
        P.pe(lambda e: e.matmul(psz[:T, :], lhsT=lrT[:, :T], rhs=self.wgk[:, :], start=True, stop=True), r=["lrT", "wgk2"], w=[pkz])
        if self.dbg_stop <= 1:
            return
        g4 = self.sb("g4", [128, 4, 512])
        spf = g4[:, 0, :]
        P.act(lambda e: e.activation(out=spf[:T, :], in_=psz[:T, :], func=AF.Exp, scale=-1.0), r=[pkz], w=["spf"])
        P.act(lambda e: e.activation(out=spf[:T, :], in_=spf[:T, :], func=AF.Ln, bias=self.onec[:T, 0:1]), r=["spf", "onec"], w=["spf"])
        emit_v(2)
        emit_v(3)
        if self.dbg_stop <= 2:
            return
        erev = g4[:, 1, :]
        psr, pkr = self.psum()
        P.pe(lambda e: e.matmul(psr[:T, :], lhsT=M["GT"][:T, :T], rhs=spf[:T, :], start=True, stop=True), r=["spf", f"m_{kind}_GT"], w=[pkr])
        P.act(lambda e: e.activation(out=erev[:T, :], in_=psr[:T, :], func=AF.Exp, scale=-1.0 / 16.0), r=[pkr], w=["erev"])
        if not state_only:
            ecum = g4[:, 2, :]
            encum = g4[:, 3, :]
            psc, pkc = self.psum()
            P.pe(lambda e: e.matmul(psc[:T, :], lhsT=M["LE"][:T, :T], rhs=spf[:T, :], start=True, stop=True), r=["spf", f"m_{kind}_LE"], w=[pkc])
            P.act(lambda e: e.activation(out=ecum[:T, :], in_=psc[:T, :], func=AF.Exp, scale=-1.0 / 16.0), r=[pkc], w=["ecum"])
            P.act(lambda e: e.activation(out=encum[:T, :], in_=psc[:T, :], func=AF.Exp, scale=1.0 / 16.0), r=[pkc], w=["encum"])
        if self.dbg_stop <= 3:
            return
        elast = self.sb("elast", [128, 4, 16])
        psl, pkl = self.psum()
        for h in range(4):
            P.pe(lambda e, h=h: e.matmul(psl[:, h * 16:h * 16 + nseq], lhsT=spf[:T, h * 128:(h + 1) * 128], rhs=M["seg"][:T, :nseq],
                                         start=True, stop=True), r=["spf", "m_s_seg", "m_p_seg"], w=[pkl])
        P.act(lambda e: e.activation(out=elast[:, :, :nseq], in_=psl[:, 0:64].rearrange("p (h j) -> p h j", h=4)[:, :, :nseq], func=AF.Exp, scale=-1.0 / 16.0),
              r=[pkl], w=["elast"])
        if self.dbg_stop <= 4:
            return
        if kind == "s":
            self.dbg("spf", spf[:T, :], ["spf"], [T, 512])
            self.dbg("erev", erev[:T, :], ["erev"], [T, 512])
            self.dbg("elast", elast[:, :, :nseq], ["elast"], [128, 4, nseq])
        if not state_only:
            for b in range(4):
                emit_r(b)
        if not state_only:
            psq, pkq = self.proj_block(hT, T, 0, 512)
            qg = self.sb("qg", [128, 512], BF16)
            P.dve(lambda e: e.scalar_tensor_tensor(out=qg[:T, :], in0=psq[:T, :], scalar=float(128 ** -0.5), in1=ecum[:T, :],
                                                   op0=ALU.mult, op1=ALU.mult), r=[pkq, "ecum"], w=["qg"])
        psk, pkk = self.proj_block(hT, T, 512, 1024)
        kh = self.sb("kh", [128, 512], BF16)
        P.dve(lambda e: e.tensor_tensor(out=kh[:T, :], in0=psk[:T, :], in1=erev[:T, :], op=ALU.mult), r=[pkk, "erev"], w=["kh"])
        if not state_only:
            kg = self.sb("kg", [128, 512], BF16)
            P.dve(lambda e: e.tensor_tensor(out=kg[:T, :], in0=psk[:T, :], in1=encum[:T, :], op=ALU.mult), r=[pkk, "encum"], w=["kg"])
        if self.dbg_stop <= 5:
            return
        if not state_only:
            if self.dbg_stop <= 6:
                return
            qT = self.sb("qT", [128, 4, 128], BF16)
            kT = self.sb("kT", [128, 4, 128], BF16)
            pt, ptk = self.psum_t()
            for h in range(4):
                P.pe(lambda e, h=h: e.transpose(out=pt[:, h * 128:h * 128 + T], in_=qg[:T, h * 128:(h + 1) * 128], identity=self.identb[:T, :T]),
                     r=["qg", "identb"], w=[ptk])
                P.pe(lambda e, h=h: e.transpose(out=pt[:, 512 + h * 128:512 + h * 128 + T], in_=kg[:T, h * 128:(h + 1) * 128],
                                                identity=self.identb[:T, :T]), r=["kg", "identb"], w=[ptk])
            ptv = pt[:].rearrange("p (k t) -> p k t", k=8)
            P.dve(lambda e: e.tensor_copy(out=qT[:, :, :T], in_=ptv[:, 0:4, :T]), r=[ptk], w=["qT"])
            P.dve(lambda e: e.tensor_copy(out=kT[:, :, :T], in_=ptv[:, 4:8, :T]), r=[ptk], w=["kT"])
            if self.dbg_stop <= 7:
                return
            psa, pka = self.psum()
            for h in range(4):
                P.pe(lambda e, h=h: e.matmul(psa[:T, h * 128:h * 128 + T], lhsT=kT[:, h, :T], rhs=qT[:, h, :T], start=True, stop=True),
                     r=["qT", "kT"], w=[pka])
            attT = self.sb("attT", [128, 4, 128], BF16)
            P.dve(lambda e: e.tensor_tensor(out=attT[:T, :, :T], in0=psa[:T, :].rearrange("p (h t) -> p h t", h=4)[:, :, :T],
                                            in1=bc_mid(M["LE"][:T, :T], 4), op=ALU.mult), r=[pka, f"m_{kind}_LE"], w=["attT"])
            if self.dbg_stop <= 8:
                return
            pso = [self.psum(hold=True) for _ in range(4)]
            for h in range(4):
                P.pe(lambda e, h=h: e.matmul(pso[h][0][:T, :], lhsT=attT[:T, h, :T], rhs=vb[:T, h * 512:(h + 1) * 512], start=True, stop=False),
                     r=["attT", ("vb", h)], w=[pso[h][1]])
        if self.dbg_stop <= 9:
            return
        for si, sq in enumerate(seqs):
            j = sq["j"]
            if sq.get("load") is not None:
                sq["load"]()
            S, Sk, Sb, Sbk = sq["S"], sq["Skey"], sq["Sb"], sq["Sbkey"]
            last = si == nseq - 1
            if not state_only:
                if sq["masked"]:
                    qTm = self.sb("qTm", [128, 4, 128], BF16)
                    self.masked_cols(qTm, "qTm", qT, "qT", si, T)
                    qsrc, qk = qTm, "qTm"
                else:
                    qsrc, qk = qT, "qT"
                for h in range(4):
                    P.pe(lambda e, h=h, qsrc=qsrc, Sb=Sb, last=last: e.matmul(pso[h][0][:T, :], lhsT=qsrc[:, h, :T], rhs=Sb[:, h, :],
                                                                             start=False, stop=last),
                         r=[qk, Sbk], w=[pso[h][1]])
            if sq["masked"]:
                khm = self.sb("khm", [128, 512], BF16)
                P.dve(lambda e, j=j: e.tensor_scalar(out=khm[:T, :], in0=kh[:T, :], scalar1=M["seg"][:T, j:j + 1], scalar2=None, op0=ALU.mult),
                      r=["kh", "m_s_seg"], w=["khm"])
                ksrc, kk = khm, "khm"
            else:
                ksrc, kk = kh, "kh"
            for h in range(4):
                psu, pku = self.psum()
                P.pe(lambda e, h=h, psu=psu, ksrc=ksrc: e.matmul(psu[:, :], lhsT=ksrc[:T, h * 128:(h + 1) * 128], rhs=vb[:T, h * 512:(h + 1) * 512],
                                                                start=True, stop=True), r=[kk, ("vb", h)], w=[pku])
                P.dve(lambda e, h=h, psu=psu, S=S, j=j: e.scalar_tensor_tensor(out=S[:, h, :], in0=S[:, h, :], scalar=elast[:, h, j:j + 1],
                                                                                in1=psu[:, :], op0=ALU.mult, op1=ALU.add),
                      r=[pku, "elast", Sk], w=[Sk])
            if sq.get("done") is not None:
                sq["done"]()
        if state_only:
            return
        if self.dbg_stop <= 10:
            for h in range(4):
                self.psum_release(pso[h][1])
            return
        st = self.sb("ostat", [128, 12])
        junk = self.sb("junk", [128, DI], BF16)
        for h in range(4):
            P.act(lambda e, h=h: e.activation(out=junk[:T, h * 512:(h + 1) * 512], in_=pso[h][0][:T, :], func=AF.Square, accum_out=st[:T, h:h + 1]),
                  r=[pso[h][1]], w=[("junk", h), ("ostat", h)])
        P.act(lambda e: e.activation(out=st[:T, 4:8], in_=st[:T, 0:4], func=AF.Sqrt, scale=1.0 / 512, bias=self.epsc[:T, 0:1]),
              r=["ostat", "epsc"], w=[("ostat", 4)])
        P.dve(lambda e: e.reciprocal(out=st[:T, 8:12], in_=st[:T, 4:8]), r=[("ostat", 4)], w=[("ostat", 8)])
        og = self.sb("og", [128, DI], BF16)
        for h in range(4):
            on = self.sb(f"on{h % 2}", [128, 512])
            onk = f"on{h % 2}"
            P.dve(lambda e, h=h, on=on: e.scalar_tensor_tensor(out=on[:T, :], in0=pso[h][0][:T, :], scalar=st[:T, 8 + h:9 + h],
                                                               in1=self.hnb[:T, :], op0=ALU.mult, op1=ALU.mult),
                  r=[pso[h][1], ("ostat", 8), "hnb"], w=[onk])
            self.psum_release(pso[h][1])
            P.dve(lambda e, h=h, on=on: e.tensor_tensor(out=og[:T, h * 512:(h + 1) * 512], in0=on[:T, :],
                                                        in1=sr[:T, h * 512:(h + 1) * 512], op=ALU.mult),
                  r=[onk, ("sr", h)], w=[("og", h)])
        self.out_proj_residual(og, "og", xt, xkey, T)

    def run_layer_gla(self, l, first, lastl):
        P, d, cfg = self.P, self.d, self.cfg
        NT, NS, TS = cfg.NT, cfg.NS, cfg.TS
        self.gla_setup(l)
        sg_in = d["sg0"] if l == 0 else d["sg3"]
        out_p = d["gla0_p"] if l == 0 else d["gla3_p"]
        out_s = d["gla0_s"] if l == 0 else d["gla3_s"]
        S = [self.sb("gS0", [128, 4, 512])]
        Sb = [self.sb("gSb0", [128, 4, 512], BF16)]
        P.dve(lambda e: e.memset(S[0][:], 0.0), w=["gS0"])
        P.dve(lambda e: e.memset(Sb[0][:], 0.0), w=["gSb0"])
        if cfg.G > 1:
            Dt = self.sb("Dtot", [128, 32])
            P.dve(lambda e: e.memset(Dt[:], 1.0), w=["Dtot"])
            for ti, xt, xkey in self.tiles_iter(first, list(range(NT))):
                seqs = [dict(j=0, S=S[0], Skey="gS0", Sb=Sb[0], Sbkey="gSb0", masked=False)]
                self.gla_tile(l, ti, xt, xkey, 128, "p", True, seqs)
                el = self.bufs["elast"]
                P.dve(lambda e, el=el: e.tensor_tensor(out=Dt[:, 0:4], in0=Dt[:, 0:4], in1=el[:, :, 0], op=ALU.mult), r=["Dtot", "elast"], w=["Dtot"])
            self.state_combine(f"g{l}", S[0][:].rearrange("p a b -> p (a b)"), "gS0", Dt[:, 0:4], "Dtot", 4)
            P.act(lambda e: e.activation(out=Sb[0][:], in_=S[0][:], func=AF.Copy), r=["gS0"], w=["gSb0"])
        ntiles = NT + 1
        for ti, xt, xkey in self.tiles_iter(first, list(range(ntiles))):
            T = 128 if ti < NT else TS
            if ti < NT:
                def done(S0=S[0], Sb0=Sb[0]):
                    P.act(lambda e: e.activation(out=Sb0[:], in_=S0[:], func=AF.Copy), r=["gS0"], w=["gSb0"])
                seqs = [dict(j=0, S=S[0], Skey="gS0", Sb=Sb[0], Sbkey="gSb0", masked=False, done=done)]
                self.gla_tile(l, ti, xt, xkey, T, "p", False, seqs)
                if ti == NT - 1:
                    P.dma("pool", out_p.rearrange("h k v -> k h v"), S[0][:], r=["gS0"], semkey="gst_p", final=True)
            else:
                seqs = []
                for j in range(NS):
                    sl = 0
                    def load(j=j, sl=sl):
                        P.dma("sp", S[sl][:], sg_in[j].rearrange("h k v -> k h v"), w=[f"gS{sl}"], semkey=("gsl", sl))
                        P.act(lambda e: e.activation(out=Sb[sl][:], in_=S[sl][:], func=AF.Copy), r=[f"gS{sl}"], w=[f"gSb{sl}"])
                    def done(j=j, sl=sl):
                        P.dma("pool", out_s[j].rearrange("h k v -> k h v"), S[sl][:], r=[f"gS{sl}"], semkey=("gss", sl), final=True)
                    seqs.append(dict(j=j, S=S[sl], Skey=f"gS{sl}", Sb=Sb[sl], Sbkey=f"gSb{sl}", masked=True, load=load, done=done))
                self.gla_tile(l, ti, xt, xkey, T, "s", False, seqs)
            self.store_x(xt, xkey, ti, T, lastl)


    def ssd_setup(self):
        P, d, cfg = self.P, self.d, self.cfg
        NS, TS = cfg.NS, cfg.TS
        P.dma("pool", d["cwb"][:, 0:4 * 3072], d["l1_conv_w"].rearrange("j c -> (j c)").partition_broadcast(128),
              w=["cwb_d"], semkey="cwbd")
        P.dma("pool", d["cwb"][:, 4 * 3072:5 * 3072], d["l1_conv_b"].partition_broadcast(128), w=["cwb_d2"], semkey="cwbd2")
        cbrow = None
        onesb = self.sb("ones_row_b", [1, 128], BF16)
        P.dve(lambda e: e.memset(onesb[:], 1.0), w=["ones_row_b"])
        dtb = self.sb("dtb", [128, 32])
        P.dma("sp", dtb[:], d["l1_dt_bias"].partition_broadcast(128), w=["dtb"], semkey="dtb")
        aneg = self.sb("aneg", [128, 32])
        P.dma("sp", aneg[:], d["l1_a_log"].partition_broadcast(128), w=["aneg"], semkey="aneg")
        P.act(lambda e: e.activation(out=aneg[:], in_=aneg[:], func=AF.Exp), r=["aneg"], w=["aneg"])
        P.dve(lambda e: e.tensor_scalar(out=aneg[:], in0=aneg[:], scalar1=-1.0, scalar2=None, op0=ALU.mult), r=["aneg"], w=["aneg"])
        dsk = self.sb("dsk", [128, 32])
        P.dma("sp", dsk[:], d["l1_d_skip"].partition_broadcast(128), w=["dsk"], semkey="dsk")
        gcol = self.sb("gcol", [128, 16])
        P.dma("sp", gcol[:], d["l1_gate_norm"].rearrange("(rc p) -> p rc", p=128), w=["gcol"], semkey="gcol",
              allow_slow_non_contiguous=True)
        for rc in range(16):
            P.dve(lambda e, rc=rc: e.tensor_scalar(out=self.wout[:, rc, :], in0=self.wout[:, rc, :], scalar1=gcol[:, rc:rc + 1],
                                                   scalar2=None, op0=ALU.mult), r=[("w_out", rc // 4), "gcol"], w=[("w_out", rc // 4)])
        self.Sh = {}
        for kind, T, K in (("p", 128, 128), ("s", TS, NS * 3)):
            sh = self.sb(f"m_{kind}_Sh", [T, 3, T], BF16)
            P.dma("pool", sh[:], d[f"c_{kind}_Sh"][:, :, :], w=[f"m_{kind}_Sh"], semkey=f"c_{kind}_Sh")
            shp = self.sb(f"m_{kind}_ShP", [128, 3, T], BF16)
            P.dma("pool", shp[:K], d[f"c_{kind}_ShP"][:, :, :], w=[f"m_{kind}_ShP"], semkey=f"c_{kind}_ShP")
            self.Sh[kind] = (sh, shp)
        self.cbrow, self.onesb, self.dtb, self.aneg, self.dsk = cbrow, onesb, dtb, aneg, dsk

    def ssd_tile(self, ti, xt, xkey, T, kind, state_only, seqs, rows, conv_out):
        P, d, cfg = self.P, self.d, self.cfg
        M = self.M[kind]
        sh, shp = self.Sh[kind]
        nseq = len(seqs)
        r0, r1 = rows
        hT = self.norm_transpose(xt, xkey, T)
        if self.dbg_stop <= 20:
            return
        xs = self.sb("vb", [128, DI], BF16)
        Bb = self.sb("qg", [128, 512], BF16)
        Cb = self.sb("kg", [128, 512], BF16)
        xtail = self.sb("xtail", [128, 3072], BF16)
        ytail = self.sb("ytail", [128, 3, 512], BF16)
        cwblk = self.sb("cwblk", [128, 5, 512], BF16)
        Yblk = self.sb("ogT", [128, 16, 128], BF16)[:].rearrange("p a b -> p (a b)").rearrange("p (j c) -> p j c", j=4)
        cwsrc = d["cwb"].rearrange("p (j c) -> p j c", j=5)
        nblk = 5 if state_only else 6
        for blk in range(nblk):
            c0 = 2048 + blk * 512
            ps, pk = self.proj_block(hT, T, c0, c0 + 512)
            P.dma("sp", cwblk[:, :, :], cwsrc[:, :, blk * 512:(blk + 1) * 512], r=["cwb_d", "cwb_d2"], w=["cwblk"], semkey="cwblk")
            psb = AP(ps[:T, :].tensor, ps[:T, :].offset, [list(ps[:T, :].ap[0]), [0, 4], list(ps[:T, :].ap[1])])
            P.dve(lambda e, psb=psb: e.tensor_tensor(out=Yblk[:T, :, :], in0=psb, in1=cwblk[:T, 0:4, :], op=ALU.mult),
                  r=[pk, "cwblk"], w=["ogT"])
            if self.dbg_stop <= 31:
                return
            xtb = xtail[r0:r1, blk * 512:(blk + 1) * 512]
            xtb3 = AP(xtb.tensor, xtb.offset, [list(xtb.ap[0]), [0, 3], list(xtb.ap[1])])
            P.dve(lambda e, xtb3=xtb3: e.tensor_tensor(out=ytail[r0:r1, :, :], in0=xtb3, in1=cwblk[r0:r1, 0:3, :], op=ALU.mult),
                  r=[("xtail", blk), "cwblk"], w=["ytail"])
            if self.dbg_stop <= 32:
                return
            if conv_out is not None:
                stg = self.sb(f"on{blk % 2}", [128, 512])
                P.dve(lambda e, ps=ps, stg=stg: e.tensor_copy(out=stg[:T, :], in_=ps[:T, :]), r=[pk], w=[f"on{blk % 2}"])
                conv_out(stg, f"on{blk % 2}", blk)
            if kind == "p":
                P.dve(lambda e, ps=ps, blk=blk: e.tensor_copy(out=xtail[64:128, blk * 512:(blk + 1) * 512], in_=ps[64:128, :]),
                      r=[pk, "ytail"], w=[("xtail", blk)])
            if self.dbg_stop <= 33:
                return
            pc, pck = self.psum()
            for j in range(3):
                P.pe(lambda e, j=j, pc=pc: e.matmul(pc[:T, :], lhsT=sh[:T, j, :T], rhs=Yblk[:T, j, :], start=(j == 0), stop=False),
                     r=["ogT", f"m_{kind}_Sh"], w=[pck])
            P.pe(lambda e, pc=pc: e.matmul(pc[:T, :], lhsT=self.identb[:T, :T], rhs=Yblk[:T, 3, :], start=False, stop=False),
                 r=["ogT", "identb"], w=[pck])
            if self.dbg_stop <= 34:
                return
            for j in range(3):
                P.pe(lambda e, j=j, pc=pc: e.matmul(pc[:T, :], lhsT=shp[r0:r1, j, :T], rhs=ytail[r0:r1, j, :], start=False, stop=False),
                     r=["ytail", f"m_{kind}_ShP"], w=[pck])
            if self.dbg_stop <= 35:
                return
            P.pe(lambda e, pc=pc, blk=blk: e.matmul(pc[:T, :], lhsT=self.onesb[0:1, :T], rhs=cwblk[0:1, 4, :],
                                                    start=False, stop=True), r=["ones_row_b", "cwblk"], w=[pck])
            if blk < 4:
                P.act(lambda e, pc=pc, blk=blk: e.activation(out=xs[:T, blk * 512:(blk + 1) * 512], in_=pc[:T, :], func=AF.Silu),
                      r=[pck], w=[("vb", blk)])
            elif blk == 4:
                P.act(lambda e, pc=pc: e.activation(out=Bb[:T, :], in_=pc[:T, :], func=AF.Silu), r=[pck], w=["qg"])
            else:
                P.act(lambda e, pc=pc: e.activation(out=Cb[:T, :], in_=pc[:T, :], func=AF.Silu), r=[pck], w=["kg"])
        if self.dbg_stop <= 41:
            return
        sm = self.sb("ssm_small", [128, 8, 32])
        psd, pkd = self.proj_block(hT, T, 5120, 5152)
        P.dve(lambda e: e.tensor_tensor(out=sm[:T, 4, :], in0=psd[:T, 0:32], in1=self.dtb[:T, :], op=ALU.add), r=[pkd, "dtb"], w=[("sm", 4)])
        P.act(lambda e: e.activation(out=sm[:T, 4, :], in_=sm[:T, 4, :], func=AF.Exp), r=[("sm", 4)], w=[("sm", 4)])
        P.act(lambda e: e.activation(out=sm[:T, 0, :], in_=sm[:T, 4, :], func=AF.Ln, bias=self.onec[:T, 0:1]), r=[("sm", 4), "onec"], w=[("sm", 0)])
        P.dve(lambda e: e.tensor_tensor(out=sm[:T, 1, :], in0=sm[:T, 0, :], in1=self.aneg[:T, :], op=ALU.mult), r=[("sm", 0), "aneg"], w=[("sm", 1)])
        la = sm[:T, 1, :]
        psr, pkr = self.psum()
        P.pe(lambda e: e.matmul(psr[:T, 0:32], lhsT=M["GT"][:T, :T], rhs=la, start=True, stop=True), r=[("sm", 1), f"m_{kind}_GT"], w=[pkr])
        P.act(lambda e: e.activation(out=sm[:T, 2, :], in_=psr[:T, 0:32], func=AF.Exp), r=[pkr], w=[("sm", 2)])
        P.dve(lambda e: e.tensor_tensor(out=sm[:T, 2, :], in0=sm[:T, 2, :], in1=sm[:T, 0, :], op=ALU.mult), r=[("sm", 2), ("sm", 0)], w=[("sm", 2)])
        if not state_only:
            psc, pkc = self.psum()
            P.pe(lambda e: e.matmul(psc[:T, 0:32], lhsT=M["LE"][:T, :T], rhs=la, start=True, stop=True), r=[("sm", 1), f"m_{kind}_LE"], w=[pkc])
            P.act(lambda e: e.activation(out=sm[:T, 3, :], in_=psc[:T, 0:32], func=AF.Exp), r=[pkc], w=[("sm", 3)])
        elb = self.sb("g4", [128, 4, 512])[:, 3, :].rearrange("p (j h) -> p j h", h=32)
        pse, pke = self.psum()
        for sq in seqs:
            j = sq["j"]
            sc = M["seg"][:T, j:j + 1]
            scb = AP(sc.tensor, sc.offset, [list(sc.ap[0]), [0, 128]])
            P.pe(lambda e, j=j, scb=scb: e.matmul(pse[:, j * 32:(j + 1) * 32], lhsT=scb, rhs=la, start=True, stop=True),
                 r=[("sm", 1), f"m_{kind}_seg"], w=[pke])
        P.act(lambda e: e.activation(out=elb[:, :nseq, :], in_=pse[:, 0:nseq * 32].rearrange("p (j h) -> p j h", h=32), func=AF.Exp),
              r=[pke], w=["encum"])
        if self.dbg_stop <= 42:
            return
        if not state_only:
            zs = self.sb("sr", [128, DI], BF16)
            for b in range(4):
                psz, pkz = self.proj_block(hT, T, b * 512, (b + 1) * 512)
                P.act(lambda e, b=b, psz=psz: e.activation(out=zs[:T, b * 512:(b + 1) * 512], in_=psz[:T, :], func=AF.Silu), r=[pkz], w=[("sr", b)])
            BT = self.sb("qT", [128, 4, 128], BF16)
            CT = self.sb("kT", [128, 4, 128], BF16)
            pt, ptk = self.psum_t()
            for g in range(4):
                P.pe(lambda e, g=g: e.transpose(out=pt[:, g * 128:g * 128 + T], in_=Bb[:T, g * 128:(g + 1) * 128], identity=self.identb[:T, :T]),
                     r=["qg", "identb"], w=[ptk])
                P.pe(lambda e, g=g: e.transpose(out=pt[:, 512 + g * 128:512 + g * 128 + T], in_=Cb[:T, g * 128:(g + 1) * 128],
                                                identity=self.identb[:T, :T]), r=["kg", "identb"], w=[ptk])
            ptv = pt[:].rearrange("p (k t) -> p k t", k=8)
            P.dve(lambda e: e.tensor_copy(out=BT[:, :, :T], in_=ptv[:, 0:4, :T]), r=[ptk], w=["qT"])
            P.dve(lambda e: e.tensor_copy(out=CT[:, :, :T], in_=ptv[:, 4:8, :T]), r=[ptk], w=["kT"])
            psa, pka = self.psum()
            for g in range(4):
                P.pe(lambda e, g=g: e.matmul(psa[:T, g * 128:g * 128 + T], lhsT=BT[:, g, :T], rhs=CT[:, g, :T], start=True, stop=True),
                     r=["qT", "kT"], w=[pka])
            cbT = self.sb("attT", [128, 4, 128], BF16)
            P.dve(lambda e: e.tensor_tensor(out=cbT[:T, :, :T], in0=psa[:T, :].rearrange("p (g t) -> p g t", g=4)[:, :, :T],
                                            in1=bc_mid(M["LE"][:T, :T], 4), op=ALU.mult), r=[pka, f"m_{kind}_LE"], w=["attT"])
            psy = [self.psum(hold=True) for _ in range(4)]
        if self.dbg_stop <= 43:
            for g in range(4):
                self.psum_release(psy[g][1])
            return
        uu = self.sb("junk", [128, DI], BF16)
        P.dve(lambda e: e.tensor_tensor(out=uu[:T, :].rearrange("p (h q) -> p h q", h=32), in0=xs[:T, :].rearrange("p (h q) -> p h q", h=32),
                                        in1=bc_last(sm[:T, 2, :], 64), op=ALU.mult), r=["vb", ("sm", 2)], w=["junk"])
        if self.dbg_stop <= 43.1:
            for g in range(4):
                self.psum_release(psy[g][1])
            return
        for si, sq in enumerate(seqs):
            j = sq["j"]
            if sq.get("load") is not None:
                sq["load"]()
            if self.dbg_stop <= 43.2:
                for g in range(4):
                    self.psum_release(psy[g][1])
                return
            ST, STk, STb, STbk = sq["S"], sq["Skey"], sq["Sb"], sq["Sbkey"]
            last = si == nseq - 1
            if not state_only:
                if sq["masked"]:
                    CTm = self.sb("qTm", [128, 4, 128], BF16)
                    self.masked_cols(CTm, "qTm", CT, "kT", si, T)
                    csrc, ck = CTm, "qTm"
                else:
                    csrc, ck = CT, "kT"
                for g in range(4):
                    P.pe(lambda e, g=g, csrc=csrc, STb=STb, si=si, last=last: e.matmul(psy[g][0][:T, :], lhsT=csrc[:, g, :T],
                                                                                      rhs=STb[:, g * 512:(g + 1) * 512],
                                                                                      start=(si == 0), stop=last),
                         r=[ck, STbk], w=[psy[g][1]])
            if self.dbg_stop <= 43.4:
                for g in range(4):
                    self.psum_release(psy[g][1])
                return
            if sq["masked"]:
                Bm = self.sb("khm", [128, 512], BF16)
                P.dve(lambda e, j=j: e.tensor_scalar(out=Bm[:T, :], in0=Bb[:T, :], scalar1=M["seg"][:T, j:j + 1], scalar2=None, op0=ALU.mult),
                      r=["qg", "m_s_seg"], w=["khm"])
                bsrc, bk = Bm, "khm"
            else:
                bsrc, bk = Bb, "qg"
            for g in range(4):
                psu, pku = self.psum()
                P.pe(lambda e, g=g, psu=psu, bsrc=bsrc: e.matmul(psu[:, :], lhsT=bsrc[:T, g * 128:(g + 1) * 128], rhs=uu[:T, g * 512:(g + 1) * 512],
                                                                start=True, stop=True), r=[bk, "junk"], w=[pku])
                stv = ST[:, g * 512:(g + 1) * 512].rearrange("p (h q) -> p h q", h=8)
                P.dve(lambda e, g=g, stv=stv, j=j: e.tensor_tensor(out=stv, in0=stv, in1=bc_last(elb[:, j, g * 8:(g + 1) * 8], 64), op=ALU.mult),
                      r=[STk, "encum"], w=[STk])
                P.dve(lambda e, g=g, psu=psu, ST=ST: e.tensor_tensor(out=ST[:, g * 512:(g + 1) * 512], in0=ST[:, g * 512:(g + 1) * 512],
                                                                     in1=psu[:, :], op=ALU.add), r=[pku, STk], w=[STk])
            if self.dbg_stop <= 43.6:
                for g in range(4):
                    self.psum_release(psy[g][1])
                return
            if sq.get("done") is not None:
                sq["done"]()
        if state_only:
            return
        if self.dbg_stop <= 44:
            for g in range(4):
                self.psum_release(psy[g][1])
            return
        og = self.sb("og", [128, DI], BF16)
        for g in range(4):
            P.dve(lambda e, g=g: e.tensor_tensor(out=og[:T, g * 512:(g + 1) * 512].rearrange("p (h q) -> p h q", h=8),
                                                 in0=psy[g][0][:T, :].rearrange("p (h q) -> p h q", h=8),
                                                 in1=bc_last(sm[:T, 3, g * 8:(g + 1) * 8], 64), op=ALU.mult),
                  r=[psy[g][1], ("sm", 3)], w=[("og", g)])
            self.psum_release(psy[g][1])
        if self.dbg_stop <= 45:
            return
        P.dve(lambda e: e.tensor_tensor(out=uu[:T, :].rearrange("p (h q) -> p h q", h=32), in0=xs[:T, :].rearrange("p (h q) -> p h q", h=32),
                                        in1=bc_last(sm[:T, 0, :], 64), op=ALU.mult), r=["vb", ("sm", 0)], w=["junk"])
        g4 = self.sb("g4", [128, 4, 512])
        laexp = g4[:, 0:2, :].rearrange("p a (b t) -> p (a b) t", t=128)
        Eg = self.sb("Eg", [128, 8, 128], BF16)
        Mg = self.sb("Mg", [128, 8, 128], BF16)
        st = self.sb("ostat", [128, 12])
        sq_junk = g4[:, 3, :]
        for g in range(4):
            P.dve(lambda e, g=g: e.tensor_tensor(out=laexp[:T, :, :T], in0=bc_last(sm[:T, 1, g * 8:(g + 1) * 8], T), in1=bc_mid(M["LE"][:T, :T], 8),
                                                 op=ALU.mult), r=[("sm", 1), f"m_{kind}_LE"], w=["spf", "erev"])
            nmm = 2 if T == 128 else 1
            for hb in range(nmm):
                psd2, pkd2 = self.psum()
                e0, e1 = (hb * 4, hb * 4 + 4) if nmm == 2 else (0, 8)
                P.pe(lambda e, psd2=psd2, e0=e0, e1=e1: e.matmul(psd2[:T, 0:(e1 - e0) * T].rearrange("p (a t) -> p a t", t=T),
                                                                 lhsT=M["GT"][:T, :T], rhs=laexp[:T, e0:e1, :T], start=True, stop=True),
                     r=["spf", "erev", f"m_{kind}_GT"], w=[pkd2])
                P.act(lambda e, psd2=psd2, e0=e0, e1=e1: e.activation(out=Eg[:T, e0:e1, :T],
                                                                      in_=psd2[:T, 0:(e1 - e0) * T].rearrange("p (a t) -> p a t", t=T), func=AF.Exp),
                      r=[pkd2], w=[("Eg", hb)])
            P.dve(lambda e, g=g: e.tensor_tensor(out=Mg[:T, :, :T], in0=Eg[:T, :, :T], in1=bc_mid(cbT[:T, g, :T], 8), op=ALU.mult),
                  r=["Eg", "attT"], w=["Mg"])
            pyi, pyik = self.psum()
            for eh in range(8):
                h = g * 8 + eh
                P.pe(lambda e, eh=eh, h=h, pyi=pyi: e.matmul(pyi[:T, eh * 64:(eh + 1) * 64], lhsT=Mg[:T, eh, :T], rhs=uu[:T, h * 64:(h + 1) * 64],
                                                             start=True, stop=True), r=["Mg", "junk"], w=[pyik])
            on = self.sb(f"on{g % 2}", [128, 512])
            onk = f"on{g % 2}"
            on2 = g4[:, 2, :]
            gs = slice(g * 512, (g + 1) * 512)
            P.dve(lambda e, on=on, pyi=pyi, gs=gs: e.tensor_tensor(out=on[:T, :], in0=pyi[:T, :], in1=og[:T, gs], op=ALU.add),
                  r=[pyik, ("og", g)], w=[onk])
            P.dve(lambda e, g=g, gs=gs: e.tensor_tensor(out=on2[:T, :].rearrange("p (h q) -> p h q", h=8),
                                                        in0=xs[:T, gs].rearrange("p (h q) -> p h q", h=8),
                                                        in1=bc_last(self.dsk[:T, g * 8:(g + 1) * 8], 64), op=ALU.mult),
                  r=[("vb", g), "dsk"], w=["ecum"])
            P.dve(lambda e, on=on: e.tensor_tensor(out=on[:T, :], in0=on[:T, :], in1=on2[:T, :], op=ALU.add), r=[onk, "ecum"], w=[onk])
            P.dve(lambda e, on=on, gs=gs: e.tensor_tensor(out=on[:T, :], in0=on[:T, :], in1=zs[:T, gs], op=ALU.mult), r=[onk, ("sr", g)], w=[onk])
            P.act(lambda e, on=on, g=g: e.activation(out=sq_junk[:T, :], in_=on[:T, :], func=AF.Square, accum_out=st[:T, g:g + 1]),
                  r=[onk], w=["encum", ("ostat", g)])
            P.act(lambda e, g=g: e.activation(out=st[:T, 4 + g:5 + g], in_=st[:T, g:g + 1], func=AF.Sqrt, scale=1.0 / 512, bias=self.epsc[:T, 0:1]),
                  r=[("ostat", g), "epsc"], w=[("ostat", 4 + g)])
            P.dve(lambda e, g=g: e.reciprocal(out=st[:T, 8 + g:9 + g], in_=st[:T, 4 + g:5 + g]), r=[("ostat", 4 + g)], w=[("ostat", 8 + g)])
            P.dve(lambda e, on=on, g=g, gs=gs: e.tensor_scalar(out=og[:T, gs], in0=on[:T, :], scalar1=st[:T, 8 + g:9 + g], scalar2=None, op0=ALU.mult),
                  r=[onk, ("ostat", 8 + g)], w=[("og", g)])
        if self.dbg_stop <= 46:
            return
        self.out_proj_residual(og, "og", xt, xkey, T)

    def run_layer_ssd(self, l, first, lastl):
        P, d, cfg = self.P, self.d, self.cfg
        NT, NS, TS = cfg.NT, cfg.NS, cfg.TS
        self.ssd_setup()
        ST = self.sb("gS0", [128, 4, 512])[:].rearrange("p a b -> p (a b)")
        STb = self.sb("gSb0", [128, 4, 512], BF16)[:].rearrange("p a b -> p (a b)")
        xtail = self.sb("xtail", [128, 3072], BF16)
        P.dve(lambda e: e.memset(ST, 0.0), w=["gS0"])
        P.dve(lambda e: e.memset(STb, 0.0), w=["gSb0"])
        P.dve(lambda e: e.memset(xtail[:, :], 0.0), w=["xtail"])
        if cfg.G > 1:
            xt = self.load_x(first, NT - 1, 0)
            hT = self.norm_transpose(xt, "xt0", 128)
            writes = []
            for blk in range(6):
                ps, pk = self.proj_block(hT, 128, 2048 + blk * 512, 2560 + blk * 512)
                so = self.sb(f"on{blk % 2}", [128, 512])
                P.dve(lambda e, ps=ps, so=so: e.tensor_copy(out=so[:, :], in_=ps[:, :]), r=[pk], w=[f"on{blk % 2}"])
                writes.append((blk * 512, (blk + 1) * 512, so[125:128, :], [f"on{blk % 2}"]))
                if blk == 0:
                    xin_cvp = self.dscr("xin_cv", [128, 256])
                    xout_cvp = self.dscr("xout_cv", [cfg.G * 128, 256])
                    xin_cv = xin_cvp.rearrange("p w -> (p w)")[0:9216].rearrange("(r c) -> r c", c=3072)
                    xout_cv = xout_cvp.rearrange("(g p) w -> g (p w)", g=cfg.G)[:, 0:9216].rearrange("g (r c) -> g r c", c=3072)
                    for hf in range(2):
                        P.dma("sp", xin_cvp[:, hf * 128:(hf + 1) * 128], d["c_ident"][:, :], w=[("xin_cv", f"f{hf}")], semkey=("xi_cv", 2 + hf))
                P.dma("sp", xin_cv[:, blk * 512:(blk + 1) * 512], so[125:128, :], r=[f"on{blk % 2}", ("xin_cv", "f0"), ("xin_cv", "f1")],
                      w=[("xin_cv", blk)], semkey=("xi_cv", blk % 2))
            groups = [list(range(b * cfg.G, (b + 1) * cfg.G)) for b in range(cfg.B)]
            P.op("pool", lambda e: e.collective_compute("AllGather", ALU.bypass, replica_groups=groups, ins=[xin_cvp[:, :]], outs=[xout_cvp[:, :]]),
                 r=["xin_cv"], w=["xout_cv"], dma=True, semkey=("dma", "cc", 3), inc=1)

            def init_tail():
                for blk in range(6):
                    so = self.sb(f"on{blk % 2}", [128, 512])
                    for gg in range(cfg.G):
                        P.dma("sp", so[gg * 3:gg * 3 + 3, :], xout_cv[gg, :, blk * 512:(blk + 1) * 512], r=["xout_cv"], w=[f"on{blk % 2}"],
                              semkey=("xo_cv", blk % 2))
                    ps, pk = self.psum()
                    P.pe(lambda e, ps=ps, so=so: e.matmul(ps[:, :], lhsT=self.sel[:, :], rhs=so[0:cfg.G * 3, :], start=True, stop=True),
                         r=[f"on{blk % 2}", "sel"], w=[pk])
                    P.dve(lambda e, ps=ps, blk=blk: e.tensor_copy(out=xtail[64:128, blk * 512:(blk + 1) * 512], in_=ps[64:128, :]),
                          r=[pk], w=[("xtail", blk)])
            init_tail()
            Dt = self.sb("Dtot", [128, 32])
            P.dve(lambda e: e.memset(Dt[:], 1.0), w=["Dtot"])
            for ti, xt, xkey in self.tiles_iter(first, list(range(NT))):
                seqs = [dict(j=0, S=ST, Skey="gS0", Sb=STb, Sbkey="gSb0", masked=False)]
                self.ssd_tile(ti, xt, xkey, 128, "p", True, seqs, (64, 128), None)
                elb = self.bufs["g4"][:, 3, :].rearrange("p (j h) -> p j h", h=32)
                P.dve(lambda e, elb=elb: e.tensor_tensor(out=Dt[:, :], in0=Dt[:, :], in1=elb[:, 0, :], op=ALU.mult), r=["Dtot", "encum"], w=["Dtot"])
            self.state_combine("s1", ST, "gS0", Dt[:, :], "Dtot", 32)
            P.act(lambda e: e.activation(out=STb, in_=ST, func=AF.Copy), r=["gS0"], w=["gSb0"])
            init_tail()

        def st_load(src_seq):
            srcv = src_seq.rearrange("(c h2) q n -> (h2 q) c n", c=16)
            for cg in range(4):
                stg = self.sb(f"on{cg % 2}", [128, 512])
                P.dma("sp", stg[:].rearrange("p (c n) -> p c n", c=4), srcv[:, cg * 4:(cg + 1) * 4, :], w=[f"on{cg % 2}"], semkey=("stl", cg % 2))
                ps, pk = self.psum()
                for c in range(4):
                    P.pe(lambda e, c=c, ps=ps, stg=stg: e.transpose(out=ps[:, c * 128:(c + 1) * 128], in_=stg[:, c * 128:(c + 1) * 128],
                                                                    identity=self.identf[:, :]), r=[f"on{cg % 2}", "identf"], w=[pk])
                P.dve(lambda e, ps=ps, cg=cg: e.tensor_copy(out=ST[:, cg * 512:(cg + 1) * 512], in_=ps[:, :]), r=[pk], w=["gS0"])
                P.dve(lambda e, ps=ps, cg=cg: e.tensor_copy(out=STb[:, cg * 512:(cg + 1) * 512], in_=ps[:, :]), r=[pk], w=["gSb0"])

        def st_store(dst_seq, semname):
            dstv = dst_seq.rearrange("(c h2) q n -> (h2 q) c n", c=16)
            for cg in range(4):
                ps, pk = self.psum()
                for c in range(4):
                    cc = cg * 4 + c
                    P.pe(lambda e, c=c, cc=cc, ps=ps: e.transpose(out=ps[:, c * 128:(c + 1) * 128], in_=ST[:, cc * 128:(cc + 1) * 128],
                                                                  identity=self.identf[:, :]), r=["gS0", "identf"], w=[pk])
                stg = self.sb(f"on{cg % 2}", [128, 512])
                P.dve(lambda e, ps=ps, stg=stg: e.tensor_copy(out=stg[:, :], in_=ps[:, :]), r=[pk], w=[f"on{cg % 2}"])
                P.dma("pool", dstv[:, cg * 4:(cg + 1) * 4, :], stg[:].rearrange("p (c n) -> p c n", c=4), r=[f"on{cg % 2}"],
                      semkey=(semname, cg % 2), final=True)

        ntiles = NT + 1
        for ti, xt, xkey in self.tiles_iter(first, list(range(ntiles))):
            T = 128 if ti < NT else TS
            if ti < NT:
                def done():
                    P.act(lambda e: e.activation(out=STb, in_=ST, func=AF.Copy), r=["gS0"], w=["gSb0"])
                seqs = [dict(j=0, S=ST, Skey="gS0", Sb=STb, Sbkey="gSb0", masked=False, done=done)]
                conv_out = None
                if ti == NT - 1:
                    def conv_out(stg, skey, blk):
                        P.dma("pool", d["conv1_p"][:, blk * 512:(blk + 1) * 512], stg[125:128, :], r=[skey], semkey=("cvo", skey), final=True)
                        if blk == 0:
                            self.dbg("stg0", stg[:, :], [skey], [128, 512])
                self.ssd_tile(ti, xt, xkey, T, "p", False, seqs, (64, 128), conv_out)
                if ti == NT - 1:
                    st_store(d["ssm1_p"], "sst_p")
            else:
                P.dma("pool", xtail[0:NS * 3, :], d["conv1"].rearrange("s r c -> (s r) c"), w=["xtail"], semkey="xtl_s")
                seqs = []
                for j in range(NS):
                    def load(j=j):
                        st_load(d["ssm1"][j])
                    def done(j=j):
                        st_store(d["ssm1_s"][j], "sst_s")
                    seqs.append(dict(j=j, S=ST, Skey="gS0", Sb=STb, Sbkey="gSb0", masked=True, load=load, done=done))
                def conv_out(stg, skey, blk):
                    P.dma("pool", d["cvs"][:, blk * 512:(blk + 1) * 512], stg[:TS, :], r=[skey], w=[("cvs", blk)], semkey=("cvo", skey))
                    P.dma("pool", d["conv1_s"][:, :, blk * 512:(blk + 1) * 512],
                          d["cvs"][:, blk * 512:(blk + 1) * 512].rearrange("(s t) c -> s t c", t=8)[:, 5:8, :],
                          r=[("cvs", blk)], semkey="fin2", final=True)
                self.ssd_tile(ti, xt, xkey, T, "s", False, seqs, (0, NS * 3), conv_out)
            self.store_x(xt, xkey, ti, T, lastl)


    def swa_setup(self):
        P, d, cfg = self.P, self.d, self.cfg
        NS, TS = cfg.NS, cfg.TS
        esink = self.sb("esink", [128, 32])
        P.dma("sp", esink[:], d["l2_sinks"].partition_broadcast(128), w=["esink"], semkey="esink")
        P.act(lambda e: e.activation(out=esink[:], in_=esink[:], func=AF.Exp), r=["esink"], w=["esink"])
        mas = self.sb("m_s_maskA", [128, TS])
        P.dma("sp", mas[:], d["c_s_maskA"][:, :], w=["m_s_maskA"], semkey="c_s_maskA")
        ma0 = self.sb("m_p_maskA0", [128, 128])
        P.dma("sp", ma0[:], d["c_p_maskA0"][:, :], w=["m_p_maskA0"], semkey="c_p_maskA0")
        zl = self.sb("zeros_b", [128, 128], BF16)
        P.dve(lambda e: e.memset(zl[:], 0.0), w=["zeros_b"])
        self.esink, self.mas, self.ma0, self.zl = esink, mas, ma0, zl

    def swa_views(self):
        cw = self.sb("cwblk", [128, 5, 512], BF16)
        yt = self.sb("ytail", [128, 3, 512], BF16)
        kT2 = [cw[:, i, :].rearrange("p (k t) -> p k t", k=4) for i in range(3)]
        vaug = [yt[:, i, 0:260].rearrange("p (k c) -> p k c", k=4) for i in range(3)]
        return kT2, vaug

    def swa_kv_prep(self, kvf, kvkey, T, slot):
        P = self.P
        kT2, vaug = self.swa_views()
        kd = self.sb("qg", [128, 512], BF16)
        kdv = kd[:T, :].rearrange("p (k r c) -> p k r c", k=4, r=2)
        kin = kvf[:T, 0:256].rearrange("p (k c) -> p k c", k=4)
        for r in range(2):
            P.dve(lambda e, r=r: e.tensor_copy(out=kdv[:, :, r, :], in_=kin), r=[kvkey], w=["qg"])
        P.dve(lambda e: e.tensor_copy(out=vaug[slot][:T, :, 0:64], in_=kvf[:T, 256:512].rearrange("p (k c) -> p k c", k=4)),
              r=[kvkey], w=[("ytail", slot)])
        P.dve(lambda e: e.memset(vaug[slot][:T, :, 64:65], 1.0), w=[("ytail", slot)])
        pt, ptk = self.psum_t()
        for k in range(4):
            P.pe(lambda e, k=k: e.transpose(out=pt[:, k * 128:k * 128 + T], in_=kd[:T, k * 128:(k + 1) * 128], identity=self.identb[:T, :T]),
                 r=["qg", "identb"], w=[ptk])
        P.dve(lambda e: e.tensor_copy(out=kT2[slot][:, :, :T], in_=pt[:, 0:512].rearrange("p (k t) -> p k t", k=4)[:, :, :T]),
              r=[ptk], w=[("cwblk", slot)])

    def swa_tile(self, ti, xt, xkey, T, kind, cur, prev, maskA, maskAkey, seqs_cache, kv_out):
        P, d, cfg = self.P, self.d, self.cfg
        M = self.M[kind]
        kT2, vaug = self.swa_views()
        hT = self.norm_transpose(xt, xkey, T)
        qb = self.sb("vb", [128, DI], BF16)
        for b in range(4):
            ps, pk = self.proj_block(hT, T, b * 512, (b + 1) * 512)
            P.act(lambda e, b=b, ps=ps: e.activation(out=qb[:T, b * 512:(b + 1) * 512], in_=ps[:T, :], func=AF.Copy, scale=0.125),
                  r=[pk], w=[("vb", b)])
        qT = self.sb("ogT", [128, 16, 128], BF16)
        for half in range(2):
            pt, ptk = self.psum_t()
            for jj in range(8):
                c = half * 8 + jj
                P.pe(lambda e, jj=jj, c=c, pt=pt: e.transpose(out=pt[:, jj * 128:jj * 128 + T], in_=qb[:T, c * 128:(c + 1) * 128],
                                                              identity=self.identb[:T, :T]), r=[("vb", c // 4), "identb"], w=[ptk])
            P.dve(lambda e, half=half, pt=pt: e.tensor_copy(out=qT[:, half * 8:half * 8 + 8, :T],
                                                            in_=pt[:].rearrange("p (k t) -> p k t", k=8)[:, :, :T]), r=[ptk], w=[("ogT", half)])
        psk, pkk = self.proj_block(hT, T, 2048, 2560)
        kvf = self.sb("on0", [128, 512])
        P.dve(lambda e: e.tensor_copy(out=kvf[:T, :], in_=psk[:T, :]), r=[pkk], w=["on0"])
        self.swa_kv_prep(kvf, "on0", T, cur)
        if kv_out is not None:
            kv_out(kvf, "on0")
        sg = self.sb("sr", [128, DI], BF16)
        for b in range(4):
            ps, pk = self.proj_block(hT, T, 2560 + b * 512, 3072 + b * 512)
            P.act(lambda e, b=b, ps=ps: e.activation(out=sg[:T, b * 512:(b + 1) * 512], in_=ps[:T, :], func=AF.Silu), r=[pk], w=[("sr", b)])
        og = self.sb("og", [128, DI], BF16)
        PA = self.sb("qT", [128, 4, 128], BF16)
        PB = self.sb("kT", [128, 4, 128], BF16)
        PAm = self.sb("qTm", [128, 4, 128], BF16)
        st = self.sb("swstat", [128, 8])
        for par in range(2):
            pbase = par * 64
            if seqs_cache is None:
                combos = [[kvh] for kvh in range(4)]
            else:
                combos = [[0, 1, 2, 3]]
            for cb in combos:
                pv = {}
                for kvh in cb:
                    pv[kvh] = self.psum(hold=True)
                    P.pe(lambda e, pv=pv, kvh=kvh: e.matmul(pv[kvh][0][:T, 0:260], lhsT=self.zl[:T, :T], rhs=vaug[cur][:T, :, :].rearrange("p k c -> p (k c)"),
                                                    start=True, stop=False), r=["zeros_b", ("ytail", cur)], w=[pv[kvh][1]])
                for kvh in cb:
                    qrhs = qT[pbase:pbase + 64, kvh * 4:kvh * 4 + 4, :T]
                    psb, pkb = self.psum()
                    klhs = kT2[cur][pbase:pbase + 64, kvh, :T]
                    P.pe(lambda e, kvh=kvh, psb=psb, qrhs=qrhs, klhs=klhs: e.matmul(psb[:T, 0:4 * T].rearrange("p (a t) -> p a t", a=4),
                                                                        lhsT=klhs, rhs=qrhs, start=True, stop=True),
                         r=[("cwblk", cur), "ogT"], w=[pkb])
                    P.act(lambda e, psb=psb: e.activation(out=PB[:T, :, :T], in_=psb[:T, 0:4 * T].rearrange("p (a t) -> p a t", a=4), func=AF.Exp),
                          r=[pkb], w=["kT"])
                    P.dve(lambda e: e.tensor_tensor(out=PB[:T, :, :T], in0=PB[:T, :, :T], in1=bc_mid(M["LE"][:T, :T], 4), op=ALU.mult),
                          r=["kT", f"m_{kind}_LE"], w=["kT"])
                    for i in range(4):
                        P.pe(lambda e, pv=pv, kvh=kvh, i=i: e.matmul(pv[kvh][0][:T, i * 65:(i + 1) * 65], lhsT=PB[:T, i, :T], rhs=vaug[cur][:T, kvh, :],
                                                             start=False, stop=False), r=["kT", ("ytail", cur)], w=[pv[kvh][1]])
                if seqs_cache is None:
                    kvh = cb[0]
                    qrhs = qT[pbase:pbase + 64, kvh * 4:kvh * 4 + 4, :T]
                    psa, pka = self.psum()
                    klhs = kT2[prev][pbase:pbase + 64, kvh, :]
                    P.pe(lambda e, kvh=kvh, psa=psa, qrhs=qrhs, klhs=klhs: e.matmul(psa[:, 0:4 * T].rearrange("p (a t) -> p a t", a=4),
                                                                        lhsT=klhs, rhs=qrhs, start=True, stop=True),
                         r=[("cwblk", prev), "ogT"], w=[pka])
                    P.act(lambda e, psa=psa: e.activation(out=PA[:, :, :T], in_=psa[:, 0:4 * T].rearrange("p (a t) -> p a t", a=4), func=AF.Exp),
                          r=[pka], w=["qT"])
                    P.dve(lambda e: e.tensor_tensor(out=PA[:, :, :T], in0=PA[:, :, :T], in1=bc_mid(maskA[:, :T], 4), op=ALU.mult),
                          r=["qT", maskAkey], w=["qT"])
                    for i in range(4):
                        P.pe(lambda e, pv=pv, kvh=kvh, i=i: e.matmul(pv[kvh][0][:T, i * 65:(i + 1) * 65], lhsT=PA[:, i, :T], rhs=vaug[prev][:, kvh, :],
                                                             start=False, stop=False), r=["qT", ("ytail", prev)], w=[pv[kvh][1]])
                else:
                    for si, sq in enumerate(seqs_cache):
                        sq["load"]()
                        for kvh in cb:
                            qrhs = qT[pbase:pbase + 64, kvh * 4:kvh * 4 + 4, 8 * si:8 * si + 8]
                            psa, pka = self.psum()
                            klhs = kT2[2][pbase:pbase + 64, kvh, :]
                            P.pe(lambda e, kvh=kvh, psa=psa, qrhs=qrhs, klhs=klhs: e.matmul(psa[:, 0:32].rearrange("p (a t) -> p a t", a=4),
                                                                                lhsT=klhs, rhs=qrhs, start=True, stop=True),
                                 r=[("cwblk", 2), "ogT"], w=[pka])
                            P.act(lambda e, psa=psa: e.activation(out=PA[:, :, 0:8], in_=psa[:, 0:32].rearrange("p (a t) -> p a t", a=4), func=AF.Exp),
                                  r=[pka], w=["qT"])
                            first_use = (si == 0 and kvh == cb[0])
                            if first_use:
                                P.dve(lambda e: e.memset(PAm[:, :, :T], 0.0), w=["qTm"])
                            elif si > 0 and kvh == cb[0]:
                                P.dve(lambda e, si=si: e.memset(PAm[:, :, 8 * (si - 1):8 * si], 0.0), w=["qTm"])
                            P.dve(lambda e, si=si: e.tensor_tensor(out=PAm[:, :, 8 * si:8 * si + 8], in0=PA[:, :, 0:8],
                                                                   in1=bc_mid(self.mas[:, 8 * si:8 * si + 8], 4), op=ALU.mult),
                                  r=["qT", "m_s_maskA"], w=["qTm"])
                            for i in range(4):
                                P.pe(lambda e, pv=pv, kvh=kvh, i=i: e.matmul(pv[kvh][0][:T, i * 65:(i + 1) * 65], lhsT=PAm[:, i, :T], rhs=vaug[2][:, kvh, :],
                                                                     start=False, stop=False), r=["qTm", ("ytail", 2)], w=[pv[kvh][1]])
                for kvh in cb:
                    P.pe(lambda e, pv=pv, kvh=kvh: e.matmul(pv[kvh][0][:T, 0:260], lhsT=self.zl[:T, :T], rhs=vaug[cur][:T, :, :].rearrange("p k c -> p (k c)"),
                                                    start=False, stop=True), r=["zeros_b", ("ytail", cur)], w=[pv[kvh][1]])
                    pvv = pv[kvh][0][:T, 0:260].rearrange("p (a c) -> p a c", a=4)
                    h0 = kvh * 8 + par
                    es = self.esink[:T, h0:h0 + 7:2]
                    P.dve(lambda e, pvv=pvv, es=es: e.tensor_tensor(out=st[:T, 0:4], in0=pvv[:, :, 64], in1=es, op=ALU.add),
                          r=[pv[kvh][1], "esink"], w=[("swstat", 0)])
                    P.dve(lambda e: e.reciprocal(out=st[:T, 4:8], in_=st[:T, 0:4]), r=[("swstat", 0)], w=[("swstat", 4)])
                    on = self.sb("on1", [128, 512])
                    onv = on[:T, 0:256].rearrange("p (a c) -> p a c", a=4)
                    P.dve(lambda e, pvv=pvv, onv=onv: e.tensor_tensor(out=onv, in0=pvv[:, :, 0:64], in1=bc_last(st[:T, 4:8], 64), op=ALU.mult),
                          r=[pv[kvh][1], ("swstat", 4)], w=["on1"])
                    self.psum_release(pv[kvh][1])
                    c0 = (kvh * 8 + par) * 64
                    ogv = AP(og[:T, c0:c0 + 64].tensor, og[:T, c0:c0 + 64].offset, [list(og[:T, c0:c0 + 64].ap[0]), [128, 4], [1, 64]])
                    sgv = AP(sg[:T, c0:c0 + 64].tensor, sg[:T, c0:c0 + 64].offset, [list(sg[:T, c0:c0 + 64].ap[0]), [128, 4], [1, 64]])
                    P.dve(lambda e, ogv=ogv, sgv=sgv, onv=onv: e.tensor_tensor(out=ogv, in0=onv, in1=sgv, op=ALU.mult),
                          r=["on1", "sr"], w=["og"])
        self.out_proj_residual(og, "og", xt, xkey, T)

    def run_layer_swa(self, l, first, lastl):
        P, d, cfg = self.P, self.d, self.cfg
        NT, NS, TS = cfg.NT, cfg.NS, cfg.TS
        self.swa_setup()
        kT2, vaug = self.swa_views()
        cw = self.sb("cwblk", [128, 5, 512], BF16)
        yt = self.sb("ytail", [128, 3, 512], BF16)
        P.dve(lambda e: e.memset(cw[:], 0.0), w=["cwblk"])
        P.dve(lambda e: e.memset(yt[:], 0.0), w=["ytail"])
        if cfg.G > 1:
            xt = self.load_x(first, NT - 1, 0)
            hT = self.norm_transpose(xt, "xt0", 128)
            psk, pkk = self.proj_block(hT, 128, 2048, 2560)
            kvf = self.sb("on0", [128, 512])
            P.dve(lambda e: e.tensor_copy(out=kvf[:, :], in_=psk[:, :]), r=[pkk], w=["on0"])
            xout, xk = self.allgather("kv", 128, 512, [(0, 512, kvf[:, :], ["on0"])], cls=2)
            P.dve(lambda e: e.memset(kvf[:, :], 0.0), r=[("xin_kv", 0)], w=["on0"])
            cand = self.sb("on1", [128, 512])
            for j in range(cfg.G):
                P.dma("sp", cand[:, :], xout[j * 128:(j + 1) * 128, :], r=[xk], w=["on1"], semkey="xkv")
                P.dve(lambda e, j=j: e.scalar_tensor_tensor(out=kvf[:, :], in0=cand[:, :], scalar=self.oh[:, j:j + 1], in1=kvf[:, :],
                                                            op0=ALU.mult, op1=ALU.add), r=["on1", "on0", "oh"], w=["on0"])
            self.swa_kv_prep(kvf, "on0", 128, 1)
        ntiles = NT + 1
        for ti, xt, xkey in self.tiles_iter(first, list(range(ntiles))):
            T = 128 if ti < NT else TS
            if ti < NT:
                cur, prev = ti % 2, 1 - ti % 2
                if ti == 0:
                    maskA, mk = self.ma0, "m_p_maskA0"
                else:
                    maskA, mk = self.M["p"]["GT"], "m_p_GT"
                kv_out = None
                if ti == NT - 1:
                    def kv_out(kvf, key):
                        P.dma("pool", d["k2_p"][:, :], kvf[:, 0:256], r=[key], semkey="k2p", final=True)
                        P.dma("pool", d["v2_p"][:, :], kvf[:, 256:512], r=[key], semkey="v2p", final=True)
                self.swa_tile(ti, xt, xkey, T, "p", cur, prev, maskA, mk, None, kv_out)
            else:
                seqs = []
                for j in range(NS):
                    def load(j=j):
                        cst = self.sb("on1", [128, 512])
                        P.dma("sp", cst[:, 0:256], d["kc2"][j], w=["on1"], semkey="kc2l")
                        P.dma("sp", cst[:, 256:512], d["vc2"][j], w=["on1"], semkey="vc2l")
                        self.swa_kv_prep(cst, "on1", 128, 2)
                    seqs.append(dict(j=j, load=load))
                def kv_out(kvf, key):
                    P.dma("pool", d["kvs"][:, :], kvf[:TS, :], r=[key], w=["kvs"], semkey="kvs")
                    kvv = d["kvs"].rearrange("(s t) c -> s t c", t=8)
                    P.dma("pool", d["k2_s"][:, 120:128, :], kvv[:, :, 0:256], r=["kvs"], semkey="fin2", final=True)
                    P.dma("pool", d["v2_s"][:, 120:128, :], kvv[:, :, 256:512], r=["kvs"], semkey="fin2", final=True)
                    P.dma("pool", d["k2_s"][:, 0:120, :], d["kc2"][:, 8:128, :], semkey="fin2", final=True)
                    P.dma("pool", d["v2_s"][:, 0:120, :], d["vc2"][:, 8:128, :], semkey="fin2", final=True)
                self.swa_tile(ti, xt, xkey, T, "s", 0, 1, None, None, seqs, kv_out)
            self.store_x(xt, xkey, ti, T, lastl)

    def build(self):
        cfg, P, d = self.cfg, self.P, self.d
        self.declare()
        self.setup_consts()
        self.epsc = self.sb("epsc", [128, 1])
        P.dve(lambda e: e.memset(self.epsc[:], EPS), w=["epsc"])
        self.onec = self.sb("onec", [128, 1])
        P.dve(lambda e: e.memset(self.onec[:], 1.0), w=["onec"])
        if cfg.G > 1:
            self.rank_consts()
        layers = cfg.layers
        for li, l in enumerate(layers):
            first = li == 0
            lastl = li == len(layers) - 1
            self.load_layer_weights(l)
            kind = LAYER_KIND[l]
            if kind == "gla":
                self.run_layer_gla(l, first, lastl)
            elif kind == "ssd":
                self.run_layer_ssd(l, first, lastl)
            elif kind == "swa":
                self.run_layer_swa(l, first, lastl)
            else:
                raise NotImplementedError(kind)
        P.emit(self.stack)
        return self.nc


_PROG_CACHE = {}


def _run(inputs, layers=(0, 1, 2, 3)):
    xp = np.asarray(inputs["x_prompt"], dtype=np.float32)
    xs = np.asarray(inputs["x_sample"], dtype=np.float32)
    B, SEQ, _ = xp.shape
    DB = xs.shape[0]
    G = NCORES // B if (NCORES % B == 0 and SEQ % ((NCORES // B) * 128) == 0) else 1
    if FORCE_G is not None:
        G = FORCE_G
    cfg = Cfg(B, SEQ, DB, layers, G=G)
    G, NT, NS, TS = cfg.G, cfg.NT, cfg.NS, cfg.TS
    key = (B, SEQ, DB, tuple(layers))
    if key not in _PROG_CACHE:
        bld = Builder(cfg)
        nc = bld.build()
        _PROG_CACHE[key] = (bld, nc)
    bld, nc = _PROG_CACHE[key]
    mp = make_masks(128, 128)
    ms = make_masks(TS, 8)
    colmask = np.ascontiguousarray(ms["seg"].T)
    shared = {}
    for k, v in inputs.items():
        if k.startswith("l") or k == "final_norm":
            shared[k] = np.ascontiguousarray(np.asarray(v, dtype=np.float32))
    shared["c_ident"] = np.eye(128, dtype=np.float32)
    shared["c_p_LE"], shared["c_p_GT"] = mp["LE"], mp["GT"]
    shared["c_s_LE"], shared["c_s_GT"] = ms["LE"], ms["GT"]
    shared["c_s_seg"] = ms["seg"]
    shared["c_p_Sh"], shared["c_s_Sh"] = mp["Sh"], ms["Sh"]
    shared["c_s_maskA"] = (np.arange(128)[:, None] > (np.arange(TS)[None, :] % 8)).astype(np.float32)
    shared["c_p_ShP"], shared["c_s_ShP"] = mp["ShP"], ms["ShP"]
    in_maps = []
    NP = NT * 128
    for c in range(NCORES):
        b, g = (c // G, c % G) if c < B * G else (0, 0)
        m = dict(shared)
        m["xp"] = np.ascontiguousarray(xp[b, g * NP:(g + 1) * NP, :])
        sl = slice(c * NS, (c + 1) * NS)
        m["xsamp"] = np.ascontiguousarray(xs[sl].reshape(TS, D))
        m["sg0"] = np.ascontiguousarray(inputs["state_gla_0"][sl])
        m["ssm1"] = np.ascontiguousarray(inputs["state_ssm_1"][sl])
        m["conv1"] = np.ascontiguousarray(inputs["state_conv_1"][sl])
        m["kc2"] = np.ascontiguousarray(np.asarray(inputs["cache_swa_k_2"][sl]).reshape(NS, 128, 256))
        m["vc2"] = np.ascontiguousarray(np.asarray(inputs["cache_swa_v_2"][sl]).reshape(NS, 128, 256))
        m["sg3"] = np.ascontiguousarray(inputs["state_gla_3"][sl])
        pm = np.zeros((1, 2 * G), np.float32)
        for j in range(G):
            pm[0, j] = 1.0 if j < g else 0.0
            pm[0, G + j] = 1.0 - pm[0, j]
        m["c_pm"] = pm
        ohv = np.zeros((1, G), np.float32)
        if g > 0:
            ohv[0, g - 1] = 1.0
        m["c_oh"] = ohv
        selv = np.zeros((G * 3, 128), np.float32)
        if g > 0:
            for r in range(3):
                selv[(g - 1) * 3 + r, 125 + r] = 1.0
        m["c_sel"] = selv
        m["c_p_maskA0"] = (mp["GT"] * (1.0 if g > 0 else 0.0)).astype(np.float32)
        in_maps.append(m)
    res = run_bass_kernel_spmd(nc, in_maps, core_ids=list(range(NCORES)))
    R = res.results
    last = [b * G + G - 1 for b in range(B)]
    y_prompt = np.stack([np.concatenate([R[b * G + g]["yp"] for g in range(G)], axis=0) for b in range(B)])
    y_sample = np.concatenate([R[c]["ysamp"].reshape(NS, 8, D) for c in range(NCORES)], axis=0)

    def pst(name, shape):
        return np.stack([R[c][name].reshape(shape) for c in last])

    def sst(name, shape):
        return np.concatenate([R[c][name].reshape((NS,) + shape) for c in range(NCORES)], axis=0)

    outs = (y_prompt, y_sample,
            pst("gla0_p", (4, 128, 512)), sst("gla0_s", (4, 128, 512)),
            pst("ssm1_p", (32, 64, 128)), sst("ssm1_s", (32, 64, 128)),
            pst("conv1_p", (3, 3072)), sst("conv1_s", (3, 3072)),
            pst("k2_p", (128, 4, 64)), sst("k2_s", (128, 4, 64)),
            pst("v2_p", (128, 4, 64)), sst("v2_s", (128, 4, 64)),
            pst("gla3_p", (4, 128, 512)), sst("gla3_s", (4, 128, 512)))
    return tuple(np.ascontiguousarray(o, dtype=np.float32) for o in outs)


def kernel(**inputs):
    return _run(inputs)
```
